# Optimizing a Trainium2 kernel written in Bass

```python
import jax, jax.numpy as jnp
from jax import lax
import numpy as np

D_MODEL = 1024
BATCH = 2
SEQ = 8192
DEPTH = 2

CHUNK = 64
N_A_LAYERS = max(1, DEPTH // 2)
N_B_LAYERS = DEPTH - N_A_LAYERS
A_HEADS = 16
A_HEAD_DIM = D_MODEL // A_HEADS
A_LEFT_CHUNKS = 8
A_BAND = (A_LEFT_CHUNKS + 1) * CHUNK
A_MAX_REL = 2 * CHUNK
B_HEADS = 16
B_NOPE_DIM = 64
B_ROPE_DIM = 32
B_V_DIM = 64
B_Q_LORA = 384
B_KV_LORA = 256
ROPE_THETA = 10000.0
Q_BLOCK = 128
FFN_DIM = 2816
CONV_WIDTH = 3
NORM_EPS = 1e-6
NEG_INF = -1e30

kernel_name = "yoco_chunkrel_mla_convffn_adaln"


def rms_norm(x, g):
    xf = x.astype(jnp.float32)
    y = xf * lax.rsqrt(jnp.mean(xf * xf, axis=-1, keepdims=True) + NORM_EPS)
    return (y * g.astype(jnp.float32)).astype(x.dtype)


def modulate(x, shift, scale):
    return x * (1.0 + scale[:, None, :]) + shift[:, None, :]


def rope_tables(positions):
    half = B_ROPE_DIM // 2
    inv_freq = jnp.power(jnp.float32(ROPE_THETA),
                         -jnp.arange(half, dtype=jnp.float32) * (2.0 / B_ROPE_DIM))
    ang = positions.astype(jnp.float32)[..., None] * inv_freq
    return jnp.cos(ang), jnp.sin(ang)


def apply_rope(t, cos, sin):
    half = t.shape[-1] // 2
    tf = t.astype(jnp.float32)
    t1, t2 = tf[..., :half], tf[..., half:]
    return jnp.concatenate([t1 * cos - t2 * sin, t2 * cos + t1 * sin], axis=-1).astype(t.dtype)


def chunk_rel_attention(hn, wqkv, wo, rel_bias):
    B, S, _ = hn.shape
    nc = S // CHUNK
    qkv = (hn @ wqkv).reshape(B, S, 3, A_HEADS, A_HEAD_DIM)
    q, k, v = qkv[:, :, 0], qkv[:, :, 1], qkv[:, :, 2]
    pad = A_LEFT_CHUNKS * CHUNK
    kp = jnp.pad(k, ((0, 0), (pad, 0), (0, 0), (0, 0)))
    vp = jnp.pad(v, ((0, 0), (pad, 0), (0, 0), (0, 0)))
    qi = jnp.arange(CHUNK)[:, None]
    kj = jnp.arange(A_BAND)[None, :]
    rel = jnp.clip(qi + pad - kj, -A_MAX_REL, A_MAX_REL) + A_MAX_REL
    bias = rel_bias.astype(jnp.float32)[:, rel]
    scale = A_HEAD_DIM ** -0.5
    band_idx = jnp.arange(A_BAND)

    def one_chunk(n):
        start = n * CHUNK
        qc = lax.dynamic_slice_in_dim(q, start, CHUNK, axis=1)
        kc = lax.dynamic_slice_in_dim(kp, start, A_BAND, axis=1)
        vc = lax.dynamic_slice_in_dim(vp, start, A_BAND, axis=1)
        s = jnp.einsum('bqhd,bkhd->bhqk', qc, kc).astype(jnp.float32) * scale + bias[None]
        valid = (start - pad + band_idx) >= 0
        s = jnp.where(valid[None, None, None, :], s, NEG_INF)
        p = jax.nn.softmax(s, axis=-1).astype(vc.dtype)
        return jnp.einsum('bhqk,bkhd->bqhd', p, vc)

    o = lax.map(one_chunk, jnp.arange(nc))
    o = jnp.moveaxis(o, 0, 1).reshape(B, S, A_HEADS * A_HEAD_DIM)
    return o @ wo


def shared_kv(h, c_act, kv_mod_w, kv_mod_b, kv_norm_g, wdkv, kv_lat_norm_g, wuk, wuv, wkr, cos, sin):
    B, S, _ = h.shape
    shift, scale = jnp.split(c_act @ kv_mod_w + kv_mod_b, 2, axis=-1)
    hn = modulate(rms_norm(h, kv_norm_g), shift, scale)
    ckv = rms_norm(hn @ wdkv, kv_lat_norm_g)
    k_nope = (ckv @ wuk).reshape(B, S, B_HEADS, B_NOPE_DIM)
    v = (ckv @ wuv).reshape(B, S, B_HEADS, B_V_DIM)
    k_rope = apply_rope(hn @ wkr, cos, sin)
    return k_nope, k_rope, v


def mla_attention(hn, wdq, q_norm_g, wuq, wqr, wo, k_nope, k_rope, v, cos, sin):
    B, S, _ = hn.shape
    cq = rms_norm(hn @ wdq, q_norm_g)
    q_nope = (cq @ wuq).reshape(B, S, B_HEADS, B_NOPE_DIM)
    q_rope = apply_rope((cq @ wqr).reshape(B, S, B_HEADS, B_ROPE_DIM),
                        cos[:, :, None, :], sin[:, :, None, :])
    key_chunk = jnp.arange(S) // CHUNK
    scale = (B_NOPE_DIM + B_ROPE_DIM) ** -0.5

    def one_block(i):
        start = i * Q_BLOCK
        qn = lax.dynamic_slice_in_dim(q_nope, start, Q_BLOCK, axis=1)
        qr = lax.dynamic_slice_in_dim(q_rope, start, Q_BLOCK, axis=1)
        s = (jnp.einsum('bqhd,bkhd->bhqk', qn, k_nope).astype(jnp.float32)
             + jnp.einsum('bqhd,bkd->bhqk', qr, k_rope).astype(jnp.float32)) * scale
        q_chunk = (start + jnp.arange(Q_BLOCK)) // CHUNK
        mask = key_chunk[None, :] <= q_chunk[:, None]
        s = jnp.where(mask[None, None], s, NEG_INF)
        p = jax.nn.softmax(s, axis=-1).astype(v.dtype)
        return jnp.einsum('bhqk,bkhd->bqhd', p, v)

    o = lax.map(one_block, jnp.arange(S // Q_BLOCK))
    o = jnp.moveaxis(o, 0, 1).reshape(B, S, B_HEADS * B_V_DIM)
    return o @ wo


def conv_ffn(hn, win, conv_w, conv_b, wout):
    S = hn.shape[1]
    u = hn @ win
    up = jnp.pad(u, ((0, 0), (CONV_WIDTH - 1, 0), (0, 0)))
    y = conv_b
    for tap in range(CONV_WIDTH):
        y = y + up[:, tap:tap + S] * conv_w[tap]
    gate, val = jnp.split(y, 2, axis=-1)
    return (jax.nn.silu(gate) * val) @ wout


def setup_inputs(seed: int = 0) -> dict:
    key = jax.random.key(seed)
    ks = iter(jax.random.split(key, 40))
    D = D_MODEL

    def nrm(shape, scale):
        return jax.random.normal(next(ks), shape, jnp.float32) * scale

    def gain(shape):
        return 1.0 + nrm(shape, 0.02)

    x = nrm((BATCH, SEQ, D), 1.0)
    c = nrm((BATCH, D), 1.0)
    positions = (jnp.arange(SEQ, dtype=jnp.int32)[None, :]
                 + jax.random.randint(next(ks), (BATCH, 1), 0, 4096, dtype=jnp.int32))
    return {
        "x": x,
        "c": c,
        "positions": positions,
        "mod_w": nrm((DEPTH, D, 6 * D), 0.5 * D ** -0.5),
        "mod_b": nrm((DEPTH, 6 * D), 0.02),
        "norm1_g": gain((DEPTH, D)),
        "norm2_g": gain((DEPTH, D)),
        "a_wqkv": nrm((N_A_LAYERS, D, 3 * A_HEADS * A_HEAD_DIM), D ** -0.5),
        "a_wo": nrm((N_A_LAYERS, A_HEADS * A_HEAD_DIM, D), (A_HEADS * A_HEAD_DIM) ** -0.5),
        "a_rel_bias": nrm((N_A_LAYERS, A_HEADS, 2 * A_MAX_REL + 1), 0.5),
        "kv_mod_w": nrm((D, 2 * D), 0.5 * D ** -0.5),
        "kv_mod_b": nrm((2 * D,), 0.02),
        "kv_norm_g": gain((D,)),
        "b_wdkv": nrm((D, B_KV_LORA), D ** -0.5),
        "b_kv_lat_norm_g": gain((B_KV_LORA,)),
        "b_wuk": nrm((B_KV_LORA, B_HEADS * B_NOPE_DIM), B_KV_LORA ** -0.5),
        "b_wuv": nrm((B_KV_LORA, B_HEADS * B_V_DIM), B_KV_LORA ** -0.5),
        "b_wkr": nrm((D, B_ROPE_DIM), D ** -0.5),
        "b_wdq": nrm((N_B_LAYERS, D, B_Q_LORA), D ** -0.5),
        "b_q_norm_g": gain((N_B_LAYERS, B_Q_LORA)),
        "b_wuq": nrm((N_B_LAYERS, B_Q_LORA, B_HEADS * B_NOPE_DIM), B_Q_LORA ** -0.5),
        "b_wqr": nrm((N_B_LAYERS, B_Q_LORA, B_HEADS * B_ROPE_DIM), B_Q_LORA ** -0.5),
        "b_wo": nrm((N_B_LAYERS, B_HEADS * B_V_DIM, D), (B_HEADS * B_V_DIM) ** -0.5),
        "f_win": nrm((DEPTH, D, 2 * FFN_DIM), D ** -0.5),
        "f_conv_w": nrm((DEPTH, CONV_WIDTH, 2 * FFN_DIM), CONV_WIDTH ** -0.5),
        "f_conv_b": nrm((DEPTH, 2 * FFN_DIM), 0.02),
        "f_wout": nrm((DEPTH, FFN_DIM, D), FFN_DIM ** -0.5),
        "final_g": gain((D,)),
    }


def reference(x, c, positions, mod_w, mod_b, norm1_g, norm2_g, a_wqkv, a_wo, a_rel_bias,
              kv_mod_w, kv_mod_b, kv_norm_g, b_wdkv, b_kv_lat_norm_g, b_wuk, b_wuv, b_wkr,
              b_wdq, b_q_norm_g, b_wuq, b_wqr, b_wo, f_win, f_conv_w, f_conv_b, f_wout, final_g):
    c_act = jax.nn.silu(c)
    cos, sin = rope_tables(positions)
    h = x
    kv = None
    for l in range(DEPTH):
        mod = c_act @ mod_w[l] + mod_b[l]
        sh1, sc1, g1, sh2, sc2, g2 = jnp.split(mod, 6, axis=-1)
        hn = modulate(rms_norm(h, norm1_g[l]), sh1, sc1)
        if l < N_A_LAYERS:
            mix = chunk_rel_attention(hn, a_wqkv[l], a_wo[l], a_rel_bias[l])
        else:
            j = l - N_A_LAYERS
            mix = mla_attention(hn, b_wdq[j], b_q_norm_g[j], b_wuq[j], b_wqr[j], b_wo[j],
                                kv[0], kv[1], kv[2], cos, sin)
        h = h + g1[:, None, :] * mix
        hn = modulate(rms_norm(h, norm2_g[l]), sh2, sc2)
        h = h + g2[:, None, :] * conv_ffn(hn, f_win[l], f_conv_w[l], f_conv_b[l], f_wout[l])
        if l == N_A_LAYERS - 1:
            kv = shared_kv(h, c_act, kv_mod_w, kv_mod_b, kv_norm_g, b_wdkv, b_kv_lat_norm_g,
                           b_wuk, b_wuv, b_wkr, cos, sin)
    return rms_norm(h, final_g)
```

```python
import os
import numpy as np
import ml_dtypes
from contextlib import ExitStack
import concourse.bass as bass
import concourse.mybir as mybir
from concourse.bass_utils import run_bass_kernel_spmd

F32 = mybir.dt.float32
BF16 = mybir.dt.bfloat16
I32 = mybir.dt.int32
AF = mybir.ActivationFunctionType
ALU = mybir.AluOpType
AX = mybir.AxisListType

NCORE = 8
D = 1024
S = 8192
OWN = 2048
NOWN = 16
NEXT = 17
NALL = 21
FF = 2816
NF = 22
NEG = -30000.0
ARENA_BYTES = 200 * 1024
TWO_PI = 6.283185307179586
PI = 3.141592653589793


class Tok:
    __slots__ = ("sem", "val", "key")

    def __init__(self, sem, val, key):
        self.sem, self.val, self.key = sem, val, key


class EQ:
    def __init__(self, kb, eng, name):
        self.kb, self.e, self.name = kb, eng, name
        self.sem = kb.newsem("q_" + name)
        self.cnt = 0
        self.seen = {}

    def wait(self, *toks):
        for t in toks:
            if t is None:
                continue
            if self.name == "pe" and t.key == "pe":
                continue
            if self.seen.get(t.key, 0) >= t.val:
                continue
            self.e.wait_ge(t.sem, t.val)
            self.seen[t.key] = t.val

    def done(self, ins):
        ins.then_inc(self.sem, 1)
        self.cnt += 1
        return Tok(self.sem, self.cnt, self.name)


class DSem:
    def __init__(self, kb, name):
        self.sem = kb.newsem("d_" + name)
        self.cnt = 0
        self.name = "d_" + name
        self.last = None
        kb.dsems.append(self)

    def add(self, ins):
        ins.then_inc(self.sem, 16)
        self.cnt += 16
        return Tok(self.sem, self.cnt, self.name)


class Buf:
    __slots__ = ("w", "r")

    def __init__(self):
        self.w = None
        self.r = {}


def _use(eq, reads, writes):
    for b in reads:
        eq.wait(b.w)
    for b in writes:
        eq.wait(b.w)
        eq.wait(*b.r.values())


def _fin(tok, reads, writes):
    for b in reads:
        b.r[tok.key] = tok
    for b in writes:
        b.w = tok
        b.r = {}


class KB:
    def __init__(self, nc):
        self.nc = nc
        self.es = ExitStack()
        self.nsem = 0
        self.dsems = []
        self.pe = EQ(self, nc.tensor, "pe")
        self.act = EQ(self, nc.scalar, "act")
        self.dve = EQ(self, nc.vector, "dve")
        self.pool = EQ(self, nc.gpsimd, "pool")
        self.sp = EQ(self, nc.sync, "sp")
        self.uid = 0
        self.arena = None
        self.peak = 0
        self.st_sem = DSem(self, "store")
        self.st_last = None

    def newsem(self, name):
        self.nsem += 1
        return self.es.enter_context(self.nc.semaphore(name))

    def sb(self, es, shape, dt, name=None):
        if self.arena is None:
            self.arena = self.es.enter_context(self.nc.sbuf_tensor("arena", [128, ARENA_BYTES // 2], BF16))
            self.free = [(0, ARENA_BYTES)]
        esz = 2 if dt == BF16 else 4
        n = 1
        for x in shape[1:]:
            n *= x
        nbytes = (n * esz + 63) // 64 * 64
        top = es is self.es
        order = range(len(self.free) - 1, -1, -1) if top else range(len(self.free))
        for i in order:
            o, sz = self.free[i]
            if sz >= nbytes:
                off = o + sz - nbytes if top else o
                if sz == nbytes:
                    self.free.pop(i)
                elif top:
                    self.free[i] = (o, sz - nbytes)
                else:
                    self.free[i] = (o + nbytes, sz - nbytes)
                break
        else:
            raise RuntimeError(f"SBUF arena full allocating {name} {shape} ({nbytes}B); free={self.free}")
        self.peak = max(self.peak, ARENA_BYTES - sum(z for _, z in self.free))

        def release(off=off, nbytes=nbytes):
            self.free.append((off, nbytes))
            self.free.sort()
            merged = []
            for o, z in self.free:
                if merged and merged[-1][0] + merged[-1][1] == o:
                    merged[-1] = (merged[-1][0], merged[-1][1] + z)
                else:
                    merged.append((o, z))
            self.free = merged
        es.callback(release)
        ap = self.arena[0:shape[0], off // 2:(off + n * esz) // 2]
        if dt != BF16:
            ap = ap.bitcast(dt)
        if len(shape) == 3:
            ap = ap.rearrange("p (a b) -> p a b", a=shape[1])
        elif len(shape) == 4:
            ap = ap.rearrange("p (a b c) -> p a b c", a=shape[1], b=shape[2])
        return ap

    def ps(self, es, shape, dt, name=None):
        self.uid += 1
        return es.enter_context(self.nc.psum_tensor(f"{name or 'p'}_{self.uid}", list(shape), dt))

    def op(self, eq, fn, reads=(), writes=(), **kw):
        _use(eq, reads, writes)
        tok = eq.done(fn(**kw))
        _fin(tok, reads, writes)
        return tok

    def mm(self, emit, reads=(), writes=()):
        _use(self.pe, reads, writes)
        ins = emit()
        tok = self.pe.done(ins)
        _fin(tok, reads, writes)
        return tok

    def load(self, q, dsem, out, in_, writes=(), reads=(), chain=True):
        _use(q, reads, writes)
        if chain:
            q.wait(dsem.last)
        tok = dsem.add(q.e.dma_start(out=out, in_=in_))
        dsem.last = tok
        _fin(tok, reads, writes)
        return tok

    def store(self, q, out, in_, reads=()):
        _use(q, reads, ())
        tok = self.st_sem.add(q.e.dma_start(out=out, in_=in_))
        _fin(tok, reads, ())
        self.st_last = tok
        return tok

    def barrier(self):
        qs = (self.pe, self.act, self.dve, self.pool, self.sp)
        toks = [Tok(q.sem, q.cnt, q.name) for q in qs if q.cnt]
        toks += [Tok(d.sem, d.cnt, d.name) for d in self.dsems if d.cnt]
        for q in qs:
            q.wait(*toks)

    def finish(self):
        if self.st_last is not None:
            self.sp.wait(self.st_last)
        for q in (self.pe, self.act, self.dve, self.pool):
            if q.cnt:
                self.sp.wait(Tok(q.sem, q.cnt, q.name))
        self.es.close()


def fm(v):
    v = np.asarray(v)
    return np.ascontiguousarray(v.reshape(-1, 128).T)


class Consts:
    pass


def emit_consts(kb, es, dram):
    nc = kb.nc
    c = Consts()
    c.dsem = DSem(kb, "const")
    c.idf = kb.sb(es, [128, 128], F32, "idf")
    c.idb = kb.sb(es, [128, 128], BF16, "idb")
    c.ones = kb.sb(es, [128, 128], F32, "ones")
    c.b_idf, c.b_idb, c.b_ones = Buf(), Buf(), Buf()
    kb.load(kb.sp, DSem(kb, "ident"), c.idf[:], dram["ident"], writes=[c.b_idf])
    kb.op(kb.dve, nc.vector.tensor_copy, reads=[c.b_idf], writes=[c.b_idb], out=c.idb[:], in_=c.idf[:])
    kb.op(kb.dve, nc.vector.memset, writes=[c.b_ones], ap=c.ones[:], constant=1.0)
    return c


def emit_mod(kb, es, c, banks, cvec_d, mw_d, mb_d, ncols, want_fm, want_bc, tag):
    nc = kb.nc
    ds = DSem(kb, "modc" + tag)
    ds2 = DSem(kb, "modb" + tag)
    out_fm, out_bc = {}, {}
    nblk = ncols // 512
    with ExitStack() as les:
        cv = kb.sb(les, [128, 8], F32, "cv")
        b_cv = Buf()
        kb.load(kb.sp, ds, cv[:], cvec_d, writes=[b_cv])
        kb.op(kb.act, nc.scalar.activation, reads=[b_cv], writes=[b_cv], out=cv[:], in_=cv[:], func=AF.Silu)
        crep = kb.sb(les, [128, 8, 128], F32, "crep")
        b_crep = Buf()
        for k in range(8):
            kb.op(kb.dve, nc.vector.tensor_scalar, reads=[b_cv, c.b_ones], writes=[b_crep],
                  out=crep[:, k, :], in0=c.ones[:], scalar1=cv[:, k:k + 1], scalar2=None, op0=ALU.mult)
        wbuf = [kb.sb(les, [128, 8, 512], F32, "mwb") for _ in range(2)]
        wsem = [DSem(kb, f"mw{tag}{i}") for i in range(2)]
        b_w = [Buf(), Buf()]
        mbb = kb.sb(les, [128, 1024], F32, "mbb")
        b_mbb = Buf()
        bc = kb.sb(les, [128, 1024], F32, "bctmp")
        b_bc = Buf()
        wanted = sorted(set(want_fm) | set(want_bc))
        blocks = [(s0, h) for s0 in wanted for h in range(2)]
        mwv = mw_d.rearrange("(c p) n -> p c n", p=128)

        def issue(i):
            s0, h = blocks[i]
            col = s0 + h * 512
            kb.load(kb.sp, wsem[i % 2], wbuf[i % 2][:], mwv[:, :, col:col + 512], writes=[b_w[i % 2]])

        issue(0)
        for i, (s0, h) in enumerate(blocks):
            if i + 1 < len(blocks):
                issue(i + 1)
            if h == 0:
                kb.load(kb.sp, ds2, mbb[:], mb_d[0:1, s0:s0 + 1024].partition_broadcast(128), writes=[b_mbb])
                if s0 in want_bc:
                    t = kb.sb(es, [128, 1024], F32, "modbc")
                    out_bc[s0] = (t, Buf())
                dst, b_dst = out_bc[s0] if s0 in want_bc else (bc, b_bc)
            pb, b_pb = banks[i % 2]
            wb = wbuf[i % 2]

            def emit():
                for k in range(8):
                    ins = nc.tensor.matmul(pb, lhsT=crep[:, k, :], rhs=wb[:, k, :], start=(k == 0), stop=(k == 7))
                return ins
            kb.mm(emit, reads=[b_crep, b_w[i % 2]], writes=[b_pb])
            kb.op(kb.dve, nc.vector.tensor_tensor, reads=[b_pb, b_mbb], writes=[b_dst],
                  out=dst[:, h * 512:(h + 1) * 512], in0=pb, in1=mbb[:, h * 512:(h + 1) * 512], op=ALU.add)
            if h == 1 and s0 in want_fm:
                t = kb.sb(es, [128, 8], F32, "modfm")
                bt = Buf()
                pb2, b_pb2 = banks[(i + 1) % 2]

                def emit2():
                    for cc in range(8):
                        ins = nc.tensor.matmul(pb2[:, cc:cc + 1], lhsT=dst[:, cc * 128:(cc + 1) * 128],
                                               rhs=c.idf[:, 0:1], start=True, stop=True)
                    return ins
                kb.mm(emit2, reads=[b_dst, c.b_idf], writes=[b_pb2])
                kb.op(kb.dve, nc.vector.tensor_copy, reads=[b_pb2], writes=[bt], out=t[:], in_=pb2[:, 0:8])
                out_fm[s0] = (t, bt)
        kb.barrier()
    return out_fm, out_bc


def psum_banks(kb, es):
    pt = [kb.ps(es, [128, 1024], F32, "pp") for _ in range(4)]
    bufs = [Buf() for _ in range(8)]
    banks = [(pt[b // 2][:, (b % 2) * 512:(b % 2) * 512 + 512], bufs[b]) for b in range(8)]
    pairs = [(pt[p][:], [bufs[2 * p], bufs[2 * p + 1]]) for p in range(4)]
    return pairs, banks


def emit_rstd(kb, es, ssq, b_ssq, n, inv_n, eps=1e-6):
    nc = kb.nc
    kb.op(kb.dve, nc.vector.tensor_scalar, reads=[b_ssq], writes=[b_ssq], out=ssq[:, 0:n], in0=ssq[:, 0:n],
          scalar1=inv_n, scalar2=eps, op0=ALU.mult, op1=ALU.add)
    kb.op(kb.act, nc.scalar.activation, reads=[b_ssq], writes=[b_ssq], out=ssq[:, 0:n], in_=ssq[:, 0:n], func=AF.Sqrt)
    kb.op(kb.dve, nc.vector.reciprocal, reads=[b_ssq], writes=[b_ssq], out=ssq[:, 0:n], in_=ssq[:, 0:n])


def emit_norm_T(kb, c, tiles, rstd, b_rstd, outs, ppairs, xnb, b_xnb, cnt=[0]):
    nc = kb.nc
    i = 0
    while i < len(tiles):
        grp = tiles[i:i + 2]
        g = cnt[0]
        cnt[0] += 1
        pair, pb = ppairs[g % 2]
        pv = pair.bitcast(BF16).rearrange("p (c t) -> p c t", c=8)
        for j, (src, b_src, rc, doff) in enumerate(grp):
            xi = (g % 2) * 2 + j
            kb.op(kb.act, nc.scalar.activation, reads=[b_src, b_rstd], writes=[b_xnb[xi]],
                  out=xnb[xi][:], in_=src, func=AF.Copy, scale=rstd[:, rc:rc + 1])
        for j, (src, b_src, rc, doff) in enumerate(grp):
            xi = (g % 2) * 2 + j

            def emit(j=j, xi=xi):
                for cc in range(8):
                    ins = nc.tensor.transpose(out=pv[:, cc, j * 128:(j + 1) * 128],
                                              in_=xnb[xi][:, cc * 128:(cc + 1) * 128], identity=c.idb[:])
                return ins
            kb.mm(emit, reads=[b_xnb[xi], c.b_idb], writes=pb)
        n = 128 * len(grp)
        doff = grp[0][3]
        k = 0
        for (dst, b_dst, a, b_a, sh, b_sh) in outs:
            for cc in range(8):
                if k % 2 == 0:
                    kb.op(kb.act, nc.scalar.activation, reads=pb + [b_a, b_sh], writes=[b_dst],
                          out=dst[:, cc, doff:doff + n], in_=pv[:, cc, 0:n], func=AF.Identity,
                          scale=a[:, cc:cc + 1], bias=sh[:, cc:cc + 1])
                else:
                    kb.op(kb.dve, nc.vector.tensor_scalar, reads=pb + [b_a, b_sh], writes=[b_dst],
                          out=dst[:, cc, doff:doff + n], in0=pv[:, cc, 0:n], scalar1=a[:, cc:cc + 1],
                          scalar2=sh[:, cc:cc + 1], op0=ALU.mult, op1=ALU.add)
                k += 1
        i += 2


def emit_ffn(kb, es, c, h, b_h, nt, rstd, b_rstd, a2, b_a2, sh2, b_sh2, g2bc, b_g2, win_d, wout_d,
             cw, b_cw, cb, b_cb, hv, b_hv, pairs, banks, fix_tok, tag):
    nc = kb.nc
    ntok = nt * 128
    blocks = [(s0, min(512, ntok - s0)) for s0 in range(0, ntok, 512)]
    with ExitStack() as les:
        wout = kb.sb(les, [128, NF, 1024], BF16, "wout")
        b_wout = Buf()
        ds_wout = DSem(kb, "wout" + tag)
        wov = wout_d.rearrange("(f p) n -> p f n", p=128)
        for f0 in range(0, NF, 6):
            f1 = min(NF, f0 + 6)
            kb.load(kb.pool, ds_wout, wout[:, f0:f1, :], wov[:, f0:f1, :], writes=[b_wout])
        hnT = kb.sb(les, [128, 8, 512], BF16, "hn2T")
        b_hnT = Buf()
        actT = kb.sb(les, [128, NF, 512], BF16, "actT")
        b_actT = Buf()
        xnb = [kb.sb(les, [128, 1024], BF16, "xnb") for _ in range(4)]
        b_xnb = [Buf() for _ in range(4)]
        wsl = [kb.sb(les, [128, 8, 256], BF16, "winsl") for _ in range(3)]
        b_wsl = [Buf() for _ in range(3)]
        ds_w = [DSem(kb, f"win{tag}{i}") for i in range(3)]
        ug = [kb.sb(les, [128, 514], F32, "ug") for _ in range(2)]
        b_ug = [Buf(), Buf()]
        yy = [kb.sb(les, [128, 512], F32, "yy") for _ in range(2)]
        b_yy = [Buf(), Buf()]
        sg = kb.sb(les, [128, 512], F32, "sg")
        b_sg = Buf()
        halo = kb.sb(les, [128, 2 * NF, 2], F32, "halo")
        b_halo = Buf()
        tmp = kb.sb(les, [128, 1024], F32, "ftmp")
        b_tmp = Buf()
        kb.op(kb.dve, nc.vector.memset, writes=[b_halo], ap=halo[:], constant=0.0)
        jobs = [(bi, f) for bi in range(len(blocks)) for f in range(NF)]

        def issue_w(ji):
            bi, f = jobs[ji]
            sl = ji % 3
            kb.load(kb.pool, ds_w[sl], wsl[sl][:], win_d[f], writes=[b_wsl[sl]])

        issue_w(0)
        issue_w(1)
        for ji, (bi, f) in enumerate(jobs):
            s0, n = blocks[bi]
            if f == 0:
                tl = [(h[:, (s0 // 128) + j, :], b_h[(s0 // 128) + j], (s0 // 128) + j, j * 128)
                      for j in range(n // 128)]
                emit_norm_T(kb, c, tl, rstd, b_rstd, [(hnT, b_hnT, a2, b_a2, sh2, b_sh2)], pairs[0:2], xnb, b_xnb)
            if ji + 2 < len(jobs):
                issue_w(ji + 2)
            sl = ji % 3
            for gv in range(2):
                fi = f + gv * NF
                pb, b_pb = banks[4 + ((ji * 2 + gv) % 4)]

                def emit(gv=gv, pb=pb):
                    for k in range(8):
                        ins = nc.tensor.matmul(pb[:, 0:n], lhsT=wsl[sl][:, k, gv * 128:(gv + 1) * 128], rhs=hnT[:, k, 0:n],
                                               start=(k == 0), stop=(k == 7))
                    return ins
                kb.mm(emit, reads=[b_wsl[sl], b_hnT], writes=[b_pb])
                u, b_u = ug[gv], b_ug[gv]
                y, b_y = yy[gv], b_yy[gv]
                kb.op(kb.act, nc.scalar.activation, reads=[b_pb], writes=[b_u], out=u[:, 2:2 + n], in_=pb[:, 0:n],
                      func=AF.Copy)
                kb.op(kb.dve, nc.vector.tensor_copy, reads=[b_halo], writes=[b_u], out=u[:, 0:2], in_=halo[:, fi, :])
                if fix_tok is not None and s0 <= fix_tok - 2 and fix_tok <= s0 + n:
                    o = fix_tok - s0
                    kb.op(kb.dve, nc.vector.tensor_scalar, reads=[b_hv], writes=[b_u], out=u[:, o:o + 2],
                          in0=u[:, o:o + 2], scalar1=hv[:, 0:1], scalar2=None, op0=ALU.mult)
                kb.op(kb.act, nc.scalar.activation, reads=[b_pb, b_cw, b_cb], writes=[b_y], out=y[:, 0:n],
                      in_=pb[:, 0:n], func=AF.Identity, scale=cw[:, fi, 2:3], bias=cb[:, fi:fi + 1])
                kb.op(kb.dve, nc.vector.tensor_copy, reads=[b_u], writes=[b_halo], out=halo[:, fi, :],
                      in_=u[:, n:n + 2])
                kb.op(kb.dve, nc.vector.scalar_tensor_tensor, reads=[b_u, b_cw], writes=[b_y], out=y[:, 0:n],
                      in0=u[:, 1:1 + n], scalar=cw[:, fi, 1:2], in1=y[:, 0:n], op0=ALU.mult, op1=ALU.add)
                kb.op(kb.dve, nc.vector.scalar_tensor_tensor, reads=[b_u, b_cw], writes=[b_y], out=y[:, 0:n],
                      in0=u[:, 0:n], scalar=cw[:, fi, 0:1], in1=y[:, 0:n], op0=ALU.mult, op1=ALU.add)
            kb.op(kb.act, nc.scalar.activation, reads=[b_yy[0]], writes=[b_sg], out=sg[:, 0:n], in_=yy[0][:, 0:n],
                  func=AF.Silu)
            kb.op(kb.dve, nc.vector.tensor_tensor, reads=[b_sg, b_yy[1]], writes=[b_actT], out=actT[:, f, 0:n],
                  in0=sg[:, 0:n], in1=yy[1][:, 0:n], op=ALU.mult)
            if f == NF - 1:
                for j in range(n // 128):
                    ti = s0 // 128 + j
                    pair, pb2 = pairs[ti % 2]

                    def emit3(j=j, pair=pair):
                        for hh in range(2):
                            for ff in range(NF):
                                ins = nc.tensor.matmul(pair[:, hh * 512:(hh + 1) * 512],
                                                       lhsT=actT[:, ff, j * 128:(j + 1) * 128],
                                                       rhs=wout[:, ff, hh * 512:(hh + 1) * 512],
                                                       start=(ff == 0), stop=(ff == NF - 1))
                        return ins
                    kb.mm(emit3, reads=[b_actT, b_wout], writes=pb2)
                    kb.op(kb.dve, nc.vector.tensor_tensor, reads=pb2 + [b_g2], writes=[b_tmp], out=tmp[:],
                          in0=pair[:], in1=g2bc[:], op=ALU.mult)
                    kb.op(kb.dve, nc.vector.tensor_tensor, reads=[b_tmp], writes=[b_h[ti]], out=h[:, ti, :],
                          in0=h[:, ti, :], in1=tmp[:], op=ALU.add)
        kb.barrier()


def _din(nc, name, shape, dt=F32):
    return nc.dram_tensor(name, list(shape), dt, kind="ExternalInput").ap()


def _dout(nc, name, shape, dt=F32):
    return nc.dram_tensor(name, list(shape), dt, kind="ExternalOutput").ap()


def load_vecs(kb, es, c, specs):
    out = {}
    for name, (ap, shape, dt) in specs.items():
        t = kb.sb(es, shape, dt, name)
        kb.sp.e.dma_start(out=t[:], in_=ap).then_inc(c.dsem.sem, 16)
        c.dsem.cnt += 16
        out[name] = t
    c.b_vec = Buf()
    c.b_vec.w = Tok(c.dsem.sem, c.dsem.cnt, c.dsem.name)
    return out


def emit_rope_tables(kb, es, pos_i, b_pos, n, invf, sgn, b_vec, cosT, sinT, b_tab):
    nc = kb.nc
    ang = kb.sb(es, [32, n], F32, "ang")
    kf = kb.sb(es, [32, n], F32, "kf")
    ki = kb.sb(es, [32, n], I32, "ki")
    m = kb.sb(es, [32, n], F32, "mm")
    b = Buf()
    C1 = 6.28125
    C2 = TWO_PI - C1
    V = nc.vector
    op = lambda fn, **kw: kb.op(kb.dve, fn, reads=[b_pos, b_vec], writes=[b, b_tab], **kw)
    op(V.tensor_copy, out=ang[:], in_=pos_i)
    op(V.tensor_scalar, out=ang[:], in0=ang[:], scalar1=invf[:, 0:1], scalar2=None, op0=ALU.mult)
    op(V.tensor_scalar, out=kf[:], in0=ang[:], scalar1=1.0 / TWO_PI, scalar2=None, op0=ALU.mult)
    op(V.tensor_copy, out=ki[:], in_=kf[:])
    op(V.tensor_copy, out=kf[:], in_=ki[:])
    op(V.scalar_tensor_tensor, out=ang[:], in0=kf[:], scalar=-C1, in1=ang[:], op0=ALU.mult, op1=ALU.add)
    op(V.scalar_tensor_tensor, out=ang[:], in0=kf[:], scalar=-C2, in1=ang[:], op0=ALU.mult, op1=ALU.add)

    def wrap(t):
        op(V.tensor_scalar, out=m[:], in0=t[:], scalar1=PI, scalar2=TWO_PI, op0=ALU.is_gt, op1=ALU.mult)
        op(V.tensor_tensor, out=t[:], in0=t[:], in1=m[:], op=ALU.subtract)
        op(V.tensor_scalar, out=m[:], in0=t[:], scalar1=-PI, scalar2=TWO_PI, op0=ALU.is_lt, op1=ALU.mult)
        op(V.tensor_tensor, out=t[:], in0=t[:], in1=m[:], op=ALU.add)
    wrap(ang)
    kb.op(kb.act, nc.scalar.activation, reads=[b], writes=[b_tab], out=sinT, in_=ang[:], func=AF.Sin)
    op(V.tensor_scalar, out=sinT, in0=sinT, scalar1=sgn[:, 0:1], scalar2=None, op0=ALU.mult)
    op(V.tensor_scalar, out=kf[:], in0=ang[:], scalar1=PI / 2, scalar2=None, op0=ALU.add)
    wrap(kf)
    kb.op(kb.act, nc.scalar.activation, reads=[b], writes=[b_tab], out=cosT, in_=kf[:], func=AF.Sin)


def emit_latents(kb, es, c, d, h, b_h, ssq, b_ssq, akv, b_akv, shkv, b_shkv, a1n, b_a1n, sh1n, b_sh1n, vec, bv,
                 pairs, banks, o_ckv, o_kr, o_cq, stop=99):
    nc = kb.nc
    V, A, G = nc.vector, nc.scalar, nc.gpsimd
    with ExitStack() as s6:
        junk = kb.sb(s6, [128, 1024], BF16, "junk6")
        b_junk = Buf()
        kb.op(kb.dve, V.memset, writes=[b_ssq], ap=ssq[:], constant=0.0)
        for te in range(1, NEXT):
            kb.op(kb.act, A.activation, reads=[b_h[te]], writes=[b_junk, b_ssq], out=junk[:], in_=h[:, te, :],
                  func=AF.Square, accum_out=ssq[:, te:te + 1])
        emit_rstd(kb, es, ssq, b_ssq, NEXT, 1.0 / D)
        wdkv = kb.sb(s6, [128, 8, 256], BF16, "wdkv")
        wdq = kb.sb(s6, [128, 8, 384], BF16, "wdq")
        wkr = kb.sb(s6, [128, 8, 64], BF16, "wkr")
        b_w = Buf()
        dsw = [DSem(kb, f"lw{i}") for i in range(3)]
        kb.load(kb.pool, dsw[0], wdkv[:], d["wdkv"].rearrange("(k p) n -> p k n", p=128), writes=[b_w])
        t2 = kb.load(kb.pool, dsw[1], wdq[:], d["wdq"].rearrange("(k p) n -> p k n", p=128), writes=[])
        t3 = kb.load(kb.pool, dsw[2], wkr[:], d["wkr2"].rearrange("(k p) n -> p k n", p=128), writes=[])
        b_w2, b_w3 = Buf(), Buf()
        b_w2.w, b_w3.w = t2, t3
        ssl = kb.sb(s6, [128, NOWN, 2], F32, "ssl")
        b_ssl = [Buf() for _ in range(NOWN)]
        kb.dve.done(V.memset(ap=ssl[:], constant=0.0))
        for ti in range(NOWN):
            b_ssl[ti].w = Tok(kb.dve.sem, kb.dve.cnt, "dve")
        krT = kb.sb(s6, [32, OWN], BF16, "krT")
        b_krT = Buf()
        ckvT = kb.sb(s6, [128, 2, OWN], BF16, "ckvTo")
        cqT = kb.sb(s6, [128, 3, OWN], BF16, "cqTo")
        b_ckvT, b_cqT = Buf(), Buf()
        cosT = kb.sb(s6, [32, OWN], F32, "cosT")
        sinT = kb.sb(s6, [32, OWN], F32, "sinT")
        b_tab = Buf()
        with ExitStack() as sr:
            posi = kb.sb(sr, [32, OWN], I32, "posi")
            b_pos = Buf()
            dsp = DSem(kb, "posk")
            kb.load(kb.sp, dsp, posi[:], d["posk"].partition_broadcast(32), writes=[b_pos])
            emit_rope_tables(kb, sr, posi[:], b_pos, OWN, vec["invf"], vec["sgn"], bv, cosT[:], sinT[:], b_tab)
            kb.barrier()
        if stop == 7:
            kb.barrier()
            return
        hkvT = kb.sb(s6, [128, 8, 512], BF16, "hkvT")
        hqT = kb.sb(s6, [128, 8, 512], BF16, "hqT")
        b_hkvT, b_hqT = Buf(), Buf()
        xnb = [kb.sb(s6, [128, 1024], BF16, "xnb6") for _ in range(4)]
        b_xnb = [Buf() for _ in range(4)]
        t1 = kb.sb(s6, [32, 512], F32, "rt1")
        t2_ = kb.sb(s6, [32, 512], F32, "rt2")
        b_t1, b_t2 = Buf(), Buf()
        lnb = [kb.sb(s6, [128, 640], BF16, "lnb") for _ in range(2)]
        b_lnb = [Buf(), Buf()]
        for blk in range(4):
            tl = [(h[:, 1 + 4 * blk + j, :], b_h[1 + 4 * blk + j], 1 + 4 * blk + j, j * 128) for j in range(4)]
            emit_norm_T(kb, c, tl, ssq, b_ssq, [(hkvT, b_hkvT, akv, b_akv, shkv, b_shkv),
                                                (hqT, b_hqT, a1n, b_a1n, sh1n, b_sh1n)], pairs[0:2], xnb, b_xnb)
            for j in range(4):
                ti = 4 * blk + j
                pbs = []
                for (src, b_src, w, b_ww, ncol, row, inv_n) in ((hkvT, b_hkvT, wdkv, b_w, 256, 0, 1.0 / 256),
                                                              (hqT, b_hqT, wdq, b_w2, 384, 1, 1.0 / 384)):
                    pb, b_pb = banks[4 + (2 * ti + row) % 4]
                    pbs.append((pb, b_pb))

                    def emit(pb=pb, src=src, w=w, ncol=ncol, j=j):
                        for k in range(8):
                            ins = nc.tensor.matmul(pb[:, 0:ncol], lhsT=src[:, k, j * 128:(j + 1) * 128], rhs=w[:, k, :],
                                                   start=(k == 0), stop=(k == 7))
                        return ins
                    kb.mm(emit, reads=[b_src, b_ww], writes=[b_pb])
                    kb.op(kb.act, A.activation, reads=[b_pb], writes=[b_junk, b_ssl[ti]], out=junk[:, 0:ncol],
                          in_=pb[:, 0:ncol], func=AF.Square, accum_out=ssl[:, ti, row:row + 1])
                    kb.op(kb.dve, V.tensor_scalar, reads=[], writes=[b_ssl[ti]], out=ssl[:, ti, row:row + 1],
                          in0=ssl[:, ti, row:row + 1], scalar1=inv_n, scalar2=1e-6, op0=ALU.mult, op1=ALU.add)
                kb.op(kb.act, A.activation, reads=[], writes=[b_ssl[ti]], out=ssl[:, ti, :], in_=ssl[:, ti, :],
                      func=AF.Sqrt)
                kb.op(kb.dve, V.reciprocal, reads=[], writes=[b_ssl[ti]], out=ssl[:, ti, :], in_=ssl[:, ti, :])
                lb, b_lb = lnb[ti % 2], b_lnb[ti % 2]
                kb.op(kb.dve, V.scalar_tensor_tensor, reads=[pbs[0][1], b_ssl[ti], bv], writes=[b_lb], out=lb[:, 0:256],
                      in0=pbs[0][0][:, 0:256], scalar=ssl[:, ti, 0:1], in1=vec["latg"][:], op0=ALU.mult, op1=ALU.mult)
                kb.op(kb.dve, V.scalar_tensor_tensor, reads=[pbs[1][1], b_ssl[ti], bv], writes=[b_lb], out=lb[:, 256:640],
                      in0=pbs[1][0][:, 0:384], scalar=ssl[:, ti, 1:2], in1=vec["qg"][:], op0=ALU.mult, op1=ALU.mult)
                if os.environ.get("SKIP_TR"):
                    continue
                pb, b_pb = banks[2 + ti % 2]
                pv = pairs[1][0].bitcast(BF16)[:, (ti % 2) * 1024:(ti % 2) * 1024 + 1024]

                def emit(pv=pv, lb=lb):
                    for cc in range(5):
                        ins = nc.tensor.transpose(out=pv[:, cc * 128:(cc + 1) * 128], in_=lb[:, cc * 128:(cc + 1) * 128],
                                                  identity=c.idb[:])
                    return ins
                kb.mm(emit, reads=[b_lb, c.b_idb], writes=[b_pb])
                for cc in range(5):
                    dst, b_dst = (ckvT[:, cc, ti * 128:(ti + 1) * 128], b_ckvT) if cc < 2 else \
                        (cqT[:, cc - 2, ti * 128:(ti + 1) * 128], b_cqT)
                    if cc % 2 == 0:
                        kb.op(kb.dve, V.tensor_copy, reads=[b_pb], writes=[b_dst], out=dst, in_=pv[:, cc * 128:(cc + 1) * 128])
                    else:
                        kb.op(kb.act, A.activation, reads=[b_pb], writes=[b_dst], out=dst, in_=pv[:, cc * 128:(cc + 1) * 128],
                              func=AF.Copy)
            if os.environ.get("SKIP_ROPE"):
                continue
            pa, b_pa = banks[0]
            pbw, b_pbw = banks[1]
            for (pp, b_pp, c0) in ((pa, b_pa, 0), (pbw, b_pbw, 32)):
                def emit(pp=pp, c0=c0):
                    for k in range(8):
                        ins = nc.tensor.matmul(pp[0:32, :], lhsT=wkr[:, k, c0:c0 + 32], rhs=hkvT[:, k, :],
                                               start=(k == 0), stop=(k == 7))
                    return ins
                kb.mm(emit, reads=[b_hkvT, b_w3], writes=[b_pp])
            tk = slice(blk * 512, (blk + 1) * 512)
            kb.op(kb.dve, V.tensor_tensor, reads=[b_pa, b_tab], writes=[b_t1], out=t1[:], in0=pa[0:32, :],
                  in1=cosT[:, tk], op=ALU.mult)
            kb.op(kb.dve, V.tensor_tensor, reads=[b_pbw, b_tab], writes=[b_t2], out=t2_[:], in0=pbw[0:32, :],
                  in1=sinT[:, tk], op=ALU.mult)
            kb.op(kb.dve, V.tensor_tensor, reads=[b_t1, b_t2], writes=[b_krT], out=krT[:, tk], in0=t1[:], in1=t2_[:],
                  op=ALU.add)
        if stop == 8:
            kb.barrier()
            return
        kb.store(kb.sp, o_ckv.rearrange("(c p) n -> p c n", p=128), ckvT[:], reads=[b_ckvT])
        kb.store(kb.sp, o_cq.rearrange("(c p) n -> p c n", p=128), cqT[:], reads=[b_cqT])
        kb.store(kb.sp, o_kr, krT[:], reads=[b_krT])
        kb.barrier()


def build_phaseA(dbg=False, stop=99):
    nc = bass.Bass("TRN2", target_bir_lowering=False)
    d = {}
    for name, shape, dt in [
        ("xe", [NALL * 128, D], F32), ("valid", [128, NALL], F32), ("posk", [1, OWN], I32), ("cvec", [128, 8], F32),
        ("invf", [32, 1], F32), ("sgn", [32, 1], F32), ("hv", [128, 1], F32), ("ident", [128, 128], F32),
        ("mw0", [D, 6 * D], F32), ("mb0", [1, 6 * D], F32), ("mw1", [D, 2 * D], F32), ("mb1", [1, 2 * D], F32),
        ("kvmw", [D, 2 * D], F32), ("kvmb", [1, 2 * D], F32),
        ("n1g0", [128, 8], F32), ("n2g0", [128, 8], F32), ("n1g1", [128, 8], F32), ("kvng", [128, 8], F32),
        ("wqkv_t", [8, 128, 8 * 384], F32), ("wo", [D, D], F32), ("tb", [128, 16 * 640], F32),
        ("win_t", [NF, 128, 8 * 256], F32), ("cw", [128, 2 * NF * 3], F32), ("cb", [128, 2 * NF], F32),
        ("wout", [FF, D], F32), ("wdkv", [D, 256], F32), ("latg", [1, 256], F32), ("wkr2", [D, 64], F32),
        ("wdq", [D, 384], F32), ("qg", [1, 384], F32),
    ]:
        d[name] = _din(nc, name, shape, dt)
    o_h1 = _dout(nc, "h1", [NEXT * 128, D], F32)
    o_ckv = _dout(nc, "ckvT", [256, OWN], BF16)
    o_kr = _dout(nc, "krT", [32, OWN], BF16)
    o_cq = _dout(nc, "cqT", [384, OWN], BF16)
    if dbg:
        o_dbg = _dout(nc, "dbg", [NEXT * 128, D], F32)

    kb = KB(nc)
    es = kb.es
    V, A, G = nc.vector, nc.scalar, nc.gpsimd
    pairs, banks = psum_banks(kb, es)
    c = emit_consts(kb, es, d)
    vec = load_vecs(kb, es, c, {
        "valid": (d["valid"], [128, NALL], F32), "hv": (d["hv"], [128, 1], F32),
        "invf": (d["invf"], [32, 1], F32), "sgn": (d["sgn"], [32, 1], F32),
        "n1g0": (d["n1g0"], [128, 8], F32), "n2g0": (d["n2g0"], [128, 8], F32),
        "n1g1": (d["n1g1"], [128, 8], F32), "kvng": (d["kvng"], [128, 8], F32),
        "cw": (d["cw"].rearrange("p (f t) -> p f t", t=3), [128, 2 * NF, 3], F32), "cb": (d["cb"], [128, 2 * NF], F32),
        "latg": (d["latg"].partition_broadcast(128), [128, 256], F32),
        "qg": (d["qg"].partition_broadcast(128), [128, 384], F32),
    })
    bv = c.b_vec
    fm0, bc0 = emit_mod(kb, es, c, [banks[0], banks[1]], d["cvec"], d["mw0"], d["mb0"], 6 * D,
                        want_fm=[0, 1024, 3072, 4096], want_bc=[2048, 5120], tag="0")
    fm1, _ = emit_mod(kb, es, c, [banks[0], banks[1]], d["cvec"], d["mw1"], d["mb1"], 2 * D,
                      want_fm=[0, 1024], want_bc=[], tag="1")
    fmk, _ = emit_mod(kb, es, c, [banks[0], banks[1]], d["cvec"], d["kvmw"], d["kvmb"], 2 * D,
                      want_fm=[0, 1024], want_bc=[], tag="k")

    def mk_a(sc, gname):
        t = kb.sb(es, [128, 8], F32, "a_" + gname)
        b = Buf()
        kb.op(kb.dve, V.scalar_tensor_tensor, reads=[sc[1], bv], writes=[b], out=t[:], in0=sc[0][:], scalar=1.0,
              in1=vec[gname][:], op0=ALU.add, op1=ALU.mult)
        return t, b
    a1, b_a1 = mk_a(fm0[1024], "n1g0")
    a2, b_a2 = mk_a(fm0[4096], "n2g0")
    a1n, b_a1n = mk_a(fm1[1024], "n1g1")
    akv, b_akv = mk_a(fmk[1024], "kvng")
    sh1, b_sh1 = fm0[0]
    sh2, b_sh2 = fm0[3072]
    sh1n, b_sh1n = fm1[0]
    shkv, b_shkv = fmk[0]
    g1bc, b_g1 = bc0[2048]
    g2bc, b_g2 = bc0[5120]

    if stop == 0:
        kb.finish()
        return nc
    ssq = kb.sb(es, [128, 32], F32, "ssq")
    b_ssq = Buf()
    xev = d["xe"].rearrange("(t p) n -> p t n", p=128)
    with ExitStack() as s12:
        hnT = kb.sb(s12, [128, 8, NALL * 128], BF16, "hnT")
        b_hnT = Buf()
        with ExitStack() as s1:
            xall = kb.sb(s1, [128, NALL, 1024], F32, "xall")
            b_x = [Buf() for _ in range(NALL)]
            junk = kb.sb(s1, [128, 1024], BF16, "junk")
            b_junk = Buf()
            xnb = [kb.sb(s1, [128, 1024], BF16, "xnb") for _ in range(4)]
            b_xnb = [Buf() for _ in range(4)]
            dsx = [DSem(kb, f"x{i}") for i in range(6)]
            kb.op(kb.dve, V.memset, writes=[b_ssq], ap=ssq[:], constant=0.0)
            for t in range(NALL):
                kb.load(kb.sp, dsx[t % 6], xall[:, t, :], xev[:, t, :], writes=[b_x[t]])
            for t in range(NALL):
                kb.op(kb.act, A.activation, reads=[b_x[t]], writes=[b_junk, b_ssq], out=junk[:], in_=xall[:, t, :],
                      func=AF.Square, accum_out=ssq[:, t:t + 1])
            emit_rstd(kb, es, ssq, b_ssq, NALL, 1.0 / D)
            tl = [(xall[:, t, :], b_x[t], t, t * 128) for t in range(NALL)]
            emit_norm_T(kb, c, tl, ssq, b_ssq, [(hnT, b_hnT, a1, b_a1, sh1, b_sh1)], pairs[0:2], xnb, b_xnb)
            kb.barrier()
        if stop == 1:
            kb.finish()
            return nc
        s_o = ExitStack()
        oT = kb.sb(s_o, [128, 8, NEXT * 128], BF16, "oT")
        if True:
            b_oT = Buf()
            with ExitStack() as s2:
                tb = kb.sb(s2, [128, 16, 640], BF16, "tb")
                b_tb = Buf()
                ds_tb = DSem(kb, "tb")
                tbv = d["tb"].rearrange("p (h n) -> p h n", h=16)
                for q4 in range(4):
                    kb.load(kb.pool, ds_tb, tb[:, q4 * 4:q4 * 4 + 4, :], tbv[:, q4 * 4:q4 * 4 + 4, :], writes=[b_tb])
                wsl = [kb.sb(s2, [128, 8, 384], BF16, "wqkv") for _ in range(2)]
                b_wsl = [Buf(), Buf()]
                ds_w = [DSem(kb, f"wqkv{i}") for i in range(2)]
                qT = [kb.sb(s2, [128, NEXT * 128], BF16, "qT") for _ in range(2)]
                kT = [kb.sb(s2, [128, NALL * 128], BF16, "kT") for _ in range(2)]
                va = [kb.sb(s2, [128, NALL, 2, 128], BF16, "va") for _ in range(2)]
                b_qT, b_kT, b_va = [Buf(), Buf()], [Buf(), Buf()], [Buf(), Buf()]
                PT = [kb.sb(s2, [128, 640], BF16, "PT") for _ in range(3)]
                b_PT = [Buf() for _ in range(3)]
                rs = kb.sb(s2, [128, 256], F32, "rs")
                b_rs = Buf()
                for sl in range(2):
                    for t in range(NALL):
                        kb.op(kb.dve, V.tensor_scalar, reads=[bv, c.b_ones], writes=[b_va[sl]],
                              out=va[sl][:, t, 0, 64:128], in0=c.ones[:, 0:64], scalar1=vec["valid"][:, t:t + 1],
                              scalar2=None, op0=ALU.mult)
                        kb.op(kb.dve, V.tensor_scalar, reads=[bv, c.b_ones], writes=[b_va[sl]],
                              out=va[sl][:, t, 1, 0:64], in0=c.ones[:, 0:64], scalar1=vec["valid"][:, t:t + 1],
                              scalar2=None, op0=ALU.mult)

                def load_w(p):
                    kb.load(kb.pool, ds_w[p % 2], wsl[p % 2][:],
                            d["wqkv_t"][p].rearrange("p (k n) -> p k n", k=8), writes=[b_wsl[p % 2]])
                load_w(0)
                pcnt = 0
                acnt = 0
                for p in range(8):
                    sl = p % 2
                    if p + 1 < 8:
                        load_w(p + 1)
                    w = wsl[sl]
                    for (dst, b_dst, col0, tok0, ntok, scale) in ((qT[sl], b_qT[sl], 0, 512, NEXT * 128, 0.125),
                                                                  (kT[sl], b_kT[sl], 128, 0, NALL * 128, None)):
                        for e0 in range(0, ntok, 512):
                            n = min(512, ntok - e0)
                            pb, b_pb = banks[4 + pcnt % 2]
                            pcnt += 1

                            def emit(pb=pb, col0=col0, a0=tok0 + e0, n=n):
                                for k in range(8):
                                    ins = nc.tensor.matmul(pb[:, 0:n], lhsT=w[:, k, col0:col0 + 128],
                                                           rhs=hnT[:, k, a0:a0 + n], start=(k == 0), stop=(k == 7))
                                return ins
                            kb.mm(emit, reads=[b_wsl[sl], b_hnT], writes=[b_pb])
                            if scale is not None:
                                kb.op(kb.act, A.activation, reads=[b_pb], writes=[b_dst], out=dst[:, e0:e0 + n],
                                      in_=pb[:, 0:n], func=AF.Copy, scale=scale)
                            else:
                                kb.op(kb.dve, V.tensor_copy, reads=[b_pb], writes=[b_dst], out=dst[:, e0:e0 + n],
                                      in_=pb[:, 0:n])
                    for t0 in range(0, NALL, 4):
                        nt_ = min(4, NALL - t0)
                        pb, b_pb = banks[4 + pcnt % 2]
                        pcnt += 1

                        def emit(pb=pb, t0=t0, nt_=nt_):
                            for j in range(nt_):
                                for k in range(8):
                                    ins = nc.tensor.matmul(pb[:, j * 128:(j + 1) * 128],
                                                           lhsT=hnT[:, k, (t0 + j) * 128:(t0 + j + 1) * 128],
                                                           rhs=w[:, k, 256:384], start=(k == 0), stop=(k == 7))
                            return ins
                        kb.mm(emit, reads=[b_wsl[sl], b_hnT], writes=[b_pb])
                        for j in range(nt_):
                            kb.op(kb.dve, V.tensor_scalar, reads=[b_pb, bv], writes=[b_va[sl]],
                                  out=va[sl][:, t0 + j, 0, 0:64], in0=pb[:, j * 128:j * 128 + 64],
                                  scalar1=vec["valid"][:, t0 + j:t0 + j + 1], scalar2=None, op0=ALU.mult)
                            kb.op(kb.act, A.activation, reads=[b_pb, bv], writes=[b_va[sl]],
                                  out=va[sl][:, t0 + j, 1, 64:128], in_=pb[:, j * 128 + 64:j * 128 + 128],
                                  func=AF.Copy, scale=vec["valid"][:, t0 + j:t0 + j + 1])
                    for te in range(NEXT):
                        ob, b_ob = banks[6 + te % 2]
                        pts = []
                        for hi in range(2):
                            h16 = 2 * p + hi
                            sp_, sb_ = pairs[hi]
                            r0 = hi * 64

                            def emit(sp_=sp_, h16=h16, r0=r0, te=te):
                                nc.tensor.matmul(sp_[:, 0:512], lhsT=c.idb[:], rhs=tb[:, h16, 0:512], start=True,
                                                 stop=False, skip_group_check=True)
                                nc.tensor.matmul(sp_[:, 512:640], lhsT=c.idb[:], rhs=tb[:, h16, 512:640], start=True,
                                                 stop=False, skip_group_check=True)
                                for m in range(5):
                                    ins = nc.tensor.matmul(sp_[:, m * 128:(m + 1) * 128],
                                                           lhsT=kT[sl][r0:r0 + 64, (te + m) * 128:(te + m + 1) * 128],
                                                           rhs=qT[sl][r0:r0 + 64, te * 128:(te + 1) * 128],
                                                           start=False, stop=True, skip_group_check=True)
                                return ins
                            kb.mm(emit, reads=[c.b_idb, b_tb, b_kT[sl], b_qT[sl]], writes=sb_)
                            pi = acnt % 3
                            acnt += 1
                            kb.op(kb.act, A.activation, reads=sb_, writes=[b_PT[pi]], out=PT[pi][:], in_=sp_[:, 0:640],
                                  func=AF.Exp)
                            pts.append(pi)
                        for hi in range(2):
                            pi = pts[hi]

                            def emit(hi=hi, pi=pi, te=te):
                                for m in range(5):
                                    ins = nc.tensor.matmul(ob[:, hi * 128:(hi + 1) * 128], lhsT=va[sl][:, te + m, hi, :],
                                                           rhs=PT[pi][:, m * 128:(m + 1) * 128], start=(m == 0),
                                                           stop=(m == 4), skip_group_check=True)
                                return ins
                            kb.mm(emit, reads=[b_va[sl], b_PT[pi]], writes=[b_ob])
                        kb.op(kb.dve, V.tensor_scalar, reads=[b_ob], writes=[b_rs], out=rs[0:64, 0:128],
                              in0=ob[64:128, 0:128], scalar1=1e-30, scalar2=None, op0=ALU.max)
                        kb.op(kb.dve, V.tensor_scalar, reads=[b_ob], writes=[b_rs], out=rs[64:128, 128:256],
                              in0=ob[0:64, 128:256], scalar1=1e-30, scalar2=None, op0=ALU.max)
                        kb.op(kb.dve, V.reciprocal, reads=[b_rs], writes=[b_rs], out=rs[0:64, 0:128], in_=rs[0:64, 0:128])
                        kb.op(kb.dve, V.reciprocal, reads=[b_rs], writes=[b_rs], out=rs[64:128, 128:256],
                              in_=rs[64:128, 128:256])
                        kb.op(kb.dve, V.tensor_tensor, reads=[b_ob, b_rs], writes=[b_oT],
                              out=oT[0:64, p, te * 128:(te + 1) * 128], in0=ob[0:64, 0:128], in1=rs[0:64, 0:128],
                              op=ALU.mult)
                        kb.op(kb.dve, V.tensor_tensor, reads=[b_ob, b_rs], writes=[b_oT],
                              out=oT[64:128, p, te * 128:(te + 1) * 128], in0=ob[64:128, 128:256],
                              in1=rs[64:128, 128:256], op=ALU.mult)
                kb.barrier()
    if stop == 2:
        kb.finish()
        return nc
    h = kb.sb(es, [128, NEXT, 1024], F32, "h")
    b_h = [Buf() for _ in range(NEXT)]
    with ExitStack() as s3:
        wo = kb.sb(s3, [128, 8, 1024], BF16, "wo")
        b_wo = Buf()
        ds_wo = DSem(kb, "wo")
        kb.load(kb.pool, ds_wo, wo[:], d["wo"].rearrange("(c p) n -> p c n", p=128), writes=[b_wo])
        tmp = kb.sb(s3, [128, 1024], F32, "tmp3")
        b_tmp = Buf()
        dsh = [DSem(kb, f"h{i}") for i in range(4)]
        for te in range(NEXT):
            kb.load(kb.sp, dsh[te % 4], h[:, te, :], xev[:, te + 4, :], writes=[b_h[te]])
        for te in range(NEXT):
            pair, pb2 = pairs[te % 2]

            def emit(pair=pair, te=te):
                for hh in range(2):
                    for cc in range(8):
                        ins = nc.tensor.matmul(pair[:, hh * 512:(hh + 1) * 512],
                                               lhsT=oT[:, cc, te * 128:(te + 1) * 128],
                                               rhs=wo[:, cc, hh * 512:(hh + 1) * 512], start=(cc == 0),
                                               stop=(cc == 7))
                return ins
            kb.mm(emit, reads=[b_oT, b_wo], writes=pb2)
            kb.op(kb.dve, V.tensor_tensor, reads=pb2 + [b_g1], writes=[b_tmp], out=tmp[:], in0=pair[:],
                  in1=g1bc[:], op=ALU.mult)
            kb.op(kb.dve, V.tensor_tensor, reads=[b_tmp], writes=[b_h[te]], out=h[:, te, :], in0=h[:, te, :],
                  in1=tmp[:], op=ALU.add)
        kb.barrier()
    s_o.close()
    if dbg:
        for te in range(NEXT):
            kb.store(kb.sp, o_dbg[te * 128:(te + 1) * 128, :], h[:, te, :], reads=[b_h[te]])
    if stop == 3:
        kb.finish()
        return nc
    with ExitStack() as s4:
        junk = kb.sb(s4, [128, 1024], BF16, "junk2")
        b_junk = Buf()
        kb.op(kb.dve, V.memset, writes=[b_ssq], ap=ssq[:], constant=0.0)
        for te in range(NEXT):
            kb.op(kb.act, A.activation, reads=[b_h[te]], writes=[b_junk, b_ssq], out=junk[:], in_=h[:, te, :],
                  func=AF.Square, accum_out=ssq[:, te:te + 1])
        emit_rstd(kb, es, ssq, b_ssq, NEXT, 1.0 / D)
        kb.barrier()
    emit_ffn(kb, es, c, h, b_h, NEXT, ssq, b_ssq, a2, b_a2, sh2, b_sh2, g2bc, b_g2, d["win_t"], d["wout"],
             vec["cw"], bv, vec["cb"], bv, vec["hv"], bv, pairs, banks, 128, "A")
    if stop == 5:
        kb.finish()
        return nc
    for te in range(NEXT):
        kb.store(kb.sp, o_h1[te * 128:(te + 1) * 128, :], h[:, te, :], reads=[b_h[te]])
    if stop == 6:
        kb.finish()
        return nc
    emit_latents(kb, es, c, d, h, b_h, ssq, b_ssq, akv, b_akv, shkv, b_shkv, a1n, b_a1n, sh1n, b_sh1n, vec, bv,
                 pairs, banks, o_ckv, o_kr, o_cq, stop=stop)
    kb.finish()
    return nc


def toeplitz_bias(relb):
    k = np.arange(128)[:, None, None]
    m = np.arange(5)[None, :, None]
    q = np.arange(128)[None, None, :]
    idx = np.clip(q - k + 128 * (4 - m), -128, 128) + 128
    T = relb[:, idx]
    a, j = q // 64, k // 64
    masked = ((m == 0) & (j == 0) & (a == 1)) | ((m == 4) & (j == 1) & (a == 0))
    T = np.where(masked[None], np.float32(NEG), T).astype(np.float32)
    return np.ascontiguousarray(T.transpose(1, 0, 2, 3).reshape(128, 16 * 640))


def tile_win(win):
    w = np.asarray(win).reshape(8, 128, 2, NF, 128)
    return np.ascontiguousarray(w.transpose(3, 1, 0, 2, 4).reshape(NF, 128, 8 * 256))


def tile_wqkv(w):
    w = np.asarray(w).reshape(8, 128, 3, 8, 128)
    return np.ascontiguousarray(w.transpose(3, 1, 0, 2, 4).reshape(8, 128, 8 * 384))


def conv_fm(conv_w, conv_b):
    cw = np.ascontiguousarray(np.asarray(conv_w).T.reshape(2 * NF, 128, 3).transpose(1, 0, 2).reshape(128, 2 * NF * 3))
    return cw, fm(conv_b)


def rope_consts():
    invf = np.power(np.float32(10000.0), -np.arange(16, dtype=np.float32) * np.float32(2.0 / 32)).astype(np.float32)
    invf = np.concatenate([invf, invf]).reshape(32, 1)
    sgn = np.concatenate([-np.ones(16, np.float32), np.ones(16, np.float32)]).reshape(32, 1)
    return invf, sgn


def prep_A(inp):
    x, cc, pos = inp["x"], inp["c"], inp["positions"]
    invf, sgn = rope_consts()
    cw0, cb0 = conv_fm(inp["f_conv_w"][0], inp["f_conv_b"][0])
    wkr = np.asarray(inp["b_wkr"])
    shared = {
        "invf": invf, "sgn": sgn, "ident": np.eye(128, dtype=np.float32),
        "mw0": np.ascontiguousarray(inp["mod_w"][0]), "mb0": np.ascontiguousarray(inp["mod_b"][0][None]),
        "mw1": np.ascontiguousarray(inp["mod_w"][1][:, 0:2 * D]), "mb1": np.ascontiguousarray(inp["mod_b"][1][None, 0:2 * D]),
        "kvmw": np.ascontiguousarray(inp["kv_mod_w"]), "kvmb": np.ascontiguousarray(inp["kv_mod_b"][None]),
        "n1g0": fm(inp["norm1_g"][0]), "n2g0": fm(inp["norm2_g"][0]), "n1g1": fm(inp["norm1_g"][1]),
        "kvng": fm(inp["kv_norm_g"]),
        "wqkv_t": tile_wqkv(inp["a_wqkv"][0]), "wo": np.ascontiguousarray(inp["a_wo"][0]),
        "tb": toeplitz_bias(np.asarray(inp["a_rel_bias"][0])),
        "win_t": tile_win(inp["f_win"][0]), "cw": cw0, "cb": cb0, "wout": np.ascontiguousarray(inp["f_wout"][0]),
        "wdkv": np.ascontiguousarray(inp["b_wdkv"]), "latg": np.ascontiguousarray(inp["b_kv_lat_norm_g"][None]),
        "wkr2": np.ascontiguousarray(np.concatenate([wkr, wkr[:, 16:32], wkr[:, 0:16]], axis=1)),
        "wdq": np.ascontiguousarray(inp["b_wdq"][0]), "qg": np.ascontiguousarray(inp["b_q_norm_g"][0][None]),
    }
    maps = []
    for core in range(NCORE):
        b, j = core // 4, core % 4
        s0 = j * OWN
        xe = np.zeros((NALL * 128, D), np.float32)
        lo = s0 - 640
        src0 = max(lo, 0)
        xe[src0 - lo:] = x[b, src0:s0 + OWN]
        tok = lo + np.arange(NALL * 128)
        valid = np.ascontiguousarray((tok >= 0).astype(np.float32).reshape(NALL, 128).T)
        m = dict(shared)
        m.update({"xe": xe, "valid": valid, "posk": np.ascontiguousarray(pos[b, s0:s0 + OWN][None]).astype(np.int32),
                  "cvec": fm(cc[b]), "hv": np.full((128, 1), 1.0 if s0 > 0 else 0.0, np.float32)})
        maps.append(m)
    return maps


def mla_mask():
    k = np.arange(128)[:, None, None]
    jj = np.arange(4)[None, :, None]
    q = np.arange(512)[None, None, :]
    allowed = (2 * jj + k // 64) <= (q // 64)
    return np.ascontiguousarray(np.where(allowed, np.float32(0.0), np.float32(NEG)).astype(np.float32).reshape(128, 4 * 512))


def build_phaseB():
    nc = bass.Bass("TRN2", target_bir_lowering=False)
    d = {}
    for name, shape, dt in [
        ("cqT", [384, S], BF16), ("ckvT", [256, S], BF16), ("krT", [32, S], BF16), ("posq", [1, S], I32),
        ("invf", [32, 1], F32), ("sgn", [32, 1], F32), ("ident", [128, 128], F32),
        ("wuq", [384, 256], F32), ("wqr2", [384, 256], F32), ("wuk", [256, 256], F32), ("wuv", [256, 256], F32),
        ("mask", [128, 4 * 512], F32),
    ]:
        d[name] = _din(nc, name, shape, dt)
    o_oT = _dout(nc, "oT", [256, S], BF16)
    kb = KB(nc)
    es = kb.es
    V, A, G = nc.vector, nc.scalar, nc.gpsimd
    pairs, banks = psum_banks(kb, es)
    c = emit_consts(kb, es, d)
    vec = load_vecs(kb, es, c, {"invf": (d["invf"], [32, 1], F32), "sgn": (d["sgn"], [32, 1], F32)})
    bv = c.b_vec
    SC = 96.0 ** -0.5
    NQB = S // 512
    cqs = [kb.sb(es, [128, 3, 512], BF16, "cqs") for _ in range(3)]
    b_cqs = [Buf() for _ in range(3)]
    ds_cq = [DSem(kb, f"cqs{i}") for i in range(3)]
    cq_d = d["cqT"].rearrange("(c p) n -> p c n", p=128)
    ckvT = kb.sb(es, [128, 2, S], BF16, "ckvT")
    krT = kb.sb(es, [32, S], BF16, "krT")
    b_ckv, b_kr = Buf(), Buf()
    dsl = [DSem(kb, f"bl{i}") for i in range(3)]
    kb.load(kb.sp, dsl[1], ckvT[:], d["ckvT"].rearrange("(c p) n -> p c n", p=128), writes=[b_ckv])
    kb.load(kb.sp, dsl[2], krT[:], d["krT"], writes=[b_kr])
    wuq = kb.sb(es, [128, 3, 256], BF16, "wuq")
    wqr = kb.sb(es, [128, 3, 256], BF16, "wqr")
    wuk = kb.sb(es, [128, 2, 256], BF16, "wuk")
    wuv = kb.sb(es, [128, 2, 256], BF16, "wuv")
    mask = kb.sb(es, [128, 4, 512], BF16, "mask")
    b_w = Buf()
    dsw = [DSem(kb, f"bw{i}") for i in range(5)]
    toks = [
        kb.load(kb.pool, dsw[0], wuq[:], d["wuq"].rearrange("(c p) n -> p c n", p=128)),
        kb.load(kb.pool, dsw[1], wqr[:], d["wqr2"].rearrange("(c p) n -> p c n", p=128)),
        kb.load(kb.pool, dsw[2], wuk[:], d["wuk"].rearrange("(c p) n -> p c n", p=128)),
        kb.load(kb.pool, dsw[3], wuv[:], d["wuv"].rearrange("(c p) n -> p c n", p=128)),
        kb.load(kb.pool, dsw[4], mask[:], d["mask"].rearrange("p (j n) -> p j n", j=4)),
    ]

    b_wl = [Buf() for _ in toks]
    for bb, t in zip(b_wl, toks):
        bb.w = t
    cosb = kb.sb(es, [32, S], BF16, "cosb")
    sinb = kb.sb(es, [32, S], BF16, "sinb")
    b_tab = Buf()
    with ExitStack() as sr:
        posi = kb.sb(sr, [32, 2048], I32, "posi")
        b_pos = Buf()
        cosf = kb.sb(sr, [32, 2048], F32, "cosf")
        sinf = kb.sb(sr, [32, 2048], F32, "sinf")
        b_tf = Buf()
        dsp = DSem(kb, "posq")
        for part in range(4):
            tk = slice(part * 2048, (part + 1) * 2048)
            kb.load(kb.sp, dsp, posi[:], d["posq"][0:1, tk].partition_broadcast(32), writes=[b_pos])
            with ExitStack() as sr2:
                emit_rope_tables(kb, sr2, posi[:], b_pos, 2048, vec["invf"], vec["sgn"], bv, cosf[:], sinf[:], b_tf)
                kb.op(kb.dve, V.tensor_copy, reads=[b_tf], writes=[b_tab], out=cosb[:, tk], in_=cosf[:])
                kb.op(kb.dve, V.tensor_copy, reads=[b_tf], writes=[b_tab], out=sinb[:, tk], in_=sinf[:])
                kb.barrier()
    QnT = kb.sb(es, [64, S], BF16, "QnT")
    QrT = kb.sb(es, [32, S], BF16, "QrT")
    KnT = kb.sb(es, [64, S], BF16, "KnT")
    va = kb.sb(es, [128, S // 128, 128], BF16, "va")
    b_Qn, b_Qr, b_Kn, b_va = Buf(), Buf(), Buf(), Buf()
    kb.op(kb.dve, V.memset, writes=[b_va], ap=va[:, :, 64:128], constant=1.0)
    PT = [kb.sb(es, [128, 1024], BF16, "PT") for _ in range(3)]
    b_PT = [Buf() for _ in range(3)]
    rs = kb.sb(es, [64, 512], F32, "rs")
    b_rs = Buf()
    ost = [kb.sb(es, [64, 512], BF16, "ost") for _ in range(3)]
    b_ost = [Buf() for _ in range(3)]
    t1 = kb.sb(es, [32, 512], F32, "t1")
    t2 = kb.sb(es, [32, 512], F32, "t2")
    b_t1, b_t2 = Buf(), Buf()
    pcnt = 0
    acnt = 0
    ocnt = 0
    def load_cq(i):
        blk = i % NQB
        kb.load(kb.sp, ds_cq[i % 3], cqs[i % 3][:], cq_d[:, :, blk * 512:(blk + 1) * 512], writes=[b_cqs[i % 3]])
    load_cq(0)
    load_cq(1)
    for hh in range(4):
        hc = slice(hh * 64, (hh + 1) * 64)
        for blk in range(NQB):
            ci = hh * NQB + blk
            if ci + 2 < 4 * NQB:
                load_cq(ci + 2)
            cqT, b_cq = cqs[ci % 3], b_cqs[ci % 3]
            tk = slice(blk * 512, (blk + 1) * 512)
            pb, b_pb = banks[6 + pcnt % 2]
            pcnt += 1

            def emit(pb=pb, cqT=cqT):
                for kc in range(3):
                    ins = nc.tensor.matmul(pb[0:64, :], lhsT=wuq[:, kc, hc], rhs=cqT[:, kc, :], start=(kc == 0),
                                           stop=(kc == 2))
                return ins
            kb.mm(emit, reads=[b_wl[0], b_cq], writes=[b_pb])
            kb.op(kb.act, A.activation, reads=[b_pb], writes=[b_Qn], out=QnT[:, tk], in_=pb[0:64, :], func=AF.Copy,
                  scale=SC)
            for q2 in range(2):
                t0 = blk * 512 + q2 * 256
                pa, b_pa = banks[6 + pcnt % 2]
                pcnt += 1

                def emit(pa=pa, q2=q2, cqT=cqT):
                    for half in range(2):
                        for kc in range(3):
                            ins = nc.tensor.matmul(pa[0:32, half * 256:half * 256 + 256],
                                                   lhsT=wqr[:, kc, hh * 64 + half * 32:hh * 64 + half * 32 + 32],
                                                   rhs=cqT[:, kc, q2 * 256:q2 * 256 + 256], start=(kc == 0),
                                                   stop=(kc == 2))
                    return ins
                kb.mm(emit, reads=[b_wl[1], b_cq], writes=[b_pa])
                kb.op(kb.dve, V.tensor_tensor, reads=[b_pa, b_tab], writes=[b_t1], out=t1[:, 0:256], in0=pa[0:32, 0:256],
                      in1=cosb[:, t0:t0 + 256], op=ALU.mult)
                kb.op(kb.dve, V.scalar_tensor_tensor, reads=[b_pa, b_tab], writes=[b_t2], out=t2[:, 0:256],
                      in0=pa[0:32, 256:512], scalar=SC, in1=sinb[:, t0:t0 + 256], op0=ALU.mult, op1=ALU.mult)
                kb.op(kb.dve, V.scalar_tensor_tensor, reads=[b_t1, b_t2], writes=[b_Qr], out=QrT[:, t0:t0 + 256],
                      in0=t1[:, 0:256], scalar=SC, in1=t2[:, 0:256], op0=ALU.mult, op1=ALU.add)
            pb, b_pb = banks[6 + pcnt % 2]
            pcnt += 1

            def emit(pb=pb, tk=tk):
                for kc in range(2):
                    ins = nc.tensor.matmul(pb[0:64, :], lhsT=wuk[:, kc, hc], rhs=ckvT[:, kc, tk], start=(kc == 0),
                                           stop=(kc == 1))
                return ins
            kb.mm(emit, reads=[b_wl[2], b_ckv], writes=[b_pb])
            kb.op(kb.dve, V.tensor_copy, reads=[b_pb], writes=[b_Kn], out=KnT[:, tk], in_=pb[0:64, :])
        for t0 in range(0, S // 128, 8):
            pb, b_pb = banks[6 + pcnt % 2]
            pcnt += 1

            def emit(pb=pb, t0=t0):
                for j in range(8):
                    for kc in range(2):
                        ins = nc.tensor.matmul(pb[:, j * 64:(j + 1) * 64],
                                               lhsT=ckvT[:, kc, (t0 + j) * 128:(t0 + j + 1) * 128], rhs=wuv[:, kc, hc],
                                               start=(kc == 0), stop=(kc == 1))
                return ins
            kb.mm(emit, reads=[b_wl[3], b_ckv], writes=[b_pb])
            for j in range(8):
                if j % 2 == 0:
                    kb.op(kb.act, A.activation, reads=[b_pb], writes=[b_va], out=va[:, t0 + j, 0:64],
                          in_=pb[:, j * 64:(j + 1) * 64], func=AF.Copy)
                else:
                    kb.op(kb.dve, V.tensor_copy, reads=[b_pb], writes=[b_va], out=va[:, t0 + j, 0:64],
                          in_=pb[:, j * 64:(j + 1) * 64])
        for qb in range(NQB):
            qk = slice(qb * 512, (qb + 1) * 512)
            nkt = 4 * (qb + 1)
            ob, b_ob = banks[4 + ocnt % 2]
            ocnt += 1
            pend = []
            for kp in range(nkt // 2):
                sp_, sbufs = pairs[acnt % 2]
                pi = acnt % 3
                acnt += 1

                def emit(sp_=sp_, kp=kp):
                    for u in range(2):
                        kt = 2 * kp + u
                        ks = slice(kt * 128, (kt + 1) * 128)
                        o_ = sp_[:, u * 512:(u + 1) * 512]
                        diag = kt >= 4 * qb
                        nc.tensor.matmul(o_, lhsT=KnT[:, ks], rhs=QnT[:, qk], start=True, stop=False,
                                         skip_group_check=True)
                        ins = nc.tensor.matmul(o_, lhsT=krT[:, ks], rhs=QrT[:, qk], start=False, stop=not diag,
                                               skip_group_check=True)
                        if diag:
                            ins = nc.tensor.matmul(o_, lhsT=c.idb[:], rhs=mask[:, kt - 4 * qb, :], start=False,
                                                   stop=True, skip_group_check=True)
                    return ins
                kb.mm(emit, reads=[b_Kn, b_Qn, b_kr, b_Qr, c.b_idb, b_wl[4]], writes=sbufs)
                kb.op(kb.act, A.activation, reads=sbufs, writes=[b_PT[pi]], out=PT[pi][:], in_=sp_, func=AF.Exp)

                def emit2(pi=pi, kp=kp):
                    for u in range(2):
                        kt = 2 * kp + u
                        ins = nc.tensor.matmul(ob, lhsT=va[:, kt, :], rhs=PT[pi][:, u * 512:(u + 1) * 512],
                                               start=(kt == 0), stop=(kt == nkt - 1), skip_group_check=True)
                    return ins
                pend.append((emit2, pi))
                if len(pend) > 1:
                    e2, p2 = pend.pop(0)
                    kb.mm(e2, reads=[b_va, b_PT[p2]], writes=[b_ob])
            while pend:
                e2, p2 = pend.pop(0)
                kb.mm(e2, reads=[b_va, b_PT[p2]], writes=[b_ob])
            oi = ocnt % 3
            kb.op(kb.dve, V.tensor_scalar, reads=[b_ob], writes=[b_rs], out=rs[:], in0=ob[64:128, :], scalar1=1e-30,
                  scalar2=None, op0=ALU.max)
            kb.op(kb.dve, V.reciprocal, reads=[b_rs], writes=[b_rs], out=rs[:], in_=rs[:])
            kb.op(kb.dve, V.tensor_tensor, reads=[b_ob, b_rs], writes=[b_ost[oi]], out=ost[oi][:], in0=ob[0:64, :],
                  in1=rs[:], op=ALU.mult)
            kb.store(kb.sp, o_oT[hh * 64:(hh + 1) * 64, qk], ost[oi][:], reads=[b_ost[oi]])
    kb.finish()
    return nc


def build_phaseC():
    nc = bass.Bass("TRN2", target_bir_lowering=False)
    d = {}
    for name, shape, dt in [
        ("h1e", [NEXT * 128, D], F32), ("oTe", [D, NEXT * 128], BF16), ("cvec", [128, 8], F32), ("hv", [128, 1], F32),
        ("ident", [128, 128], F32), ("mw1", [D, 6 * D], F32), ("mb1", [1, 6 * D], F32), ("n2g1", [128, 8], F32),
        ("wo1", [D, D], F32), ("win_t", [NF, 128, 8 * 256], F32), ("cw", [128, 2 * NF * 3], F32),
        ("cb", [128, 2 * NF], F32), ("wout", [FF, D], F32), ("fg", [1, D], F32),
    ]:
        d[name] = _din(nc, name, shape, dt)
    o_out = _dout(nc, "out", [OWN, D], F32)
    kb = KB(nc)
    es = kb.es
    V, A, G = nc.vector, nc.scalar, nc.gpsimd
    pairs, banks = psum_banks(kb, es)
    c = emit_consts(kb, es, d)
    vec = load_vecs(kb, es, c, {
        "hv": (d["hv"], [128, 1], F32), "n2g1": (d["n2g1"], [128, 8], F32),
        "cw": (d["cw"].rearrange("p (f t) -> p f t", t=3), [128, 2 * NF, 3], F32), "cb": (d["cb"], [128, 2 * NF], F32),
        "fg": (d["fg"].partition_broadcast(128), [128, D], F32),
    })
    bv = c.b_vec
    fm1, bc1 = emit_mod(kb, es, c, [banks[0], banks[1]], d["cvec"], d["mw1"], d["mb1"], 6 * D,
                        want_fm=[3072, 4096], want_bc=[2048, 5120], tag="c")
    a2 = kb.sb(es, [128, 8], F32, "a2")
    b_a2 = Buf()
    kb.op(kb.dve, V.scalar_tensor_tensor, reads=[fm1[4096][1], bv], writes=[b_a2], out=a2[:], in0=fm1[4096][0][:],
          scalar=1.0, in1=vec["n2g1"][:], op0=ALU.add, op1=ALU.mult)
    sh2, b_sh2 = fm1[3072]
    g1bc, b_g1 = bc1[2048]
    g2bc, b_g2 = bc1[5120]
    ssq = kb.sb(es, [128, 32], F32, "ssq")
    b_ssq = Buf()
    h = kb.sb(es, [128, NEXT, 1024], F32, "h")
    b_h = [Buf() for _ in range(NEXT)]
    hv_d = d["h1e"].rearrange("(t p) n -> p t n", p=128)
    with ExitStack() as s3:
        oT = kb.sb(s3, [128, 8, NEXT * 128], BF16, "oT")
        b_oT = Buf()
        ds_o = DSem(kb, "oT")
        kb.load(kb.sp, ds_o, oT[:], d["oTe"].rearrange("(c p) n -> p c n", p=128), writes=[b_oT])
        wo = kb.sb(s3, [128, 8, 1024], BF16, "wo")
        b_wo = Buf()
        ds_wo = DSem(kb, "wo")
        kb.load(kb.pool, ds_wo, wo[:], d["wo1"].rearrange("(c p) n -> p c n", p=128), writes=[b_wo])
        tmp = kb.sb(s3, [128, 1024], F32, "tmp3")
        b_tmp = Buf()
        dsh = [DSem(kb, f"h{i}") for i in range(4)]
        for te in range(NEXT):
            kb.load(kb.sp, dsh[te % 4], h[:, te, :], hv_d[:, te, :], writes=[b_h[te]])
        for te in range(NEXT):
            pair, pb2 = pairs[te % 2]

            def emit(pair=pair, te=te):
                for hh in range(2):
                    for cc in range(8):
                        ins = nc.tensor.matmul(pair[:, hh * 512:(hh + 1) * 512], lhsT=oT[:, cc, te * 128:(te + 1) * 128],
                                               rhs=wo[:, cc, hh * 512:(hh + 1) * 512], start=(cc == 0), stop=(cc == 7))
                return ins
            kb.mm(emit, reads=[b_oT, b_wo], writes=pb2)
            kb.op(kb.dve, V.tensor_tensor, reads=pb2 + [b_g1], writes=[b_tmp], out=tmp[:], in0=pair[:], in1=g1bc[:],
                  op=ALU.mult)
            kb.op(kb.dve, V.tensor_tensor, reads=[b_tmp], writes=[b_h[te]], out=h[:, te, :], in0=h[:, te, :],
                  in1=tmp[:], op=ALU.add)
        kb.barrier()

    def stats(lo):
        with ExitStack() as s4:
            junk = kb.sb(s4, [128, 1024], BF16, "junk2")
            b_junk = Buf()
            kb.op(kb.dve, V.memset, writes=[b_ssq], ap=ssq[:], constant=0.0)
            for te in range(lo, NEXT):
                kb.op(kb.act, A.activation, reads=[b_h[te]], writes=[b_junk, b_ssq], out=junk[:], in_=h[:, te, :],
                      func=AF.Square, accum_out=ssq[:, te:te + 1])
            emit_rstd(kb, es, ssq, b_ssq, NEXT, 1.0 / D)
            kb.barrier()
    stats(0)
    emit_ffn(kb, es, c, h, b_h, NEXT, ssq, b_ssq, a2, b_a2, sh2, b_sh2, g2bc, b_g2, d["win_t"], d["wout"],
             vec["cw"], bv, vec["cb"], bv, vec["hv"], bv, pairs, banks, 128, "C")
    stats(1)
    with ExitStack() as s5:
        ot = [kb.sb(s5, [128, 1024], F32, "ot") for _ in range(3)]
        b_ot = [Buf() for _ in range(3)]
        for te in range(1, NEXT):
            i = te % 3
            kb.op(kb.dve, V.scalar_tensor_tensor, reads=[b_h[te], b_ssq, bv], writes=[b_ot[i]], out=ot[i][:],
                  in0=h[:, te, :], scalar=ssq[:, te:te + 1], in1=vec["fg"][:], op0=ALU.mult, op1=ALU.mult)
            kb.store(kb.sp, o_out[(te - 1) * 128:te * 128, :], ot[i][:], reads=[b_ot[i]])
        kb.barrier()
    kb.finish()
    return nc


_CACHE = {}


def _get(name, fn):
    if name not in _CACHE:
        _CACHE[name] = fn()
    return _CACHE[name]


def kernel(**inp):
    inp = {k: np.asarray(v) for k, v in inp.items()}
    ident = np.eye(128, dtype=np.float32)
    invf, sgn = rope_consts()
    cores = list(range(NCORE))
    ncA = _get("A", build_phaseA)
    resA = run_bass_kernel_spmd(ncA, prep_A(inp), core_ids=cores).results
    ncB = _get("B", build_phaseB)
    wqr = np.asarray(inp["b_wqr"][0]).reshape(384, 16, 32)
    wqr2 = np.concatenate([wqr, wqr[:, :, 16:32], wqr[:, :, 0:16]], axis=2).reshape(384, 16 * 64)
    mask = mla_mask()
    mapsB = []
    for core in cores:
        b, j = core // 4, core % 4
        g = [resA[b * 4 + q] for q in range(4)]
        hs = slice(j * 256, (j + 1) * 256)
        mapsB.append({
            "cqT": np.ascontiguousarray(np.concatenate([np.asarray(r["cqT"]) for r in g], axis=1)),
            "ckvT": np.ascontiguousarray(np.concatenate([np.asarray(r["ckvT"]) for r in g], axis=1)),
            "krT": np.ascontiguousarray(np.concatenate([np.asarray(r["krT"]) for r in g], axis=1)),
            "posq": np.ascontiguousarray(inp["positions"][b][None]).astype(np.int32),
            "invf": invf, "sgn": sgn, "ident": ident,
            "wuq": np.ascontiguousarray(inp["b_wuq"][0][:, hs]), "wqr2": np.ascontiguousarray(wqr2[:, hs]),
            "wuk": np.ascontiguousarray(inp["b_wuk"][:, hs]), "wuv": np.ascontiguousarray(inp["b_wuv"][:, hs]),
            "mask": mask,
        })
    resB = run_bass_kernel_spmd(ncB, mapsB, core_ids=cores).results
    ncC = _get("C", build_phaseC)
    cw1, cb1 = conv_fm(inp["f_conv_w"][1], inp["f_conv_b"][1])
    sharedC = {
        "ident": ident, "mw1": np.ascontiguousarray(inp["mod_w"][1]), "mb1": np.ascontiguousarray(inp["mod_b"][1][None]),
        "n2g1": fm(inp["norm2_g"][1]), "wo1": np.ascontiguousarray(inp["b_wo"][0]),
        "win_t": tile_win(inp["f_win"][1]), "cw": cw1, "cb": cb1, "wout": np.ascontiguousarray(inp["f_wout"][1]),
        "fg": np.ascontiguousarray(inp["final_g"][None]),
    }
    mapsC = []
    for core in cores:
        b, j = core // 4, core % 4
        s0 = j * OWN
        oT_b = np.concatenate([np.asarray(resB[b * 4 + q]["oT"]) for q in range(4)], axis=0)
        oTe = np.zeros((D, NEXT * 128), oT_b.dtype)
        lo = s0 - 128
        src0 = max(lo, 0)
        oTe[:, src0 - lo:] = oT_b[:, src0:s0 + OWN]
        m = dict(sharedC)
        m.update({"h1e": np.ascontiguousarray(np.asarray(resA[core]["h1"])), "oTe": oTe, "cvec": fm(inp["c"][b]),
                  "hv": np.full((128, 1), 1.0 if s0 > 0 else 0.0, np.float32)})
        mapsC.append(m)
    resC = run_bass_kernel_spmd(ncC, mapsC, core_ids=cores).results
    out = np.zeros((2, S, D), np.float32)
    for core in cores:
        b, j = core // 4, core % 4
        out[b, j * OWN:(j + 1) * OWN] = np.asarray(resC[core]["out"])
    return out
```

```python
import os
import numpy as np
import ml_dtypes
from contextlib import ExitStack
import concourse.bass as bass
import concourse.mybir as mybir
from concourse.bass_utils import run_bass_kernel_spmd

F32 = mybir.dt.float32
BF16 = mybir.dt.bfloat16
I32 = mybir.dt.int32
AF = mybir.ActivationFunctionType
ALU = mybir.AluOpType
AX = mybir.AxisListType

NCORE = 8
D = 1024
S = 8192
OWN = 2048
NOWN = 16
NEXT = 17
NALL = 21
FF = 2816
NF = 22
NEG = -30000.0
ARENA_BYTES = 200 * 1024
TWO_PI = 6.283185307179586
PI = 3.141592653589793


class Tok:
    __slots__ = ("sem", "val", "key")

    def __init__(self, sem, val, key):
        self.sem, self.val, self.key = sem, val, key


class EQ:
    def __init__(self, kb, eng, name):
        self.kb, self.e, self.name = kb, eng, name
        self.sem = kb.newsem("q_" + name)
        self.cnt = 0
        self.seen = {}

    def wait(self, *toks):
        for t in toks:
            if t is None:
                continue
            if self.name == "pe" and t.key == "pe":
                continue
            if self.seen.get(t.key, 0) >= t.val:
                continue
            self.e.wait_ge(t.sem, t.val)
            self.seen[t.key] = t.val

    def done(self, ins):
        ins.then_inc(self.sem, 1)
        self.cnt += 1
        return Tok(self.sem, self.cnt, self.name)


class DSem:
    def __init__(self, kb, name):
        self.sem = kb.newsem("d_" + name)
        self.cnt = 0
        self.name = "d_" + name
        self.last = None
        kb.dsems.append(self)

    def add(self, ins):
        ins.then_inc(self.sem, 16)
        self.cnt += 16
        return Tok(self.sem, self.cnt, self.name)


class Buf:
    __slots__ = ("w", "r")

    def __init__(self):
        self.w = None
        self.r = {}


def _use(eq, reads, writes):
    for b in reads:
        eq.wait(b.w)
    for b in writes:
        eq.wait(b.w)
        eq.wait(*b.r.values())


def _fin(tok, reads, writes):
    for b in reads:
        b.r[tok.key] = tok
    for b in writes:
        b.w = tok
        b.r = {}


class KB:
    def __init__(self, nc):
        self.nc = nc
        self.es = ExitStack()
        self.nsem = 0
        self.dsems = []
        self.pe = EQ(self, nc.tensor, "pe")
        self.act = EQ(self, nc.scalar, "act")
        self.dve = EQ(self, nc.vector, "dve")
        self.pool = EQ(self, nc.gpsimd, "pool")
        self.sp = EQ(self, nc.sync, "sp")
        self.uid = 0
        self.arena = None
        self.peak = 0
        self.st_sem = DSem(self, "store")
        self.st_last = None

    def newsem(self, name):
        self.nsem += 1
        return self.es.enter_context(self.nc.semaphore(name))

    def sb(self, es, shape, dt, name=None):
        if self.arena is None:
            self.arena = self.es.enter_context(self.nc.sbuf_tensor("arena", [128, ARENA_BYTES // 2], BF16))
            self.free = [(0, ARENA_BYTES)]
        esz = 2 if dt == BF16 else 4
        n = 1
        for x in shape[1:]:
            n *= x
        nbytes = (n * esz + 63) // 64 * 64
        top = es is self.es
        order = range(len(self.free) - 1, -1, -1) if top else range(len(self.free))
        for i in order:
            o, sz = self.free[i]
            if sz >= nbytes:
                off = o + sz - nbytes if top else o
                if sz == nbytes:
                    self.free.pop(i)
                elif top:
                    self.free[i] = (o, sz - nbytes)
                else:
                    self.free[i] = (o + nbytes, sz - nbytes)
                break
        else:
            raise RuntimeError(f"SBUF arena full allocating {name} {shape} ({nbytes}B); free={self.free}")
        self.peak = max(self.peak, ARENA_BYTES - sum(z for _, z in self.free))

        def release(off=off, nbytes=nbytes):
            self.free.append((off, nbytes))
            self.free.sort()
            merged = []
            for o, z in self.free:
                if merged and merged[-1][0] + merged[-1][1] == o:
                    merged[-1] = (merged[-1][0], merged[-1][1] + z)
                else:
                    merged.append((o, z))
            self.free = merged
        es.callback(release)
        ap = self.arena[0:shape[0], off // 2:(off + n * esz) // 2]
        if dt != BF16:
            ap = ap.bitcast(dt)
        if len(shape) == 3:
            ap = ap.rearrange("p (a b) -> p a b", a=shape[1])
        elif len(shape) == 4:
            ap = ap.rearrange("p (a b c) -> p a b c", a=shape[1], b=shape[2])
        return ap

    def ps(self, es, shape, dt, name=None):
        self.uid += 1
        return es.enter_context(self.nc.psum_tensor(f"{name or 'p'}_{self.uid}", list(shape), dt))

    def op(self, eq, fn, reads=(), writes=(), **kw):
        _use(eq, reads, writes)
        tok = eq.done(fn(**kw))
        _fin(tok, reads, writes)
        return tok

    def mm(self, emit, reads=(), writes=()):
        _use(self.pe, reads, writes)
        ins = emit()
        tok = self.pe.done(ins)
        _fin(tok, reads, writes)
        return tok

    def load(self, q, dsem, out, in_, writes=(), reads=(), chain=True):
        _use(q, reads, writes)
        if chain:
            q.wait(dsem.last)
        tok = dsem.add(q.e.dma_start(out=out, in_=in_))
        dsem.last = tok
        _fin(tok, reads, writes)
        return tok

    def store(self, q, out, in_, reads=()):
        _use(q, reads, ())
        tok = self.st_sem.add(q.e.dma_start(out=out, in_=in_))
        _fin(tok, reads, ())
        self.st_last = tok
        return tok

    def barrier(self):
        qs = (self.pe, self.act, self.dve, self.pool, self.sp)
        toks = [Tok(q.sem, q.cnt, q.name) for q in qs if q.cnt]
        toks += [Tok(d.sem, d.cnt, d.name) for d in self.dsems if d.cnt]
        for q in qs:
            q.wait(*toks)

    def finish(self):
        if self.st_last is not None:
            self.sp.wait(self.st_last)
        for q in (self.pe, self.act, self.dve, self.pool):
            if q.cnt:
                self.sp.wait(Tok(q.sem, q.cnt, q.name))
        self.es.close()


def fm(v):
    v = np.asarray(v)
    return np.ascontiguousarray(v.reshape(-1, 128).T)


class Consts:
    pass


def emit_consts(kb, es, dram):
    nc = kb.nc
    c = Consts()
    c.dsem = DSem(kb, "const")
    c.idf = kb.sb(es, [128, 128], F32, "idf")
    c.idb = kb.sb(es, [128, 128], BF16, "idb")
    c.ones = kb.sb(es, [128, 128], F32, "ones")
    c.b_idf, c.b_idb, c.b_ones = Buf(), Buf(), Buf()
    kb.load(kb.sp, DSem(kb, "ident"), c.idf[:], dram["ident"], writes=[c.b_idf])
    kb.op(kb.dve, nc.vector.tensor_copy, reads=[c.b_idf], writes=[c.b_idb], out=c.idb[:], in_=c.idf[:])
    kb.op(kb.dve, nc.vector.memset, writes=[c.b_ones], ap=c.ones[:], constant=1.0)
    return c


def emit_mod(kb, es, c, banks, cvec_d, mw_d, mb_d, ncols, want_fm, want_bc, tag):
    nc = kb.nc
    ds = DSem(kb, "modc" + tag)
    ds2 = DSem(kb, "modb" + tag)
    out_fm, out_bc = {}, {}
    nblk = ncols // 512
    with ExitStack() as les:
        cv = kb.sb(les, [128, 8], F32, "cv")
        b_cv = Buf()
        kb.load(kb.sp, ds, cv[:], cvec_d, writes=[b_cv])
        kb.op(kb.act, nc.scalar.activation, reads=[b_cv], writes=[b_cv], out=cv[:], in_=cv[:], func=AF.Silu)
        crep = kb.sb(les, [128, 8, 128], F32, "crep")
        b_crep = Buf()
        for k in range(8):
            kb.op(kb.dve, nc.vector.tensor_scalar, reads=[b_cv, c.b_ones], writes=[b_crep],
                  out=crep[:, k, :], in0=c.ones[:], scalar1=cv[:, k:k + 1], scalar2=None, op0=ALU.mult)
        wbuf = [kb.sb(les, [128, 8, 512], F32, "mwb") for _ in range(2)]
        wsem = [DSem(kb, f"mw{tag}{i}") for i in range(2)]
        b_w = [Buf(), Buf()]
        mbb = kb.sb(les, [128, 1024], F32, "mbb")
        b_mbb = Buf()
        bc = kb.sb(les, [128, 1024], F32, "bctmp")
        b_bc = Buf()
        wanted = sorted(set(want_fm) | set(want_bc))
        blocks = [(s0, h) for s0 in wanted for h in range(2)]
        mwv = mw_d.rearrange("(c p) n -> p c n", p=128)

        def issue(i):
            s0, h = blocks[i]
            col = s0 + h * 512
            kb.load(kb.sp, wsem[i % 2], wbuf[i % 2][:], mwv[:, :, col:col + 512], writes=[b_w[i % 2]])

        issue(0)
        for i, (s0, h) in enumerate(blocks):
            if i + 1 < len(blocks):
                issue(i + 1)
            if h == 0:
                kb.load(kb.sp, ds2, mbb[:], mb_d[0:1, s0:s0 + 1024].partition_broadcast(128), writes=[b_mbb])
                if s0 in want_bc:
                    t = kb.sb(es, [128, 1024], F32, "modbc")
                    out_bc[s0] = (t, Buf())
                dst, b_dst = out_bc[s0] if s0 in want_bc else (bc, b_bc)
            pb, b_pb = banks[i % 2]
            wb = wbuf[i % 2]

            def emit():
                for k in range(8):
                    ins = nc.tensor.matmul(pb, lhsT=crep[:, k, :], rhs=wb[:, k, :], start=(k == 0), stop=(k == 7))
                return ins
            kb.mm(emit, reads=[b_crep, b_w[i % 2]], writes=[b_pb])
            kb.op(kb.dve, nc.vector.tensor_tensor, reads=[b_pb, b_mbb], writes=[b_dst],
                  out=dst[:, h * 512:(h + 1) * 512], in0=pb, in1=mbb[:, h * 512:(h + 1) * 512], op=ALU.add)
            if h == 1 and s0 in want_fm:
                t = kb.sb(es, [128, 8], F32, "modfm")
                bt = Buf()
                pb2, b_pb2 = banks[(i + 1) % 2]

                def emit2():
                    for cc in range(8):
                        ins = nc.tensor.matmul(pb2[:, cc:cc + 1], lhsT=dst[:, cc * 128:(cc + 1) * 128],
                                               rhs=c.idf[:, 0:1], start=True, stop=True)
                    return ins
                kb.mm(emit2, reads=[b_dst, c.b_idf], writes=[b_pb2])
                kb.op(kb.dve, nc.vector.tensor_copy, reads=[b_pb2], writes=[bt], out=t[:], in_=pb2[:, 0:8])
                out_fm[s0] = (t, bt)
        kb.barrier()
    return out_fm, out_bc


def psum_banks(kb, es):
    pt = [kb.ps(es, [128, 1024], F32, "pp") for _ in range(4)]
    bufs = [Buf() for _ in range(8)]
    banks = [(pt[b // 2][:, (b % 2) * 512:(b % 2) * 512 + 512], bufs[b]) for b in range(8)]
    pairs = [(pt[p][:], [bufs[2 * p], bufs[2 * p + 1]]) for p in range(4)]
    return pairs, banks


def emit_rstd(kb, es, ssq, b_ssq, n, inv_n, eps=1e-6):
    nc = kb.nc
    kb.op(kb.dve, nc.vector.tensor_scalar, reads=[b_ssq], writes=[b_ssq], out=ssq[:, 0:n], in0=ssq[:, 0:n],
          scalar1=inv_n, scalar2=eps, op0=ALU.mult, op1=ALU.add)
    kb.op(kb.act, nc.scalar.activation, reads=[b_ssq], writes=[b_ssq], out=ssq[:, 0:n], in_=ssq[:, 0:n], func=AF.Sqrt)
    kb.op(kb.dve, nc.vector.reciprocal, reads=[b_ssq], writes=[b_ssq], out=ssq[:, 0:n], in_=ssq[:, 0:n])


def emit_norm_T(kb, c, tiles, rstd, b_rstd, outs, ppairs, xnb, b_xnb, cnt=[0]):
    nc = kb.nc
    i = 0
    while i < len(tiles):
        grp = tiles[i:i + 2]
        g = cnt[0]
        cnt[0] += 1
        pair, pb = ppairs[g % 2]
        pv = pair.bitcast(BF16).rearrange("p (c t) -> p c t", c=8)
        for j, (src, b_src, rc, doff) in enumerate(grp):
            xi = (g % 2) * 2 + j
            kb.op(kb.act, nc.scalar.activation, reads=[b_src, b_rstd], writes=[b_xnb[xi]],
                  out=xnb[xi][:], in_=src, func=AF.Copy, scale=rstd[:, rc:rc + 1])
        for j, (src, b_src, rc, doff) in enumerate(grp):
            xi = (g % 2) * 2 + j

            def emit(j=j, xi=xi):
                for cc in range(8):
                    ins = nc.tensor.transpose(out=pv[:, cc, j * 128:(j + 1) * 128],
                                              in_=xnb[xi][:, cc * 128:(cc + 1) * 128], identity=c.idb[:])
                return ins
            kb.mm(emit, reads=[b_xnb[xi], c.b_idb], writes=pb)
        n = 128 * len(grp)
        doff = grp[0][3]
        k = 0
        for (dst, b_dst, a, b_a, sh, b_sh) in outs:
            for cc in range(8):
                if k % 2 == 0:
                    kb.op(kb.act, nc.scalar.activation, reads=pb + [b_a, b_sh], writes=[b_dst],
                          out=dst[:, cc, doff:doff + n], in_=pv[:, cc, 0:n], func=AF.Identity,
                          scale=a[:, cc:cc + 1], bias=sh[:, cc:cc + 1])
                else:
                    kb.op(kb.dve, nc.vector.tensor_scalar, reads=pb + [b_a, b_sh], writes=[b_dst],
                          out=dst[:, cc, doff:doff + n], in0=pv[:, cc, 0:n], scalar1=a[:, cc:cc + 1],
                          scalar2=sh[:, cc:cc + 1], op0=ALU.mult, op1=ALU.add)
                k += 1
        i += 2


def emit_ffn(kb, es, c, h, b_h, nt, rstd, b_rstd, a2, b_a2, sh2, b_sh2, g2bc, b_g2, win_d, wout_d,
             cw, b_cw, cb, b_cb, hv, b_hv, pairs, banks, fix_tok, tag):
    nc = kb.nc
    ntok = nt * 128
    blocks = [(s0, min(512, ntok - s0)) for s0 in range(0, ntok, 512)]
    with ExitStack() as les:
        wout = kb.sb(les, [128, NF, 1024], BF16, "wout")
        b_wout = Buf()
        ds_wout = DSem(kb, "wout" + tag)
        wov = wout_d.rearrange("(f p) n -> p f n", p=128)
        for f0 in range(0, NF, 6):
            f1 = min(NF, f0 + 6)
            kb.load(kb.pool, ds_wout, wout[:, f0:f1, :], wov[:, f0:f1, :], writes=[b_wout])
        hnT = kb.sb(les, [128, 8, 512], BF16, "hn2T")
        b_hnT = Buf()
        actT = kb.sb(les, [128, NF, 512], BF16, "actT")
        b_actT = Buf()
        xnb = [kb.sb(les, [128, 1024], BF16, "xnb") for _ in range(4)]
        b_xnb = [Buf() for _ in range(4)]
        wsl = [kb.sb(les, [128, 8, 256], BF16, "winsl") for _ in range(3)]
        b_wsl = [Buf() for _ in range(3)]
        ds_w = [DSem(kb, f"win{tag}{i}") for i in range(3)]
        ug = [kb.sb(les, [128, 514], F32, "ug") for _ in range(2)]
        b_ug = [Buf() for _ in range(2)]
        yy = [kb.sb(les, [128, 512], F32, "yy") for _ in range(2)]
        b_yy = [Buf() for _ in range(2)]
        sgs = [kb.sb(les, [128, 512], F32, "sg") for _ in range(1)]
        b_sgs = [Buf()]
        halo = kb.sb(les, [128, 2 * NF, 2], F32, "halo")
        b_halo = Buf()
        tmp = kb.sb(les, [128, 1024], F32, "ftmp")
        b_tmp = Buf()
        kb.op(kb.dve, nc.vector.memset, writes=[b_halo], ap=halo[:], constant=0.0)
        jobs = [(bi, f) for bi in range(len(blocks)) for f in range(NF)]

        def issue_w(ji):
            bi, f = jobs[ji]
            sl = ji % 3
            kb.load(kb.pool, ds_w[sl], wsl[sl][:], win_d[f], writes=[b_wsl[sl]])

        issue_w(0)
        issue_w(1)
        for ji, (bi, f) in enumerate(jobs):
            s0, n = blocks[bi]
            if f == 0:
                tl = [(h[:, (s0 // 128) + j, :], b_h[(s0 // 128) + j], (s0 // 128) + j, j * 128)
                      for j in range(n // 128)]
                emit_norm_T(kb, c, tl, rstd, b_rstd, [(hnT, b_hnT, a2, b_a2, sh2, b_sh2)], pairs[0:2], xnb, b_xnb)
            if ji + 2 < len(jobs):
                issue_w(ji + 2)
            sl = ji % 3
            for gv in range(2):
                fi = f + gv * NF
                pb, b_pb = banks[4 + ((ji * 2 + gv) % 4)]

                def emit(gv=gv, pb=pb):
                    for k in range(8):
                        ins = nc.tensor.matmul(pb[:, 0:n], lhsT=wsl[sl][:, k, gv * 128:(gv + 1) * 128], rhs=hnT[:, k, 0:n],
                                               start=(k == 0), stop=(k == 7))
                    return ins
                kb.mm(emit, reads=[b_wsl[sl], b_hnT], writes=[b_pb])
                u, b_u = ug[gv], b_ug[gv]
                y, b_y = yy[gv], b_yy[gv]
                kb.op(kb.act, nc.scalar.activation, reads=[b_pb], writes=[b_u], out=u[:, 2:2 + n], in_=pb[:, 0:n],
                      func=AF.Copy)
                kb.op(kb.dve, nc.vector.tensor_copy, reads=[b_halo], writes=[b_u], out=u[:, 0:2], in_=halo[:, fi, :])
                if fix_tok is not None and s0 <= fix_tok - 2 and fix_tok <= s0 + n:
                    o = fix_tok - s0
                    kb.op(kb.dve, nc.vector.tensor_scalar, reads=[b_hv], writes=[b_u], out=u[:, o:o + 2],
                          in0=u[:, o:o + 2], scalar1=hv[:, 0:1], scalar2=None, op0=ALU.mult)
                kb.op(kb.act, nc.scalar.activation, reads=[b_pb, b_cw, b_cb], writes=[b_y], out=y[:, 0:n],
                      in_=pb[:, 0:n], func=AF.Identity, scale=cw[:, fi, 2:3], bias=cb[:, fi:fi + 1])
                kb.op(kb.dve, nc.vector.tensor_copy, reads=[b_u], writes=[b_halo], out=halo[:, fi, :],
                      in_=u[:, n:n + 2])
                kb.op(kb.dve, nc.vector.scalar_tensor_tensor, reads=[b_u, b_cw], writes=[b_y], out=y[:, 0:n],
                      in0=u[:, 1:1 + n], scalar=cw[:, fi, 1:2], in1=y[:, 0:n], op0=ALU.mult, op1=ALU.add)
                kb.op(kb.dve, nc.vector.scalar_tensor_tensor, reads=[b_u, b_cw], writes=[b_y], out=y[:, 0:n],
                      in0=u[:, 0:n], scalar=cw[:, fi, 0:1], in1=y[:, 0:n], op0=ALU.mult, op1=ALU.add)
            yg_, b_yg_ = yy[0], b_yy[0]
            yv_, b_yv_ = yy[1], b_yy[1]
            sg, b_sg = sgs[0], b_sgs[0]
            kb.op(kb.act, nc.scalar.activation, reads=[b_yg_], writes=[b_sg], out=sg[:, 0:n], in_=yg_[:, 0:n],
                  func=AF.Silu)
            kb.op(kb.dve, nc.vector.tensor_tensor, reads=[b_sg, b_yv_], writes=[b_actT], out=actT[:, f, 0:n],
                  in0=sg[:, 0:n], in1=yv_[:, 0:n], op=ALU.mult)
            if f == NF - 1:
                for j in range(n // 128):
                    ti = s0 // 128 + j
                    pair, pb2 = pairs[ti % 2]

                    def emit3(j=j, pair=pair):
                        for hh in range(2):
                            for ff in range(NF):
                                ins = nc.tensor.matmul(pair[:, hh * 512:(hh + 1) * 512],
                                                       lhsT=actT[:, ff, j * 128:(j + 1) * 128],
                                                       rhs=wout[:, ff, hh * 512:(hh + 1) * 512],
                                                       start=(ff == 0), stop=(ff == NF - 1))
                        return ins
                    kb.mm(emit3, reads=[b_actT, b_wout], writes=pb2)
                    kb.op(kb.dve, nc.vector.tensor_tensor, reads=pb2 + [b_g2], writes=[b_tmp], out=tmp[:],
                          in0=pair[:], in1=g2bc[:], op=ALU.mult)
                    kb.op(kb.dve, nc.vector.tensor_tensor, reads=[b_tmp], writes=[b_h[ti]], out=h[:, ti, :],
                          in0=h[:, ti, :], in1=tmp[:], op=ALU.add)
        kb.barrier()


def _din(nc, name, shape, dt=F32):
    return nc.dram_tensor(name, list(shape), dt, kind="ExternalInput").ap()


def _dout(nc, name, shape, dt=F32):
    return nc.dram_tensor(name, list(shape), dt, kind="ExternalOutput").ap()


def load_vecs(kb, es, c, specs):
    out = {}
    for name, (ap, shape, dt) in specs.items():
        t = kb.sb(es, shape, dt, name)
        kb.sp.e.dma_start(out=t[:], in_=ap).then_inc(c.dsem.sem, 16)
        c.dsem.cnt += 16
        out[name] = t
    c.b_vec = Buf()
    c.b_vec.w = Tok(c.dsem.sem, c.dsem.cnt, c.dsem.name)
    return out


def emit_rope_tables(kb, es, pos_i, b_pos, n, invf, sgn, b_vec, cosT, sinT, b_tab, p0=0):
    nc = kb.nc
    ang = kb.sb(es, [p0 + 32, n], F32, "ang")[p0:p0 + 32]
    kf = kb.sb(es, [p0 + 32, n], F32, "kf")[p0:p0 + 32]
    ki = kb.sb(es, [p0 + 32, n], I32, "ki")[p0:p0 + 32]
    m = kb.sb(es, [p0 + 32, n], F32, "mm")[p0:p0 + 32]
    b = Buf()
    C1 = 6.28125
    C2 = TWO_PI - C1
    V = nc.vector
    op = lambda fn, **kw: kb.op(kb.dve, fn, reads=[b_pos, b_vec], writes=[b, b_tab], **kw)
    op(V.tensor_copy, out=ang[:], in_=pos_i)
    op(V.tensor_scalar, out=ang[:], in0=ang[:], scalar1=invf[:, 0:1], scalar2=None, op0=ALU.mult)
    op(V.tensor_scalar, out=kf[:], in0=ang[:], scalar1=1.0 / TWO_PI, scalar2=None, op0=ALU.mult)
    op(V.tensor_copy, out=ki[:], in_=kf[:])
    op(V.tensor_copy, out=kf[:], in_=ki[:])
    op(V.scalar_tensor_tensor, out=ang[:], in0=kf[:], scalar=-C1, in1=ang[:], op0=ALU.mult, op1=ALU.add)
    op(V.scalar_tensor_tensor, out=ang[:], in0=kf[:], scalar=-C2, in1=ang[:], op0=ALU.mult, op1=ALU.add)

    def wrap(t):
        op(V.tensor_scalar, out=m[:], in0=t[:], scalar1=PI, scalar2=TWO_PI, op0=ALU.is_gt, op1=ALU.mult)
        op(V.tensor_tensor, out=t[:], in0=t[:], in1=m[:], op=ALU.subtract)
        op(V.tensor_scalar, out=m[:], in0=t[:], scalar1=-PI, scalar2=TWO_PI, op0=ALU.is_lt, op1=ALU.mult)
        op(V.tensor_tensor, out=t[:], in0=t[:], in1=m[:], op=ALU.add)
    wrap(ang)
    kb.op(kb.act, nc.scalar.activation, reads=[b], writes=[b_tab], out=sinT, in_=ang[:], func=AF.Sin)
    op(V.tensor_scalar, out=sinT, in0=sinT, scalar1=sgn[:, 0:1], scalar2=None, op0=ALU.mult)
    op(V.tensor_scalar, out=kf[:], in0=ang[:], scalar1=PI / 2, scalar2=None, op0=ALU.add)
    wrap(kf)
    kb.op(kb.act, nc.scalar.activation, reads=[b], writes=[b_tab], out=cosT, in_=kf[:], func=AF.Sin)


def emit_latents(kb, es, c, d, h, b_h, ssq, b_ssq, akv, b_akv, shkv, b_shkv, a1n, b_a1n, sh1n, b_sh1n, vec, bv,
                 pairs, banks, o_ckv, o_kr, o_cq, stop=99):
    nc = kb.nc
    V, A, G = nc.vector, nc.scalar, nc.gpsimd
    with ExitStack() as s6:
        junk = kb.sb(s6, [128, 1024], BF16, "junk6")
        b_junk = Buf()
        kb.op(kb.dve, V.memset, writes=[b_ssq], ap=ssq[:], constant=0.0)
        for te in range(1, NEXT):
            kb.op(kb.act, A.activation, reads=[b_h[te]], writes=[b_junk, b_ssq], out=junk[:], in_=h[:, te, :],
                  func=AF.Square, accum_out=ssq[:, te:te + 1])
        emit_rstd(kb, es, ssq, b_ssq, NEXT, 1.0 / D)
        wdkv = kb.sb(s6, [128, 8, 256], BF16, "wdkv")
        wdq = kb.sb(s6, [128, 8, 384], BF16, "wdq")
        wkr = kb.sb(s6, [128, 8, 64], BF16, "wkr")
        b_w = Buf()
        dsw = [DSem(kb, f"lw{i}") for i in range(3)]
        kb.load(kb.pool, dsw[0], wdkv[:], d["wdkv"].rearrange("(k p) n -> p k n", p=128), writes=[b_w])
        t2 = kb.load(kb.pool, dsw[1], wdq[:], d["wdq"].rearrange("(k p) n -> p k n", p=128), writes=[])
        t3 = kb.load(kb.pool, dsw[2], wkr[:], d["wkr2"].rearrange("(k p) n -> p k n", p=128), writes=[])
        b_w2, b_w3 = Buf(), Buf()
        b_w2.w, b_w3.w = t2, t3
        ssl = kb.sb(s6, [128, NOWN, 2], F32, "ssl")
        b_ssl = [Buf() for _ in range(NOWN)]
        kb.dve.done(V.memset(ap=ssl[:], constant=0.0))
        for ti in range(NOWN):
            b_ssl[ti].w = Tok(kb.dve.sem, kb.dve.cnt, "dve")
        krT = kb.sb(s6, [32, OWN], BF16, "krT")
        b_krT = Buf()
        ckvT = kb.sb(s6, [128, 2, OWN], BF16, "ckvTo")
        cqT = kb.sb(s6, [128, 3, OWN], BF16, "cqTo")
        b_ckvT, b_cqT = Buf(), Buf()
        cosT = kb.sb(s6, [32, OWN], F32, "cosT")
        sinT = kb.sb(s6, [32, OWN], F32, "sinT")
        b_tab = Buf()
        with ExitStack() as sr:
            posi = kb.sb(sr, [32, OWN], I32, "posi")
            b_pos = Buf()
            dsp = DSem(kb, "posk")
            kb.load(kb.sp, dsp, posi[:], d["posk"].partition_broadcast(32), writes=[b_pos])
            emit_rope_tables(kb, sr, posi[:], b_pos, OWN, vec["invf"], vec["sgn"], bv, cosT[:], sinT[:], b_tab)
            kb.barrier()
        if stop == 7:
            kb.barrier()
            return
        hkvT = kb.sb(s6, [128, 8, 512], BF16, "hkvT")
        hqT = kb.sb(s6, [128, 8, 512], BF16, "hqT")
        b_hkvT, b_hqT = Buf(), Buf()
        xnb = [kb.sb(s6, [128, 1024], BF16, "xnb6") for _ in range(4)]
        b_xnb = [Buf() for _ in range(4)]
        t1 = kb.sb(s6, [32, 512], F32, "rt1")
        t2_ = kb.sb(s6, [32, 512], F32, "rt2")
        b_t1, b_t2 = Buf(), Buf()
        lnb = [kb.sb(s6, [128, 640], BF16, "lnb") for _ in range(2)]
        b_lnb = [Buf(), Buf()]
        for blk in range(4):
            tl = [(h[:, 1 + 4 * blk + j, :], b_h[1 + 4 * blk + j], 1 + 4 * blk + j, j * 128) for j in range(4)]
            emit_norm_T(kb, c, tl, ssq, b_ssq, [(hkvT, b_hkvT, akv, b_akv, shkv, b_shkv),
                                                (hqT, b_hqT, a1n, b_a1n, sh1n, b_sh1n)], pairs[0:2], xnb, b_xnb)
            for j in range(4):
                ti = 4 * blk + j
                pbs = []
                for (src, b_src, w, b_ww, ncol, row, inv_n) in ((hkvT, b_hkvT, wdkv, b_w, 256, 0, 1.0 / 256),
                                                              (hqT, b_hqT, wdq, b_w2, 384, 1, 1.0 / 384)):
                    pb, b_pb = banks[4 + (2 * ti + row) % 4]
                    pbs.append((pb, b_pb))

                    def emit(pb=pb, src=src, w=w, ncol=ncol, j=j):
                        for k in range(8):
                            ins = nc.tensor.matmul(pb[:, 0:ncol], lhsT=src[:, k, j * 128:(j + 1) * 128], rhs=w[:, k, :],
                                                   start=(k == 0), stop=(k == 7))
                        return ins
                    kb.mm(emit, reads=[b_src, b_ww], writes=[b_pb])
                    kb.op(kb.act, A.activation, reads=[b_pb], writes=[b_junk, b_ssl[ti]], out=junk[:, 0:ncol],
                          in_=pb[:, 0:ncol], func=AF.Square, accum_out=ssl[:, ti, row:row + 1])
                    kb.op(kb.dve, V.tensor_scalar, reads=[], writes=[b_ssl[ti]], out=ssl[:, ti, row:row + 1],
                          in0=ssl[:, ti, row:row + 1], scalar1=inv_n, scalar2=1e-6, op0=ALU.mult, op1=ALU.add)
                kb.op(kb.act, A.activation, reads=[], writes=[b_ssl[ti]], out=ssl[:, ti, :], in_=ssl[:, ti, :],
                      func=AF.Sqrt)
                kb.op(kb.dve, V.reciprocal, reads=[], writes=[b_ssl[ti]], out=ssl[:, ti, :], in_=ssl[:, ti, :])
                lb, b_lb = lnb[ti % 2], b_lnb[ti % 2]
                kb.op(kb.dve, V.scalar_tensor_tensor, reads=[pbs[0][1], b_ssl[ti], bv], writes=[b_lb], out=lb[:, 0:256],
                      in0=pbs[0][0][:, 0:256], scalar=ssl[:, ti, 0:1], in1=vec["latg"][:], op0=ALU.mult, op1=ALU.mult)
                kb.op(kb.dve, V.scalar_tensor_tensor, reads=[pbs[1][1], b_ssl[ti], bv], writes=[b_lb], out=lb[:, 256:640],
                      in0=pbs[1][0][:, 0:384], scalar=ssl[:, ti, 1:2], in1=vec["qg"][:], op0=ALU.mult, op1=ALU.mult)
                if os.environ.get("SKIP_TR"):
                    continue
                pb, b_pb = banks[2 + ti % 2]
                pv = pairs[1][0].bitcast(BF16)[:, (ti % 2) * 1024:(ti % 2) * 1024 + 1024]

                def emit(pv=pv, lb=lb):
                    for cc in range(5):
                        ins = nc.tensor.transpose(out=pv[:, cc * 128:(cc + 1) * 128], in_=lb[:, cc * 128:(cc + 1) * 128],
                                                  identity=c.idb[:])
                    return ins
                kb.mm(emit, reads=[b_lb, c.b_idb], writes=[b_pb])
                for cc in range(5):
                    dst, b_dst = (ckvT[:, cc, ti * 128:(ti + 1) * 128], b_ckvT) if cc < 2 else \
                        (cqT[:, cc - 2, ti * 128:(ti + 1) * 128], b_cqT)
                    if cc % 2 == 0:
                        kb.op(kb.dve, V.tensor_copy, reads=[b_pb], writes=[b_dst], out=dst, in_=pv[:, cc * 128:(cc + 1) * 128])
                    else:
                        kb.op(kb.act, A.activation, reads=[b_pb], writes=[b_dst], out=dst, in_=pv[:, cc * 128:(cc + 1) * 128],
                              func=AF.Copy)
            if os.environ.get("SKIP_ROPE"):
                continue
            pa, b_pa = banks[0]
            pbw, b_pbw = banks[1]
            for (pp, b_pp, c0) in ((pa, b_pa, 0), (pbw, b_pbw, 32)):
                def emit(pp=pp, c0=c0):
                    for k in range(8):
                        ins = nc.tensor.matmul(pp[0:32, :], lhsT=wkr[:, k, c0:c0 + 32], rhs=hkvT[:, k, :],
                                               start=(k == 0), stop=(k == 7))
                    return ins
                kb.mm(emit, reads=[b_hkvT, b_w3], writes=[b_pp])
            tk = slice(blk * 512, (blk + 1) * 512)
            kb.op(kb.dve, V.tensor_tensor, reads=[b_pa, b_tab], writes=[b_t1], out=t1[:], in0=pa[0:32, :],
                  in1=cosT[:, tk], op=ALU.mult)
            kb.op(kb.dve, V.tensor_tensor, reads=[b_pbw, b_tab], writes=[b_t2], out=t2_[:], in0=pbw[0:32, :],
                  in1=sinT[:, tk], op=ALU.mult)
            kb.op(kb.dve, V.tensor_tensor, reads=[b_t1, b_t2], writes=[b_krT], out=krT[:, tk], in0=t1[:], in1=t2_[:],
                  op=ALU.add)
        if stop == 8:
            kb.barrier()
            return
        kb.store(kb.sp, o_ckv.rearrange("(c p) n -> p c n", p=128), ckvT[:], reads=[b_ckvT])
        kb.store(kb.sp, o_cq.rearrange("(c p) n -> p c n", p=128), cqT[:], reads=[b_cqT])
        kb.store(kb.sp, o_kr, krT[:], reads=[b_krT])
        kb.barrier()


def build_phaseA(dbg=False, stop=99):
    nc = bass.Bass("TRN2", target_bir_lowering=False)
    d = {}
    for name, shape, dt in [
        ("xe", [NALL * 128, D], F32), ("valid", [128, NALL], F32), ("posk", [1, OWN], I32), ("cvec", [128, 8], F32),
        ("invf", [32, 1], F32), ("sgn", [32, 1], F32), ("hv", [128, 1], F32), ("ident", [128, 128], F32),
        ("mw0", [D, 6 * D], F32), ("mb0", [1, 6 * D], F32), ("mw1", [D, 2 * D], F32), ("mb1", [1, 2 * D], F32),
        ("kvmw", [D, 2 * D], F32), ("kvmb", [1, 2 * D], F32),
        ("n1g0", [128, 8], F32), ("n2g0", [128, 8], F32), ("n1g1", [128, 8], F32), ("kvng", [128, 8], F32),
        ("wqkv_t", [8, 128, 8 * 384], F32), ("wo", [D, D], F32), ("tb", [128, 16 * 640], F32),
        ("win_t", [NF, 128, 8 * 256], F32), ("cw", [128, 2 * NF * 3], F32), ("cb", [128, 2 * NF], F32),
        ("wout", [FF, D], F32), ("wdkv", [D, 256], F32), ("latg", [1, 256], F32), ("wkr2", [D, 64], F32),
        ("wdq", [D, 384], F32), ("qg", [1, 384], F32),
    ]:
        d[name] = _din(nc, name, shape, dt)
    o_h1 = _dout(nc, "h1", [NEXT * 128, D], F32)
    o_ckv = _dout(nc, "ckvT", [256, OWN], BF16)
    o_kr = _dout(nc, "krT", [32, OWN], BF16)
    o_cq = _dout(nc, "cqT", [384, OWN], BF16)
    if dbg:
        o_dbg = _dout(nc, "dbg", [NEXT * 128, D], F32)

    kb = KB(nc)
    es = kb.es
    V, A, G = nc.vector, nc.scalar, nc.gpsimd
    pairs, banks = psum_banks(kb, es)
    c = emit_consts(kb, es, d)
    vec = load_vecs(kb, es, c, {
        "valid": (d["valid"], [128, NALL], F32), "hv": (d["hv"], [128, 1], F32),
        "invf": (d["invf"], [32, 1], F32), "sgn": (d["sgn"], [32, 1], F32),
        "n1g0": (d["n1g0"], [128, 8], F32), "n2g0": (d["n2g0"], [128, 8], F32),
        "n1g1": (d["n1g1"], [128, 8], F32), "kvng": (d["kvng"], [128, 8], F32),
        "cw": (d["cw"].rearrange("p (f t) -> p f t", t=3), [128, 2 * NF, 3], F32), "cb": (d["cb"], [128, 2 * NF], F32),
        "latg": (d["latg"].partition_broadcast(128), [128, 256], F32),
        "qg": (d["qg"].partition_broadcast(128), [128, 384], F32),
    })
    bv = c.b_vec
    fm0, bc0 = emit_mod(kb, es, c, [banks[0], banks[1]], d["cvec"], d["mw0"], d["mb0"], 6 * D,
                        want_fm=[0, 1024, 3072, 4096], want_bc=[2048, 5120], tag="0")
    fm1, _ = emit_mod(kb, es, c, [banks[0], banks[1]], d["cvec"], d["mw1"], d["mb1"], 2 * D,
                      want_fm=[0, 1024], want_bc=[], tag="1")
    fmk, _ = emit_mod(kb, es, c, [banks[0], banks[1]], d["cvec"], d["kvmw"], d["kvmb"], 2 * D,
                      want_fm=[0, 1024], want_bc=[], tag="k")

    def mk_a(sc, gname):
        t = kb.sb(es, [128, 8], F32, "a_" + gname)
        b = Buf()
        kb.op(kb.dve, V.scalar_tensor_tensor, reads=[sc[1], bv], writes=[b], out=t[:], in0=sc[0][:], scalar=1.0,
              in1=vec[gname][:], op0=ALU.add, op1=ALU.mult)
        return t, b
    a1, b_a1 = mk_a(fm0[1024], "n1g0")
    a2, b_a2 = mk_a(fm0[4096], "n2g0")
    a1n, b_a1n = mk_a(fm1[1024], "n1g1")
    akv, b_akv = mk_a(fmk[1024], "kvng")
    sh1, b_sh1 = fm0[0]
    sh2, b_sh2 = fm0[3072]
    sh1n, b_sh1n = fm1[0]
    shkv, b_shkv = fmk[0]
    g1bc, b_g1 = bc0[2048]
    g2bc, b_g2 = bc0[5120]

    if stop == 0:
        kb.finish()
        return nc
    ssq = kb.sb(es, [128, 32], F32, "ssq")
    b_ssq = Buf()
    xev = d["xe"].rearrange("(t p) n -> p t n", p=128)
    with ExitStack() as s12:
        hnT = kb.sb(s12, [128, 8, NALL * 128], BF16, "hnT")
        b_hnT = Buf()
        with ExitStack() as s1:
            xall = kb.sb(s1, [128, NALL, 1024], F32, "xall")
            b_x = [Buf() for _ in range(NALL)]
            junk = kb.sb(s1, [128, 1024], BF16, "junk")
            b_junk = Buf()
            xnb = [kb.sb(s1, [128, 1024], BF16, "xnb") for _ in range(4)]
            b_xnb = [Buf() for _ in range(4)]
            dsx = [DSem(kb, f"x{i}") for i in range(6)]
            kb.op(kb.dve, V.memset, writes=[b_ssq], ap=ssq[:], constant=0.0)
            for t in range(NALL):
                kb.load(kb.sp, dsx[t % 6], xall[:, t, :], xev[:, t, :], writes=[b_x[t]])
            for t in range(NALL):
                kb.op(kb.act, A.activation, reads=[b_x[t]], writes=[b_junk, b_ssq], out=junk[:], in_=xall[:, t, :],
                      func=AF.Square, accum_out=ssq[:, t:t + 1])
            emit_rstd(kb, es, ssq, b_ssq, NALL, 1.0 / D)
            tl = [(xall[:, t, :], b_x[t], t, t * 128) for t in range(NALL)]
            emit_norm_T(kb, c, tl, ssq, b_ssq, [(hnT, b_hnT, a1, b_a1, sh1, b_sh1)], pairs[0:2], xnb, b_xnb)
            kb.barrier()
        if stop == 1:
            kb.finish()
            return nc
        s_o = ExitStack()
        oT = kb.sb(s_o, [128, 8, NEXT * 128], BF16, "oT")
        if True:
            b_oT = Buf()
            with ExitStack() as s2:
                tb = kb.sb(s2, [128, 16, 640], BF16, "tb")
                b_tb = Buf()
                ds_tb = DSem(kb, "tb")
                tbv = d["tb"].rearrange("p (h n) -> p h n", h=16)
                for q4 in range(4):
                    kb.load(kb.pool, ds_tb, tb[:, q4 * 4:q4 * 4 + 4, :], tbv[:, q4 * 4:q4 * 4 + 4, :], writes=[b_tb])
                wsl = [kb.sb(s2, [128, 8, 384], BF16, "wqkv") for _ in range(2)]
                b_wsl = [Buf(), Buf()]
                ds_w = [DSem(kb, f"wqkv{i}") for i in range(2)]
                qT = [kb.sb(s2, [128, NEXT * 128], BF16, "qT") for _ in range(2)]
                kT = [kb.sb(s2, [128, NALL * 128], BF16, "kT") for _ in range(2)]
                va = [kb.sb(s2, [128, NALL, 2, 128], BF16, "va") for _ in range(2)]
                b_qT, b_kT, b_va = [Buf(), Buf()], [Buf(), Buf()], [Buf(), Buf()]
                PT = [kb.sb(s2, [128, 640], BF16, "PT") for _ in range(3)]
                b_PT = [Buf() for _ in range(3)]
                rs = kb.sb(s2, [128, 256], F32, "rs")
                b_rs = Buf()
                for sl in range(2):
                    for t in range(NALL):
                        kb.op(kb.dve, V.tensor_scalar, reads=[bv, c.b_ones], writes=[b_va[sl]],
                              out=va[sl][:, t, 0, 64:128], in0=c.ones[:, 0:64], scalar1=vec["valid"][:, t:t + 1],
                              scalar2=None, op0=ALU.mult)
                        kb.op(kb.dve, V.tensor_scalar, reads=[bv, c.b_ones], writes=[b_va[sl]],
                              out=va[sl][:, t, 1, 0:64], in0=c.ones[:, 0:64], scalar1=vec["valid"][:, t:t + 1],
                              scalar2=None, op0=ALU.mult)

                def load_w(p):
                    kb.load(kb.pool, ds_w[p % 2], wsl[p % 2][:],
                            d["wqkv_t"][p].rearrange("p (k n) -> p k n", k=8), writes=[b_wsl[p % 2]])
                load_w(0)
                pcnt = 0
                acnt = 0
                for p in range(8):
                    sl = p % 2
                    if p + 1 < 8:
                        load_w(p + 1)
                    w = wsl[sl]
                    for (dst, b_dst, col0, tok0, ntok, scale) in ((qT[sl], b_qT[sl], 0, 512, NEXT * 128, 0.125),
                                                                  (kT[sl], b_kT[sl], 128, 0, NALL * 128, None)):
                        for e0 in range(0, ntok, 512):
                            n = min(512, ntok - e0)
                            pb, b_pb = banks[4 + pcnt % 2]
                            pcnt += 1

                            def emit(pb=pb, col0=col0, a0=tok0 + e0, n=n):
                                for k in range(8):
                                    ins = nc.tensor.matmul(pb[:, 0:n], lhsT=w[:, k, col0:col0 + 128],
                                                           rhs=hnT[:, k, a0:a0 + n], start=(k == 0), stop=(k == 7))
                                return ins
                            kb.mm(emit, reads=[b_wsl[sl], b_hnT], writes=[b_pb])
                            if scale is not None:
                                kb.op(kb.act, A.activation, reads=[b_pb], writes=[b_dst], out=dst[:, e0:e0 + n],
                                      in_=pb[:, 0:n], func=AF.Copy, scale=scale)
                            else:
                                kb.op(kb.dve, V.tensor_copy, reads=[b_pb], writes=[b_dst], out=dst[:, e0:e0 + n],
                                      in_=pb[:, 0:n])
                    for t0 in range(0, NALL, 4):
                        nt_ = min(4, NALL - t0)
                        pb, b_pb = banks[4 + pcnt % 2]
                        pcnt += 1

                        def emit(pb=pb, t0=t0, nt_=nt_):
                            for j in range(nt_):
                                for k in range(8):
                                    ins = nc.tensor.matmul(pb[:, j * 128:(j + 1) * 128],
                                                           lhsT=hnT[:, k, (t0 + j) * 128:(t0 + j + 1) * 128],
                                                           rhs=w[:, k, 256:384], start=(k == 0), stop=(k == 7))
                            return ins
                        kb.mm(emit, reads=[b_wsl[sl], b_hnT], writes=[b_pb])
                        for j in range(nt_):
                            kb.op(kb.dve, V.tensor_scalar, reads=[b_pb, bv], writes=[b_va[sl]],
                                  out=va[sl][:, t0 + j, 0, 0:64], in0=pb[:, j * 128:j * 128 + 64],
                                  scalar1=vec["valid"][:, t0 + j:t0 + j + 1], scalar2=None, op0=ALU.mult)
                            kb.op(kb.act, A.activation, reads=[b_pb, bv], writes=[b_va[sl]],
                                  out=va[sl][:, t0 + j, 1, 64:128], in_=pb[:, j * 128 + 64:j * 128 + 128],
                                  func=AF.Copy, scale=vec["valid"][:, t0 + j:t0 + j + 1])
                    for te in range(NEXT):
                        ob, b_ob = banks[6 + te % 2]
                        pts = []
                        for hi in range(2):
                            h16 = 2 * p + hi
                            sp_, sb_ = pairs[hi]
                            r0 = hi * 64

                            def emit(sp_=sp_, h16=h16, r0=r0, te=te):
                                nc.tensor.matmul(sp_[:, 0:512], lhsT=c.idb[:], rhs=tb[:, h16, 0:512], start=True,
                                                 stop=False, skip_group_check=True)
                                nc.tensor.matmul(sp_[:, 512:640], lhsT=c.idb[:], rhs=tb[:, h16, 512:640], start=True,
                                                 stop=False, skip_group_check=True)
                                for m in range(5):
                                    ins = nc.tensor.matmul(sp_[:, m * 128:(m + 1) * 128],
                                                           lhsT=kT[sl][r0:r0 + 64, (te + m) * 128:(te + m + 1) * 128],
                                                           rhs=qT[sl][r0:r0 + 64, te * 128:(te + 1) * 128],
                                                           start=False, stop=True, skip_group_check=True)
                                return ins
                            kb.mm(emit, reads=[c.b_idb, b_tb, b_kT[sl], b_qT[sl]], writes=sb_)
                            pi = acnt % 3
                            acnt += 1
                            kb.op(kb.act, A.activation, reads=sb_, writes=[b_PT[pi]], out=PT[pi][:], in_=sp_[:, 0:640],
                                  func=AF.Exp)
                            pts.append(pi)
                        for hi in range(2):
                            pi = pts[hi]

                            def emit(hi=hi, pi=pi, te=te):
                                for m in range(5):
                                    ins = nc.tensor.matmul(ob[:, hi * 128:(hi + 1) * 128], lhsT=va[sl][:, te + m, hi, :],
                                                           rhs=PT[pi][:, m * 128:(m + 1) * 128], start=(m == 0),
                                                           stop=(m == 4), skip_group_check=True)
                                return ins
                            kb.mm(emit, reads=[b_va[sl], b_PT[pi]], writes=[b_ob])
                        kb.op(kb.dve, V.tensor_scalar, reads=[b_ob], writes=[b_rs], out=rs[0:64, 0:128],
                              in0=ob[64:128, 0:128], scalar1=1e-30, scalar2=None, op0=ALU.max)
                        kb.op(kb.dve, V.tensor_scalar, reads=[b_ob], writes=[b_rs], out=rs[64:128, 128:256],
                              in0=ob[0:64, 128:256], scalar1=1e-30, scalar2=None, op0=ALU.max)
                        kb.op(kb.dve, V.reciprocal, reads=[b_rs], writes=[b_rs], out=rs[0:64, 0:128], in_=rs[0:64, 0:128])
                        kb.op(kb.dve, V.reciprocal, reads=[b_rs], writes=[b_rs], out=rs[64:128, 128:256],
                              in_=rs[64:128, 128:256])
                        kb.op(kb.dve, V.tensor_tensor, reads=[b_ob, b_rs], writes=[b_oT],
                              out=oT[0:64, p, te * 128:(te + 1) * 128], in0=ob[0:64, 0:128], in1=rs[0:64, 0:128],
                              op=ALU.mult)
                        kb.op(kb.dve, V.tensor_tensor, reads=[b_ob, b_rs], writes=[b_oT],
                              out=oT[64:128, p, te * 128:(te + 1) * 128], in0=ob[64:128, 128:256],
                              in1=rs[64:128, 128:256], op=ALU.mult)
                kb.barrier()
    if stop == 2:
        kb.finish()
        return nc
    h = kb.sb(es, [128, NEXT, 1024], F32, "h")
    b_h = [Buf() for _ in range(NEXT)]
    with ExitStack() as s3:
        wo = kb.sb(s3, [128, 8, 1024], BF16, "wo")
        b_wo = Buf()
        ds_wo = DSem(kb, "wo")
        kb.load(kb.pool, ds_wo, wo[:], d["wo"].rearrange("(c p) n -> p c n", p=128), writes=[b_wo])
        tmp = kb.sb(s3, [128, 1024], F32, "tmp3")
        b_tmp = Buf()
        dsh = [DSem(kb, f"h{i}") for i in range(4)]
        for te in range(NEXT):
            kb.load(kb.sp, dsh[te % 4], h[:, te, :], xev[:, te + 4, :], writes=[b_h[te]])
        for te in range(NEXT):
            pair, pb2 = pairs[te % 2]

            def emit(pair=pair, te=te):
                for hh in range(2):
                    for cc in range(8):
                        ins = nc.tensor.matmul(pair[:, hh * 512:(hh + 1) * 512],
                                               lhsT=oT[:, cc, te * 128:(te + 1) * 128],
                                               rhs=wo[:, cc, hh * 512:(hh + 1) * 512], start=(cc == 0),
                                               stop=(cc == 7))
                return ins
            kb.mm(emit, reads=[b_oT, b_wo], writes=pb2)
            kb.op(kb.dve, V.tensor_tensor, reads=pb2 + [b_g1], writes=[b_tmp], out=tmp[:], in0=pair[:],
                  in1=g1bc[:], op=ALU.mult)
            kb.op(kb.dve, V.tensor_tensor, reads=[b_tmp], writes=[b_h[te]], out=h[:, te, :], in0=h[:, te, :],
                  in1=tmp[:], op=ALU.add)
        kb.barrier()
    s_o.close()
    if dbg:
        for te in range(NEXT):
            kb.store(kb.sp, o_dbg[te * 128:(te + 1) * 128, :], h[:, te, :], reads=[b_h[te]])
    if stop == 3:
        kb.finish()
        return nc
    with ExitStack() as s4:
        junk = kb.sb(s4, [128, 1024], BF16, "junk2")
        b_junk = Buf()
        kb.op(kb.dve, V.memset, writes=[b_ssq], ap=ssq[:], constant=0.0)
        for te in range(NEXT):
            kb.op(kb.act, A.activation, reads=[b_h[te]], writes=[b_junk, b_ssq], out=junk[:], in_=h[:, te, :],
                  func=AF.Square, accum_out=ssq[:, te:te + 1])
        emit_rstd(kb, es, ssq, b_ssq, NEXT, 1.0 / D)
        kb.barrier()
    emit_ffn(kb, es, c, h, b_h, NEXT, ssq, b_ssq, a2, b_a2, sh2, b_sh2, g2bc, b_g2, d["win_t"], d["wout"],
             vec["cw"], bv, vec["cb"], bv, vec["hv"], bv, pairs, banks, 128, "A")
    if stop == 5:
        kb.finish()
        return nc
    for te in range(NEXT):
        kb.store(kb.sp, o_h1[te * 128:(te + 1) * 128, :], h[:, te, :], reads=[b_h[te]])
    if stop == 6:
        kb.finish()
        return nc
    emit_latents(kb, es, c, d, h, b_h, ssq, b_ssq, akv, b_akv, shkv, b_shkv, a1n, b_a1n, sh1n, b_sh1n, vec, bv,
                 pairs, banks, o_ckv, o_kr, o_cq, stop=stop)
    kb.finish()
    return nc


def toeplitz_bias(relb):
    k = np.arange(128)[:, None, None]
    m = np.arange(5)[None, :, None]
    q = np.arange(128)[None, None, :]
    idx = np.clip(q - k + 128 * (4 - m), -128, 128) + 128
    T = relb[:, idx]
    a, j = q // 64, k // 64
    masked = ((m == 0) & (j == 0) & (a == 1)) | ((m == 4) & (j == 1) & (a == 0))
    T = np.where(masked[None], np.float32(NEG), T).astype(np.float32)
    return np.ascontiguousarray(T.transpose(1, 0, 2, 3).reshape(128, 16 * 640))


def tile_win(win):
    w = np.asarray(win).reshape(8, 128, 2, NF, 128)
    return np.ascontiguousarray(w.transpose(3, 1, 0, 2, 4).reshape(NF, 128, 8 * 256))


def tile_wqkv(w):
    w = np.asarray(w).reshape(8, 128, 3, 8, 128)
    return np.ascontiguousarray(w.transpose(3, 1, 0, 2, 4).reshape(8, 128, 8 * 384))


def conv_fm(conv_w, conv_b):
    cw = np.ascontiguousarray(np.asarray(conv_w).T.reshape(2 * NF, 128, 3).transpose(1, 0, 2).reshape(128, 2 * NF * 3))
    return cw, fm(conv_b)


def rope_consts():
    invf = np.power(np.float32(10000.0), -np.arange(16, dtype=np.float32) * np.float32(2.0 / 32)).astype(np.float32)
    invf = np.concatenate([invf, invf]).reshape(32, 1)
    sgn = np.concatenate([-np.ones(16, np.float32), np.ones(16, np.float32)]).reshape(32, 1)
    return invf, sgn


def prep_A(inp):
    x, cc, pos = inp["x"], inp["c"], inp["positions"]
    invf, sgn = rope_consts()
    cw0, cb0 = conv_fm(inp["f_conv_w"][0], inp["f_conv_b"][0])
    wkr = np.asarray(inp["b_wkr"])
    shared = {
        "invf": invf, "sgn": sgn, "ident": np.eye(128, dtype=np.float32),
        "mw0": np.ascontiguousarray(inp["mod_w"][0]), "mb0": np.ascontiguousarray(inp["mod_b"][0][None]),
        "mw1": np.ascontiguousarray(inp["mod_w"][1][:, 0:2 * D]), "mb1": np.ascontiguousarray(inp["mod_b"][1][None, 0:2 * D]),
        "kvmw": np.ascontiguousarray(inp["kv_mod_w"]), "kvmb": np.ascontiguousarray(inp["kv_mod_b"][None]),
        "n1g0": fm(inp["norm1_g"][0]), "n2g0": fm(inp["norm2_g"][0]), "n1g1": fm(inp["norm1_g"][1]),
        "kvng": fm(inp["kv_norm_g"]),
        "wqkv_t": tile_wqkv(inp["a_wqkv"][0]), "wo": np.ascontiguousarray(inp["a_wo"][0]),
        "tb": toeplitz_bias(np.asarray(inp["a_rel_bias"][0])),
        "win_t": tile_win(inp["f_win"][0]), "cw": cw0, "cb": cb0, "wout": np.ascontiguousarray(inp["f_wout"][0]),
        "wdkv": np.ascontiguousarray(inp["b_wdkv"]), "latg": np.ascontiguousarray(inp["b_kv_lat_norm_g"][None]),
        "wkr2": np.ascontiguousarray(np.concatenate([wkr, wkr[:, 16:32], wkr[:, 0:16]], axis=1)),
        "wdq": np.ascontiguousarray(inp["b_wdq"][0]), "qg": np.ascontiguousarray(inp["b_q_norm_g"][0][None]),
    }
    maps = []
    for core in range(NCORE):
        b, j = core // 4, core % 4
        s0 = j * OWN
        xe = np.zeros((NALL * 128, D), np.float32)
        lo = s0 - 640
        src0 = max(lo, 0)
        xe[src0 - lo:] = x[b, src0:s0 + OWN]
        tok = lo + np.arange(NALL * 128)
        valid = np.ascontiguousarray((tok >= 0).astype(np.float32).reshape(NALL, 128).T)
        m = dict(shared)
        m.update({"xe": xe, "valid": valid, "posk": np.ascontiguousarray(pos[b, s0:s0 + OWN][None]).astype(np.int32),
                  "cvec": fm(cc[b]), "hv": np.full((128, 1), 1.0 if s0 > 0 else 0.0, np.float32)})
        maps.append(m)
    return maps


def mla_mask():
    k = np.arange(128)[:, None, None]
    jj = np.arange(4)[None, :, None]
    q = np.arange(512)[None, None, :]
    allowed = (2 * jj + k // 64) <= (q // 64)
    return np.ascontiguousarray(np.where(allowed, np.float32(0.0), np.float32(NEG)).astype(np.float32).reshape(128, 4 * 512))


def build_phaseB():
    nc = bass.Bass("TRN2", target_bir_lowering=False)
    d = {}
    for name, shape, dt in [
        ("cqT", [384, S], BF16), ("ckvT", [256, S], BF16), ("krT", [32, S], BF16), ("posq", [1, S], I32),
        ("invf", [32, 1], F32), ("sgn", [32, 1], F32), ("ident", [128, 128], F32),
        ("wq96", [384, 4 * 96], F32), ("wq96s", [384, 4 * 96], F32), ("wuk", [256, 256], F32), ("wuv", [256, 256], F32),
        ("mask", [128, 4 * 512], F32),
    ]:
        d[name] = _din(nc, name, shape, dt)
    o_oT = _dout(nc, "oT", [256, S], BF16)
    kb = KB(nc)
    es = kb.es
    V, A, G = nc.vector, nc.scalar, nc.gpsimd
    pairs, banks = psum_banks(kb, es)
    c = emit_consts(kb, es, d)
    invf = kb.sb(es, [96, 1], F32, "invf")[64:96]
    sgn = kb.sb(es, [96, 1], F32, "sgn")[64:96]
    dsv = DSem(kb, "bvec")
    bv = Buf()
    kb.load(kb.sp, dsv, invf, d["invf"], writes=[bv], chain=False)
    kb.load(kb.sp, dsv, sgn, d["sgn"], writes=[bv], chain=False)
    SC = 96.0 ** -0.5
    NQB = S // 512
    cqs = [kb.sb(es, [128, 3, 512], BF16, "cqs") for _ in range(3)]
    b_cqs = [Buf() for _ in range(3)]
    ds_cq = [DSem(kb, f"cqs{i}") for i in range(3)]
    cq_d = d["cqT"].rearrange("(c p) n -> p c n", p=128)
    ckvT = kb.sb(es, [128, 2, S], BF16, "ckvT")
    KT = kb.sb(es, [96, S], BF16, "KT")
    QT = kb.sb(es, [96, S], BF16, "QT")
    b_ckv, b_KT, b_QT = Buf(), Buf(), Buf()
    dsl = [DSem(kb, f"bl{i}") for i in range(3)]
    kb.load(kb.sp, dsl[1], ckvT[:], d["ckvT"].rearrange("(c p) n -> p c n", p=128), writes=[b_ckv])
    kb.load(kb.sp, dsl[2], KT[64:96, :], d["krT"], writes=[b_KT])
    wq = kb.sb(es, [128, 3, 4 * 96], BF16, "wq96")
    wqs = kb.sb(es, [128, 3, 4 * 96], BF16, "wq96s")
    wuk = kb.sb(es, [128, 2, 256], BF16, "wuk")
    wuv = kb.sb(es, [128, 2, 256], BF16, "wuv")
    mask = kb.sb(es, [128, 4, 512], BF16, "mask")
    dsw = [DSem(kb, f"bw{i}") for i in range(5)]
    toks = [
        kb.load(kb.pool, dsw[0], wq[:], d["wq96"].rearrange("(c p) n -> p c n", p=128)),
        kb.load(kb.pool, dsw[1], wqs[:], d["wq96s"].rearrange("(c p) n -> p c n", p=128)),
        kb.load(kb.pool, dsw[2], wuk[:], d["wuk"].rearrange("(c p) n -> p c n", p=128)),
        kb.load(kb.pool, dsw[3], wuv[:], d["wuv"].rearrange("(c p) n -> p c n", p=128)),
        kb.load(kb.pool, dsw[4], mask[:], d["mask"].rearrange("p (j n) -> p j n", j=4)),
    ]
    b_wl = [Buf() for _ in toks]
    for bb, t in zip(b_wl, toks):
        bb.w = t
    cosb = kb.sb(es, [96, S], BF16, "cosb")[64:96]
    sinb = kb.sb(es, [96, S], BF16, "sinb")[64:96]
    b_tab = Buf()
    with ExitStack() as sr:
        posi = kb.sb(sr, [96, 2048], I32, "posi")[64:96]
        b_pos = Buf()
        cosf = kb.sb(sr, [96, 2048], F32, "cosf")[64:96]
        sinf = kb.sb(sr, [96, 2048], F32, "sinf")[64:96]
        b_tf = Buf()
        dsp = DSem(kb, "posq")
        for part in range(4):
            tk = slice(part * 2048, (part + 1) * 2048)
            kb.load(kb.sp, dsp, posi, d["posq"][0:1, tk].partition_broadcast(32), writes=[b_pos])
            with ExitStack() as sr2:
                emit_rope_tables(kb, sr2, posi, b_pos, 2048, invf, sgn, bv, cosf, sinf, b_tf, p0=64)
                kb.op(kb.dve, V.tensor_copy, reads=[b_tf], writes=[b_tab], out=cosb[:, tk], in_=cosf)
                kb.op(kb.dve, V.tensor_copy, reads=[b_tf], writes=[b_tab], out=sinb[:, tk], in_=sinf)
                kb.barrier()
    va = kb.sb(es, [128, S // 128, 128], BF16, "va")
    b_va = Buf()
    kb.op(kb.dve, V.memset, writes=[b_va], ap=va[:, :, 64:128], constant=1.0)
    PT = [kb.sb(es, [128, 1024], BF16, "PT") for _ in range(3)]
    b_PT = [Buf() for _ in range(3)]
    rs = kb.sb(es, [64, 512], F32, "rs")
    b_rs = Buf()
    ost = [kb.sb(es, [64, 512], BF16, "ost") for _ in range(3)]
    b_ost = [Buf() for _ in range(3)]
    t1 = [kb.sb(es, [96, 512], F32, "t1")[64:96] for _ in range(2)]
    t2 = [kb.sb(es, [96, 512], F32, "t2")[64:96] for _ in range(2)]
    b_t1, b_t2 = [Buf(), Buf()], [Buf(), Buf()]
    pcnt = 0
    acnt = 0
    ocnt = 0

    def load_cq(i):
        blk = i % NQB
        kb.load(kb.sp, ds_cq[i % 3], cqs[i % 3][:], cq_d[:, :, blk * 512:(blk + 1) * 512], writes=[b_cqs[i % 3]])
    load_cq(0)
    load_cq(1)
    for hh in range(4):
        hc = slice(hh * 64, (hh + 1) * 64)
        h96 = slice(hh * 96, (hh + 1) * 96)
        for blk in range(NQB):
            ci = hh * NQB + blk
            if ci + 2 < 4 * NQB:
                load_cq(ci + 2)
            cqT, b_cq = cqs[ci % 3], b_cqs[ci % 3]
            tk = slice(blk * 512, (blk + 1) * 512)
            p1, b_p1 = banks[4 + pcnt % 4]
            pcnt += 1
            p2, b_p2 = banks[4 + pcnt % 4]
            pcnt += 1

            def emit(p1=p1, cqT=cqT):
                for kc in range(3):
                    ins = nc.tensor.matmul(p1[0:96, :], lhsT=wq[:, kc, h96], rhs=cqT[:, kc, :], start=(kc == 0),
                                           stop=(kc == 2))
                return ins
            kb.mm(emit, reads=[b_wl[0], b_cq], writes=[b_p1])

            def emit(p2=p2, cqT=cqT):
                for kc in range(3):
                    ins = nc.tensor.matmul(p2[0:96, :], lhsT=wqs[:, kc, h96], rhs=cqT[:, kc, :], start=(kc == 0),
                                           stop=(kc == 2))
                return ins
            kb.mm(emit, reads=[b_wl[1], b_cq], writes=[b_p2])
            kb.op(kb.act, A.activation, reads=[b_p1], writes=[b_QT], out=QT[0:64, tk], in_=p1[0:64, :], func=AF.Copy,
                  scale=SC)
            ti_ = blk % 2
            kb.op(kb.dve, V.tensor_tensor, reads=[b_p1, b_tab], writes=[b_t1[ti_]], out=t1[ti_], in0=p1[64:96, :],
                  in1=cosb[:, tk], op=ALU.mult)
            kb.op(kb.dve, V.scalar_tensor_tensor, reads=[b_p2, b_tab], writes=[b_t2[ti_]], out=t2[ti_],
                  in0=p2[64:96, :], scalar=SC, in1=sinb[:, tk], op0=ALU.mult, op1=ALU.mult)
            kb.op(kb.dve, V.scalar_tensor_tensor, reads=[b_t1[ti_], b_t2[ti_]], writes=[b_QT], out=QT[64:96, tk],
                  in0=t1[ti_], scalar=SC, in1=t2[ti_], op0=ALU.mult, op1=ALU.add)
            pb, b_pb = banks[4 + pcnt % 4]
            pcnt += 1

            def emit(pb=pb, tk=tk):
                for kc in range(2):
                    ins = nc.tensor.matmul(pb[0:64, :], lhsT=wuk[:, kc, hc], rhs=ckvT[:, kc, tk], start=(kc == 0),
                                           stop=(kc == 1))
                return ins
            kb.mm(emit, reads=[b_wl[2], b_ckv], writes=[b_pb])
            kb.op(kb.act, A.activation, reads=[b_pb], writes=[b_KT], out=KT[0:64, tk], in_=pb[0:64, :], func=AF.Copy)
        for t0 in range(0, S // 128, 8):
            pb, b_pb = banks[4 + pcnt % 4]
            pcnt += 1

            def emit(pb=pb, t0=t0):
                for j in range(8):
                    for kc in range(2):
                        ins = nc.tensor.matmul(pb[:, j * 64:(j + 1) * 64],
                                               lhsT=ckvT[:, kc, (t0 + j) * 128:(t0 + j + 1) * 128], rhs=wuv[:, kc, hc],
                                               start=(kc == 0), stop=(kc == 1))
                return ins
            kb.mm(emit, reads=[b_wl[3], b_ckv], writes=[b_pb])
            for j in range(8):
                if j % 2 == 0:
                    kb.op(kb.act, A.activation, reads=[b_pb], writes=[b_va], out=va[:, t0 + j, 0:64],
                          in_=pb[:, j * 64:(j + 1) * 64], func=AF.Copy)
                else:
                    kb.op(kb.dve, V.tensor_copy, reads=[b_pb], writes=[b_va], out=va[:, t0 + j, 0:64],
                          in_=pb[:, j * 64:(j + 1) * 64])
        for qb in range(NQB):
            qk = slice(qb * 512, (qb + 1) * 512)
            nkt = 4 * (qb + 1)
            ob, b_ob = banks[4 + ocnt % 2]
            ocnt += 1
            pend = []
            for kp in range(nkt // 2):
                sp_, sbufs = pairs[acnt % 2]
                pi = acnt % 3
                acnt += 1

                def emit(sp_=sp_, kp=kp):
                    for u in range(2):
                        kt = 2 * kp + u
                        ks = slice(kt * 128, (kt + 1) * 128)
                        o_ = sp_[:, u * 512:(u + 1) * 512]
                        diag = kt >= 4 * qb
                        ins = nc.tensor.matmul(o_, lhsT=KT[:, ks], rhs=QT[:, qk], start=True, stop=not diag,
                                               skip_group_check=True)
                        if diag:
                            ins = nc.tensor.matmul(o_, lhsT=c.idb[:], rhs=mask[:, kt - 4 * qb, :], start=False,
                                                   stop=True, skip_group_check=True)
                    return ins
                kb.mm(emit, reads=[b_KT, b_QT, c.b_idb, b_wl[4]], writes=sbufs)
                kb.op(kb.act, A.activation, reads=sbufs, writes=[b_PT[pi]], out=PT[pi][:], in_=sp_, func=AF.Exp)

                def emit2(pi=pi, kp=kp):
                    for u in range(2):
                        kt = 2 * kp + u
                        ins = nc.tensor.matmul(ob, lhsT=va[:, kt, :], rhs=PT[pi][:, u * 512:(u + 1) * 512],
                                               start=(kt == 0), stop=(kt == nkt - 1), skip_group_check=True)
                    return ins
                pend.append((emit2, pi))
                if len(pend) > 1:
                    e2, p2_ = pend.pop(0)
                    kb.mm(e2, reads=[b_va, b_PT[p2_]], writes=[b_ob])
            while pend:
                e2, p2_ = pend.pop(0)
                kb.mm(e2, reads=[b_va, b_PT[p2_]], writes=[b_ob])
            oi = ocnt % 3
            kb.op(kb.dve, V.tensor_scalar, reads=[b_ob], writes=[b_rs], out=rs[:], in0=ob[64:128, :], scalar1=1e-30,
                  scalar2=None, op0=ALU.max)
            kb.op(kb.dve, V.reciprocal, reads=[b_rs], writes=[b_rs], out=rs[:], in_=rs[:])
            kb.op(kb.dve, V.tensor_tensor, reads=[b_ob, b_rs], writes=[b_ost[oi]], out=ost[oi][:], in0=ob[0:64, :],
                  in1=rs[:], op=ALU.mult)
            kb.store(kb.sp, o_oT[hh * 64:(hh + 1) * 64, qk], ost[oi][:], reads=[b_ost[oi]])
    kb.finish()
    return nc


def build_phaseC():
    nc = bass.Bass("TRN2", target_bir_lowering=False)
    d = {}
    for name, shape, dt in [
        ("h1e", [NEXT * 128, D], F32), ("oTe", [D, NEXT * 128], BF16), ("cvec", [128, 8], F32), ("hv", [128, 1], F32),
        ("ident", [128, 128], F32), ("mw1", [D, 6 * D], F32), ("mb1", [1, 6 * D], F32), ("n2g1", [128, 8], F32),
        ("wo1", [D, D], F32), ("win_t", [NF, 128, 8 * 256], F32), ("cw", [128, 2 * NF * 3], F32),
        ("cb", [128, 2 * NF], F32), ("wout", [FF, D], F32), ("fg", [1, D], F32),
    ]:
        d[name] = _din(nc, name, shape, dt)
    o_out = _dout(nc, "out", [OWN, D], F32)
    kb = KB(nc)
    es = kb.es
    V, A, G = nc.vector, nc.scalar, nc.gpsimd
    pairs, banks = psum_banks(kb, es)
    c = emit_consts(kb, es, d)
    vec = load_vecs(kb, es, c, {
        "hv": (d["hv"], [128, 1], F32), "n2g1": (d["n2g1"], [128, 8], F32),
        "cw": (d["cw"].rearrange("p (f t) -> p f t", t=3), [128, 2 * NF, 3], F32), "cb": (d["cb"], [128, 2 * NF], F32),
        "fg": (d["fg"].partition_broadcast(128), [128, D], F32),
    })
    bv = c.b_vec
    fm1, bc1 = emit_mod(kb, es, c, [banks[0], banks[1]], d["cvec"], d["mw1"], d["mb1"], 6 * D,
                        want_fm=[3072, 4096], want_bc=[2048, 5120], tag="c")
    a2 = kb.sb(es, [128, 8], F32, "a2")
    b_a2 = Buf()
    kb.op(kb.dve, V.scalar_tensor_tensor, reads=[fm1[4096][1], bv], writes=[b_a2], out=a2[:], in0=fm1[4096][0][:],
          scalar=1.0, in1=vec["n2g1"][:], op0=ALU.add, op1=ALU.mult)
    sh2, b_sh2 = fm1[3072]
    g1bc, b_g1 = bc1[2048]
    g2bc, b_g2 = bc1[5120]
    ssq = kb.sb(es, [128, 32], F32, "ssq")
    b_ssq = Buf()
    h = kb.sb(es, [128, NEXT, 1024], F32, "h")
    b_h = [Buf() for _ in range(NEXT)]
    hv_d = d["h1e"].rearrange("(t p) n -> p t n", p=128)
    with ExitStack() as s3:
        oT = kb.sb(s3, [128, 8, NEXT * 128], BF16, "oT")
        b_oT = Buf()
        ds_o = DSem(kb, "oT")
        kb.load(kb.sp, ds_o, oT[:], d["oTe"].rearrange("(c p) n -> p c n", p=128), writes=[b_oT])
        wo = kb.sb(s3, [128, 8, 1024], BF16, "wo")
        b_wo = Buf()
        ds_wo = DSem(kb, "wo")
        kb.load(kb.pool, ds_wo, wo[:], d["wo1"].rearrange("(c p) n -> p c n", p=128), writes=[b_wo])
        tmp = kb.sb(s3, [128, 1024], F32, "tmp3")
        b_tmp = Buf()
        dsh = [DSem(kb, f"h{i}") for i in range(4)]
        for te in range(NEXT):
            kb.load(kb.sp, dsh[te % 4], h[:, te, :], hv_d[:, te, :], writes=[b_h[te]])
        for te in range(NEXT):
            pair, pb2 = pairs[te % 2]

            def emit(pair=pair, te=te):
                for hh in range(2):
                    for cc in range(8):
                        ins = nc.tensor.matmul(pair[:, hh * 512:(hh + 1) * 512], lhsT=oT[:, cc, te * 128:(te + 1) * 128],
                                               rhs=wo[:, cc, hh * 512:(hh + 1) * 512], start=(cc == 0), stop=(cc == 7))
                return ins
            kb.mm(emit, reads=[b_oT, b_wo], writes=pb2)
            kb.op(kb.dve, V.tensor_tensor, reads=pb2 + [b_g1], writes=[b_tmp], out=tmp[:], in0=pair[:], in1=g1bc[:],
                  op=ALU.mult)
            kb.op(kb.dve, V.tensor_tensor, reads=[b_tmp], writes=[b_h[te]], out=h[:, te, :], in0=h[:, te, :],
                  in1=tmp[:], op=ALU.add)
        kb.barrier()

    def stats(lo):
        with ExitStack() as s4:
            junk = kb.sb(s4, [128, 1024], BF16, "junk2")
            b_junk = Buf()
            kb.op(kb.dve, V.memset, writes=[b_ssq], ap=ssq[:], constant=0.0)
            for te in range(lo, NEXT):
                kb.op(kb.act, A.activation, reads=[b_h[te]], writes=[b_junk, b_ssq], out=junk[:], in_=h[:, te, :],
                      func=AF.Square, accum_out=ssq[:, te:te + 1])
            emit_rstd(kb, es, ssq, b_ssq, NEXT, 1.0 / D)
            kb.barrier()
    stats(0)
    emit_ffn(kb, es, c, h, b_h, NEXT, ssq, b_ssq, a2, b_a2, sh2, b_sh2, g2bc, b_g2, d["win_t"], d["wout"],
             vec["cw"], bv, vec["cb"], bv, vec["hv"], bv, pairs, banks, 128, "C")
    stats(1)
    with ExitStack() as s5:
        ot = [kb.sb(s5, [128, 1024], F32, "ot") for _ in range(3)]
        b_ot = [Buf() for _ in range(3)]
        for te in range(1, NEXT):
            i = te % 3
            kb.op(kb.dve, V.scalar_tensor_tensor, reads=[b_h[te], b_ssq, bv], writes=[b_ot[i]], out=ot[i][:],
                  in0=h[:, te, :], scalar=ssq[:, te:te + 1], in1=vec["fg"][:], op0=ALU.mult, op1=ALU.mult)
            kb.store(kb.sp, o_out[(te - 1) * 128:te * 128, :], ot[i][:], reads=[b_ot[i]])
        kb.barrier()
    kb.finish()
    return nc


_CACHE = {}


def _get(name, fn):
    if name not in _CACHE:
        _CACHE[name] = fn()
    return _CACHE[name]


def kernel(**inp):
    inp = {k: np.asarray(v) for k, v in inp.items()}
    ident = np.eye(128, dtype=np.float32)
    invf, sgn = rope_consts()
    cores = list(range(NCORE))
    ncA = _get("A", build_phaseA)
    resA = run_bass_kernel_spmd(ncA, prep_A(inp), core_ids=cores).results
    ncB = _get("B", build_phaseB)
    wqr = np.asarray(inp["b_wqr"][0]).reshape(384, 16, 32)
    wuq_ = np.asarray(inp["b_wuq"][0]).reshape(384, 16, 64)
    wq96 = np.concatenate([wuq_, wqr], axis=2)
    wq96s = np.concatenate([np.zeros_like(wuq_), wqr[:, :, 16:32], wqr[:, :, 0:16]], axis=2)
    mask = mla_mask()
    mapsB = []
    for core in cores:
        b, j = core // 4, core % 4
        g = [resA[b * 4 + q] for q in range(4)]
        hs = slice(j * 256, (j + 1) * 256)
        mapsB.append({
            "cqT": np.ascontiguousarray(np.concatenate([np.asarray(r["cqT"]) for r in g], axis=1)),
            "ckvT": np.ascontiguousarray(np.concatenate([np.asarray(r["ckvT"]) for r in g], axis=1)),
            "krT": np.ascontiguousarray(np.concatenate([np.asarray(r["krT"]) for r in g], axis=1)),
            "posq": np.ascontiguousarray(inp["positions"][b][None]).astype(np.int32),
            "invf": invf, "sgn": sgn, "ident": ident,
            "wq96": np.ascontiguousarray(wq96[:, 4 * j:4 * j + 4, :].reshape(384, 4 * 96)),
            "wq96s": np.ascontiguousarray(wq96s[:, 4 * j:4 * j + 4, :].reshape(384, 4 * 96)),
            "wuk": np.ascontiguousarray(inp["b_wuk"][:, hs]), "wuv": np.ascontiguousarray(inp["b_wuv"][:, hs]),
            "mask": mask,
        })
    resB = run_bass_kernel_spmd(ncB, mapsB, core_ids=cores).results
    ncC = _get("C", build_phaseC)
    cw1, cb1 = conv_fm(inp["f_conv_w"][1], inp["f_conv_b"][1])
    sharedC = {
        "ident": ident, "mw1": np.ascontiguousarray(inp["mod_w"][1]), "mb1": np.ascontiguousarray(inp["mod_b"][1][None]),
        "n2g1": fm(inp["norm2_g"][1]), "wo1": np.ascontiguousarray(inp["b_wo"][0]),
        "win_t": tile_win(inp["f_win"][1]), "cw": cw1, "cb": cb1, "wout": np.ascontiguousarray(inp["f_wout"][1]),
        "fg": np.ascontiguousarray(inp["final_g"][None]),
    }
    mapsC = []
    for core in cores:
        b, j = core // 4, core % 4
        s0 = j * OWN
        oT_b = np.concatenate([np.asarray(resB[b * 4 + q]["oT"]) for q in range(4)], axis=0)
        oTe = np.zeros((D, NEXT * 128), oT_b.dtype)
        lo = s0 - 128
        src0 = max(lo, 0)
        oTe[:, src0 - lo:] = oT_b[:, src0:s0 + OWN]
        m = dict(sharedC)
        m.update({"h1e": np.ascontiguousarray(np.asarray(resA[core]["h1"])), "oTe": oTe, "cvec": fm(inp["c"][b]),
                  "hv": np.full((128, 1), 1.0 if s0 > 0 else 0.0, np.float32)})
        mapsC.append(m)
    resC = run_bass_kernel_spmd(ncC, mapsC, core_ids=cores).results
    out = np.zeros((2, S, D), np.float32)
    for core in cores:
        b, j = core // 4, core % 4
        out[b, j * OWN:(j + 1) * OWN] = np.asarray(resC[core]["out"])
    return out
```

```python
import os
import numpy as np
import ml_dtypes
from contextlib import ExitStack
import concourse.bass as bass
import concourse.mybir as mybir
from concourse.bass_utils import run_bass_kernel_spmd

F32 = mybir.dt.float32
BF16 = mybir.dt.bfloat16
I32 = mybir.dt.int32
AF = mybir.ActivationFunctionType
ALU = mybir.AluOpType
AX = mybir.AxisListType

NCORE = 8
D = 1024
S = 8192
OWN = 2048
NOWN = 16
NEXT = 17
NALL = 21
FF = 2816
NF = 22
NEG = -30000.0
ARENA_BYTES = 200 * 1024
TWO_PI = 6.283185307179586
PI = 3.141592653589793


class Tok:
    __slots__ = ("sem", "val", "key")

    def __init__(self, sem, val, key):
        self.sem, self.val, self.key = sem, val, key


class EQ:
    def __init__(self, kb, eng, name):
        self.kb, self.e, self.name = kb, eng, name
        self.sem = kb.newsem("q_" + name)
        self.cnt = 0
        self.seen = {}

    def wait(self, *toks):
        for t in toks:
            if t is None:
                continue
            if self.name == "pe" and t.key == "pe":
                continue
            if self.seen.get(t.key, 0) >= t.val:
                continue
            self.e.wait_ge(t.sem, t.val)
            self.seen[t.key] = t.val

    def done(self, ins):
        ins.then_inc(self.sem, 1)
        self.cnt += 1
        return Tok(self.sem, self.cnt, self.name)


class DSem:
    def __init__(self, kb, name):
        self.sem = kb.newsem("d_" + name)
        self.cnt = 0
        self.name = "d_" + name
        self.last = None
        kb.dsems.append(self)

    def add(self, ins):
        ins.then_inc(self.sem, 16)
        self.cnt += 16
        return Tok(self.sem, self.cnt, self.name)


class Buf:
    __slots__ = ("w", "r")

    def __init__(self):
        self.w = None
        self.r = {}


def _use(eq, reads, writes):
    for b in reads:
        eq.wait(b.w)
    for b in writes:
        eq.wait(b.w)
        eq.wait(*b.r.values())


def _fin(tok, reads, writes):
    for b in reads:
        b.r[tok.key] = tok
    for b in writes:
        b.w = tok
        b.r = {}


class KB:
    def __init__(self, nc):
        self.nc = nc
        self.es = ExitStack()
        self.nsem = 0
        self.dsems = []
        self.pe = EQ(self, nc.tensor, "pe")
        self.act = EQ(self, nc.scalar, "act")
        self.dve = EQ(self, nc.vector, "dve")
        self.pool = EQ(self, nc.gpsimd, "pool")
        self.sp = EQ(self, nc.sync, "sp")
        self.uid = 0
        self.arena = None
        self.peak = 0
        self.st_sem = DSem(self, "store")
        self.st_last = None

    def newsem(self, name):
        self.nsem += 1
        return self.es.enter_context(self.nc.semaphore(name))

    def sb(self, es, shape, dt, name=None):
        if self.arena is None:
            self.arena = self.es.enter_context(self.nc.sbuf_tensor("arena", [128, ARENA_BYTES // 2], BF16))
            self.free = [(0, ARENA_BYTES)]
        esz = 2 if dt == BF16 else 4
        n = 1
        for x in shape[1:]:
            n *= x
        nbytes = (n * esz + 63) // 64 * 64
        top = es is self.es
        order = range(len(self.free) - 1, -1, -1) if top else range(len(self.free))
        for i in order:
            o, sz = self.free[i]
            if sz >= nbytes:
                off = o + sz - nbytes if top else o
                if sz == nbytes:
                    self.free.pop(i)
                elif top:
                    self.free[i] = (o, sz - nbytes)
                else:
                    self.free[i] = (o + nbytes, sz - nbytes)
                break
        else:
            raise RuntimeError(f"SBUF arena full allocating {name} {shape} ({nbytes}B); free={self.free}")
        self.peak = max(self.peak, ARENA_BYTES - sum(z for _, z in self.free))

        def release(off=off, nbytes=nbytes):
            self.free.append((off, nbytes))
            self.free.sort()
            merged = []
            for o, z in self.free:
                if merged and merged[-1][0] + merged[-1][1] == o:
                    merged[-1] = (merged[-1][0], merged[-1][1] + z)
                else:
                    merged.append((o, z))
            self.free = merged
        es.callback(release)
        ap = self.arena[0:shape[0], off // 2:(off + n * esz) // 2]
        if dt != BF16:
            ap = ap.bitcast(dt)
        if len(shape) == 3:
            ap = ap.rearrange("p (a b) -> p a b", a=shape[1])
        elif len(shape) == 4:
            ap = ap.rearrange("p (a b c) -> p a b c", a=shape[1], b=shape[2])
        return ap

    def ps(self, es, shape, dt, name=None):
        self.uid += 1
        return es.enter_context(self.nc.psum_tensor(f"{name or 'p'}_{self.uid}", list(shape), dt))

    def op(self, eq, fn, reads=(), writes=(), **kw):
        _use(eq, reads, writes)
        tok = eq.done(fn(**kw))
        _fin(tok, reads, writes)
        return tok

    def mm(self, emit, reads=(), writes=()):
        _use(self.pe, reads, writes)
        ins = emit()
        tok = self.pe.done(ins)
        _fin(tok, reads, writes)
        return tok

    def load(self, q, dsem, out, in_, writes=(), reads=(), chain=True):
        _use(q, reads, writes)
        if chain:
            q.wait(dsem.last)
        tok = dsem.add(q.e.dma_start(out=out, in_=in_))
        dsem.last = tok
        _fin(tok, reads, writes)
        return tok

    def store(self, q, out, in_, reads=()):
        _use(q, reads, ())
        tok = self.st_sem.add(q.e.dma_start(out=out, in_=in_))
        _fin(tok, reads, ())
        self.st_last = tok
        return tok

    def barrier(self):
        qs = (self.pe, self.act, self.dve, self.pool, self.sp)
        toks = [Tok(q.sem, q.cnt, q.name) for q in qs if q.cnt]
        toks += [Tok(d.sem, d.cnt, d.name) for d in self.dsems if d.cnt]
        for q in qs:
            q.wait(*toks)

    def finish(self):
        if self.st_last is not None:
            self.sp.wait(self.st_last)
        for q in (self.pe, self.act, self.dve, self.pool):
            if q.cnt:
                self.sp.wait(Tok(q.sem, q.cnt, q.name))
        self.es.close()


def fm(v):
    v = np.asarray(v)
    return np.ascontiguousarray(v.reshape(-1, 128).T)


class Consts:
    pass


def emit_consts(kb, es, dram):
    nc = kb.nc
    c = Consts()
    c.dsem = DSem(kb, "const")
    c.idf = kb.sb(es, [128, 128], F32, "idf")
    c.idb = kb.sb(es, [128, 128], BF16, "idb")
    c.ones = kb.sb(es, [128, 128], F32, "ones")
    c.b_idf, c.b_idb, c.b_ones = Buf(), Buf(), Buf()
    kb.load(kb.sp, DSem(kb, "ident"), c.idf[:], dram["ident"], writes=[c.b_idf])
    kb.op(kb.dve, nc.vector.tensor_copy, reads=[c.b_idf], writes=[c.b_idb], out=c.idb[:], in_=c.idf[:])
    kb.op(kb.dve, nc.vector.memset, writes=[c.b_ones], ap=c.ones[:], constant=1.0)
    return c


def emit_mod(kb, es, c, banks, cvec_d, mw_d, mb_d, ncols, want_fm, want_bc, tag):
    nc = kb.nc
    ds = DSem(kb, "modc" + tag)
    ds2 = DSem(kb, "modb" + tag)
    out_fm, out_bc = {}, {}
    nblk = ncols // 512
    with ExitStack() as les:
        cv = kb.sb(les, [128, 8], F32, "cv")
        b_cv = Buf()
        kb.load(kb.sp, ds, cv[:], cvec_d, writes=[b_cv])
        kb.op(kb.act, nc.scalar.activation, reads=[b_cv], writes=[b_cv], out=cv[:], in_=cv[:], func=AF.Silu)
        crep = kb.sb(les, [128, 8, 128], BF16, "crep")
        b_crep = Buf()
        for k in range(8):
            kb.op(kb.dve, nc.vector.tensor_scalar, reads=[b_cv, c.b_ones], writes=[b_crep],
                  out=crep[:, k, :], in0=c.ones[:], scalar1=cv[:, k:k + 1], scalar2=None, op0=ALU.mult)
        wbuf = [kb.sb(les, [128, 8, 512], BF16, "mwb") for _ in range(3)]
        wsem = [DSem(kb, f"mw{tag}{i}") for i in range(3)]
        b_w = [Buf(), Buf(), Buf()]
        mbb = kb.sb(les, [128, 1024], F32, "mbb")
        b_mbb = Buf()
        bc = kb.sb(les, [128, 1024], F32, "bctmp")
        b_bc = Buf()
        wanted = sorted(set(want_fm) | set(want_bc))
        blocks = [(s0, h) for s0 in wanted for h in range(2)]
        mwv = mw_d.rearrange("(c p) n -> p c n", p=128)

        def issue(i):
            s0, h = blocks[i]
            col = s0 + h * 512
            kb.load(kb.pool, wsem[i % 3], wbuf[i % 3][:], mwv[:, :, col:col + 512], writes=[b_w[i % 3]])

        issue(0)
        if len(blocks) > 1:
            issue(1)
        for i, (s0, h) in enumerate(blocks):
            if i + 2 < len(blocks):
                issue(i + 2)
            if h == 0:
                kb.load(kb.sp, ds2, mbb[:], mb_d[0:1, s0:s0 + 1024].partition_broadcast(128), writes=[b_mbb])
                if s0 in want_bc:
                    t = kb.sb(es, [128, 1024], F32, "modbc")
                    out_bc[s0] = (t, Buf())
                dst, b_dst = out_bc[s0] if s0 in want_bc else (bc, b_bc)
            pb, b_pb = banks[i % 2]
            wb = wbuf[i % 3]

            def emit():
                for k in range(8):
                    ins = nc.tensor.matmul(pb, lhsT=crep[:, k, :], rhs=wb[:, k, :], start=(k == 0), stop=(k == 7))
                return ins
            kb.mm(emit, reads=[b_crep, b_w[i % 3]], writes=[b_pb])
            kb.op(kb.dve, nc.vector.tensor_tensor, reads=[b_pb, b_mbb], writes=[b_dst],
                  out=dst[:, h * 512:(h + 1) * 512], in0=pb, in1=mbb[:, h * 512:(h + 1) * 512], op=ALU.add)
            if h == 1 and s0 in want_fm:
                t = kb.sb(es, [128, 8], F32, "modfm")
                bt = Buf()
                pb2, b_pb2 = banks[(i + 1) % 2]

                def emit2():
                    for cc in range(8):
                        ins = nc.tensor.matmul(pb2[:, cc:cc + 1], lhsT=dst[:, cc * 128:(cc + 1) * 128],
                                               rhs=c.idf[:, 0:1], start=True, stop=True)
                    return ins
                kb.mm(emit2, reads=[b_dst, c.b_idf], writes=[b_pb2])
                kb.op(kb.dve, nc.vector.tensor_copy, reads=[b_pb2], writes=[bt], out=t[:], in_=pb2[:, 0:8])
                out_fm[s0] = (t, bt)
        kb.barrier()
    return out_fm, out_bc


def psum_banks(kb, es):
    pt = [kb.ps(es, [128, 1024], F32, "pp") for _ in range(4)]
    bufs = [Buf() for _ in range(8)]
    banks = [(pt[b // 2][:, (b % 2) * 512:(b % 2) * 512 + 512], bufs[b]) for b in range(8)]
    pairs = [(pt[p][:], [bufs[2 * p], bufs[2 * p + 1]]) for p in range(4)]
    return pairs, banks


def emit_rstd(kb, es, ssq, b_ssq, n, inv_n, eps=1e-6):
    nc = kb.nc
    kb.op(kb.dve, nc.vector.tensor_scalar, reads=[b_ssq], writes=[b_ssq], out=ssq[:, 0:n], in0=ssq[:, 0:n],
          scalar1=inv_n, scalar2=eps, op0=ALU.mult, op1=ALU.add)
    kb.op(kb.act, nc.scalar.activation, reads=[b_ssq], writes=[b_ssq], out=ssq[:, 0:n], in_=ssq[:, 0:n], func=AF.Sqrt)
    kb.op(kb.dve, nc.vector.reciprocal, reads=[b_ssq], writes=[b_ssq], out=ssq[:, 0:n], in_=ssq[:, 0:n])


def emit_norm_T(kb, c, tiles, rstd, b_rstd, outs, ppairs, xnb, b_xnb, cnt=[0]):
    nc = kb.nc
    i = 0
    while i < len(tiles):
        grp = tiles[i:i + 2]
        g = cnt[0]
        cnt[0] += 1
        pair, pb = ppairs[g % 2]
        pv = pair.bitcast(BF16).rearrange("p (c t) -> p c t", c=8)
        for j, (src, b_src, rc, doff) in enumerate(grp):
            xi = (g % 2) * 2 + j
            kb.op(kb.act, nc.scalar.activation, reads=[b_src, b_rstd], writes=[b_xnb[xi]],
                  out=xnb[xi][:], in_=src, func=AF.Copy, scale=rstd[:, rc:rc + 1])
        for j, (src, b_src, rc, doff) in enumerate(grp):
            xi = (g % 2) * 2 + j

            def emit(j=j, xi=xi):
                for cc in range(8):
                    ins = nc.tensor.transpose(out=pv[:, cc, j * 128:(j + 1) * 128],
                                              in_=xnb[xi][:, cc * 128:(cc + 1) * 128], identity=c.idb[:])
                return ins
            kb.mm(emit, reads=[b_xnb[xi], c.b_idb], writes=pb)
        n = 128 * len(grp)
        doff = grp[0][3]
        k = 0
        for (dst, b_dst, a, b_a, sh, b_sh) in outs:
            for cc in range(8):
                if k % 2 == 0:
                    kb.op(kb.act, nc.scalar.activation, reads=pb + [b_a, b_sh], writes=[b_dst],
                          out=dst[:, cc, doff:doff + n], in_=pv[:, cc, 0:n], func=AF.Identity,
                          scale=a[:, cc:cc + 1], bias=sh[:, cc:cc + 1])
                else:
                    kb.op(kb.dve, nc.vector.tensor_scalar, reads=pb + [b_a, b_sh], writes=[b_dst],
                          out=dst[:, cc, doff:doff + n], in0=pv[:, cc, 0:n], scalar1=a[:, cc:cc + 1],
                          scalar2=sh[:, cc:cc + 1], op0=ALU.mult, op1=ALU.add)
                k += 1
        i += 2


def emit_ffn(kb, es, c, h, b_h, nt, rstd, b_rstd, a2, b_a2, sh2, b_sh2, g2bc, b_g2, win_d, wout_d,
             cw, b_cw, cb, b_cb, hv, b_hv, pairs, banks, fix_tok, tag):
    nc = kb.nc
    ntok = nt * 128
    blocks = [(s0, min(512, ntok - s0)) for s0 in range(0, ntok, 512)]
    with ExitStack() as les:
        wout = kb.sb(les, [128, NF, 1024], BF16, "wout")
        b_wout = Buf()
        ds_wout = DSem(kb, "wout" + tag)
        wov = wout_d.rearrange("(f p) n -> p f n", p=128)
        for f0 in range(0, NF, 6):
            f1 = min(NF, f0 + 6)
            kb.load(kb.pool, ds_wout, wout[:, f0:f1, :], wov[:, f0:f1, :], writes=[b_wout])
        hnT = kb.sb(les, [128, 8, 512], BF16, "hn2T")
        b_hnT = Buf()
        actT = kb.sb(les, [128, NF, 512], BF16, "actT")
        b_actT = Buf()
        xnb = [kb.sb(les, [128, 1024], BF16, "xnb") for _ in range(4)]
        b_xnb = [Buf() for _ in range(4)]
        wsl = [kb.sb(les, [128, 8, 256], BF16, "winsl") for _ in range(3)]
        b_wsl = [Buf() for _ in range(3)]
        ds_w = [DSem(kb, f"win{tag}{i}") for i in range(3)]
        ug = [kb.sb(les, [128, 514], F32, "ug") for _ in range(2)]
        b_ug = [Buf() for _ in range(2)]
        yy = [kb.sb(les, [128, 512], F32, "yy") for _ in range(2)]
        b_yy = [Buf() for _ in range(2)]
        sgs = [kb.sb(les, [128, 512], F32, "sg") for _ in range(1)]
        b_sgs = [Buf()]
        halo = kb.sb(les, [128, 2 * NF, 2], F32, "halo")
        b_halo = Buf()
        tmp = kb.sb(les, [128, 1024], F32, "ftmp")
        b_tmp = Buf()
        kb.op(kb.dve, nc.vector.memset, writes=[b_halo], ap=halo[:], constant=0.0)
        jobs = [(bi, f) for bi in range(len(blocks)) for f in range(NF)]

        def issue_w(ji):
            bi, f = jobs[ji]
            sl = ji % 3
            kb.load(kb.pool, ds_w[sl], wsl[sl][:], win_d[f], writes=[b_wsl[sl]])

        issue_w(0)
        issue_w(1)
        for ji, (bi, f) in enumerate(jobs):
            s0, n = blocks[bi]
            if f == 0:
                tl = [(h[:, (s0 // 128) + j, :], b_h[(s0 // 128) + j], (s0 // 128) + j, j * 128)
                      for j in range(n // 128)]
                emit_norm_T(kb, c, tl, rstd, b_rstd, [(hnT, b_hnT, a2, b_a2, sh2, b_sh2)], pairs[0:2], xnb, b_xnb)
            if ji + 2 < len(jobs):
                issue_w(ji + 2)
            sl = ji % 3
            for gv in range(2):
                fi = f + gv * NF
                pb, b_pb = banks[4 + ((ji * 2 + gv) % 4)]

                def emit(gv=gv, pb=pb):
                    for k in range(8):
                        ins = nc.tensor.matmul(pb[:, 0:n], lhsT=wsl[sl][:, k, gv * 128:(gv + 1) * 128], rhs=hnT[:, k, 0:n],
                                               start=(k == 0), stop=(k == 7))
                    return ins
                kb.mm(emit, reads=[b_wsl[sl], b_hnT], writes=[b_pb])
                u, b_u = ug[gv], b_ug[gv]
                y, b_y = yy[gv], b_yy[gv]
                kb.op(kb.act, nc.scalar.activation, reads=[b_pb], writes=[b_u], out=u[:, 2:2 + n], in_=pb[:, 0:n],
                      func=AF.Copy)
                kb.op(kb.act, nc.scalar.activation, reads=[b_halo], writes=[b_u], out=u[:, 0:2], in_=halo[:, fi, :],
                      func=AF.Copy)
                if fix_tok is not None and s0 <= fix_tok - 2 and fix_tok <= s0 + n:
                    o = fix_tok - s0
                    kb.op(kb.dve, nc.vector.tensor_scalar, reads=[b_hv], writes=[b_u], out=u[:, o:o + 2],
                          in0=u[:, o:o + 2], scalar1=hv[:, 0:1], scalar2=None, op0=ALU.mult)
                kb.op(kb.act, nc.scalar.activation, reads=[b_pb, b_cw, b_cb], writes=[b_y], out=y[:, 0:n],
                      in_=pb[:, 0:n], func=AF.Identity, scale=cw[:, fi, 2:3], bias=cb[:, fi:fi + 1])
                kb.op(kb.act, nc.scalar.activation, reads=[b_u], writes=[b_halo], out=halo[:, fi, :],
                      in_=u[:, n:n + 2], func=AF.Copy)
                kb.op(kb.dve, nc.vector.scalar_tensor_tensor, reads=[b_u, b_cw], writes=[b_y], out=y[:, 0:n],
                      in0=u[:, 1:1 + n], scalar=cw[:, fi, 1:2], in1=y[:, 0:n], op0=ALU.mult, op1=ALU.add)
                kb.op(kb.dve, nc.vector.scalar_tensor_tensor, reads=[b_u, b_cw], writes=[b_y], out=y[:, 0:n],
                      in0=u[:, 0:n], scalar=cw[:, fi, 0:1], in1=y[:, 0:n], op0=ALU.mult, op1=ALU.add)
            yg_, b_yg_ = yy[0], b_yy[0]
            yv_, b_yv_ = yy[1], b_yy[1]
            sg, b_sg = sgs[0], b_sgs[0]
            kb.op(kb.act, nc.scalar.activation, reads=[b_yg_], writes=[b_sg], out=sg[:, 0:n], in_=yg_[:, 0:n],
                  func=AF.Silu)
            kb.op(kb.dve, nc.vector.tensor_tensor, reads=[b_sg, b_yv_], writes=[b_actT], out=actT[:, f, 0:n],
                  in0=sg[:, 0:n], in1=yv_[:, 0:n], op=ALU.mult)
            if f == NF - 1:
                for j in range(n // 128):
                    ti = s0 // 128 + j
                    pair, pb2 = pairs[ti % 2]

                    def emit3(j=j, pair=pair):
                        for hh in range(2):
                            for ff in range(NF):
                                ins = nc.tensor.matmul(pair[:, hh * 512:(hh + 1) * 512],
                                                       lhsT=actT[:, ff, j * 128:(j + 1) * 128],
                                                       rhs=wout[:, ff, hh * 512:(hh + 1) * 512],
                                                       start=(ff == 0), stop=(ff == NF - 1))
                        return ins
                    kb.mm(emit3, reads=[b_actT, b_wout], writes=pb2)
                    kb.op(kb.dve, nc.vector.tensor_tensor, reads=pb2 + [b_g2], writes=[b_tmp], out=tmp[:],
                          in0=pair[:], in1=g2bc[:], op=ALU.mult)
                    kb.op(kb.dve, nc.vector.tensor_tensor, reads=[b_tmp], writes=[b_h[ti]], out=h[:, ti, :],
                          in0=h[:, ti, :], in1=tmp[:], op=ALU.add)
        kb.barrier()


def _din(nc, name, shape, dt=F32):
    return nc.dram_tensor(name, list(shape), dt, kind="ExternalInput").ap()


def _dout(nc, name, shape, dt=F32):
    return nc.dram_tensor(name, list(shape), dt, kind="ExternalOutput").ap()


def load_vecs(kb, es, c, specs):
    out = {}
    for name, (ap, shape, dt) in specs.items():
        t = kb.sb(es, shape, dt, name)
        kb.sp.e.dma_start(out=t[:], in_=ap).then_inc(c.dsem.sem, 16)
        c.dsem.cnt += 16
        out[name] = t
    c.b_vec = Buf()
    c.b_vec.w = Tok(c.dsem.sem, c.dsem.cnt, c.dsem.name)
    return out


def emit_rope_tables(kb, es, pos_i, b_pos, n, invf, sgn, b_vec, cosT, sinT, b_tab, p0=0):
    nc = kb.nc
    ang = kb.sb(es, [p0 + 32, n], F32, "ang")[p0:p0 + 32]
    kf = kb.sb(es, [p0 + 32, n], F32, "kf")[p0:p0 + 32]
    ki = kb.sb(es, [p0 + 32, n], I32, "ki")[p0:p0 + 32]
    m = kb.sb(es, [p0 + 32, n], F32, "mm")[p0:p0 + 32]
    b = Buf()
    C1 = 6.28125
    C2 = TWO_PI - C1
    V = nc.vector
    op = lambda fn, **kw: kb.op(kb.dve, fn, reads=[b_pos, b_vec], writes=[b, b_tab], **kw)
    op(V.tensor_copy, out=ang[:], in_=pos_i)
    op(V.tensor_scalar, out=ang[:], in0=ang[:], scalar1=invf[:, 0:1], scalar2=None, op0=ALU.mult)
    op(V.tensor_scalar, out=kf[:], in0=ang[:], scalar1=1.0 / TWO_PI, scalar2=None, op0=ALU.mult)
    op(V.tensor_copy, out=ki[:], in_=kf[:])
    op(V.tensor_copy, out=kf[:], in_=ki[:])
    op(V.scalar_tensor_tensor, out=ang[:], in0=kf[:], scalar=-C1, in1=ang[:], op0=ALU.mult, op1=ALU.add)
    op(V.scalar_tensor_tensor, out=ang[:], in0=kf[:], scalar=-C2, in1=ang[:], op0=ALU.mult, op1=ALU.add)

    def wrap(t):
        op(V.tensor_scalar, out=m[:], in0=t[:], scalar1=PI, scalar2=TWO_PI, op0=ALU.is_gt, op1=ALU.mult)
        op(V.tensor_tensor, out=t[:], in0=t[:], in1=m[:], op=ALU.subtract)
        op(V.tensor_scalar, out=m[:], in0=t[:], scalar1=-PI, scalar2=TWO_PI, op0=ALU.is_lt, op1=ALU.mult)
        op(V.tensor_tensor, out=t[:], in0=t[:], in1=m[:], op=ALU.add)
    wrap(ang)
    kb.op(kb.act, nc.scalar.activation, reads=[b], writes=[b_tab], out=sinT, in_=ang[:], func=AF.Sin)
    op(V.tensor_scalar, out=sinT, in0=sinT, scalar1=sgn[:, 0:1], scalar2=None, op0=ALU.mult)
    op(V.tensor_scalar, out=kf[:], in0=ang[:], scalar1=PI / 2, scalar2=None, op0=ALU.add)
    wrap(kf)
    kb.op(kb.act, nc.scalar.activation, reads=[b], writes=[b_tab], out=cosT, in_=kf[:], func=AF.Sin)


def emit_latents(kb, es, c, d, h, b_h, ssq, b_ssq, akv, b_akv, shkv, b_shkv, a1n, b_a1n, sh1n, b_sh1n, vec, bv,
                 pairs, banks, o_ckv, o_kr, o_cq, stop=99):
    nc = kb.nc
    V, A, G = nc.vector, nc.scalar, nc.gpsimd
    with ExitStack() as s6:
        junk = kb.sb(s6, [128, 1024], BF16, "junk6")
        b_junk = Buf()
        kb.op(kb.dve, V.memset, writes=[b_ssq], ap=ssq[:], constant=0.0)
        for te in range(1, NEXT):
            kb.op(kb.act, A.activation, reads=[b_h[te]], writes=[b_junk, b_ssq], out=junk[:], in_=h[:, te, :],
                  func=AF.Square, accum_out=ssq[:, te:te + 1])
        emit_rstd(kb, es, ssq, b_ssq, NEXT, 1.0 / D)
        wdkv = kb.sb(s6, [128, 8, 256], BF16, "wdkv")
        wdq = kb.sb(s6, [128, 8, 384], BF16, "wdq")
        wkr = kb.sb(s6, [128, 8, 64], BF16, "wkr")
        b_w = Buf()
        dsw = [DSem(kb, f"lw{i}") for i in range(3)]
        kb.load(kb.pool, dsw[0], wdkv[:], d["wdkv"].rearrange("(k p) n -> p k n", p=128), writes=[b_w])
        t2 = kb.load(kb.pool, dsw[1], wdq[:], d["wdq"].rearrange("(k p) n -> p k n", p=128), writes=[])
        t3 = kb.load(kb.pool, dsw[2], wkr[:], d["wkr2"].rearrange("(k p) n -> p k n", p=128), writes=[])
        b_w2, b_w3 = Buf(), Buf()
        b_w2.w, b_w3.w = t2, t3
        ssl = kb.sb(s6, [128, NOWN, 2], F32, "ssl")
        b_ssl = [Buf() for _ in range(NOWN)]
        kb.dve.done(V.memset(ap=ssl[:], constant=0.0))
        for ti in range(NOWN):
            b_ssl[ti].w = Tok(kb.dve.sem, kb.dve.cnt, "dve")
        krT = kb.sb(s6, [32, OWN], BF16, "krT")
        b_krT = Buf()
        ckvT = kb.sb(s6, [128, 2, OWN], BF16, "ckvTo")
        cqT = kb.sb(s6, [128, 3, OWN], BF16, "cqTo")
        b_ckvT, b_cqT = Buf(), Buf()
        cosT = kb.sb(s6, [32, OWN], F32, "cosT")
        sinT = kb.sb(s6, [32, OWN], F32, "sinT")
        b_tab = Buf()
        with ExitStack() as sr:
            posi = kb.sb(sr, [32, OWN], I32, "posi")
            b_pos = Buf()
            dsp = DSem(kb, "posk")
            kb.load(kb.sp, dsp, posi[:], d["posk"].partition_broadcast(32), writes=[b_pos])
            emit_rope_tables(kb, sr, posi[:], b_pos, OWN, vec["invf"], vec["sgn"], bv, cosT[:], sinT[:], b_tab)
            kb.barrier()
        if stop == 7:
            kb.barrier()
            return
        hkvT = kb.sb(s6, [128, 8, 512], BF16, "hkvT")
        hqT = kb.sb(s6, [128, 8, 512], BF16, "hqT")
        b_hkvT, b_hqT = Buf(), Buf()
        xnb = [kb.sb(s6, [128, 1024], BF16, "xnb6") for _ in range(4)]
        b_xnb = [Buf() for _ in range(4)]
        t1 = kb.sb(s6, [32, 512], F32, "rt1")
        t2_ = kb.sb(s6, [32, 512], F32, "rt2")
        b_t1, b_t2 = Buf(), Buf()
        lnb = [kb.sb(s6, [128, 640], BF16, "lnb") for _ in range(2)]
        b_lnb = [Buf(), Buf()]
        for blk in range(4):
            tl = [(h[:, 1 + 4 * blk + j, :], b_h[1 + 4 * blk + j], 1 + 4 * blk + j, j * 128) for j in range(4)]
            emit_norm_T(kb, c, tl, ssq, b_ssq, [(hkvT, b_hkvT, akv, b_akv, shkv, b_shkv),
                                                (hqT, b_hqT, a1n, b_a1n, sh1n, b_sh1n)], pairs[0:2], xnb, b_xnb)
            for j in range(4):
                ti = 4 * blk + j
                pbs = []
                for (src, b_src, w, b_ww, ncol, row, inv_n) in ((hkvT, b_hkvT, wdkv, b_w, 256, 0, 1.0 / 256),
                                                              (hqT, b_hqT, wdq, b_w2, 384, 1, 1.0 / 384)):
                    pb, b_pb = banks[4 + (2 * ti + row) % 4]
                    pbs.append((pb, b_pb))

                    def emit(pb=pb, src=src, w=w, ncol=ncol, j=j):
                        for k in range(8):
                            ins = nc.tensor.matmul(pb[:, 0:ncol], lhsT=src[:, k, j * 128:(j + 1) * 128], rhs=w[:, k, :],
                                                   start=(k == 0), stop=(k == 7))
                        return ins
                    kb.mm(emit, reads=[b_src, b_ww], writes=[b_pb])
                    kb.op(kb.act, A.activation, reads=[b_pb], writes=[b_junk, b_ssl[ti]], out=junk[:, 0:ncol],
                          in_=pb[:, 0:ncol], func=AF.Square, accum_out=ssl[:, ti, row:row + 1])
                    kb.op(kb.dve, V.tensor_scalar, reads=[], writes=[b_ssl[ti]], out=ssl[:, ti, row:row + 1],
                          in0=ssl[:, ti, row:row + 1], scalar1=inv_n, scalar2=1e-6, op0=ALU.mult, op1=ALU.add)
                kb.op(kb.act, A.activation, reads=[], writes=[b_ssl[ti]], out=ssl[:, ti, :], in_=ssl[:, ti, :],
                      func=AF.Sqrt)
                kb.op(kb.dve, V.reciprocal, reads=[], writes=[b_ssl[ti]], out=ssl[:, ti, :], in_=ssl[:, ti, :])
                lb, b_lb = lnb[ti % 2], b_lnb[ti % 2]
                kb.op(kb.dve, V.scalar_tensor_tensor, reads=[pbs[0][1], b_ssl[ti], bv], writes=[b_lb], out=lb[:, 0:256],
                      in0=pbs[0][0][:, 0:256], scalar=ssl[:, ti, 0:1], in1=vec["latg"][:], op0=ALU.mult, op1=ALU.mult)
                kb.op(kb.dve, V.scalar_tensor_tensor, reads=[pbs[1][1], b_ssl[ti], bv], writes=[b_lb], out=lb[:, 256:640],
                      in0=pbs[1][0][:, 0:384], scalar=ssl[:, ti, 1:2], in1=vec["qg"][:], op0=ALU.mult, op1=ALU.mult)
                if os.environ.get("SKIP_TR"):
                    continue
                pb, b_pb = banks[2 + ti % 2]
                pv = pairs[1][0].bitcast(BF16)[:, (ti % 2) * 1024:(ti % 2) * 1024 + 1024]

                def emit(pv=pv, lb=lb):
                    for cc in range(5):
                        ins = nc.tensor.transpose(out=pv[:, cc * 128:(cc + 1) * 128], in_=lb[:, cc * 128:(cc + 1) * 128],
                                                  identity=c.idb[:])
                    return ins
                kb.mm(emit, reads=[b_lb, c.b_idb], writes=[b_pb])
                for cc in range(5):
                    dst, b_dst = (ckvT[:, cc, ti * 128:(ti + 1) * 128], b_ckvT) if cc < 2 else \
                        (cqT[:, cc - 2, ti * 128:(ti + 1) * 128], b_cqT)
                    if cc % 2 == 0:
                        kb.op(kb.dve, V.tensor_copy, reads=[b_pb], writes=[b_dst], out=dst, in_=pv[:, cc * 128:(cc + 1) * 128])
                    else:
                        kb.op(kb.act, A.activation, reads=[b_pb], writes=[b_dst], out=dst, in_=pv[:, cc * 128:(cc + 1) * 128],
                              func=AF.Copy)
            if os.environ.get("SKIP_ROPE"):
                continue
            pa, b_pa = banks[0]
            pbw, b_pbw = banks[1]
            for (pp, b_pp, c0) in ((pa, b_pa, 0), (pbw, b_pbw, 32)):
                def emit(pp=pp, c0=c0):
                    for k in range(8):
                        ins = nc.tensor.matmul(pp[0:32, :], lhsT=wkr[:, k, c0:c0 + 32], rhs=hkvT[:, k, :],
                                               start=(k == 0), stop=(k == 7))
                    return ins
                kb.mm(emit, reads=[b_hkvT, b_w3], writes=[b_pp])
            tk = slice(blk * 512, (blk + 1) * 512)
            kb.op(kb.dve, V.tensor_tensor, reads=[b_pa, b_tab], writes=[b_t1], out=t1[:], in0=pa[0:32, :],
                  in1=cosT[:, tk], op=ALU.mult)
            kb.op(kb.dve, V.tensor_tensor, reads=[b_pbw, b_tab], writes=[b_t2], out=t2_[:], in0=pbw[0:32, :],
                  in1=sinT[:, tk], op=ALU.mult)
            kb.op(kb.dve, V.tensor_tensor, reads=[b_t1, b_t2], writes=[b_krT], out=krT[:, tk], in0=t1[:], in1=t2_[:],
                  op=ALU.add)
        if stop == 8:
            kb.barrier()
            return
        kb.store(kb.sp, o_ckv.rearrange("(c p) n -> p c n", p=128), ckvT[:], reads=[b_ckvT])
        kb.store(kb.sp, o_cq.rearrange("(c p) n -> p c n", p=128), cqT[:], reads=[b_cqT])
        kb.store(kb.sp, o_kr, krT[:], reads=[b_krT])
        kb.barrier()


def build_phaseA(dbg=False, stop=99):
    nc = bass.Bass("TRN2", target_bir_lowering=False)
    d = {}
    for name, shape, dt in [
        ("xe", [NALL * 128, D], F32), ("valid", [128, NALL], F32), ("posk", [1, OWN], I32), ("cvec", [128, 8], F32),
        ("invf", [32, 1], F32), ("sgn", [32, 1], F32), ("hv", [128, 1], F32), ("ident", [128, 128], F32),
        ("mw0", [D, 6 * D], F32), ("mb0", [1, 6 * D], F32), ("mw1", [D, 2 * D], F32), ("mb1", [1, 2 * D], F32),
        ("kvmw", [D, 2 * D], F32), ("kvmb", [1, 2 * D], F32),
        ("n1g0", [128, 8], F32), ("n2g0", [128, 8], F32), ("n1g1", [128, 8], F32), ("kvng", [128, 8], F32),
        ("wqkv_t", [8, 128, 8 * 384], F32), ("wo", [D, D], F32), ("tb", [128, 16 * 640], F32),
        ("win_t", [NF, 128, 8 * 256], F32), ("cw", [128, 2 * NF * 3], F32), ("cb", [128, 2 * NF], F32),
        ("wout", [FF, D], F32), ("wdkv", [D, 256], F32), ("latg", [1, 256], F32), ("wkr2", [D, 64], F32),
        ("wdq", [D, 384], F32), ("qg", [1, 384], F32),
    ]:
        d[name] = _din(nc, name, shape, dt)
    o_h1 = _dout(nc, "h1", [NEXT * 128, D], F32)
    o_ckv = _dout(nc, "ckvT", [256, OWN], BF16)
    o_kr = _dout(nc, "krT", [32, OWN], BF16)
    o_cq = _dout(nc, "cqT", [384, OWN], BF16)
    if dbg:
        o_dbg = _dout(nc, "dbg", [NEXT * 128, D], F32)

    kb = KB(nc)
    es = kb.es
    V, A, G = nc.vector, nc.scalar, nc.gpsimd
    pairs, banks = psum_banks(kb, es)
    c = emit_consts(kb, es, d)
    vec = load_vecs(kb, es, c, {
        "valid": (d["valid"], [128, NALL], F32), "hv": (d["hv"], [128, 1], F32),
        "invf": (d["invf"], [32, 1], F32), "sgn": (d["sgn"], [32, 1], F32),
        "n1g0": (d["n1g0"], [128, 8], F32), "n2g0": (d["n2g0"], [128, 8], F32),
        "n1g1": (d["n1g1"], [128, 8], F32), "kvng": (d["kvng"], [128, 8], F32),
        "cw": (d["cw"].rearrange("p (f t) -> p f t", t=3), [128, 2 * NF, 3], F32), "cb": (d["cb"], [128, 2 * NF], F32),
        "latg": (d["latg"].partition_broadcast(128), [128, 256], F32),
        "qg": (d["qg"].partition_broadcast(128), [128, 384], F32),
    })
    bv = c.b_vec
    fm0, bc0 = emit_mod(kb, es, c, [banks[0], banks[1]], d["cvec"], d["mw0"], d["mb0"], 6 * D,
                        want_fm=[0, 1024, 3072, 4096], want_bc=[2048, 5120], tag="0")
    fm1, _ = emit_mod(kb, es, c, [banks[0], banks[1]], d["cvec"], d["mw1"], d["mb1"], 2 * D,
                      want_fm=[0, 1024], want_bc=[], tag="1")
    fmk, _ = emit_mod(kb, es, c, [banks[0], banks[1]], d["cvec"], d["kvmw"], d["kvmb"], 2 * D,
                      want_fm=[0, 1024], want_bc=[], tag="k")

    def mk_a(sc, gname):
        t = kb.sb(es, [128, 8], F32, "a_" + gname)
        b = Buf()
        kb.op(kb.dve, V.scalar_tensor_tensor, reads=[sc[1], bv], writes=[b], out=t[:], in0=sc[0][:], scalar=1.0,
              in1=vec[gname][:], op0=ALU.add, op1=ALU.mult)
        return t, b
    a1, b_a1 = mk_a(fm0[1024], "n1g0")
    a2, b_a2 = mk_a(fm0[4096], "n2g0")
    a1n, b_a1n = mk_a(fm1[1024], "n1g1")
    akv, b_akv = mk_a(fmk[1024], "kvng")
    sh1, b_sh1 = fm0[0]
    sh2, b_sh2 = fm0[3072]
    sh1n, b_sh1n = fm1[0]
    shkv, b_shkv = fmk[0]
    g1bc, b_g1 = bc0[2048]
    g2bc, b_g2 = bc0[5120]

    if stop == 0:
        kb.finish()
        return nc
    ssq = kb.sb(es, [128, 32], F32, "ssq")
    b_ssq = Buf()
    xev = d["xe"].rearrange("(t p) n -> p t n", p=128)
    with ExitStack() as s12:
        hnT = kb.sb(s12, [128, 8, NALL * 128], BF16, "hnT")
        b_hnT = Buf()
        with ExitStack() as s1:
            xall = kb.sb(s1, [128, NALL, 1024], F32, "xall")
            b_x = [Buf() for _ in range(NALL)]
            junk = kb.sb(s1, [128, 1024], BF16, "junk")
            b_junk = Buf()
            xnb = [kb.sb(s1, [128, 1024], BF16, "xnb") for _ in range(4)]
            b_xnb = [Buf() for _ in range(4)]
            dsx = [DSem(kb, f"x{i}") for i in range(6)]
            kb.op(kb.dve, V.memset, writes=[b_ssq], ap=ssq[:], constant=0.0)
            for t in range(NALL):
                kb.load(kb.sp, dsx[t % 6], xall[:, t, :], xev[:, t, :], writes=[b_x[t]])
            for t in range(NALL):
                kb.op(kb.act, A.activation, reads=[b_x[t]], writes=[b_junk, b_ssq], out=junk[:], in_=xall[:, t, :],
                      func=AF.Square, accum_out=ssq[:, t:t + 1])
            emit_rstd(kb, es, ssq, b_ssq, NALL, 1.0 / D)
            tl = [(xall[:, t, :], b_x[t], t, t * 128) for t in range(NALL)]
            emit_norm_T(kb, c, tl, ssq, b_ssq, [(hnT, b_hnT, a1, b_a1, sh1, b_sh1)], pairs[0:2], xnb, b_xnb)
            kb.barrier()
        if stop == 1:
            kb.finish()
            return nc
        s_o = ExitStack()
        oT = kb.sb(s_o, [128, 8, NEXT * 128], BF16, "oT")
        if True:
            b_oT = Buf()
            with ExitStack() as s2:
                tb = kb.sb(s2, [128, 16, 640], BF16, "tb")
                b_tb = Buf()
                ds_tb = DSem(kb, "tb")
                tbv = d["tb"].rearrange("p (h n) -> p h n", h=16)
                for q4 in range(4):
                    kb.load(kb.pool, ds_tb, tb[:, q4 * 4:q4 * 4 + 4, :], tbv[:, q4 * 4:q4 * 4 + 4, :], writes=[b_tb])
                wsl = [kb.sb(s2, [128, 8, 384], BF16, "wqkv") for _ in range(2)]
                b_wsl = [Buf(), Buf()]
                ds_w = [DSem(kb, f"wqkv{i}") for i in range(2)]
                qT = [kb.sb(s2, [128, NEXT * 128], BF16, "qT") for _ in range(2)]
                kT = [kb.sb(s2, [128, NALL * 128], BF16, "kT") for _ in range(2)]
                va = [kb.sb(s2, [128, NALL, 2, 128], BF16, "va") for _ in range(2)]
                b_qT, b_kT, b_va = [Buf(), Buf()], [Buf(), Buf()], [Buf(), Buf()]
                PT = [kb.sb(s2, [128, 640], BF16, "PT") for _ in range(6)]
                b_PT = [Buf() for _ in range(6)]
                rs = [kb.sb(s2, [128, 256], F32, "rs") for _ in range(2)]
                b_rs = [Buf(), Buf()]
                for sl in range(2):
                    for t in range(NALL):
                        kb.op(kb.dve, V.tensor_scalar, reads=[bv, c.b_ones], writes=[b_va[sl]],
                              out=va[sl][:, t, 0, 64:128], in0=c.ones[:, 0:64], scalar1=vec["valid"][:, t:t + 1],
                              scalar2=None, op0=ALU.mult)
                        kb.op(kb.dve, V.tensor_scalar, reads=[bv, c.b_ones], writes=[b_va[sl]],
                              out=va[sl][:, t, 1, 0:64], in0=c.ones[:, 0:64], scalar1=vec["valid"][:, t:t + 1],
                              scalar2=None, op0=ALU.mult)

                def load_w(p):
                    kb.load(kb.pool, ds_w[p % 2], wsl[p % 2][:],
                            d["wqkv_t"][p].rearrange("p (k n) -> p k n", k=8), writes=[b_wsl[p % 2]])
                load_w(0)
                pcnt = 0
                acnt = 0
                for p in range(8):
                    sl = p % 2
                    if p + 1 < 8:
                        load_w(p + 1)
                    w = wsl[sl]
                    for (dst, b_dst, col0, tok0, ntok, scale) in ((qT[sl], b_qT[sl], 0, 512, NEXT * 128, 0.125),
                                                                  (kT[sl], b_kT[sl], 128, 0, NALL * 128, None)):
                        for e0 in range(0, ntok, 512):
                            n = min(512, ntok - e0)
                            pb, b_pb = banks[4 + pcnt % 2]
                            pcnt += 1

                            def emit(pb=pb, col0=col0, a0=tok0 + e0, n=n):
                                for k in range(8):
                                    ins = nc.tensor.matmul(pb[:, 0:n], lhsT=w[:, k, col0:col0 + 128],
                                                           rhs=hnT[:, k, a0:a0 + n], start=(k == 0), stop=(k == 7))
                                return ins
                            kb.mm(emit, reads=[b_wsl[sl], b_hnT], writes=[b_pb])
                            if scale is not None:
                                kb.op(kb.act, A.activation, reads=[b_pb], writes=[b_dst], out=dst[:, e0:e0 + n],
                                      in_=pb[:, 0:n], func=AF.Copy, scale=scale)
                            else:
                                kb.op(kb.dve, V.tensor_copy, reads=[b_pb], writes=[b_dst], out=dst[:, e0:e0 + n],
                                      in_=pb[:, 0:n])
                    for t0 in range(0, NALL, 4):
                        nt_ = min(4, NALL - t0)
                        pb, b_pb = banks[4 + pcnt % 2]
                        pcnt += 1

                        def emit(pb=pb, t0=t0, nt_=nt_):
                            for j in range(nt_):
                                for k in range(8):
                                    ins = nc.tensor.matmul(pb[:, j * 128:(j + 1) * 128],
                                                           lhsT=hnT[:, k, (t0 + j) * 128:(t0 + j + 1) * 128],
                                                           rhs=w[:, k, 256:384], start=(k == 0), stop=(k == 7))
                            return ins
                        kb.mm(emit, reads=[b_wsl[sl], b_hnT], writes=[b_pb])
                        for j in range(nt_):
                            kb.op(kb.dve, V.tensor_scalar, reads=[b_pb, bv], writes=[b_va[sl]],
                                  out=va[sl][:, t0 + j, 0, 0:64], in0=pb[:, j * 128:j * 128 + 64],
                                  scalar1=vec["valid"][:, t0 + j:t0 + j + 1], scalar2=None, op0=ALU.mult)
                            kb.op(kb.act, A.activation, reads=[b_pb, bv], writes=[b_va[sl]],
                                  out=va[sl][:, t0 + j, 1, 64:128], in_=pb[:, j * 128 + 64:j * 128 + 128],
                                  func=AF.Copy, scale=vec["valid"][:, t0 + j:t0 + j + 1])
                    def tail_fn(te, pts):
                        ob, b_ob = banks[6 + te % 2]
                        for hi in range(2):
                            pi = pts[hi]

                            def emit(hi=hi, pi=pi, te=te, ob=ob):
                                for m in range(5):
                                    ins = nc.tensor.matmul(ob[:, hi * 128:(hi + 1) * 128], lhsT=va[sl][:, te + m, hi, :],
                                                           rhs=PT[pi][:, m * 128:(m + 1) * 128], start=(m == 0),
                                                           stop=(m == 4), skip_group_check=True)
                                return ins
                            kb.mm(emit, reads=[b_va[sl], b_PT[pi]], writes=[b_ob])
                        rsb, b_rsb = rs[te % 2], b_rs[te % 2]
                        kb.op(kb.dve, V.tensor_scalar, reads=[b_ob], writes=[b_rsb], out=rsb[0:64, 0:128],
                              in0=ob[64:128, 0:128], scalar1=1e-30, scalar2=None, op0=ALU.max)
                        kb.op(kb.dve, V.tensor_scalar, reads=[b_ob], writes=[b_rsb], out=rsb[64:128, 128:256],
                              in0=ob[0:64, 128:256], scalar1=1e-30, scalar2=None, op0=ALU.max)
                        kb.op(kb.dve, V.reciprocal, reads=[b_rsb], writes=[b_rsb], out=rsb[0:64, 0:128],
                              in_=rsb[0:64, 0:128])
                        kb.op(kb.dve, V.reciprocal, reads=[b_rsb], writes=[b_rsb], out=rsb[64:128, 128:256],
                              in_=rsb[64:128, 128:256])
                        kb.op(kb.dve, V.tensor_tensor, reads=[b_ob, b_rsb], writes=[b_oT],
                              out=oT[0:64, p, te * 128:(te + 1) * 128], in0=ob[0:64, 0:128], in1=rsb[0:64, 0:128],
                              op=ALU.mult)
                        kb.op(kb.dve, V.tensor_tensor, reads=[b_ob, b_rsb], writes=[b_oT],
                              out=oT[64:128, p, te * 128:(te + 1) * 128], in0=ob[64:128, 128:256],
                              in1=rsb[64:128, 128:256], op=ALU.mult)

                    prev = None
                    for te in range(NEXT):
                        pts = []
                        for hi in range(2):
                            h16 = 2 * p + hi
                            sp_, sb_ = pairs[hi]
                            r0 = hi * 64

                            def emit(sp_=sp_, h16=h16, r0=r0, te=te):
                                nc.tensor.matmul(sp_[:, 0:512], lhsT=c.idb[:], rhs=tb[:, h16, 0:512], start=True,
                                                 stop=False, skip_group_check=True)
                                nc.tensor.matmul(sp_[:, 512:640], lhsT=c.idb[:], rhs=tb[:, h16, 512:640], start=True,
                                                 stop=False, skip_group_check=True)
                                for m in range(5):
                                    ins = nc.tensor.matmul(sp_[:, m * 128:(m + 1) * 128],
                                                           lhsT=kT[sl][r0:r0 + 64, (te + m) * 128:(te + m + 1) * 128],
                                                           rhs=qT[sl][r0:r0 + 64, te * 128:(te + 1) * 128],
                                                           start=False, stop=True, skip_group_check=True)
                                return ins
                            kb.mm(emit, reads=[c.b_idb, b_tb, b_kT[sl], b_qT[sl]], writes=sb_)
                            pi = acnt % 6
                            acnt += 1
                            kb.op(kb.act, A.activation, reads=sb_, writes=[b_PT[pi]], out=PT[pi][:], in_=sp_[:, 0:640],
                                  func=AF.Exp)
                            pts.append(pi)
                        if prev is not None:
                            tail_fn(*prev)
                        prev = (te, pts)
                    tail_fn(*prev)
                kb.barrier()
    if stop == 2:
        kb.finish()
        return nc
    h = kb.sb(es, [128, NEXT, 1024], F32, "h")
    b_h = [Buf() for _ in range(NEXT)]
    with ExitStack() as s3:
        wo = kb.sb(s3, [128, 8, 1024], BF16, "wo")
        b_wo = Buf()
        ds_wo = DSem(kb, "wo")
        kb.load(kb.pool, ds_wo, wo[:], d["wo"].rearrange("(c p) n -> p c n", p=128), writes=[b_wo])
        tmp = kb.sb(s3, [128, 1024], F32, "tmp3")
        b_tmp = Buf()
        dsh = [DSem(kb, f"h{i}") for i in range(4)]
        for te in range(NEXT):
            kb.load(kb.sp, dsh[te % 4], h[:, te, :], xev[:, te + 4, :], writes=[b_h[te]])
        for te in range(NEXT):
            pair, pb2 = pairs[te % 2]

            def emit(pair=pair, te=te):
                for hh in range(2):
                    for cc in range(8):
                        ins = nc.tensor.matmul(pair[:, hh * 512:(hh + 1) * 512],
                                               lhsT=oT[:, cc, te * 128:(te + 1) * 128],
                                               rhs=wo[:, cc, hh * 512:(hh + 1) * 512], start=(cc == 0),
                                               stop=(cc == 7))
                return ins
            kb.mm(emit, reads=[b_oT, b_wo], writes=pb2)
            kb.op(kb.dve, V.tensor_tensor, reads=pb2 + [b_g1], writes=[b_tmp], out=tmp[:], in0=pair[:],
                  in1=g1bc[:], op=ALU.mult)
            kb.op(kb.dve, V.tensor_tensor, reads=[b_tmp], writes=[b_h[te]], out=h[:, te, :], in0=h[:, te, :],
                  in1=tmp[:], op=ALU.add)
        kb.barrier()
    s_o.close()
    if dbg:
        for te in range(NEXT):
            kb.store(kb.sp, o_dbg[te * 128:(te + 1) * 128, :], h[:, te, :], reads=[b_h[te]])
    if stop == 3:
        kb.finish()
        return nc
    with ExitStack() as s4:
        junk = kb.sb(s4, [128, 1024], BF16, "junk2")
        b_junk = Buf()
        kb.op(kb.dve, V.memset, writes=[b_ssq], ap=ssq[:], constant=0.0)
        for te in range(NEXT):
            kb.op(kb.act, A.activation, reads=[b_h[te]], writes=[b_junk, b_ssq], out=junk[:], in_=h[:, te, :],
                  func=AF.Square, accum_out=ssq[:, te:te + 1])
        emit_rstd(kb, es, ssq, b_ssq, NEXT, 1.0 / D)
        kb.barrier()
    emit_ffn(kb, es, c, h, b_h, NEXT, ssq, b_ssq, a2, b_a2, sh2, b_sh2, g2bc, b_g2, d["win_t"], d["wout"],
             vec["cw"], bv, vec["cb"], bv, vec["hv"], bv, pairs, banks, 128, "A")
    if stop == 5:
        kb.finish()
        return nc
    for te in range(NEXT):
        kb.store(kb.sp, o_h1[te * 128:(te + 1) * 128, :], h[:, te, :], reads=[b_h[te]])
    if stop == 6:
        kb.finish()
        return nc
    emit_latents(kb, es, c, d, h, b_h, ssq, b_ssq, akv, b_akv, shkv, b_shkv, a1n, b_a1n, sh1n, b_sh1n, vec, bv,
                 pairs, banks, o_ckv, o_kr, o_cq, stop=stop)
    kb.finish()
    return nc


def toeplitz_bias(relb):
    k = np.arange(128)[:, None, None]
    m = np.arange(5)[None, :, None]
    q = np.arange(128)[None, None, :]
    idx = np.clip(q - k + 128 * (4 - m), -128, 128) + 128
    T = relb[:, idx]
    a, j = q // 64, k // 64
    masked = ((m == 0) & (j == 0) & (a == 1)) | ((m == 4) & (j == 1) & (a == 0))
    T = np.where(masked[None], np.float32(NEG), T).astype(np.float32)
    return np.ascontiguousarray(T.transpose(1, 0, 2, 3).reshape(128, 16 * 640))


def tile_win(win):
    w = np.asarray(win).reshape(8, 128, 2, NF, 128)
    return np.ascontiguousarray(w.transpose(3, 1, 0, 2, 4).reshape(NF, 128, 8 * 256))


def tile_wqkv(w):
    w = np.asarray(w).reshape(8, 128, 3, 8, 128)
    return np.ascontiguousarray(w.transpose(3, 1, 0, 2, 4).reshape(8, 128, 8 * 384))


def conv_fm(conv_w, conv_b):
    cw = np.ascontiguousarray(np.asarray(conv_w).T.reshape(2 * NF, 128, 3).transpose(1, 0, 2).reshape(128, 2 * NF * 3))
    return cw, fm(conv_b)


def rope_consts():
    invf = np.power(np.float32(10000.0), -np.arange(16, dtype=np.float32) * np.float32(2.0 / 32)).astype(np.float32)
    invf = np.concatenate([invf, invf]).reshape(32, 1)
    sgn = np.concatenate([-np.ones(16, np.float32), np.ones(16, np.float32)]).reshape(32, 1)
    return invf, sgn


def prep_A(inp):
    x, cc, pos = inp["x"], inp["c"], inp["positions"]
    invf, sgn = rope_consts()
    cw0, cb0 = conv_fm(inp["f_conv_w"][0], inp["f_conv_b"][0])
    wkr = np.asarray(inp["b_wkr"])
    shared = {
        "invf": invf, "sgn": sgn, "ident": np.eye(128, dtype=np.float32),
        "mw0": np.ascontiguousarray(inp["mod_w"][0]), "mb0": np.ascontiguousarray(inp["mod_b"][0][None]),
        "mw1": np.ascontiguousarray(inp["mod_w"][1][:, 0:2 * D]), "mb1": np.ascontiguousarray(inp["mod_b"][1][None, 0:2 * D]),
        "kvmw": np.ascontiguousarray(inp["kv_mod_w"]), "kvmb": np.ascontiguousarray(inp["kv_mod_b"][None]),
        "n1g0": fm(inp["norm1_g"][0]), "n2g0": fm(inp["norm2_g"][0]), "n1g1": fm(inp["norm1_g"][1]),
        "kvng": fm(inp["kv_norm_g"]),
        "wqkv_t": tile_wqkv(inp["a_wqkv"][0]), "wo": np.ascontiguousarray(inp["a_wo"][0]),
        "tb": toeplitz_bias(np.asarray(inp["a_rel_bias"][0])),
        "win_t": tile_win(inp["f_win"][0]), "cw": cw0, "cb": cb0, "wout": np.ascontiguousarray(inp["f_wout"][0]),
        "wdkv": np.ascontiguousarray(inp["b_wdkv"]), "latg": np.ascontiguousarray(inp["b_kv_lat_norm_g"][None]),
        "wkr2": np.ascontiguousarray(np.concatenate([wkr, wkr[:, 16:32], wkr[:, 0:16]], axis=1)),
        "wdq": np.ascontiguousarray(inp["b_wdq"][0]), "qg": np.ascontiguousarray(inp["b_q_norm_g"][0][None]),
    }
    maps = []
    for core in range(NCORE):
        b, j = core // 4, core % 4
        s0 = j * OWN
        xe = np.zeros((NALL * 128, D), np.float32)
        lo = s0 - 640
        src0 = max(lo, 0)
        xe[src0 - lo:] = x[b, src0:s0 + OWN]
        tok = lo + np.arange(NALL * 128)
        valid = np.ascontiguousarray((tok >= 0).astype(np.float32).reshape(NALL, 128).T)
        m = dict(shared)
        m.update({"xe": xe, "valid": valid, "posk": np.ascontiguousarray(pos[b, s0:s0 + OWN][None]).astype(np.int32),
                  "cvec": fm(cc[b]), "hv": np.full((128, 1), 1.0 if s0 > 0 else 0.0, np.float32)})
        maps.append(m)
    return maps


def mla_mask():
    k = np.arange(128)[:, None, None]
    jj = np.arange(4)[None, :, None]
    q = np.arange(512)[None, None, :]
    allowed = (2 * jj + k // 64) <= (q // 64)
    return np.ascontiguousarray(np.where(allowed, np.float32(0.0), np.float32(NEG)).astype(np.float32).reshape(128, 4 * 512))


def build_phaseB():
    nc = bass.Bass("TRN2", target_bir_lowering=False)
    d = {}
    for name, shape, dt in [
        ("cqT", [384, S], BF16), ("ckvT", [256, S], BF16), ("krT", [32, S], BF16), ("posq", [1, S], I32),
        ("invf", [32, 1], F32), ("sgn", [32, 1], F32), ("ident", [128, 128], F32),
        ("wq96", [384, 4 * 96], F32), ("wq96s", [384, 4 * 96], F32), ("wuk", [256, 256], F32), ("wuv", [256, 256], F32),
        ("mask", [128, 4 * 512], F32),
    ]:
        d[name] = _din(nc, name, shape, dt)
    o_oT = _dout(nc, "oT", [256, S], BF16)
    kb = KB(nc)
    es = kb.es
    V, A, G = nc.vector, nc.scalar, nc.gpsimd
    pairs, banks = psum_banks(kb, es)
    c = emit_consts(kb, es, d)
    invf = kb.sb(es, [96, 1], F32, "invf")[64:96]
    sgn = kb.sb(es, [96, 1], F32, "sgn")[64:96]
    dsv = DSem(kb, "bvec")
    bv = Buf()
    kb.load(kb.sp, dsv, invf, d["invf"], writes=[bv], chain=False)
    kb.load(kb.sp, dsv, sgn, d["sgn"], writes=[bv], chain=False)
    SC = 96.0 ** -0.5
    NQB = S // 512
    cqs = [kb.sb(es, [128, 3, 512], BF16, "cqs") for _ in range(3)]
    b_cqs = [Buf() for _ in range(3)]
    ds_cq = [DSem(kb, f"cqs{i}") for i in range(3)]
    cq_d = d["cqT"].rearrange("(c p) n -> p c n", p=128)
    ckvT = kb.sb(es, [128, 2, S], BF16, "ckvT")
    KT = kb.sb(es, [96, S], BF16, "KT")
    QT = kb.sb(es, [96, S], BF16, "QT")
    b_ckv, b_KT, b_QT = Buf(), Buf(), Buf()
    dsl = [DSem(kb, f"bl{i}") for i in range(3)]
    kb.load(kb.sp, dsl[1], ckvT[:], d["ckvT"].rearrange("(c p) n -> p c n", p=128), writes=[b_ckv])
    kb.load(kb.sp, dsl[2], KT[64:96, :], d["krT"], writes=[b_KT])
    wq = kb.sb(es, [128, 3, 4 * 96], BF16, "wq96")
    wqs = kb.sb(es, [128, 3, 4 * 96], BF16, "wq96s")
    wuk = kb.sb(es, [128, 2, 256], BF16, "wuk")
    wuv = kb.sb(es, [128, 2, 256], BF16, "wuv")
    mask = kb.sb(es, [128, 4, 512], BF16, "mask")
    dsw = [DSem(kb, f"bw{i}") for i in range(5)]
    toks = [
        kb.load(kb.pool, dsw[0], wq[:], d["wq96"].rearrange("(c p) n -> p c n", p=128)),
        kb.load(kb.pool, dsw[1], wqs[:], d["wq96s"].rearrange("(c p) n -> p c n", p=128)),
        kb.load(kb.pool, dsw[2], wuk[:], d["wuk"].rearrange("(c p) n -> p c n", p=128)),
        kb.load(kb.pool, dsw[3], wuv[:], d["wuv"].rearrange("(c p) n -> p c n", p=128)),
        kb.load(kb.pool, dsw[4], mask[:], d["mask"].rearrange("p (j n) -> p j n", j=4)),
    ]
    b_wl = [Buf() for _ in toks]
    for bb, t in zip(b_wl, toks):
        bb.w = t
    cosb = kb.sb(es, [96, S], BF16, "cosb")[64:96]
    sinb = kb.sb(es, [96, S], BF16, "sinb")[64:96]
    b_tab = Buf()
    with ExitStack() as sr:
        posi = kb.sb(sr, [96, 2048], I32, "posi")[64:96]
        b_pos = Buf()
        cosf = kb.sb(sr, [96, 2048], F32, "cosf")[64:96]
        sinf = kb.sb(sr, [96, 2048], F32, "sinf")[64:96]
        b_tf = Buf()
        dsp = DSem(kb, "posq")
        for part in range(4):
            tk = slice(part * 2048, (part + 1) * 2048)
            kb.load(kb.sp, dsp, posi, d["posq"][0:1, tk].partition_broadcast(32), writes=[b_pos])
            with ExitStack() as sr2:
                emit_rope_tables(kb, sr2, posi, b_pos, 2048, invf, sgn, bv, cosf, sinf, b_tf, p0=64)
                kb.op(kb.dve, V.tensor_copy, reads=[b_tf], writes=[b_tab], out=cosb[:, tk], in_=cosf)
                kb.op(kb.dve, V.tensor_copy, reads=[b_tf], writes=[b_tab], out=sinb[:, tk], in_=sinf)
                kb.barrier()
    va = kb.sb(es, [128, S // 128, 128], BF16, "va")
    b_va = Buf()
    kb.op(kb.dve, V.memset, writes=[b_va], ap=va[:, :, 64:128], constant=1.0)
    PT = [kb.sb(es, [128, 1024], BF16, "PT") for _ in range(3)]
    b_PT = [Buf() for _ in range(3)]
    rs = kb.sb(es, [64, 512], F32, "rs")
    b_rs = Buf()
    ost = [kb.sb(es, [64, 512], BF16, "ost") for _ in range(3)]
    b_ost = [Buf() for _ in range(3)]
    t1 = [kb.sb(es, [96, 512], F32, "t1")[64:96] for _ in range(2)]
    t2 = [kb.sb(es, [96, 512], F32, "t2")[64:96] for _ in range(2)]
    b_t1, b_t2 = [Buf(), Buf()], [Buf(), Buf()]
    pcnt = 0
    acnt = 0
    ocnt = 0

    def load_cq(i):
        blk = i % NQB
        kb.load(kb.sp, ds_cq[i % 3], cqs[i % 3][:], cq_d[:, :, blk * 512:(blk + 1) * 512], writes=[b_cqs[i % 3]])
    load_cq(0)
    load_cq(1)
    for hh in range(4):
        hc = slice(hh * 64, (hh + 1) * 64)
        h96 = slice(hh * 96, (hh + 1) * 96)
        for blk in range(NQB):
            ci = hh * NQB + blk
            if ci + 2 < 4 * NQB:
                load_cq(ci + 2)
            cqT, b_cq = cqs[ci % 3], b_cqs[ci % 3]
            tk = slice(blk * 512, (blk + 1) * 512)
            p1, b_p1 = banks[4 + pcnt % 4]
            pcnt += 1
            p2, b_p2 = banks[4 + pcnt % 4]
            pcnt += 1

            def emit(p1=p1, cqT=cqT):
                for kc in range(3):
                    ins = nc.tensor.matmul(p1[0:96, :], lhsT=wq[:, kc, h96], rhs=cqT[:, kc, :], start=(kc == 0),
                                           stop=(kc == 2))
                return ins
            kb.mm(emit, reads=[b_wl[0], b_cq], writes=[b_p1])

            def emit(p2=p2, cqT=cqT):
                for kc in range(3):
                    ins = nc.tensor.matmul(p2[0:96, :], lhsT=wqs[:, kc, h96], rhs=cqT[:, kc, :], start=(kc == 0),
                                           stop=(kc == 2))
                return ins
            kb.mm(emit, reads=[b_wl[1], b_cq], writes=[b_p2])
            kb.op(kb.act, A.activation, reads=[b_p1], writes=[b_QT], out=QT[0:64, tk], in_=p1[0:64, :], func=AF.Copy,
                  scale=SC)
            ti_ = blk % 2
            kb.op(kb.dve, V.tensor_tensor, reads=[b_p1, b_tab], writes=[b_t1[ti_]], out=t1[ti_], in0=p1[64:96, :],
                  in1=cosb[:, tk], op=ALU.mult)
            kb.op(kb.dve, V.scalar_tensor_tensor, reads=[b_p2, b_tab], writes=[b_t2[ti_]], out=t2[ti_],
                  in0=p2[64:96, :], scalar=SC, in1=sinb[:, tk], op0=ALU.mult, op1=ALU.mult)
            kb.op(kb.dve, V.scalar_tensor_tensor, reads=[b_t1[ti_], b_t2[ti_]], writes=[b_QT], out=QT[64:96, tk],
                  in0=t1[ti_], scalar=SC, in1=t2[ti_], op0=ALU.mult, op1=ALU.add)
            pb, b_pb = banks[4 + pcnt % 4]
            pcnt += 1

            def emit(pb=pb, tk=tk):
                for kc in range(2):
                    ins = nc.tensor.matmul(pb[0:64, :], lhsT=wuk[:, kc, hc], rhs=ckvT[:, kc, tk], start=(kc == 0),
                                           stop=(kc == 1))
                return ins
            kb.mm(emit, reads=[b_wl[2], b_ckv], writes=[b_pb])
            kb.op(kb.act, A.activation, reads=[b_pb], writes=[b_KT], out=KT[0:64, tk], in_=pb[0:64, :], func=AF.Copy)
        for t0 in range(0, S // 128, 8):
            pb, b_pb = banks[4 + pcnt % 4]
            pcnt += 1

            def emit(pb=pb, t0=t0):
                for j in range(8):
                    for kc in range(2):
                        ins = nc.tensor.matmul(pb[:, j * 64:(j + 1) * 64],
                                               lhsT=ckvT[:, kc, (t0 + j) * 128:(t0 + j + 1) * 128], rhs=wuv[:, kc, hc],
                                               start=(kc == 0), stop=(kc == 1))
                return ins
            kb.mm(emit, reads=[b_wl[3], b_ckv], writes=[b_pb])
            for j in range(8):
                if j % 2 == 0:
                    kb.op(kb.act, A.activation, reads=[b_pb], writes=[b_va], out=va[:, t0 + j, 0:64],
                          in_=pb[:, j * 64:(j + 1) * 64], func=AF.Copy)
                else:
                    kb.op(kb.dve, V.tensor_copy, reads=[b_pb], writes=[b_va], out=va[:, t0 + j, 0:64],
                          in_=pb[:, j * 64:(j + 1) * 64])
        for qb in range(NQB):
            qk = slice(qb * 512, (qb + 1) * 512)
            nkt = 4 * (qb + 1)
            ob, b_ob = banks[4 + ocnt % 2]
            ocnt += 1
            pend = []
            for kp in range(nkt // 2):
                sp_, sbufs = pairs[acnt % 2]
                pi = acnt % 3
                acnt += 1

                def emit(sp_=sp_, kp=kp):
                    for u in range(2):
                        kt = 2 * kp + u
                        ks = slice(kt * 128, (kt + 1) * 128)
                        o_ = sp_[:, u * 512:(u + 1) * 512]
                        diag = kt >= 4 * qb
                        ins = nc.tensor.matmul(o_, lhsT=KT[:, ks], rhs=QT[:, qk], start=True, stop=not diag,
                                               skip_group_check=True)
                        if diag:
                            ins = nc.tensor.matmul(o_, lhsT=c.idb[:], rhs=mask[:, kt - 4 * qb, :], start=False,
                                                   stop=True, skip_group_check=True)
                    return ins
                kb.mm(emit, reads=[b_KT, b_QT, c.b_idb, b_wl[4]], writes=sbufs)
                kb.op(kb.act, A.activation, reads=sbufs, writes=[b_PT[pi]], out=PT[pi][:], in_=sp_, func=AF.Exp)

                def emit2(pi=pi, kp=kp):
                    for u in range(2):
                        kt = 2 * kp + u
                        ins = nc.tensor.matmul(ob, lhsT=va[:, kt, :], rhs=PT[pi][:, u * 512:(u + 1) * 512],
                                               start=(kt == 0), stop=(kt == nkt - 1), skip_group_check=True)
                    return ins
                pend.append((emit2, pi))
                if len(pend) > 1:
                    e2, p2_ = pend.pop(0)
                    kb.mm(e2, reads=[b_va, b_PT[p2_]], writes=[b_ob])
            while pend:
                e2, p2_ = pend.pop(0)
                kb.mm(e2, reads=[b_va, b_PT[p2_]], writes=[b_ob])
            oi = ocnt % 3
            kb.op(kb.dve, V.tensor_scalar, reads=[b_ob], writes=[b_rs], out=rs[:], in0=ob[64:128, :], scalar1=1e-30,
                  scalar2=None, op0=ALU.max)
            kb.op(kb.dve, V.reciprocal, reads=[b_rs], writes=[b_rs], out=rs[:], in_=rs[:])
            kb.op(kb.dve, V.tensor_tensor, reads=[b_ob, b_rs], writes=[b_ost[oi]], out=ost[oi][:], in0=ob[0:64, :],
                  in1=rs[:], op=ALU.mult)
            kb.store(kb.sp, o_oT[hh * 64:(hh + 1) * 64, qk], ost[oi][:], reads=[b_ost[oi]])
    kb.finish()
    return nc


def build_phaseC():
    nc = bass.Bass("TRN2", target_bir_lowering=False)
    d = {}
    for name, shape, dt in [
        ("h1e", [NEXT * 128, D], F32), ("oTe", [D, NEXT * 128], BF16), ("cvec", [128, 8], F32), ("hv", [128, 1], F32),
        ("ident", [128, 128], F32), ("mw1", [D, 6 * D], F32), ("mb1", [1, 6 * D], F32), ("n2g1", [128, 8], F32),
        ("wo1", [D, D], F32), ("win_t", [NF, 128, 8 * 256], F32), ("cw", [128, 2 * NF * 3], F32),
        ("cb", [128, 2 * NF], F32), ("wout", [FF, D], F32), ("fg", [1, D], F32),
    ]:
        d[name] = _din(nc, name, shape, dt)
    o_out = _dout(nc, "out", [OWN, D], F32)
    kb = KB(nc)
    es = kb.es
    V, A, G = nc.vector, nc.scalar, nc.gpsimd
    pairs, banks = psum_banks(kb, es)
    c = emit_consts(kb, es, d)
    vec = load_vecs(kb, es, c, {
        "hv": (d["hv"], [128, 1], F32), "n2g1": (d["n2g1"], [128, 8], F32),
        "cw": (d["cw"].rearrange("p (f t) -> p f t", t=3), [128, 2 * NF, 3], F32), "cb": (d["cb"], [128, 2 * NF], F32),
        "fg": (d["fg"].partition_broadcast(128), [128, D], F32),
    })
    bv = c.b_vec
    fm1, bc1 = emit_mod(kb, es, c, [banks[0], banks[1]], d["cvec"], d["mw1"], d["mb1"], 6 * D,
                        want_fm=[3072, 4096], want_bc=[2048, 5120], tag="c")
    a2 = kb.sb(es, [128, 8], F32, "a2")
    b_a2 = Buf()
    kb.op(kb.dve, V.scalar_tensor_tensor, reads=[fm1[4096][1], bv], writes=[b_a2], out=a2[:], in0=fm1[4096][0][:],
          scalar=1.0, in1=vec["n2g1"][:], op0=ALU.add, op1=ALU.mult)
    sh2, b_sh2 = fm1[3072]
    g1bc, b_g1 = bc1[2048]
    g2bc, b_g2 = bc1[5120]
    ssq = kb.sb(es, [128, 32], F32, "ssq")
    b_ssq = Buf()
    h = kb.sb(es, [128, NEXT, 1024], F32, "h")
    b_h = [Buf() for _ in range(NEXT)]
    hv_d = d["h1e"].rearrange("(t p) n -> p t n", p=128)
    with ExitStack() as s3:
        oT = kb.sb(s3, [128, 8, NEXT * 128], BF16, "oT")
        b_oT = Buf()
        ds_o = DSem(kb, "oT")
        kb.load(kb.sp, ds_o, oT[:], d["oTe"].rearrange("(c p) n -> p c n", p=128), writes=[b_oT])
        wo = kb.sb(s3, [128, 8, 1024], BF16, "wo")
        b_wo = Buf()
        ds_wo = DSem(kb, "wo")
        kb.load(kb.pool, ds_wo, wo[:], d["wo1"].rearrange("(c p) n -> p c n", p=128), writes=[b_wo])
        tmp = kb.sb(s3, [128, 1024], F32, "tmp3")
        b_tmp = Buf()
        dsh = [DSem(kb, f"h{i}") for i in range(4)]
        for te in range(NEXT):
            kb.load(kb.sp, dsh[te % 4], h[:, te, :], hv_d[:, te, :], writes=[b_h[te]])
        for te in range(NEXT):
            pair, pb2 = pairs[te % 2]

            def emit(pair=pair, te=te):
                for hh in range(2):
                    for cc in range(8):
                        ins = nc.tensor.matmul(pair[:, hh * 512:(hh + 1) * 512], lhsT=oT[:, cc, te * 128:(te + 1) * 128],
                                               rhs=wo[:, cc, hh * 512:(hh + 1) * 512], start=(cc == 0), stop=(cc == 7))
                return ins
            kb.mm(emit, reads=[b_oT, b_wo], writes=pb2)
            kb.op(kb.dve, V.tensor_tensor, reads=pb2 + [b_g1], writes=[b_tmp], out=tmp[:], in0=pair[:], in1=g1bc[:],
                  op=ALU.mult)
            kb.op(kb.dve, V.tensor_tensor, reads=[b_tmp], writes=[b_h[te]], out=h[:, te, :], in0=h[:, te, :],
                  in1=tmp[:], op=ALU.add)
        kb.barrier()

    def stats(lo):
        with ExitStack() as s4:
            junk = kb.sb(s4, [128, 1024], BF16, "junk2")
            b_junk = Buf()
            kb.op(kb.dve, V.memset, writes=[b_ssq], ap=ssq[:], constant=0.0)
            for te in range(lo, NEXT):
                kb.op(kb.act, A.activation, reads=[b_h[te]], writes=[b_junk, b_ssq], out=junk[:], in_=h[:, te, :],
                      func=AF.Square, accum_out=ssq[:, te:te + 1])
            emit_rstd(kb, es, ssq, b_ssq, NEXT, 1.0 / D)
            kb.barrier()
    stats(0)
    emit_ffn(kb, es, c, h, b_h, NEXT, ssq, b_ssq, a2, b_a2, sh2, b_sh2, g2bc, b_g2, d["win_t"], d["wout"],
             vec["cw"], bv, vec["cb"], bv, vec["hv"], bv, pairs, banks, 128, "C")
    stats(1)
    with ExitStack() as s5:
        ot = [kb.sb(s5, [128, 1024], F32, "ot") for _ in range(3)]
        b_ot = [Buf() for _ in range(3)]
        for te in range(1, NEXT):
            i = te % 3
            kb.op(kb.dve, V.scalar_tensor_tensor, reads=[b_h[te], b_ssq, bv], writes=[b_ot[i]], out=ot[i][:],
                  in0=h[:, te, :], scalar=ssq[:, te:te + 1], in1=vec["fg"][:], op0=ALU.mult, op1=ALU.mult)
            kb.store(kb.sp, o_out[(te - 1) * 128:te * 128, :], ot[i][:], reads=[b_ot[i]])
        kb.barrier()
    kb.finish()
    return nc


_CACHE = {}


def _get(name, fn):
    if name not in _CACHE:
        _CACHE[name] = fn()
    return _CACHE[name]


def kernel(**inp):
    inp = {k: np.asarray(v) for k, v in inp.items()}
    ident = np.eye(128, dtype=np.float32)
    invf, sgn = rope_consts()
    cores = list(range(NCORE))
    ncA = _get("A", build_phaseA)
    resA = run_bass_kernel_spmd(ncA, prep_A(inp), core_ids=cores).results
    ncB = _get("B", build_phaseB)
    wqr = np.asarray(inp["b_wqr"][0]).reshape(384, 16, 32)
    wuq_ = np.asarray(inp["b_wuq"][0]).reshape(384, 16, 64)
    wq96 = np.concatenate([wuq_, wqr], axis=2)
    wq96s = np.concatenate([np.zeros_like(wuq_), wqr[:, :, 16:32], wqr[:, :, 0:16]], axis=2)
    mask = mla_mask()
    mapsB = []
    for core in cores:
        b, j = core // 4, core % 4
        g = [resA[b * 4 + q] for q in range(4)]
        hs = slice(j * 256, (j + 1) * 256)
        mapsB.append({
            "cqT": np.ascontiguousarray(np.concatenate([np.asarray(r["cqT"]) for r in g], axis=1)),
            "ckvT": np.ascontiguousarray(np.concatenate([np.asarray(r["ckvT"]) for r in g], axis=1)),
            "krT": np.ascontiguousarray(np.concatenate([np.asarray(r["krT"]) for r in g], axis=1)),
            "posq": np.ascontiguousarray(inp["positions"][b][None]).astype(np.int32),
            "invf": invf, "sgn": sgn, "ident": ident,
            "wq96": np.ascontiguousarray(wq96[:, 4 * j:4 * j + 4, :].reshape(384, 4 * 96)),
            "wq96s": np.ascontiguousarray(wq96s[:, 4 * j:4 * j + 4, :].reshape(384, 4 * 96)),
            "wuk": np.ascontiguousarray(inp["b_wuk"][:, hs]), "wuv": np.ascontiguousarray(inp["b_wuv"][:, hs]),
            "mask": mask,
        })
    resB = run_bass_kernel_spmd(ncB, mapsB, core_ids=cores).results
    ncC = _get("C", build_phaseC)
    cw1, cb1 = conv_fm(inp["f_conv_w"][1], inp["f_conv_b"][1])
    sharedC = {
        "ident": ident, "mw1": np.ascontiguousarray(inp["mod_w"][1]), "mb1": np.ascontiguousarray(inp["mod_b"][1][None]),
        "n2g1": fm(inp["norm2_g"][1]), "wo1": np.ascontiguousarray(inp["b_wo"][0]),
        "win_t": tile_win(inp["f_win"][1]), "cw": cw1, "cb": cb1, "wout": np.ascontiguousarray(inp["f_wout"][1]),
        "fg": np.ascontiguousarray(inp["final_g"][None]),
    }
    mapsC = []
    for core in cores:
        b, j = core // 4, core % 4
        s0 = j * OWN
        oT_b = np.concatenate([np.asarray(resB[b * 4 + q]["oT"]) for q in range(4)], axis=0)
        oTe = np.zeros((D, NEXT * 128), oT_b.dtype)
        lo = s0 - 128
        src0 = max(lo, 0)
        oTe[:, src0 - lo:] = oT_b[:, src0:s0 + OWN]
        m = dict(sharedC)
        m.update({"h1e": np.ascontiguousarray(np.asarray(resA[core]["h1"])), "oTe": oTe, "cvec": fm(inp["c"][b]),
                  "hv": np.full((128, 1), 1.0 if s0 > 0 else 0.0, np.float32)})
        mapsC.append(m)
    resC = run_bass_kernel_spmd(ncC, mapsC, core_ids=cores).results
    out = np.zeros((2, S, D), np.float32)
    for core in cores:
        b, j = core // 4, core % 4
        out[b, j * OWN:(j + 1) * OWN] = np.asarray(resC[core]["out"])
    return out
```

```python
import os
import numpy as np
import ml_dtypes
from contextlib import ExitStack
import concourse.bass as bass
import concourse.mybir as mybir
from concourse.bass_utils import run_bass_kernel_spmd

F32 = mybir.dt.float32
BF16 = mybir.dt.bfloat16
I32 = mybir.dt.int32
AF = mybir.ActivationFunctionType
ALU = mybir.AluOpType
AX = mybir.AxisListType

NCORE = 8
D = 1024
S = 8192
OWN = 2048
NOWN = 16
NEXT = 17
NALL = 21
FF = 2816
NF = 22
NEG = -30000.0
ARENA_BYTES = 204 * 1024
TWO_PI = 6.283185307179586
PI = 3.141592653589793


class Tok:
    __slots__ = ("sem", "val", "key")

    def __init__(self, sem, val, key):
        self.sem, self.val, self.key = sem, val, key


class EQ:
    def __init__(self, kb, eng, name):
        self.kb, self.e, self.name = kb, eng, name
        self.sem = kb.newsem("q_" + name)
        self.cnt = 0
        self.seen = {}

    def wait(self, *toks):
        for t in toks:
            if t is None:
                continue
            if self.name == "pe" and t.key == "pe":
                continue
            if self.seen.get(t.key, 0) >= t.val:
                continue
            self.e.wait_ge(t.sem, t.val)
            self.seen[t.key] = t.val

    def done(self, ins):
        ins.then_inc(self.sem, 1)
        self.cnt += 1
        return Tok(self.sem, self.cnt, self.name)


class DSem:
    def __init__(self, kb, name):
        self.sem = kb.newsem("d_" + name)
        self.cnt = 0
        self.name = "d_" + name
        self.last = None
        kb.dsems.append(self)

    def add(self, ins):
        ins.then_inc(self.sem, 16)
        self.cnt += 16
        return Tok(self.sem, self.cnt, self.name)


class Buf:
    __slots__ = ("w", "r")

    def __init__(self):
        self.w = None
        self.r = {}


def _use(eq, reads, writes):
    for b in reads:
        eq.wait(b.w)
    for b in writes:
        eq.wait(b.w)
        eq.wait(*b.r.values())


def _fin(tok, reads, writes):
    for b in reads:
        b.r[tok.key] = tok
    for b in writes:
        b.w = tok
        b.r = {}


class KB:
    def __init__(self, nc):
        self.nc = nc
        self.es = ExitStack()
        self.nsem = 0
        self.dsems = []
        self.pe = EQ(self, nc.tensor, "pe")
        self.act = EQ(self, nc.scalar, "act")
        self.dve = EQ(self, nc.vector, "dve")
        self.pool = EQ(self, nc.gpsimd, "pool")
        self.sp = EQ(self, nc.sync, "sp")
        self.uid = 0
        self.arena = None
        self.peak = 0
        self.st_sem = DSem(self, "store")
        self.st_last = None

    def newsem(self, name):
        self.nsem += 1
        return self.es.enter_context(self.nc.semaphore(name))

    def sb(self, es, shape, dt, name=None):
        if self.arena is None:
            self.arena = self.es.enter_context(self.nc.sbuf_tensor("arena", [128, ARENA_BYTES // 2], BF16))
            self.free = [(0, ARENA_BYTES)]
        esz = 2 if dt == BF16 else 4
        n = 1
        for x in shape[1:]:
            n *= x
        nbytes = (n * esz + 63) // 64 * 64
        top = es is self.es
        order = range(len(self.free) - 1, -1, -1) if top else range(len(self.free))
        for i in order:
            o, sz = self.free[i]
            if sz >= nbytes:
                off = o + sz - nbytes if top else o
                if sz == nbytes:
                    self.free.pop(i)
                elif top:
                    self.free[i] = (o, sz - nbytes)
                else:
                    self.free[i] = (o + nbytes, sz - nbytes)
                break
        else:
            raise RuntimeError(f"SBUF arena full allocating {name} {shape} ({nbytes}B); free={self.free}")
        self.peak = max(self.peak, ARENA_BYTES - sum(z for _, z in self.free))

        def release(off=off, nbytes=nbytes):
            self.free.append((off, nbytes))
            self.free.sort()
            merged = []
            for o, z in self.free:
                if merged and merged[-1][0] + merged[-1][1] == o:
                    merged[-1] = (merged[-1][0], merged[-1][1] + z)
                else:
                    merged.append((o, z))
            self.free = merged
        es.callback(release)
        ap = self.arena[0:shape[0], off // 2:(off + n * esz) // 2]
        if dt != BF16:
            ap = ap.bitcast(dt)
        if len(shape) == 3:
            ap = ap.rearrange("p (a b) -> p a b", a=shape[1])
        elif len(shape) == 4:
            ap = ap.rearrange("p (a b c) -> p a b c", a=shape[1], b=shape[2])
        return ap

    def ps(self, es, shape, dt, name=None):
        self.uid += 1
        return es.enter_context(self.nc.psum_tensor(f"{name or 'p'}_{self.uid}", list(shape), dt))

    def op(self, eq, fn, reads=(), writes=(), **kw):
        _use(eq, reads, writes)
        tok = eq.done(fn(**kw))
        _fin(tok, reads, writes)
        return tok

    def mm(self, emit, reads=(), writes=()):
        _use(self.pe, reads, writes)
        ins = emit()
        tok = self.pe.done(ins)
        _fin(tok, reads, writes)
        return tok

    def load(self, q, dsem, out, in_, writes=(), reads=(), chain=True):
        _use(q, reads, writes)
        if chain:
            q.wait(dsem.last)
        tok = dsem.add(q.e.dma_start(out=out, in_=in_))
        dsem.last = tok
        _fin(tok, reads, writes)
        return tok

    def store(self, q, out, in_, reads=()):
        _use(q, reads, ())
        tok = self.st_sem.add(q.e.dma_start(out=out, in_=in_))
        _fin(tok, reads, ())
        self.st_last = tok
        return tok

    def barrier(self):
        qs = (self.pe, self.act, self.dve, self.pool, self.sp)
        toks = [Tok(q.sem, q.cnt, q.name) for q in qs if q.cnt]
        toks += [Tok(d.sem, d.cnt, d.name) for d in self.dsems if d.cnt]
        for q in qs:
            q.wait(*toks)

    def finish(self):
        if self.st_last is not None:
            self.sp.wait(self.st_last)
        for q in (self.pe, self.act, self.dve, self.pool):
            if q.cnt:
                self.sp.wait(Tok(q.sem, q.cnt, q.name))
        self.es.close()


def fm(v):
    v = np.asarray(v)
    return np.ascontiguousarray(v.reshape(-1, 128).T)


class Consts:
    pass


def emit_consts(kb, es, dram):
    nc = kb.nc
    c = Consts()
    c.dsem = DSem(kb, "const")
    c.idf = kb.sb(es, [128, 128], F32, "idf")
    c.idb = kb.sb(es, [128, 128], BF16, "idb")
    c.ones = kb.sb(es, [128, 128], F32, "ones")
    c.b_idf, c.b_idb, c.b_ones = Buf(), Buf(), Buf()
    kb.load(kb.sp, DSem(kb, "ident"), c.idf[:], dram["ident"], writes=[c.b_idf])
    kb.op(kb.dve, nc.vector.tensor_copy, reads=[c.b_idf], writes=[c.b_idb], out=c.idb[:], in_=c.idf[:])
    kb.op(kb.dve, nc.vector.memset, writes=[c.b_ones], ap=c.ones[:], constant=1.0)
    return c


def emit_mod(kb, es, c, banks, cvec_d, mw_d, mb_d, ncols, want_fm, want_bc, tag):
    nc = kb.nc
    ds = DSem(kb, "modc" + tag)
    ds2 = DSem(kb, "modb" + tag)
    out_fm, out_bc = {}, {}
    nblk = ncols // 512
    with ExitStack() as les:
        cv = kb.sb(les, [128, 8], F32, "cv")
        b_cv = Buf()
        kb.load(kb.sp, ds, cv[:], cvec_d, writes=[b_cv])
        kb.op(kb.act, nc.scalar.activation, reads=[b_cv], writes=[b_cv], out=cv[:], in_=cv[:], func=AF.Silu)
        crep = kb.sb(les, [128, 8, 128], BF16, "crep")
        b_crep = Buf()
        for k in range(8):
            kb.op(kb.dve, nc.vector.tensor_scalar, reads=[b_cv, c.b_ones], writes=[b_crep],
                  out=crep[:, k, :], in0=c.ones[:], scalar1=cv[:, k:k + 1], scalar2=None, op0=ALU.mult)
        wbuf = [kb.sb(les, [128, 8, 512], BF16, "mwb") for _ in range(3)]
        wsem = [DSem(kb, f"mw{tag}{i}") for i in range(3)]
        b_w = [Buf(), Buf(), Buf()]
        mbb = kb.sb(les, [128, 1024], F32, "mbb")
        b_mbb = Buf()
        bc = kb.sb(les, [128, 1024], F32, "bctmp")
        b_bc = Buf()
        wanted = sorted(set(want_fm) | set(want_bc))
        blocks = [(s0, h) for s0 in wanted for h in range(2)]
        mwv = mw_d.rearrange("(c p) n -> p c n", p=128)

        def issue(i):
            s0, h = blocks[i]
            col = s0 + h * 512
            kb.load(kb.pool, wsem[i % 3], wbuf[i % 3][:], mwv[:, :, col:col + 512], writes=[b_w[i % 3]])

        issue(0)
        if len(blocks) > 1:
            issue(1)
        for i, (s0, h) in enumerate(blocks):
            if i + 2 < len(blocks):
                issue(i + 2)
            if h == 0:
                kb.load(kb.sp, ds2, mbb[:], mb_d[0:1, s0:s0 + 1024].partition_broadcast(128), writes=[b_mbb])
                if s0 in want_bc:
                    t = kb.sb(es, [128, 1024], F32, "modbc")
                    out_bc[s0] = (t, Buf())
                dst, b_dst = out_bc[s0] if s0 in want_bc else (bc, b_bc)
            pb, b_pb = banks[i % 2]
            wb = wbuf[i % 3]

            def emit():
                for k in range(8):
                    ins = nc.tensor.matmul(pb, lhsT=crep[:, k, :], rhs=wb[:, k, :], start=(k == 0), stop=(k == 7))
                return ins
            kb.mm(emit, reads=[b_crep, b_w[i % 3]], writes=[b_pb])
            kb.op(kb.dve, nc.vector.tensor_tensor, reads=[b_pb, b_mbb], writes=[b_dst],
                  out=dst[:, h * 512:(h + 1) * 512], in0=pb, in1=mbb[:, h * 512:(h + 1) * 512], op=ALU.add)
            if h == 1 and s0 in want_fm:
                t = kb.sb(es, [128, 8], F32, "modfm")
                bt = Buf()
                pb2, b_pb2 = banks[(i + 1) % 2]

                def emit2():
                    for cc in range(8):
                        ins = nc.tensor.matmul(pb2[:, cc:cc + 1], lhsT=dst[:, cc * 128:(cc + 1) * 128],
                                               rhs=c.idf[:, 0:1], start=True, stop=True)
                    return ins
                kb.mm(emit2, reads=[b_dst, c.b_idf], writes=[b_pb2])
                kb.op(kb.dve, nc.vector.tensor_copy, reads=[b_pb2], writes=[bt], out=t[:], in_=pb2[:, 0:8])
                out_fm[s0] = (t, bt)
        kb.barrier()
    return out_fm, out_bc


def psum_banks(kb, es):
    pt = [kb.ps(es, [128, 1024], F32, "pp") for _ in range(4)]
    bufs = [Buf() for _ in range(8)]
    banks = [(pt[b // 2][:, (b % 2) * 512:(b % 2) * 512 + 512], bufs[b]) for b in range(8)]
    pairs = [(pt[p][:], [bufs[2 * p], bufs[2 * p + 1]]) for p in range(4)]
    return pairs, banks


def emit_rstd(kb, es, ssq, b_ssq, n, inv_n, eps=1e-6):
    nc = kb.nc
    kb.op(kb.dve, nc.vector.tensor_scalar, reads=[b_ssq], writes=[b_ssq], out=ssq[:, 0:n], in0=ssq[:, 0:n],
          scalar1=inv_n, scalar2=eps, op0=ALU.mult, op1=ALU.add)
    kb.op(kb.act, nc.scalar.activation, reads=[b_ssq], writes=[b_ssq], out=ssq[:, 0:n], in_=ssq[:, 0:n], func=AF.Sqrt)
    kb.op(kb.dve, nc.vector.reciprocal, reads=[b_ssq], writes=[b_ssq], out=ssq[:, 0:n], in_=ssq[:, 0:n])


def emit_norm_T(kb, c, tiles, rstd, b_rstd, outs, ppairs, xnb, b_xnb, cnt=[0]):
    nc = kb.nc
    i = 0
    while i < len(tiles):
        grp = tiles[i:i + 2]
        g = cnt[0]
        cnt[0] += 1
        pair, pb = ppairs[g % 2]
        pv = pair.bitcast(BF16).rearrange("p (c t) -> p c t", c=8)
        for j, (src, b_src, rc, doff) in enumerate(grp):
            xi = ((g % 2) * 2 + j) % len(xnb)
            kb.op(kb.act, nc.scalar.activation, reads=[b_src, b_rstd], writes=[b_xnb[xi]],
                  out=xnb[xi][:], in_=src, func=AF.Copy, scale=rstd[:, rc:rc + 1])
        for j, (src, b_src, rc, doff) in enumerate(grp):
            xi = ((g % 2) * 2 + j) % len(xnb)

            def emit(j=j, xi=xi):
                for cc in range(8):
                    ins = nc.tensor.transpose(out=pv[:, cc, j * 128:(j + 1) * 128],
                                              in_=xnb[xi][:, cc * 128:(cc + 1) * 128], identity=c.idb[:])
                return ins
            kb.mm(emit, reads=[b_xnb[xi], c.b_idb], writes=pb)
        n = 128 * len(grp)
        doff = grp[0][3]
        k = 0
        for (dst, b_dst, a, b_a, sh, b_sh) in outs:
            for cc in range(8):
                if k % 2 == 0:
                    kb.op(kb.act, nc.scalar.activation, reads=pb + [b_a, b_sh], writes=[b_dst],
                          out=dst[:, cc, doff:doff + n], in_=pv[:, cc, 0:n], func=AF.Identity,
                          scale=a[:, cc:cc + 1], bias=sh[:, cc:cc + 1])
                else:
                    kb.op(kb.dve, nc.vector.tensor_scalar, reads=pb + [b_a, b_sh], writes=[b_dst],
                          out=dst[:, cc, doff:doff + n], in0=pv[:, cc, 0:n], scalar1=a[:, cc:cc + 1],
                          scalar2=sh[:, cc:cc + 1], op0=ALU.mult, op1=ALU.add)
                k += 1
        i += 2


def emit_ffn(kb, es, c, h, b_h, nt, rstd, b_rstd, a2, b_a2, sh2, b_sh2, g2bc, b_g2, win_d, wout_d,
             cw, b_cw, cb, b_cb, hv, b_hv, pairs, banks, fix_tok, tag):
    nc = kb.nc
    ntok = nt * 128
    blocks = [(s0, min(512, ntok - s0)) for s0 in range(0, ntok, 512)]
    with ExitStack() as les:
        wout = kb.sb(les, [128, NF, 1024], BF16, "wout")
        b_wout = Buf()
        ds_wout = DSem(kb, "wout" + tag)
        wov = wout_d.rearrange("(f p) n -> p f n", p=128)
        for f0 in range(0, NF, 6):
            f1 = min(NF, f0 + 6)
            kb.load(kb.pool, ds_wout, wout[:, f0:f1, :], wov[:, f0:f1, :], writes=[b_wout])
        hnT = kb.sb(les, [128, 8, 512], BF16, "hn2T")
        b_hnT = Buf()
        actTs = [kb.sb(les, [128, NF, 512], BF16, "actT") for _ in range(2)]
        b_actTs = [Buf(), Buf()]
        xnb = [kb.sb(les, [128, 1024], BF16, "xnb") for _ in range(2)]
        b_xnb = [Buf() for _ in range(2)]
        wsl = [kb.sb(les, [128, 8, 256], BF16, "winsl") for _ in range(3)]
        b_wsl = [Buf() for _ in range(3)]
        ds_w = [DSem(kb, f"win{tag}{i}") for i in range(3)]
        ug = [kb.sb(les, [128, 514], F32, "ug") for _ in range(2)]
        b_ug = [Buf() for _ in range(2)]
        yy = [kb.sb(les, [128, 512], F32, "yy") for _ in range(2)]
        b_yy = [Buf() for _ in range(2)]
        sgs = [kb.sb(les, [128, 512], F32, "sg") for _ in range(1)]
        b_sgs = [Buf()]
        halo = kb.sb(les, [128, 2 * NF, 2], F32, "halo")
        b_halo = Buf()
        kb.op(kb.dve, nc.vector.memset, writes=[b_halo], ap=halo[:], constant=0.0)
        jobs = [(bi, f) for bi in range(len(blocks)) for f in range(NF)]

        def issue_w(ji):
            bi, f = jobs[ji]
            sl = ji % 3
            kb.load(kb.pool, ds_w[sl], wsl[sl][:], win_d[f], writes=[b_wsl[sl]])

        issue_w(0)
        issue_w(1)
        pending = []
        for ji, (bi, f) in enumerate(jobs):
            s0, n = blocks[bi]
            actT, b_actT = actTs[bi % 2], b_actTs[bi % 2]
            if f == 0:
                tl = [(h[:, (s0 // 128) + j, :], b_h[(s0 // 128) + j], (s0 // 128) + j, j * 128)
                      for j in range(n // 128)]
                emit_norm_T(kb, c, tl, rstd, b_rstd, [(hnT, b_hnT, a2, b_a2, sh2, b_sh2)], pairs[0:2], xnb, b_xnb)
            if ji + 2 < len(jobs):
                issue_w(ji + 2)
            sl = ji % 3
            for gv in range(2):
                fi = f + gv * NF
                pb, b_pb = banks[4 + ((ji * 2 + gv) % 4)]

                def emit(gv=gv, pb=pb):
                    for k in range(8):
                        ins = nc.tensor.matmul(pb[:, 0:n], lhsT=wsl[sl][:, k, gv * 128:(gv + 1) * 128], rhs=hnT[:, k, 0:n],
                                               start=(k == 0), stop=(k == 7))
                    return ins
                kb.mm(emit, reads=[b_wsl[sl], b_hnT], writes=[b_pb])
                u, b_u = ug[gv], b_ug[gv]
                y, b_y = yy[gv], b_yy[gv]
                kb.op(kb.act, nc.scalar.activation, reads=[b_pb], writes=[b_u], out=u[:, 2:2 + n], in_=pb[:, 0:n],
                      func=AF.Copy)
                kb.op(kb.act, nc.scalar.activation, reads=[b_halo], writes=[b_u], out=u[:, 0:2], in_=halo[:, fi, :],
                      func=AF.Copy)
                if fix_tok is not None and s0 <= fix_tok - 2 and fix_tok <= s0 + n:
                    o = fix_tok - s0
                    kb.op(kb.dve, nc.vector.tensor_scalar, reads=[b_hv], writes=[b_u], out=u[:, o:o + 2],
                          in0=u[:, o:o + 2], scalar1=hv[:, 0:1], scalar2=None, op0=ALU.mult)
                kb.op(kb.act, nc.scalar.activation, reads=[b_pb, b_cw, b_cb], writes=[b_y], out=y[:, 0:n],
                      in_=pb[:, 0:n], func=AF.Identity, scale=cw[:, fi, 2:3], bias=cb[:, fi:fi + 1])
                kb.op(kb.act, nc.scalar.activation, reads=[b_u], writes=[b_halo], out=halo[:, fi, :],
                      in_=u[:, n:n + 2], func=AF.Copy)
                kb.op(kb.dve, nc.vector.scalar_tensor_tensor, reads=[b_u, b_cw], writes=[b_y], out=y[:, 0:n],
                      in0=u[:, 1:1 + n], scalar=cw[:, fi, 1:2], in1=y[:, 0:n], op0=ALU.mult, op1=ALU.add)
                kb.op(kb.dve, nc.vector.scalar_tensor_tensor, reads=[b_u, b_cw], writes=[b_y], out=y[:, 0:n],
                      in0=u[:, 0:n], scalar=cw[:, fi, 0:1], in1=y[:, 0:n], op0=ALU.mult, op1=ALU.add)
            yg_, b_yg_ = yy[0], b_yy[0]
            yv_, b_yv_ = yy[1], b_yy[1]
            sg, b_sg = sgs[0], b_sgs[0]
            kb.op(kb.act, nc.scalar.activation, reads=[b_yg_], writes=[b_sg], out=sg[:, 0:n], in_=yg_[:, 0:n],
                  func=AF.Silu)
            kb.op(kb.dve, nc.vector.tensor_tensor, reads=[b_sg, b_yv_], writes=[b_actT], out=actT[:, f, 0:n],
                  in0=sg[:, 0:n], in1=yv_[:, 0:n], op=ALU.mult)
            if pending and f % 4 == 3:
                pending.pop(0)()
            if f == NF - 1:
                while pending:
                    pending.pop(0)()
                for j in range(n // 128):
                    def out_tile(j=j, s0=s0, actT=actT, b_actT=b_actT):
                        ti = s0 // 128 + j
                        pair, pb2 = pairs[ti % 2]

                        def emit3():
                            for hh in range(2):
                                for ff in range(NF):
                                    ins = nc.tensor.matmul(pair[:, hh * 512:(hh + 1) * 512],
                                                           lhsT=actT[:, ff, j * 128:(j + 1) * 128],
                                                           rhs=wout[:, ff, hh * 512:(hh + 1) * 512],
                                                           start=(ff == 0), stop=(ff == NF - 1))
                            return ins
                        kb.mm(emit3, reads=[b_actT, b_wout], writes=pb2)
                        kb.op(kb.dve, nc.vector.tensor_tensor, reads=[b_g2], writes=pb2, out=pair[:],
                              in0=pair[:], in1=g2bc[:], op=ALU.mult)
                        kb.op(kb.dve, nc.vector.tensor_tensor, reads=pb2, writes=[b_h[ti]], out=h[:, ti, :],
                              in0=pair[:], in1=h[:, ti, :], op=ALU.add)
                    pending.append(out_tile)
        while pending:
            pending.pop(0)()
        kb.barrier()


def _din(nc, name, shape, dt=F32):
    return nc.dram_tensor(name, list(shape), dt, kind="ExternalInput").ap()


def _dout(nc, name, shape, dt=F32):
    return nc.dram_tensor(name, list(shape), dt, kind="ExternalOutput").ap()


def load_vecs(kb, es, c, specs):
    out = {}
    for name, (ap, shape, dt) in specs.items():
        t = kb.sb(es, shape, dt, name)
        kb.sp.e.dma_start(out=t[:], in_=ap).then_inc(c.dsem.sem, 16)
        c.dsem.cnt += 16
        out[name] = t
    c.b_vec = Buf()
    c.b_vec.w = Tok(c.dsem.sem, c.dsem.cnt, c.dsem.name)
    return out


def emit_rope_tables(kb, es, pos_i, b_pos, n, invf, sgn, b_vec, cosT, sinT, b_tab, p0=0):
    nc = kb.nc
    ang = kb.sb(es, [p0 + 32, n], F32, "ang")[p0:p0 + 32]
    kf = kb.sb(es, [p0 + 32, n], F32, "kf")[p0:p0 + 32]
    ki = kb.sb(es, [p0 + 32, n], I32, "ki")[p0:p0 + 32]
    m = kb.sb(es, [p0 + 32, n], F32, "mm")[p0:p0 + 32]
    b = Buf()
    C1 = 6.28125
    C2 = TWO_PI - C1
    V = nc.vector
    op = lambda fn, **kw: kb.op(kb.dve, fn, reads=[b_pos, b_vec], writes=[b, b_tab], **kw)
    op(V.tensor_copy, out=ang[:], in_=pos_i)
    op(V.tensor_scalar, out=ang[:], in0=ang[:], scalar1=invf[:, 0:1], scalar2=None, op0=ALU.mult)
    op(V.tensor_scalar, out=kf[:], in0=ang[:], scalar1=1.0 / TWO_PI, scalar2=None, op0=ALU.mult)
    op(V.tensor_copy, out=ki[:], in_=kf[:])
    op(V.tensor_copy, out=kf[:], in_=ki[:])
    op(V.scalar_tensor_tensor, out=ang[:], in0=kf[:], scalar=-C1, in1=ang[:], op0=ALU.mult, op1=ALU.add)
    op(V.scalar_tensor_tensor, out=ang[:], in0=kf[:], scalar=-C2, in1=ang[:], op0=ALU.mult, op1=ALU.add)

    def wrap(t):
        op(V.tensor_scalar, out=m[:], in0=t[:], scalar1=PI, scalar2=TWO_PI, op0=ALU.is_gt, op1=ALU.mult)
        op(V.tensor_tensor, out=t[:], in0=t[:], in1=m[:], op=ALU.subtract)
        op(V.tensor_scalar, out=m[:], in0=t[:], scalar1=-PI, scalar2=TWO_PI, op0=ALU.is_lt, op1=ALU.mult)
        op(V.tensor_tensor, out=t[:], in0=t[:], in1=m[:], op=ALU.add)
    wrap(ang)
    kb.op(kb.act, nc.scalar.activation, reads=[b], writes=[b_tab], out=sinT, in_=ang[:], func=AF.Sin)
    op(V.tensor_scalar, out=sinT, in0=sinT, scalar1=sgn[:, 0:1], scalar2=None, op0=ALU.mult)
    op(V.tensor_scalar, out=kf[:], in0=ang[:], scalar1=PI / 2, scalar2=None, op0=ALU.add)
    wrap(kf)
    kb.op(kb.act, nc.scalar.activation, reads=[b], writes=[b_tab], out=cosT, in_=kf[:], func=AF.Sin)


def emit_latents(kb, es, c, d, h, b_h, ssq, b_ssq, akv, b_akv, shkv, b_shkv, a1n, b_a1n, sh1n, b_sh1n, vec, bv,
                 pairs, banks, o_ckv, o_kr, o_cq, stop=99):
    nc = kb.nc
    V, A, G = nc.vector, nc.scalar, nc.gpsimd
    with ExitStack() as s6:
        junk = kb.sb(s6, [128, 1024], BF16, "junk6")
        b_junk = Buf()
        kb.op(kb.dve, V.memset, writes=[b_ssq], ap=ssq[:], constant=0.0)
        for te in range(1, NEXT):
            kb.op(kb.act, A.activation, reads=[b_h[te]], writes=[b_junk, b_ssq], out=junk[:], in_=h[:, te, :],
                  func=AF.Square, accum_out=ssq[:, te:te + 1])
        emit_rstd(kb, es, ssq, b_ssq, NEXT, 1.0 / D)
        vec = dict(vec)
        vec["latg"] = kb.sb(s6, [128, 256], F32, "latg")
        vec["qg"] = kb.sb(s6, [128, 384], F32, "qg")
        b_lg = Buf()
        ds_lg = DSem(kb, "latg")
        kb.load(kb.sp, ds_lg, vec["latg"][:], d["latg"].partition_broadcast(128), writes=[b_lg], chain=False)
        kb.load(kb.sp, ds_lg, vec["qg"][:], d["qg"].partition_broadcast(128), writes=[b_lg], chain=False)
        wdkv = kb.sb(s6, [128, 8, 256], BF16, "wdkv")
        wdq = kb.sb(s6, [128, 8, 384], BF16, "wdq")
        wkr = kb.sb(s6, [128, 8, 64], BF16, "wkr")
        b_w = Buf()
        dsw = [DSem(kb, f"lw{i}") for i in range(3)]
        kb.load(kb.pool, dsw[0], wdkv[:], d["wdkv"].rearrange("(k p) n -> p k n", p=128), writes=[b_w])
        t2 = kb.load(kb.pool, dsw[1], wdq[:], d["wdq"].rearrange("(k p) n -> p k n", p=128), writes=[])
        t3 = kb.load(kb.pool, dsw[2], wkr[:], d["wkr2"].rearrange("(k p) n -> p k n", p=128), writes=[])
        b_w2, b_w3 = Buf(), Buf()
        b_w2.w, b_w3.w = t2, t3
        ssl = kb.sb(s6, [128, NOWN, 2], F32, "ssl")
        b_ssl = [Buf() for _ in range(NOWN)]
        kb.dve.done(V.memset(ap=ssl[:], constant=0.0))
        for ti in range(NOWN):
            b_ssl[ti].w = Tok(kb.dve.sem, kb.dve.cnt, "dve")
        krT = kb.sb(s6, [32, OWN], BF16, "krT")
        b_krT = Buf()
        ckvT = kb.sb(s6, [128, 2, OWN], BF16, "ckvTo")
        cqT = kb.sb(s6, [128, 3, OWN], BF16, "cqTo")
        b_ckvT, b_cqT = Buf(), Buf()
        cosT = kb.sb(s6, [32, OWN], F32, "cosT")
        sinT = kb.sb(s6, [32, OWN], F32, "sinT")
        b_tab = Buf()
        with ExitStack() as sr:
            posi = kb.sb(sr, [32, OWN], I32, "posi")
            b_pos = Buf()
            dsp = DSem(kb, "posk")
            kb.load(kb.sp, dsp, posi[:], d["posk"].partition_broadcast(32), writes=[b_pos])
            emit_rope_tables(kb, sr, posi[:], b_pos, OWN, vec["invf"], vec["sgn"], bv, cosT[:], sinT[:], b_tab)
            kb.barrier()
        if stop == 7:
            kb.barrier()
            return
        hkvT = kb.sb(s6, [128, 8, 512], BF16, "hkvT")
        hqT = kb.sb(s6, [128, 8, 512], BF16, "hqT")
        b_hkvT, b_hqT = Buf(), Buf()
        xnb = [kb.sb(s6, [128, 1024], BF16, "xnb6") for _ in range(4)]
        b_xnb = [Buf() for _ in range(4)]
        t1 = kb.sb(s6, [32, 512], F32, "rt1")
        t2_ = kb.sb(s6, [32, 512], F32, "rt2")
        b_t1, b_t2 = Buf(), Buf()
        lnb = [kb.sb(s6, [128, 640], BF16, "lnb") for _ in range(2)]
        b_lnb = [Buf(), Buf()]
        for blk in range(4):
            tl = [(h[:, 1 + 4 * blk + j, :], b_h[1 + 4 * blk + j], 1 + 4 * blk + j, j * 128) for j in range(4)]
            emit_norm_T(kb, c, tl, ssq, b_ssq, [(hkvT, b_hkvT, akv, b_akv, shkv, b_shkv),
                                                (hqT, b_hqT, a1n, b_a1n, sh1n, b_sh1n)], pairs[0:2], xnb, b_xnb)
            for j in range(4):
                ti = 4 * blk + j
                pbs = []
                for (src, b_src, w, b_ww, ncol, row, inv_n) in ((hkvT, b_hkvT, wdkv, b_w, 256, 0, 1.0 / 256),
                                                              (hqT, b_hqT, wdq, b_w2, 384, 1, 1.0 / 384)):
                    pb, b_pb = banks[4 + (2 * ti + row) % 4]
                    pbs.append((pb, b_pb))

                    def emit(pb=pb, src=src, w=w, ncol=ncol, j=j):
                        for k in range(8):
                            ins = nc.tensor.matmul(pb[:, 0:ncol], lhsT=src[:, k, j * 128:(j + 1) * 128], rhs=w[:, k, :],
                                                   start=(k == 0), stop=(k == 7))
                        return ins
                    kb.mm(emit, reads=[b_src, b_ww], writes=[b_pb])
                    kb.op(kb.act, A.activation, reads=[b_pb], writes=[b_junk, b_ssl[ti]], out=junk[:, 0:ncol],
                          in_=pb[:, 0:ncol], func=AF.Square, accum_out=ssl[:, ti, row:row + 1])
                    kb.op(kb.dve, V.tensor_scalar, reads=[], writes=[b_ssl[ti]], out=ssl[:, ti, row:row + 1],
                          in0=ssl[:, ti, row:row + 1], scalar1=inv_n, scalar2=1e-6, op0=ALU.mult, op1=ALU.add)
                kb.op(kb.act, A.activation, reads=[], writes=[b_ssl[ti]], out=ssl[:, ti, :], in_=ssl[:, ti, :],
                      func=AF.Sqrt)
                kb.op(kb.dve, V.reciprocal, reads=[], writes=[b_ssl[ti]], out=ssl[:, ti, :], in_=ssl[:, ti, :])
                lb, b_lb = lnb[ti % 2], b_lnb[ti % 2]
                kb.op(kb.dve, V.scalar_tensor_tensor, reads=[pbs[0][1], b_ssl[ti], b_lg], writes=[b_lb], out=lb[:, 0:256],
                      in0=pbs[0][0][:, 0:256], scalar=ssl[:, ti, 0:1], in1=vec["latg"][:], op0=ALU.mult, op1=ALU.mult)
                kb.op(kb.dve, V.scalar_tensor_tensor, reads=[pbs[1][1], b_ssl[ti], b_lg], writes=[b_lb], out=lb[:, 256:640],
                      in0=pbs[1][0][:, 0:384], scalar=ssl[:, ti, 1:2], in1=vec["qg"][:], op0=ALU.mult, op1=ALU.mult)
                if os.environ.get("SKIP_TR"):
                    continue
                pb, b_pb = banks[2 + ti % 2]
                pv = pairs[1][0].bitcast(BF16)[:, (ti % 2) * 1024:(ti % 2) * 1024 + 1024]

                def emit(pv=pv, lb=lb):
                    for cc in range(5):
                        ins = nc.tensor.transpose(out=pv[:, cc * 128:(cc + 1) * 128], in_=lb[:, cc * 128:(cc + 1) * 128],
                                                  identity=c.idb[:])
                    return ins
                kb.mm(emit, reads=[b_lb, c.b_idb], writes=[b_pb])
                for cc in range(5):
                    dst, b_dst = (ckvT[:, cc, ti * 128:(ti + 1) * 128], b_ckvT) if cc < 2 else \
                        (cqT[:, cc - 2, ti * 128:(ti + 1) * 128], b_cqT)
                    if cc % 2 == 0:
                        kb.op(kb.dve, V.tensor_copy, reads=[b_pb], writes=[b_dst], out=dst, in_=pv[:, cc * 128:(cc + 1) * 128])
                    else:
                        kb.op(kb.act, A.activation, reads=[b_pb], writes=[b_dst], out=dst, in_=pv[:, cc * 128:(cc + 1) * 128],
                              func=AF.Copy)
            if os.environ.get("SKIP_ROPE"):
                continue
            pa, b_pa = banks[0]
            pbw, b_pbw = banks[1]
            for (pp, b_pp, c0) in ((pa, b_pa, 0), (pbw, b_pbw, 32)):
                def emit(pp=pp, c0=c0):
                    for k in range(8):
                        ins = nc.tensor.matmul(pp[0:32, :], lhsT=wkr[:, k, c0:c0 + 32], rhs=hkvT[:, k, :],
                                               start=(k == 0), stop=(k == 7))
                    return ins
                kb.mm(emit, reads=[b_hkvT, b_w3], writes=[b_pp])
            tk = slice(blk * 512, (blk + 1) * 512)
            kb.op(kb.dve, V.tensor_tensor, reads=[b_pa, b_tab], writes=[b_t1], out=t1[:], in0=pa[0:32, :],
                  in1=cosT[:, tk], op=ALU.mult)
            kb.op(kb.dve, V.tensor_tensor, reads=[b_pbw, b_tab], writes=[b_t2], out=t2_[:], in0=pbw[0:32, :],
                  in1=sinT[:, tk], op=ALU.mult)
            kb.op(kb.dve, V.tensor_tensor, reads=[b_t1, b_t2], writes=[b_krT], out=krT[:, tk], in0=t1[:], in1=t2_[:],
                  op=ALU.add)
        if stop == 8:
            kb.barrier()
            return
        kb.store(kb.sp, o_ckv.rearrange("(c p) n -> p c n", p=128), ckvT[:], reads=[b_ckvT])
        kb.store(kb.sp, o_cq.rearrange("(c p) n -> p c n", p=128), cqT[:], reads=[b_cqT])
        kb.store(kb.sp, o_kr, krT[:], reads=[b_krT])
        kb.barrier()


def build_phaseA(dbg=False, stop=99):
    nc = bass.Bass("TRN2", target_bir_lowering=False)
    d = {}
    for name, shape, dt in [
        ("xe", [NALL * 128, D], F32), ("valid", [128, NALL], F32), ("posk", [1, OWN], I32), ("cvec", [128, 8], F32),
        ("invf", [32, 1], F32), ("sgn", [32, 1], F32), ("hv", [128, 1], F32), ("ident", [128, 128], F32),
        ("mw0", [D, 6 * D], F32), ("mb0", [1, 6 * D], F32), ("mw1", [D, 2 * D], F32), ("mb1", [1, 2 * D], F32),
        ("kvmw", [D, 2 * D], F32), ("kvmb", [1, 2 * D], F32),
        ("n1g0", [128, 8], F32), ("n2g0", [128, 8], F32), ("n1g1", [128, 8], F32), ("kvng", [128, 8], F32),
        ("wqkv_t", [8, 128, 8 * 384], F32), ("wo", [D, D], F32), ("tb", [128, 16 * 640], F32),
        ("win_t", [NF, 128, 8 * 256], F32), ("cw", [128, 2 * NF * 3], F32), ("cb", [128, 2 * NF], F32),
        ("wout", [FF, D], F32), ("wdkv", [D, 256], F32), ("latg", [1, 256], F32), ("wkr2", [D, 64], F32),
        ("wdq", [D, 384], F32), ("qg", [1, 384], F32),
    ]:
        d[name] = _din(nc, name, shape, dt)
    o_h1 = _dout(nc, "h1", [NEXT * 128, D], F32)
    o_ckv = _dout(nc, "ckvT", [256, OWN], BF16)
    o_kr = _dout(nc, "krT", [32, OWN], BF16)
    o_cq = _dout(nc, "cqT", [384, OWN], BF16)
    if dbg:
        o_dbg = _dout(nc, "dbg", [NEXT * 128, D], F32)

    kb = KB(nc)
    es = kb.es
    V, A, G = nc.vector, nc.scalar, nc.gpsimd
    pairs, banks = psum_banks(kb, es)
    c = emit_consts(kb, es, d)
    vec = load_vecs(kb, es, c, {
        "valid": (d["valid"], [128, NALL], F32), "hv": (d["hv"], [128, 1], F32),
        "invf": (d["invf"], [32, 1], F32), "sgn": (d["sgn"], [32, 1], F32),
        "n1g0": (d["n1g0"], [128, 8], F32), "n2g0": (d["n2g0"], [128, 8], F32),
        "n1g1": (d["n1g1"], [128, 8], F32), "kvng": (d["kvng"], [128, 8], F32),
        "cw": (d["cw"].rearrange("p (f t) -> p f t", t=3), [128, 2 * NF, 3], F32), "cb": (d["cb"], [128, 2 * NF], F32),
    })
    bv = c.b_vec
    fm0, bc0 = emit_mod(kb, es, c, [banks[0], banks[1]], d["cvec"], d["mw0"], d["mb0"], 6 * D,
                        want_fm=[0, 1024, 3072, 4096], want_bc=[2048, 5120], tag="0")
    fm1, _ = emit_mod(kb, es, c, [banks[0], banks[1]], d["cvec"], d["mw1"], d["mb1"], 2 * D,
                      want_fm=[0, 1024], want_bc=[], tag="1")
    fmk, _ = emit_mod(kb, es, c, [banks[0], banks[1]], d["cvec"], d["kvmw"], d["kvmb"], 2 * D,
                      want_fm=[0, 1024], want_bc=[], tag="k")

    def mk_a(sc, gname):
        t = kb.sb(es, [128, 8], F32, "a_" + gname)
        b = Buf()
        kb.op(kb.dve, V.scalar_tensor_tensor, reads=[sc[1], bv], writes=[b], out=t[:], in0=sc[0][:], scalar=1.0,
              in1=vec[gname][:], op0=ALU.add, op1=ALU.mult)
        return t, b
    a1, b_a1 = mk_a(fm0[1024], "n1g0")
    a2, b_a2 = mk_a(fm0[4096], "n2g0")
    a1n, b_a1n = mk_a(fm1[1024], "n1g1")
    akv, b_akv = mk_a(fmk[1024], "kvng")
    sh1, b_sh1 = fm0[0]
    sh2, b_sh2 = fm0[3072]
    sh1n, b_sh1n = fm1[0]
    shkv, b_shkv = fmk[0]
    g1bc, b_g1 = bc0[2048]
    g2bc, b_g2 = bc0[5120]

    if stop == 0:
        kb.finish()
        return nc
    ssq = kb.sb(es, [128, 32], F32, "ssq")
    b_ssq = Buf()
    xev = d["xe"].rearrange("(t p) n -> p t n", p=128)
    with ExitStack() as s12:
        hnT = kb.sb(s12, [128, 8, NALL * 128], BF16, "hnT")
        b_hnT = Buf()
        with ExitStack() as s1:
            xall = kb.sb(s1, [128, NALL, 1024], F32, "xall")
            b_x = [Buf() for _ in range(NALL)]
            junk = kb.sb(s1, [128, 1024], BF16, "junk")
            b_junk = Buf()
            xnb = [kb.sb(s1, [128, 1024], BF16, "xnb") for _ in range(4)]
            b_xnb = [Buf() for _ in range(4)]
            dsx = [DSem(kb, f"x{i}") for i in range(6)]
            kb.op(kb.dve, V.memset, writes=[b_ssq], ap=ssq[:], constant=0.0)
            for t in range(NALL):
                kb.load(kb.sp, dsx[t % 6], xall[:, t, :], xev[:, t, :], writes=[b_x[t]])
            for t in range(NALL):
                kb.op(kb.act, A.activation, reads=[b_x[t]], writes=[b_junk, b_ssq], out=junk[:], in_=xall[:, t, :],
                      func=AF.Square, accum_out=ssq[:, t:t + 1])
            emit_rstd(kb, es, ssq, b_ssq, NALL, 1.0 / D)
            tl = [(xall[:, t, :], b_x[t], t, t * 128) for t in range(NALL)]
            emit_norm_T(kb, c, tl, ssq, b_ssq, [(hnT, b_hnT, a1, b_a1, sh1, b_sh1)], pairs[0:2], xnb, b_xnb)
            kb.barrier()
        if stop == 1:
            kb.finish()
            return nc
        s_o = ExitStack()
        oT = kb.sb(s_o, [128, 8, NEXT * 128], BF16, "oT")
        if True:
            b_oT = Buf()
            with ExitStack() as s2:
                tb = kb.sb(s2, [128, 16, 640], BF16, "tb")
                b_tb = Buf()
                ds_tb = DSem(kb, "tb")
                tbv = d["tb"].rearrange("p (h n) -> p h n", h=16)
                for q4 in range(4):
                    kb.load(kb.pool, ds_tb, tb[:, q4 * 4:q4 * 4 + 4, :], tbv[:, q4 * 4:q4 * 4 + 4, :], writes=[b_tb])
                wsl = [kb.sb(s2, [128, 8, 384], BF16, "wqkv") for _ in range(2)]
                b_wsl = [Buf(), Buf()]
                ds_w = [DSem(kb, f"wqkv{i}") for i in range(2)]
                qT = [kb.sb(s2, [128, NEXT * 128], BF16, "qT") for _ in range(2)]
                kT = [kb.sb(s2, [128, NALL * 128], BF16, "kT") for _ in range(2)]
                va = [kb.sb(s2, [128, NALL, 2, 128], BF16, "va") for _ in range(2)]
                b_qT, b_kT, b_va = [Buf(), Buf()], [Buf(), Buf()], [Buf(), Buf()]
                PT = [kb.sb(s2, [128, 640], BF16, "PT") for _ in range(6)]
                b_PT = [Buf() for _ in range(6)]
                rs = [kb.sb(s2, [128, 256], F32, "rs") for _ in range(2)]
                b_rs = [Buf(), Buf()]
                for sl in range(2):
                    for t in range(NALL):
                        kb.op(kb.dve, V.tensor_scalar, reads=[bv, c.b_ones], writes=[b_va[sl]],
                              out=va[sl][:, t, 0, 64:128], in0=c.ones[:, 0:64], scalar1=vec["valid"][:, t:t + 1],
                              scalar2=None, op0=ALU.mult)
                        kb.op(kb.dve, V.tensor_scalar, reads=[bv, c.b_ones], writes=[b_va[sl]],
                              out=va[sl][:, t, 1, 0:64], in0=c.ones[:, 0:64], scalar1=vec["valid"][:, t:t + 1],
                              scalar2=None, op0=ALU.mult)

                def load_w(p):
                    kb.load(kb.pool, ds_w[p % 2], wsl[p % 2][:],
                            d["wqkv_t"][p].rearrange("p (k n) -> p k n", k=8), writes=[b_wsl[p % 2]])
                load_w(0)
                pcnt = 0
                acnt = 0
                for p in range(8):
                    sl = p % 2
                    if p + 1 < 8:
                        load_w(p + 1)
                    w = wsl[sl]
                    for (dst, b_dst, col0, tok0, ntok, scale) in ((qT[sl], b_qT[sl], 0, 512, NEXT * 128, 0.125),
                                                                  (kT[sl], b_kT[sl], 128, 0, NALL * 128, None)):
                        for e0 in range(0, ntok, 512):
                            n = min(512, ntok - e0)
                            pb, b_pb = banks[4 + pcnt % 2]
                            pcnt += 1

                            def emit(pb=pb, col0=col0, a0=tok0 + e0, n=n):
                                for k in range(8):
                                    ins = nc.tensor.matmul(pb[:, 0:n], lhsT=w[:, k, col0:col0 + 128],
                                                           rhs=hnT[:, k, a0:a0 + n], start=(k == 0), stop=(k == 7))
                                return ins
                            kb.mm(emit, reads=[b_wsl[sl], b_hnT], writes=[b_pb])
                            if scale is not None:
                                kb.op(kb.act, A.activation, reads=[b_pb], writes=[b_dst], out=dst[:, e0:e0 + n],
                                      in_=pb[:, 0:n], func=AF.Copy, scale=scale)
                            else:
                                kb.op(kb.dve, V.tensor_copy, reads=[b_pb], writes=[b_dst], out=dst[:, e0:e0 + n],
                                      in_=pb[:, 0:n])
                    for t0 in range(0, NALL, 4):
                        nt_ = min(4, NALL - t0)
                        pb, b_pb = banks[4 + pcnt % 2]
                        pcnt += 1

                        def emit(pb=pb, t0=t0, nt_=nt_):
                            for j in range(nt_):
                                for k in range(8):
                                    ins = nc.tensor.matmul(pb[:, j * 128:(j + 1) * 128],
                                                           lhsT=hnT[:, k, (t0 + j) * 128:(t0 + j + 1) * 128],
                                                           rhs=w[:, k, 256:384], start=(k == 0), stop=(k == 7))
                            return ins
                        kb.mm(emit, reads=[b_wsl[sl], b_hnT], writes=[b_pb])
                        for j in range(nt_):
                            kb.op(kb.dve, V.tensor_scalar, reads=[b_pb, bv], writes=[b_va[sl]],
                                  out=va[sl][:, t0 + j, 0, 0:64], in0=pb[:, j * 128:j * 128 + 64],
                                  scalar1=vec["valid"][:, t0 + j:t0 + j + 1], scalar2=None, op0=ALU.mult)
                            kb.op(kb.act, A.activation, reads=[b_pb, bv], writes=[b_va[sl]],
                                  out=va[sl][:, t0 + j, 1, 64:128], in_=pb[:, j * 128 + 64:j * 128 + 128],
                                  func=AF.Copy, scale=vec["valid"][:, t0 + j:t0 + j + 1])
                    def tail_fn(te, pts):
                        ob, b_ob = banks[6 + te % 2]
                        for hi in range(2):
                            pi = pts[hi]

                            def emit(hi=hi, pi=pi, te=te, ob=ob):
                                for m in range(5):
                                    ins = nc.tensor.matmul(ob[:, hi * 128:(hi + 1) * 128], lhsT=va[sl][:, te + m, hi, :],
                                                           rhs=PT[pi][:, m * 128:(m + 1) * 128], start=(m == 0),
                                                           stop=(m == 4), skip_group_check=True)
                                return ins
                            kb.mm(emit, reads=[b_va[sl], b_PT[pi]], writes=[b_ob])
                        rsb, b_rsb = rs[te % 2], b_rs[te % 2]
                        kb.op(kb.dve, V.tensor_scalar, reads=[b_ob], writes=[b_rsb], out=rsb[0:64, 0:128],
                              in0=ob[64:128, 0:128], scalar1=1e-30, scalar2=None, op0=ALU.max)
                        kb.op(kb.dve, V.tensor_scalar, reads=[b_ob], writes=[b_rsb], out=rsb[64:128, 128:256],
                              in0=ob[0:64, 128:256], scalar1=1e-30, scalar2=None, op0=ALU.max)
                        kb.op(kb.dve, V.reciprocal, reads=[b_rsb], writes=[b_rsb], out=rsb[0:64, 0:128],
                              in_=rsb[0:64, 0:128])
                        kb.op(kb.dve, V.reciprocal, reads=[b_rsb], writes=[b_rsb], out=rsb[64:128, 128:256],
                              in_=rsb[64:128, 128:256])
                        kb.op(kb.dve, V.tensor_tensor, reads=[b_ob, b_rsb], writes=[b_oT],
                              out=oT[0:64, p, te * 128:(te + 1) * 128], in0=ob[0:64, 0:128], in1=rsb[0:64, 0:128],
                              op=ALU.mult)
                        kb.op(kb.dve, V.tensor_tensor, reads=[b_ob, b_rsb], writes=[b_oT],
                              out=oT[64:128, p, te * 128:(te + 1) * 128], in0=ob[64:128, 128:256],
                              in1=rsb[64:128, 128:256], op=ALU.mult)

                    prev = None
                    for te in range(NEXT):
                        pts = []
                        for hi in range(2):
                            h16 = 2 * p + hi
                            sp_, sb_ = pairs[hi]
                            r0 = hi * 64

                            def emit(sp_=sp_, h16=h16, r0=r0, te=te):
                                nc.tensor.matmul(sp_[:, 0:512], lhsT=c.idb[:], rhs=tb[:, h16, 0:512], start=True,
                                                 stop=False, skip_group_check=True)
                                nc.tensor.matmul(sp_[:, 512:640], lhsT=c.idb[:], rhs=tb[:, h16, 512:640], start=True,
                                                 stop=False, skip_group_check=True)
                                for m in range(5):
                                    ins = nc.tensor.matmul(sp_[:, m * 128:(m + 1) * 128],
                                                           lhsT=kT[sl][r0:r0 + 64, (te + m) * 128:(te + m + 1) * 128],
                                                           rhs=qT[sl][r0:r0 + 64, te * 128:(te + 1) * 128],
                                                           start=False, stop=True, skip_group_check=True)
                                return ins
                            kb.mm(emit, reads=[c.b_idb, b_tb, b_kT[sl], b_qT[sl]], writes=sb_)
                            pi = acnt % 6
                            acnt += 1
                            kb.op(kb.act, A.activation, reads=sb_, writes=[b_PT[pi]], out=PT[pi][:], in_=sp_[:, 0:640],
                                  func=AF.Exp)
                            pts.append(pi)
                        if prev is not None:
                            tail_fn(*prev)
                        prev = (te, pts)
                    tail_fn(*prev)
                kb.barrier()
    if stop == 2:
        kb.finish()
        return nc
    h = kb.sb(es, [128, NEXT, 1024], F32, "h")
    b_h = [Buf() for _ in range(NEXT)]
    with ExitStack() as s3:
        wo = kb.sb(s3, [128, 8, 1024], BF16, "wo")
        b_wo = Buf()
        ds_wo = DSem(kb, "wo")
        kb.load(kb.pool, ds_wo, wo[:], d["wo"].rearrange("(c p) n -> p c n", p=128), writes=[b_wo])
        tmp = kb.sb(s3, [128, 1024], F32, "tmp3")
        b_tmp = Buf()
        dsh = [DSem(kb, f"h{i}") for i in range(4)]
        for te in range(NEXT):
            kb.load(kb.sp, dsh[te % 4], h[:, te, :], xev[:, te + 4, :], writes=[b_h[te]])
        for te in range(NEXT):
            pair, pb2 = pairs[te % 2]

            def emit(pair=pair, te=te):
                for hh in range(2):
                    for cc in range(8):
                        ins = nc.tensor.matmul(pair[:, hh * 512:(hh + 1) * 512],
                                               lhsT=oT[:, cc, te * 128:(te + 1) * 128],
                                               rhs=wo[:, cc, hh * 512:(hh + 1) * 512], start=(cc == 0),
                                               stop=(cc == 7))
                return ins
            kb.mm(emit, reads=[b_oT, b_wo], writes=pb2)
            kb.op(kb.dve, V.tensor_tensor, reads=pb2 + [b_g1], writes=[b_tmp], out=tmp[:], in0=pair[:],
                  in1=g1bc[:], op=ALU.mult)
            kb.op(kb.dve, V.tensor_tensor, reads=[b_tmp], writes=[b_h[te]], out=h[:, te, :], in0=h[:, te, :],
                  in1=tmp[:], op=ALU.add)
        kb.barrier()
    s_o.close()
    if dbg:
        for te in range(NEXT):
            kb.store(kb.sp, o_dbg[te * 128:(te + 1) * 128, :], h[:, te, :], reads=[b_h[te]])
    if stop == 3:
        kb.finish()
        return nc
    with ExitStack() as s4:
        junk = kb.sb(s4, [128, 1024], BF16, "junk2")
        b_junk = Buf()
        kb.op(kb.dve, V.memset, writes=[b_ssq], ap=ssq[:], constant=0.0)
        for te in range(NEXT):
            kb.op(kb.act, A.activation, reads=[b_h[te]], writes=[b_junk, b_ssq], out=junk[:], in_=h[:, te, :],
                  func=AF.Square, accum_out=ssq[:, te:te + 1])
        emit_rstd(kb, es, ssq, b_ssq, NEXT, 1.0 / D)
        kb.barrier()
    emit_ffn(kb, es, c, h, b_h, NEXT, ssq, b_ssq, a2, b_a2, sh2, b_sh2, g2bc, b_g2, d["win_t"], d["wout"],
             vec["cw"], bv, vec["cb"], bv, vec["hv"], bv, pairs, banks, 128, "A")
    if stop == 5:
        kb.finish()
        return nc
    for te in range(NEXT):
        kb.store(kb.sp, o_h1[te * 128:(te + 1) * 128, :], h[:, te, :], reads=[b_h[te]])
    if stop == 6:
        kb.finish()
        return nc
    emit_latents(kb, es, c, d, h, b_h, ssq, b_ssq, akv, b_akv, shkv, b_shkv, a1n, b_a1n, sh1n, b_sh1n, vec, bv,
                 pairs, banks, o_ckv, o_kr, o_cq, stop=stop)
    kb.finish()
    return nc


def toeplitz_bias(relb):
    k = np.arange(128)[:, None, None]
    m = np.arange(5)[None, :, None]
    q = np.arange(128)[None, None, :]
    idx = np.clip(q - k + 128 * (4 - m), -128, 128) + 128
    T = relb[:, idx]
    a, j = q // 64, k // 64
    masked = ((m == 0) & (j == 0) & (a == 1)) | ((m == 4) & (j == 1) & (a == 0))
    T = np.where(masked[None], np.float32(NEG), T).astype(np.float32)
    return np.ascontiguousarray(T.transpose(1, 0, 2, 3).reshape(128, 16 * 640))


def tile_win(win):
    w = np.asarray(win).reshape(8, 128, 2, NF, 128)
    return np.ascontiguousarray(w.transpose(3, 1, 0, 2, 4).reshape(NF, 128, 8 * 256))


def tile_wqkv(w):
    w = np.asarray(w).reshape(8, 128, 3, 8, 128)
    return np.ascontiguousarray(w.transpose(3, 1, 0, 2, 4).reshape(8, 128, 8 * 384))


def conv_fm(conv_w, conv_b):
    cw = np.ascontiguousarray(np.asarray(conv_w).T.reshape(2 * NF, 128, 3).transpose(1, 0, 2).reshape(128, 2 * NF * 3))
    return cw, fm(conv_b)


def rope_consts():
    invf = np.power(np.float32(10000.0), -np.arange(16, dtype=np.float32) * np.float32(2.0 / 32)).astype(np.float32)
    invf = np.concatenate([invf, invf]).reshape(32, 1)
    sgn = np.concatenate([-np.ones(16, np.float32), np.ones(16, np.float32)]).reshape(32, 1)
    return invf, sgn


def prep_A(inp):
    x, cc, pos = inp["x"], inp["c"], inp["positions"]
    invf, sgn = rope_consts()
    cw0, cb0 = conv_fm(inp["f_conv_w"][0], inp["f_conv_b"][0])
    wkr = np.asarray(inp["b_wkr"])
    shared = {
        "invf": invf, "sgn": sgn, "ident": np.eye(128, dtype=np.float32),
        "mw0": np.ascontiguousarray(inp["mod_w"][0]), "mb0": np.ascontiguousarray(inp["mod_b"][0][None]),
        "mw1": np.ascontiguousarray(inp["mod_w"][1][:, 0:2 * D]), "mb1": np.ascontiguousarray(inp["mod_b"][1][None, 0:2 * D]),
        "kvmw": np.ascontiguousarray(inp["kv_mod_w"]), "kvmb": np.ascontiguousarray(inp["kv_mod_b"][None]),
        "n1g0": fm(inp["norm1_g"][0]), "n2g0": fm(inp["norm2_g"][0]), "n1g1": fm(inp["norm1_g"][1]),
        "kvng": fm(inp["kv_norm_g"]),
        "wqkv_t": tile_wqkv(inp["a_wqkv"][0]), "wo": np.ascontiguousarray(inp["a_wo"][0]),
        "tb": toeplitz_bias(np.asarray(inp["a_rel_bias"][0])),
        "win_t": tile_win(inp["f_win"][0]), "cw": cw0, "cb": cb0, "wout": np.ascontiguousarray(inp["f_wout"][0]),
        "wdkv": np.ascontiguousarray(inp["b_wdkv"]), "latg": np.ascontiguousarray(inp["b_kv_lat_norm_g"][None]),
        "wkr2": np.ascontiguousarray(np.concatenate([wkr, wkr[:, 16:32], wkr[:, 0:16]], axis=1)),
        "wdq": np.ascontiguousarray(inp["b_wdq"][0]), "qg": np.ascontiguousarray(inp["b_q_norm_g"][0][None]),
    }
    maps = []
    for core in range(NCORE):
        b, j = core // 4, core % 4
        s0 = j * OWN
        xe = np.zeros((NALL * 128, D), np.float32)
        lo = s0 - 640
        src0 = max(lo, 0)
        xe[src0 - lo:] = x[b, src0:s0 + OWN]
        tok = lo + np.arange(NALL * 128)
        valid = np.ascontiguousarray((tok >= 0).astype(np.float32).reshape(NALL, 128).T)
        m = dict(shared)
        m.update({"xe": xe, "valid": valid, "posk": np.ascontiguousarray(pos[b, s0:s0 + OWN][None]).astype(np.int32),
                  "cvec": fm(cc[b]), "hv": np.full((128, 1), 1.0 if s0 > 0 else 0.0, np.float32)})
        maps.append(m)
    return maps


def mla_mask():
    k = np.arange(128)[:, None, None]
    jj = np.arange(4)[None, :, None]
    q = np.arange(512)[None, None, :]
    allowed = (2 * jj + k // 64) <= (q // 64)
    return np.ascontiguousarray(np.where(allowed, np.float32(0.0), np.float32(NEG)).astype(np.float32).reshape(128, 4 * 512))


def build_phaseB():
    nc = bass.Bass("TRN2", target_bir_lowering=False)
    d = {}
    for name, shape, dt in [
        ("cqT", [384, S], BF16), ("ckvT", [256, S], BF16), ("krT", [32, S], BF16), ("posq", [1, S], I32),
        ("invf", [32, 1], F32), ("sgn", [32, 1], F32), ("ident", [128, 128], F32),
        ("wq96", [384, 4 * 96], F32), ("wq96s", [384, 4 * 96], F32), ("wuk", [256, 256], F32), ("wuv", [256, 256], F32),
        ("mask", [128, 4 * 512], F32),
    ]:
        d[name] = _din(nc, name, shape, dt)
    o_oT = _dout(nc, "oT", [256, S], BF16)
    kb = KB(nc)
    es = kb.es
    V, A, G = nc.vector, nc.scalar, nc.gpsimd
    pairs, banks = psum_banks(kb, es)
    c = emit_consts(kb, es, d)
    invf = kb.sb(es, [96, 1], F32, "invf")[64:96]
    sgn = kb.sb(es, [96, 1], F32, "sgn")[64:96]
    dsv = DSem(kb, "bvec")
    bv = Buf()
    kb.load(kb.sp, dsv, invf, d["invf"], writes=[bv], chain=False)
    kb.load(kb.sp, dsv, sgn, d["sgn"], writes=[bv], chain=False)
    SC = 96.0 ** -0.5
    NQB = S // 512
    cqs = [kb.sb(es, [128, 3, 512], BF16, "cqs") for _ in range(3)]
    b_cqs = [Buf() for _ in range(3)]
    ds_cq = [DSem(kb, f"cqs{i}") for i in range(3)]
    cq_d = d["cqT"].rearrange("(c p) n -> p c n", p=128)
    ckvT = kb.sb(es, [128, 2, S], BF16, "ckvT")
    KT = kb.sb(es, [96, S], BF16, "KT")
    QT = kb.sb(es, [96, S], BF16, "QT")
    b_ckv, b_KT, b_QT = Buf(), Buf(), Buf()
    dsl = [DSem(kb, f"bl{i}") for i in range(3)]
    kb.load(kb.sp, dsl[1], ckvT[:], d["ckvT"].rearrange("(c p) n -> p c n", p=128), writes=[b_ckv])
    kb.load(kb.sp, dsl[2], KT[64:96, :], d["krT"], writes=[b_KT])
    wq = kb.sb(es, [128, 3, 4 * 96], BF16, "wq96")
    wqs = kb.sb(es, [128, 3, 4 * 96], BF16, "wq96s")
    wuk = kb.sb(es, [128, 2, 256], BF16, "wuk")
    wuv = kb.sb(es, [128, 2, 256], BF16, "wuv")
    mask = kb.sb(es, [128, 4, 512], BF16, "mask")
    dsw = [DSem(kb, f"bw{i}") for i in range(5)]
    toks = [
        kb.load(kb.pool, dsw[0], wq[:], d["wq96"].rearrange("(c p) n -> p c n", p=128)),
        kb.load(kb.pool, dsw[1], wqs[:], d["wq96s"].rearrange("(c p) n -> p c n", p=128)),
        kb.load(kb.pool, dsw[2], wuk[:], d["wuk"].rearrange("(c p) n -> p c n", p=128)),
        kb.load(kb.pool, dsw[3], wuv[:], d["wuv"].rearrange("(c p) n -> p c n", p=128)),
        kb.load(kb.pool, dsw[4], mask[:], d["mask"].rearrange("p (j n) -> p j n", j=4)),
    ]
    b_wl = [Buf() for _ in toks]
    for bb, t in zip(b_wl, toks):
        bb.w = t
    cosb = kb.sb(es, [96, S], BF16, "cosb")[64:96]
    sinb = kb.sb(es, [96, S], BF16, "sinb")[64:96]
    b_tab = Buf()
    with ExitStack() as sr:
        posi = kb.sb(sr, [96, 2048], I32, "posi")[64:96]
        b_pos = Buf()
        cosf = kb.sb(sr, [96, 2048], F32, "cosf")[64:96]
        sinf = kb.sb(sr, [96, 2048], F32, "sinf")[64:96]
        b_tf = Buf()
        dsp = DSem(kb, "posq")
        for part in range(4):
            tk = slice(part * 2048, (part + 1) * 2048)
            kb.load(kb.sp, dsp, posi, d["posq"][0:1, tk].partition_broadcast(32), writes=[b_pos])
            with ExitStack() as sr2:
                emit_rope_tables(kb, sr2, posi, b_pos, 2048, invf, sgn, bv, cosf, sinf, b_tf, p0=64)
                kb.op(kb.dve, V.tensor_copy, reads=[b_tf], writes=[b_tab], out=cosb[:, tk], in_=cosf)
                kb.op(kb.dve, V.tensor_copy, reads=[b_tf], writes=[b_tab], out=sinb[:, tk], in_=sinf)
                kb.barrier()
    va = kb.sb(es, [128, S // 128, 128], BF16, "va")
    b_va = Buf()
    kb.op(kb.dve, V.memset, writes=[b_va], ap=va[:, :, 64:128], constant=1.0)
    PT = [kb.sb(es, [128, 1024], BF16, "PT") for _ in range(3)]
    b_PT = [Buf() for _ in range(3)]
    rs = kb.sb(es, [64, 512], F32, "rs")
    b_rs = Buf()
    ost = [kb.sb(es, [64, 512], BF16, "ost") for _ in range(3)]
    b_ost = [Buf() for _ in range(3)]
    t1 = [kb.sb(es, [96, 512], F32, "t1")[64:96] for _ in range(2)]
    t2 = [kb.sb(es, [96, 512], F32, "t2")[64:96] for _ in range(2)]
    b_t1, b_t2 = [Buf(), Buf()], [Buf(), Buf()]
    pcnt = 0
    acnt = 0
    ocnt = 0

    def load_cq(i):
        blk = i % NQB
        kb.load(kb.sp, ds_cq[i % 3], cqs[i % 3][:], cq_d[:, :, blk * 512:(blk + 1) * 512], writes=[b_cqs[i % 3]])
    load_cq(0)
    load_cq(1)
    for hh in range(4):
        hc = slice(hh * 64, (hh + 1) * 64)
        h96 = slice(hh * 96, (hh + 1) * 96)
        for blk in range(NQB):
            ci = hh * NQB + blk
            if ci + 2 < 4 * NQB:
                load_cq(ci + 2)
            cqT, b_cq = cqs[ci % 3], b_cqs[ci % 3]
            tk = slice(blk * 512, (blk + 1) * 512)
            p1, b_p1 = banks[4 + pcnt % 4]
            pcnt += 1
            p2, b_p2 = banks[4 + pcnt % 4]
            pcnt += 1

            def emit(p1=p1, cqT=cqT):
                for kc in range(3):
                    ins = nc.tensor.matmul(p1[0:96, :], lhsT=wq[:, kc, h96], rhs=cqT[:, kc, :], start=(kc == 0),
                                           stop=(kc == 2))
                return ins
            kb.mm(emit, reads=[b_wl[0], b_cq], writes=[b_p1])

            def emit(p2=p2, cqT=cqT):
                for kc in range(3):
                    ins = nc.tensor.matmul(p2[0:96, :], lhsT=wqs[:, kc, h96], rhs=cqT[:, kc, :], start=(kc == 0),
                                           stop=(kc == 2))
                return ins
            kb.mm(emit, reads=[b_wl[1], b_cq], writes=[b_p2])
            kb.op(kb.act, A.activation, reads=[b_p1], writes=[b_QT], out=QT[0:64, tk], in_=p1[0:64, :], func=AF.Copy,
                  scale=SC)
            ti_ = blk % 2
            kb.op(kb.dve, V.tensor_tensor, reads=[b_p1, b_tab], writes=[b_t1[ti_]], out=t1[ti_], in0=p1[64:96, :],
                  in1=cosb[:, tk], op=ALU.mult)
            kb.op(kb.dve, V.scalar_tensor_tensor, reads=[b_p2, b_tab], writes=[b_t2[ti_]], out=t2[ti_],
                  in0=p2[64:96, :], scalar=SC, in1=sinb[:, tk], op0=ALU.mult, op1=ALU.mult)
            kb.op(kb.dve, V.scalar_tensor_tensor, reads=[b_t1[ti_], b_t2[ti_]], writes=[b_QT], out=QT[64:96, tk],
                  in0=t1[ti_], scalar=SC, in1=t2[ti_], op0=ALU.mult, op1=ALU.add)
            pb, b_pb = banks[4 + pcnt % 4]
            pcnt += 1

            def emit(pb=pb, tk=tk):
                for kc in range(2):
                    ins = nc.tensor.matmul(pb[0:64, :], lhsT=wuk[:, kc, hc], rhs=ckvT[:, kc, tk], start=(kc == 0),
                                           stop=(kc == 1))
                return ins
            kb.mm(emit, reads=[b_wl[2], b_ckv], writes=[b_pb])
            kb.op(kb.act, A.activation, reads=[b_pb], writes=[b_KT], out=KT[0:64, tk], in_=pb[0:64, :], func=AF.Copy)
        for t0 in range(0, S // 128, 8):
            pb, b_pb = banks[4 + pcnt % 4]
            pcnt += 1

            def emit(pb=pb, t0=t0):
                for j in range(8):
                    for kc in range(2):
                        ins = nc.tensor.matmul(pb[:, j * 64:(j + 1) * 64],
                                               lhsT=ckvT[:, kc, (t0 + j) * 128:(t0 + j + 1) * 128], rhs=wuv[:, kc, hc],
                                               start=(kc == 0), stop=(kc == 1))
                return ins
            kb.mm(emit, reads=[b_wl[3], b_ckv], writes=[b_pb])
            for j in range(8):
                if j % 2 == 0:
                    kb.op(kb.act, A.activation, reads=[b_pb], writes=[b_va], out=va[:, t0 + j, 0:64],
                          in_=pb[:, j * 64:(j + 1) * 64], func=AF.Copy)
                else:
                    kb.op(kb.dve, V.tensor_copy, reads=[b_pb], writes=[b_va], out=va[:, t0 + j, 0:64],
                          in_=pb[:, j * 64:(j + 1) * 64])
        for qb in range(NQB):
            qk = slice(qb * 512, (qb + 1) * 512)
            nkt = 4 * (qb + 1)
            ob, b_ob = banks[4 + ocnt % 2]
            ocnt += 1
            pend = []
            for kp in range(nkt // 2):
                sp_, sbufs = pairs[acnt % 2]
                pi = acnt % 3
                acnt += 1

                def emit(sp_=sp_, kp=kp):
                    for u in range(2):
                        kt = 2 * kp + u
                        ks = slice(kt * 128, (kt + 1) * 128)
                        o_ = sp_[:, u * 512:(u + 1) * 512]
                        diag = kt >= 4 * qb
                        ins = nc.tensor.matmul(o_, lhsT=KT[:, ks], rhs=QT[:, qk], start=True, stop=not diag,
                                               skip_group_check=True)
                        if diag:
                            ins = nc.tensor.matmul(o_, lhsT=c.idb[:], rhs=mask[:, kt - 4 * qb, :], start=False,
                                                   stop=True, skip_group_check=True)
                    return ins
                kb.mm(emit, reads=[b_KT, b_QT, c.b_idb, b_wl[4]], writes=sbufs)
                kb.op(kb.act, A.activation, reads=sbufs, writes=[b_PT[pi]], out=PT[pi][:], in_=sp_, func=AF.Exp)

                def emit2(pi=pi, kp=kp):
                    for u in range(2):
                        kt = 2 * kp + u
                        ins = nc.tensor.matmul(ob, lhsT=va[:, kt, :], rhs=PT[pi][:, u * 512:(u + 1) * 512],
                                               start=(kt == 0), stop=(kt == nkt - 1), skip_group_check=True)
                    return ins
                pend.append((emit2, pi))
                if len(pend) > 1:
                    e2, p2_ = pend.pop(0)
                    kb.mm(e2, reads=[b_va, b_PT[p2_]], writes=[b_ob])
            while pend:
                e2, p2_ = pend.pop(0)
                kb.mm(e2, reads=[b_va, b_PT[p2_]], writes=[b_ob])
            oi = ocnt % 3
            kb.op(kb.dve, V.tensor_scalar, reads=[b_ob], writes=[b_rs], out=rs[:], in0=ob[64:128, :], scalar1=1e-30,
                  scalar2=None, op0=ALU.max)
            kb.op(kb.dve, V.reciprocal, reads=[b_rs], writes=[b_rs], out=rs[:], in_=rs[:])
            kb.op(kb.dve, V.tensor_tensor, reads=[b_ob, b_rs], writes=[b_ost[oi]], out=ost[oi][:], in0=ob[0:64, :],
                  in1=rs[:], op=ALU.mult)
            kb.store(kb.sp, o_oT[hh * 64:(hh + 1) * 64, qk], ost[oi][:], reads=[b_ost[oi]])
    kb.finish()
    return nc


def build_phaseC():
    nc = bass.Bass("TRN2", target_bir_lowering=False)
    d = {}
    for name, shape, dt in [
        ("h1e", [NEXT * 128, D], F32), ("oTe", [D, NEXT * 128], BF16), ("cvec", [128, 8], F32), ("hv", [128, 1], F32),
        ("ident", [128, 128], F32), ("mw1", [D, 6 * D], F32), ("mb1", [1, 6 * D], F32), ("n2g1", [128, 8], F32),
        ("wo1", [D, D], F32), ("win_t", [NF, 128, 8 * 256], F32), ("cw", [128, 2 * NF * 3], F32),
        ("cb", [128, 2 * NF], F32), ("wout", [FF, D], F32), ("fg", [1, D], F32),
    ]:
        d[name] = _din(nc, name, shape, dt)
    o_out = _dout(nc, "out", [OWN, D], F32)
    kb = KB(nc)
    es = kb.es
    V, A, G = nc.vector, nc.scalar, nc.gpsimd
    pairs, banks = psum_banks(kb, es)
    c = emit_consts(kb, es, d)
    vec = load_vecs(kb, es, c, {
        "hv": (d["hv"], [128, 1], F32), "n2g1": (d["n2g1"], [128, 8], F32),
        "cw": (d["cw"].rearrange("p (f t) -> p f t", t=3), [128, 2 * NF, 3], F32), "cb": (d["cb"], [128, 2 * NF], F32),
    })
    bv = c.b_vec
    fm1, bc1 = emit_mod(kb, es, c, [banks[0], banks[1]], d["cvec"], d["mw1"], d["mb1"], 6 * D,
                        want_fm=[3072, 4096], want_bc=[2048, 5120], tag="c")
    a2 = kb.sb(es, [128, 8], F32, "a2")
    b_a2 = Buf()
    kb.op(kb.dve, V.scalar_tensor_tensor, reads=[fm1[4096][1], bv], writes=[b_a2], out=a2[:], in0=fm1[4096][0][:],
          scalar=1.0, in1=vec["n2g1"][:], op0=ALU.add, op1=ALU.mult)
    sh2, b_sh2 = fm1[3072]
    g1bc, b_g1 = bc1[2048]
    g2bc, b_g2 = bc1[5120]
    ssq = kb.sb(es, [128, 32], F32, "ssq")
    b_ssq = Buf()
    h = kb.sb(es, [128, NEXT, 1024], F32, "h")
    b_h = [Buf() for _ in range(NEXT)]
    hv_d = d["h1e"].rearrange("(t p) n -> p t n", p=128)
    with ExitStack() as s3:
        oT = kb.sb(s3, [128, 8, NEXT * 128], BF16, "oT")
        b_oT = Buf()
        ds_o = DSem(kb, "oT")
        kb.load(kb.sp, ds_o, oT[:], d["oTe"].rearrange("(c p) n -> p c n", p=128), writes=[b_oT])
        wo = kb.sb(s3, [128, 8, 1024], BF16, "wo")
        b_wo = Buf()
        ds_wo = DSem(kb, "wo")
        kb.load(kb.pool, ds_wo, wo[:], d["wo1"].rearrange("(c p) n -> p c n", p=128), writes=[b_wo])
        tmp = kb.sb(s3, [128, 1024], F32, "tmp3")
        b_tmp = Buf()
        dsh = [DSem(kb, f"h{i}") for i in range(4)]
        for te in range(NEXT):
            kb.load(kb.sp, dsh[te % 4], h[:, te, :], hv_d[:, te, :], writes=[b_h[te]])
        for te in range(NEXT):
            pair, pb2 = pairs[te % 2]

            def emit(pair=pair, te=te):
                for hh in range(2):
                    for cc in range(8):
                        ins = nc.tensor.matmul(pair[:, hh * 512:(hh + 1) * 512], lhsT=oT[:, cc, te * 128:(te + 1) * 128],
                                               rhs=wo[:, cc, hh * 512:(hh + 1) * 512], start=(cc == 0), stop=(cc == 7))
                return ins
            kb.mm(emit, reads=[b_oT, b_wo], writes=pb2)
            kb.op(kb.dve, V.tensor_tensor, reads=pb2 + [b_g1], writes=[b_tmp], out=tmp[:], in0=pair[:], in1=g1bc[:],
                  op=ALU.mult)
            kb.op(kb.dve, V.tensor_tensor, reads=[b_tmp], writes=[b_h[te]], out=h[:, te, :], in0=h[:, te, :],
                  in1=tmp[:], op=ALU.add)
        kb.barrier()

    def stats(lo):
        with ExitStack() as s4:
            junk = kb.sb(s4, [128, 1024], BF16, "junk2")
            b_junk = Buf()
            kb.op(kb.dve, V.memset, writes=[b_ssq], ap=ssq[:], constant=0.0)
            for te in range(lo, NEXT):
                kb.op(kb.act, A.activation, reads=[b_h[te]], writes=[b_junk, b_ssq], out=junk[:], in_=h[:, te, :],
                      func=AF.Square, accum_out=ssq[:, te:te + 1])
            emit_rstd(kb, es, ssq, b_ssq, NEXT, 1.0 / D)
            kb.barrier()
    stats(0)
    emit_ffn(kb, es, c, h, b_h, NEXT, ssq, b_ssq, a2, b_a2, sh2, b_sh2, g2bc, b_g2, d["win_t"], d["wout"],
             vec["cw"], bv, vec["cb"], bv, vec["hv"], bv, pairs, banks, 128, "C")
    stats(1)
    with ExitStack() as s5:
        ot = [kb.sb(s5, [128, 1024], F32, "ot") for _ in range(3)]
        b_ot = [Buf() for _ in range(3)]
        fg = kb.sb(s5, [128, D], F32, "fg")
        b_fg = Buf()
        kb.load(kb.sp, DSem(kb, "fg"), fg[:], d["fg"].partition_broadcast(128), writes=[b_fg])
        for te in range(1, NEXT):
            i = te % 3
            kb.op(kb.dve, V.scalar_tensor_tensor, reads=[b_h[te], b_ssq, b_fg], writes=[b_ot[i]], out=ot[i][:],
                  in0=h[:, te, :], scalar=ssq[:, te:te + 1], in1=fg[:], op0=ALU.mult, op1=ALU.mult)
            kb.store(kb.sp, o_out[(te - 1) * 128:te * 128, :], ot[i][:], reads=[b_ot[i]])
        kb.barrier()
    kb.finish()
    return nc


_CACHE = {}


def _get(name, fn):
    if name not in _CACHE:
        _CACHE[name] = fn()
    return _CACHE[name]


def kernel(**inp):
    inp = {k: np.asarray(v) for k, v in inp.items()}
    ident = np.eye(128, dtype=np.float32)
    invf, sgn = rope_consts()
    cores = list(range(NCORE))
    ncA = _get("A", build_phaseA)
    resA = run_bass_kernel_spmd(ncA, prep_A(inp), core_ids=cores).results
    ncB = _get("B", build_phaseB)
    wqr = np.asarray(inp["b_wqr"][0]).reshape(384, 16, 32)
    wuq_ = np.asarray(inp["b_wuq"][0]).reshape(384, 16, 64)
    wq96 = np.concatenate([wuq_, wqr], axis=2)
    wq96s = np.concatenate([np.zeros_like(wuq_), wqr[:, :, 16:32], wqr[:, :, 0:16]], axis=2)
    mask = mla_mask()
    mapsB = []
    for core in cores:
        b, j = core // 4, core % 4
        g = [resA[b * 4 + q] for q in range(4)]
        hs = slice(j * 256, (j + 1) * 256)
        mapsB.append({
            "cqT": np.ascontiguousarray(np.concatenate([np.asarray(r["cqT"]) for r in g], axis=1)),
            "ckvT": np.ascontiguousarray(np.concatenate([np.asarray(r["ckvT"]) for r in g], axis=1)),
            "krT": np.ascontiguousarray(np.concatenate([np.asarray(r["krT"]) for r in g], axis=1)),
            "posq": np.ascontiguousarray(inp["positions"][b][None]).astype(np.int32),
            "invf": invf, "sgn": sgn, "ident": ident,
            "wq96": np.ascontiguousarray(wq96[:, 4 * j:4 * j + 4, :].reshape(384, 4 * 96)),
            "wq96s": np.ascontiguousarray(wq96s[:, 4 * j:4 * j + 4, :].reshape(384, 4 * 96)),
            "wuk": np.ascontiguousarray(inp["b_wuk"][:, hs]), "wuv": np.ascontiguousarray(inp["b_wuv"][:, hs]),
            "mask": mask,
        })
    resB = run_bass_kernel_spmd(ncB, mapsB, core_ids=cores).results
    ncC = _get("C", build_phaseC)
    cw1, cb1 = conv_fm(inp["f_conv_w"][1], inp["f_conv_b"][1])
    sharedC = {
        "ident": ident, "mw1": np.ascontiguousarray(inp["mod_w"][1]), "mb1": np.ascontiguousarray(inp["mod_b"][1][None]),
        "n2g1": fm(inp["norm2_g"][1]), "wo1": np.ascontiguousarray(inp["b_wo"][0]),
        "win_t": tile_win(inp["f_win"][1]), "cw": cw1, "cb": cb1, "wout": np.ascontiguousarray(inp["f_wout"][1]),
        "fg": np.ascontiguousarray(inp["final_g"][None]),
    }
    mapsC = []
    for core in cores:
        b, j = core // 4, core % 4
        s0 = j * OWN
        oT_b = np.concatenate([np.asarray(resB[b * 4 + q]["oT"]) for q in range(4)], axis=0)
        oTe = np.zeros((D, NEXT * 128), oT_b.dtype)
        lo = s0 - 128
        src0 = max(lo, 0)
        oTe[:, src0 - lo:] = oT_b[:, src0:s0 + OWN]
        m = dict(sharedC)
        m.update({"h1e": np.ascontiguousarray(np.asarray(resA[core]["h1"])), "oTe": oTe, "cvec": fm(inp["c"][b]),
                  "hv": np.full((128, 1), 1.0 if s0 > 0 else 0.0, np.float32)})
        mapsC.append(m)
    resC = run_bass_kernel_spmd(ncC, mapsC, core_ids=cores).results
    out = np.zeros((2, S, D), np.float32)
    for core in cores:
        b, j = core // 4, core % 4
        out[b, j * OWN:(j + 1) * OWN] = np.asarray(resC[core]["out"])
    return out
```

```python
import os
import numpy as np
import ml_dtypes
from contextlib import ExitStack
import concourse.bass as bass
import concourse.mybir as mybir
from concourse.bass_utils import run_bass_kernel_spmd

F32 = mybir.dt.float32
BF16 = mybir.dt.bfloat16
I32 = mybir.dt.int32
AF = mybir.ActivationFunctionType
ALU = mybir.AluOpType
AX = mybir.AxisListType

NCORE = 8
D = 1024
S = 8192
OWN = 2048
NOWN = 16
NEXT = 17
NALL = 21
FF = 2816
NF = 22
NEG = -30000.0
ARENA_BYTES = 204 * 1024
TWO_PI = 6.283185307179586
PI = 3.141592653589793


class Tok:
    __slots__ = ("sem", "val", "key")

    def __init__(self, sem, val, key):
        self.sem, self.val, self.key = sem, val, key


class EQ:
    def __init__(self, kb, eng, name):
        self.kb, self.e, self.name = kb, eng, name
        self.sem = kb.newsem("q_" + name)
        self.cnt = 0
        self.seen = {}

    def wait(self, *toks):
        for t in toks:
            if t is None:
                continue
            if self.name == "pe" and t.key == "pe":
                continue
            if self.seen.get(t.key, 0) >= t.val:
                continue
            self.e.wait_ge(t.sem, t.val)
            self.seen[t.key] = t.val

    def done(self, ins):
        ins.then_inc(self.sem, 1)
        self.cnt += 1
        return Tok(self.sem, self.cnt, self.name)


class DSem:
    def __init__(self, kb, name):
        self.sem = kb.newsem("d_" + name)
        self.cnt = 0
        self.name = "d_" + name
        self.last = None
        kb.dsems.append(self)

    def add(self, ins):
        ins.then_inc(self.sem, 16)
        self.cnt += 16
        return Tok(self.sem, self.cnt, self.name)


class Buf:
    __slots__ = ("w", "r")

    def __init__(self):
        self.w = None
        self.r = {}


def _use(eq, reads, writes):
    for b in reads:
        eq.wait(b.w)
    for b in writes:
        eq.wait(b.w)
        eq.wait(*b.r.values())


def _fin(tok, reads, writes):
    for b in reads:
        b.r[tok.key] = tok
    for b in writes:
        b.w = tok
        b.r = {}


class KB:
    def __init__(self, nc):
        self.nc = nc
        self.es = ExitStack()
        self.nsem = 0
        self.dsems = []
        self.pe = EQ(self, nc.tensor, "pe")
        self.act = EQ(self, nc.scalar, "act")
        self.dve = EQ(self, nc.vector, "dve")
        self.pool = EQ(self, nc.gpsimd, "pool")
        self.sp = EQ(self, nc.sync, "sp")
        self.uid = 0
        self.arena = None
        self.peak = 0
        self.st_sem = DSem(self, "store")
        self.st_last = None

    def newsem(self, name):
        self.nsem += 1
        return self.es.enter_context(self.nc.semaphore(name))

    def sb(self, es, shape, dt, name=None):
        if self.arena is None:
            self.arena = self.es.enter_context(self.nc.sbuf_tensor("arena", [128, ARENA_BYTES // 2], BF16))
            self.free = [(0, ARENA_BYTES)]
        esz = 2 if dt == BF16 else 4
        n = 1
        for x in shape[1:]:
            n *= x
        nbytes = (n * esz + 63) // 64 * 64
        top = es is self.es
        order = range(len(self.free) - 1, -1, -1) if top else range(len(self.free))
        for i in order:
            o, sz = self.free[i]
            if sz >= nbytes:
                off = o + sz - nbytes if top else o
                if sz == nbytes:
                    self.free.pop(i)
                elif top:
                    self.free[i] = (o, sz - nbytes)
                else:
                    self.free[i] = (o + nbytes, sz - nbytes)
                break
        else:
            raise RuntimeError(f"SBUF arena full allocating {name} {shape} ({nbytes}B); free={self.free}")
        self.peak = max(self.peak, ARENA_BYTES - sum(z for _, z in self.free))

        def release(off=off, nbytes=nbytes):
            self.free.append((off, nbytes))
            self.free.sort()
            merged = []
            for o, z in self.free:
                if merged and merged[-1][0] + merged[-1][1] == o:
                    merged[-1] = (merged[-1][0], merged[-1][1] + z)
                else:
                    merged.append((o, z))
            self.free = merged
        es.callback(release)
        ap = self.arena[0:shape[0], off // 2:(off + n * esz) // 2]
        if dt != BF16:
            ap = ap.bitcast(dt)
        if len(shape) == 3:
            ap = ap.rearrange("p (a b) -> p a b", a=shape[1])
        elif len(shape) == 4:
            ap = ap.rearrange("p (a b c) -> p a b c", a=shape[1], b=shape[2])
        return ap

    def ps(self, es, shape, dt, name=None):
        self.uid += 1
        return es.enter_context(self.nc.psum_tensor(f"{name or 'p'}_{self.uid}", list(shape), dt))

    def op(self, eq, fn, reads=(), writes=(), **kw):
        _use(eq, reads, writes)
        tok = eq.done(fn(**kw))
        _fin(tok, reads, writes)
        return tok

    def mm(self, emit, reads=(), writes=()):
        _use(self.pe, reads, writes)
        ins = emit()
        tok = self.pe.done(ins)
        _fin(tok, reads, writes)
        return tok

    def load(self, q, dsem, out, in_, writes=(), reads=(), chain=True):
        _use(q, reads, writes)
        if chain:
            q.wait(dsem.last)
        tok = dsem.add(q.e.dma_start(out=out, in_=in_))
        dsem.last = tok
        _fin(tok, reads, writes)
        return tok

    def store(self, q, out, in_, reads=()):
        _use(q, reads, ())
        tok = self.st_sem.add(q.e.dma_start(out=out, in_=in_))
        _fin(tok, reads, ())
        self.st_last = tok
        return tok

    def barrier(self):
        qs = (self.pe, self.act, self.dve, self.pool, self.sp)
        toks = [Tok(q.sem, q.cnt, q.name) for q in qs if q.cnt]
        toks += [Tok(d.sem, d.cnt, d.name) for d in self.dsems if d.cnt]
        for q in qs:
            q.wait(*toks)

    def finish(self):
        if self.st_last is not None:
            self.sp.wait(self.st_last)
        for q in (self.pe, self.act, self.dve, self.pool):
            if q.cnt:
                self.sp.wait(Tok(q.sem, q.cnt, q.name))
        self.es.close()


def fm(v):
    v = np.asarray(v)
    return np.ascontiguousarray(v.reshape(-1, 128).T)


class Consts:
    pass


def emit_consts(kb, es, dram):
    nc = kb.nc
    c = Consts()
    c.dsem = DSem(kb, "const")
    c.idf = kb.sb(es, [128, 128], F32, "idf")
    c.idb = kb.sb(es, [128, 128], BF16, "idb")
    c.ones = kb.sb(es, [128, 128], F32, "ones")
    c.b_idf, c.b_idb, c.b_ones = Buf(), Buf(), Buf()
    kb.load(kb.sp, DSem(kb, "ident"), c.idf[:], dram["ident"], writes=[c.b_idf])
    kb.op(kb.dve, nc.vector.tensor_copy, reads=[c.b_idf], writes=[c.b_idb], out=c.idb[:], in_=c.idf[:])
    kb.op(kb.dve, nc.vector.memset, writes=[c.b_ones], ap=c.ones[:], constant=1.0)
    return c


def emit_mod(kb, es, c, banks, cvec_d, mw_d, mb_d, ncols, want_fm, want_bc, tag):
    nc = kb.nc
    ds = DSem(kb, "modc" + tag)
    ds2 = DSem(kb, "modb" + tag)
    out_fm, out_bc = {}, {}
    nblk = ncols // 512
    with ExitStack() as les:
        cv = kb.sb(les, [128, 8], F32, "cv")
        b_cv = Buf()
        kb.load(kb.sp, ds, cv[:], cvec_d, writes=[b_cv])
        kb.op(kb.act, nc.scalar.activation, reads=[b_cv], writes=[b_cv], out=cv[:], in_=cv[:], func=AF.Silu)
        crep = kb.sb(les, [128, 8, 128], BF16, "crep")
        b_crep = Buf()
        for k in range(8):
            kb.op(kb.dve, nc.vector.tensor_scalar, reads=[b_cv, c.b_ones], writes=[b_crep],
                  out=crep[:, k, :], in0=c.ones[:], scalar1=cv[:, k:k + 1], scalar2=None, op0=ALU.mult)
        wbuf = [kb.sb(les, [128, 8, 512], BF16, "mwb") for _ in range(3)]
        wsem = [DSem(kb, f"mw{tag}{i}") for i in range(3)]
        b_w = [Buf(), Buf(), Buf()]
        mbb = kb.sb(les, [128, 1024], F32, "mbb")
        b_mbb = Buf()
        bc = kb.sb(les, [128, 1024], F32, "bctmp")
        b_bc = Buf()
        wanted = sorted(set(want_fm) | set(want_bc))
        blocks = [(s0, h) for s0 in wanted for h in range(2)]
        mwv = mw_d.rearrange("(c p) n -> p c n", p=128)

        def issue(i):
            s0, h = blocks[i]
            col = s0 + h * 512
            kb.load(kb.pool, wsem[i % 3], wbuf[i % 3][:], mwv[:, :, col:col + 512], writes=[b_w[i % 3]])

        issue(0)
        if len(blocks) > 1:
            issue(1)
        for i, (s0, h) in enumerate(blocks):
            if i + 2 < len(blocks):
                issue(i + 2)
            if h == 0:
                kb.load(kb.sp, ds2, mbb[:], mb_d[0:1, s0:s0 + 1024].partition_broadcast(128), writes=[b_mbb])
                if s0 in want_bc:
                    t = kb.sb(es, [128, 1024], F32, "modbc")
                    out_bc[s0] = (t, Buf())
                dst, b_dst = out_bc[s0] if s0 in want_bc else (bc, b_bc)
            pb, b_pb = banks[i % 2]
            wb = wbuf[i % 3]

            def emit():
                for k in range(8):
                    ins = nc.tensor.matmul(pb, lhsT=crep[:, k, :], rhs=wb[:, k, :], start=(k == 0), stop=(k == 7))
                return ins
            kb.mm(emit, reads=[b_crep, b_w[i % 3]], writes=[b_pb])
            kb.op(kb.dve, nc.vector.tensor_tensor, reads=[b_pb, b_mbb], writes=[b_dst],
                  out=dst[:, h * 512:(h + 1) * 512], in0=pb, in1=mbb[:, h * 512:(h + 1) * 512], op=ALU.add)
            if h == 1 and s0 in want_fm:
                t = kb.sb(es, [128, 8], F32, "modfm")
                bt = Buf()
                pb2, b_pb2 = banks[(i + 1) % 2]

                def emit2():
                    for cc in range(8):
                        ins = nc.tensor.matmul(pb2[:, cc:cc + 1], lhsT=dst[:, cc * 128:(cc + 1) * 128],
                                               rhs=c.idf[:, 0:1], start=True, stop=True)
                    return ins
                kb.mm(emit2, reads=[b_dst, c.b_idf], writes=[b_pb2])
                kb.op(kb.dve, nc.vector.tensor_copy, reads=[b_pb2], writes=[bt], out=t[:], in_=pb2[:, 0:8])
                out_fm[s0] = (t, bt)
        kb.barrier()
    return out_fm, out_bc


def psum_banks(kb, es):
    pt = [kb.ps(es, [128, 1024], F32, "pp") for _ in range(4)]
    bufs = [Buf() for _ in range(8)]
    banks = [(pt[b // 2][:, (b % 2) * 512:(b % 2) * 512 + 512], bufs[b]) for b in range(8)]
    pairs = [(pt[p][:], [bufs[2 * p], bufs[2 * p + 1]]) for p in range(4)]
    return pairs, banks


def emit_rstd(kb, es, ssq, b_ssq, n, inv_n, eps=1e-6):
    nc = kb.nc
    kb.op(kb.dve, nc.vector.tensor_scalar, reads=[b_ssq], writes=[b_ssq], out=ssq[:, 0:n], in0=ssq[:, 0:n],
          scalar1=inv_n, scalar2=eps, op0=ALU.mult, op1=ALU.add)
    kb.op(kb.act, nc.scalar.activation, reads=[b_ssq], writes=[b_ssq], out=ssq[:, 0:n], in_=ssq[:, 0:n], func=AF.Sqrt)
    kb.op(kb.dve, nc.vector.reciprocal, reads=[b_ssq], writes=[b_ssq], out=ssq[:, 0:n], in_=ssq[:, 0:n])


def emit_norm_T(kb, c, tiles, rstd, b_rstd, outs, ppairs, xnb, b_xnb, cnt=[0]):
    nc = kb.nc
    i = 0
    while i < len(tiles):
        grp = tiles[i:i + 2]
        g = cnt[0]
        cnt[0] += 1
        pair, pb = ppairs[g % 2]
        pv = pair.bitcast(BF16).rearrange("p (c t) -> p c t", c=8)
        for j, (src, b_src, rc, doff) in enumerate(grp):
            xi = ((g % 2) * 2 + j) % len(xnb)
            kb.op(kb.act, nc.scalar.activation, reads=[b_src, b_rstd], writes=[b_xnb[xi]],
                  out=xnb[xi][:], in_=src, func=AF.Copy, scale=rstd[:, rc:rc + 1])
        for j, (src, b_src, rc, doff) in enumerate(grp):
            xi = ((g % 2) * 2 + j) % len(xnb)

            def emit(j=j, xi=xi):
                for cc in range(8):
                    ins = nc.tensor.transpose(out=pv[:, cc, j * 128:(j + 1) * 128],
                                              in_=xnb[xi][:, cc * 128:(cc + 1) * 128], identity=c.idb[:])
                return ins
            kb.mm(emit, reads=[b_xnb[xi], c.b_idb], writes=pb)
        n = 128 * len(grp)
        doff = grp[0][3]
        k = 0
        for (dst, b_dst, a, b_a, sh, b_sh) in outs:
            for cc in range(8):
                if k % 2 == 0:
                    kb.op(kb.act, nc.scalar.activation, reads=pb + [b_a, b_sh], writes=[b_dst],
                          out=dst[:, cc, doff:doff + n], in_=pv[:, cc, 0:n], func=AF.Identity,
                          scale=a[:, cc:cc + 1], bias=sh[:, cc:cc + 1])
                else:
                    kb.op(kb.dve, nc.vector.tensor_scalar, reads=pb + [b_a, b_sh], writes=[b_dst],
                          out=dst[:, cc, doff:doff + n], in0=pv[:, cc, 0:n], scalar1=a[:, cc:cc + 1],
                          scalar2=sh[:, cc:cc + 1], op0=ALU.mult, op1=ALU.add)
                k += 1
        i += 2


def emit_ffn(kb, es, c, h, b_h, nt, rstd, b_rstd, a2, b_a2, sh2, b_sh2, g2bc, b_g2, win_d, wout_d,
             cw, b_cw, cb, b_cb, hv, b_hv, pairs, banks, fix_tok, tag):
    nc = kb.nc
    ntok = nt * 128
    blocks = [(s0, min(512, ntok - s0)) for s0 in range(0, ntok, 512)]
    with ExitStack() as les:
        wout = kb.sb(les, [128, NF, 1024], BF16, "wout")
        b_wout = Buf()
        ds_wout = DSem(kb, "wout" + tag)
        wov = wout_d.rearrange("(f p) n -> p f n", p=128)
        for f0 in range(0, NF, 6):
            f1 = min(NF, f0 + 6)
            kb.load(kb.pool, ds_wout, wout[:, f0:f1, :], wov[:, f0:f1, :], writes=[b_wout])
        hnT = kb.sb(les, [128, 8, 512], BF16, "hn2T")
        b_hnT = Buf()
        actTs = [kb.sb(les, [128, NF, 512], BF16, "actT") for _ in range(2)]
        b_actTs = [Buf(), Buf()]
        xnb = [kb.sb(les, [128, 1024], BF16, "xnb") for _ in range(2)]
        b_xnb = [Buf() for _ in range(2)]
        wsl = [kb.sb(les, [128, 8, 256], BF16, "winsl") for _ in range(3)]
        b_wsl = [Buf() for _ in range(3)]
        ds_w = [DSem(kb, f"win{tag}{i}") for i in range(3)]
        ug = [kb.sb(les, [128, 514], F32, "ug") for _ in range(2)]
        b_ug = [Buf() for _ in range(2)]
        yy = [kb.sb(les, [128, 512], F32, "yy") for _ in range(2)]
        b_yy = [Buf() for _ in range(2)]
        sgs = [kb.sb(les, [128, 512], F32, "sg") for _ in range(1)]
        b_sgs = [Buf()]
        halo = kb.sb(les, [128, 2 * NF, 2], F32, "halo")
        b_halo = Buf()
        kb.op(kb.dve, nc.vector.memset, writes=[b_halo], ap=halo[:], constant=0.0)
        jobs = [(bi, f) for bi in range(len(blocks)) for f in range(NF)]

        def issue_w(ji):
            bi, f = jobs[ji]
            sl = ji % 3
            kb.load(kb.pool, ds_w[sl], wsl[sl][:], win_d[f], writes=[b_wsl[sl]])

        issue_w(0)
        issue_w(1)
        pending = []
        for ji, (bi, f) in enumerate(jobs):
            s0, n = blocks[bi]
            actT, b_actT = actTs[bi % 2], b_actTs[bi % 2]
            if f == 0:
                tl = [(h[:, (s0 // 128) + j, :], b_h[(s0 // 128) + j], (s0 // 128) + j, j * 128)
                      for j in range(n // 128)]
                emit_norm_T(kb, c, tl, rstd, b_rstd, [(hnT, b_hnT, a2, b_a2, sh2, b_sh2)], pairs[0:2], xnb, b_xnb)
            if ji + 2 < len(jobs):
                issue_w(ji + 2)
            sl = ji % 3
            for gv in range(2):
                fi = f + gv * NF
                pb, b_pb = banks[4 + ((ji * 2 + gv) % 4)]

                def emit(gv=gv, pb=pb):
                    for k in range(8):
                        ins = nc.tensor.matmul(pb[:, 0:n], lhsT=wsl[sl][:, k, gv * 128:(gv + 1) * 128], rhs=hnT[:, k, 0:n],
                                               start=(k == 0), stop=(k == 7))
                    return ins
                kb.mm(emit, reads=[b_wsl[sl], b_hnT], writes=[b_pb])
                u, b_u = ug[gv], b_ug[gv]
                y, b_y = yy[gv], b_yy[gv]
                kb.op(kb.act, nc.scalar.activation, reads=[b_pb], writes=[b_u], out=u[:, 2:2 + n], in_=pb[:, 0:n],
                      func=AF.Copy)
                kb.op(kb.act, nc.scalar.activation, reads=[b_halo], writes=[b_u], out=u[:, 0:2], in_=halo[:, fi, :],
                      func=AF.Copy)
                if fix_tok is not None and s0 <= fix_tok - 2 and fix_tok <= s0 + n:
                    o = fix_tok - s0
                    kb.op(kb.dve, nc.vector.tensor_scalar, reads=[b_hv], writes=[b_u], out=u[:, o:o + 2],
                          in0=u[:, o:o + 2], scalar1=hv[:, 0:1], scalar2=None, op0=ALU.mult)
                kb.op(kb.act, nc.scalar.activation, reads=[b_pb, b_cw, b_cb], writes=[b_y], out=y[:, 0:n],
                      in_=pb[:, 0:n], func=AF.Identity, scale=cw[:, fi, 2:3], bias=cb[:, fi:fi + 1])
                kb.op(kb.act, nc.scalar.activation, reads=[b_u], writes=[b_halo], out=halo[:, fi, :],
                      in_=u[:, n:n + 2], func=AF.Copy)
                kb.op(kb.dve, nc.vector.scalar_tensor_tensor, reads=[b_u, b_cw], writes=[b_y], out=y[:, 0:n],
                      in0=u[:, 1:1 + n], scalar=cw[:, fi, 1:2], in1=y[:, 0:n], op0=ALU.mult, op1=ALU.add)
                kb.op(kb.dve, nc.vector.scalar_tensor_tensor, reads=[b_u, b_cw], writes=[b_y], out=y[:, 0:n],
                      in0=u[:, 0:n], scalar=cw[:, fi, 0:1], in1=y[:, 0:n], op0=ALU.mult, op1=ALU.add)
            yg_, b_yg_ = yy[0], b_yy[0]
            yv_, b_yv_ = yy[1], b_yy[1]
            sg, b_sg = sgs[0], b_sgs[0]
            kb.op(kb.act, nc.scalar.activation, reads=[b_yg_], writes=[b_sg], out=sg[:, 0:n], in_=yg_[:, 0:n],
                  func=AF.Silu)
            kb.op(kb.dve, nc.vector.tensor_tensor, reads=[b_sg, b_yv_], writes=[b_actT], out=actT[:, f, 0:n],
                  in0=sg[:, 0:n], in1=yv_[:, 0:n], op=ALU.mult)
            if pending and f % 4 == 3:
                pending.pop(0)()
            if f == NF - 1:
                while pending:
                    pending.pop(0)()
                for j in range(n // 128):
                    def out_tile(j=j, s0=s0, actT=actT, b_actT=b_actT):
                        ti = s0 // 128 + j
                        pair, pb2 = pairs[ti % 2]

                        def emit3():
                            for hh in range(2):
                                for ff in range(NF):
                                    ins = nc.tensor.matmul(pair[:, hh * 512:(hh + 1) * 512],
                                                           lhsT=actT[:, ff, j * 128:(j + 1) * 128],
                                                           rhs=wout[:, ff, hh * 512:(hh + 1) * 512],
                                                           start=(ff == 0), stop=(ff == NF - 1))
                            return ins
                        kb.mm(emit3, reads=[b_actT, b_wout], writes=pb2)
                        kb.op(kb.dve, nc.vector.tensor_tensor, reads=[b_g2], writes=pb2, out=pair[:],
                              in0=pair[:], in1=g2bc[:], op=ALU.mult)
                        kb.op(kb.dve, nc.vector.tensor_tensor, reads=pb2, writes=[b_h[ti]], out=h[:, ti, :],
                              in0=pair[:], in1=h[:, ti, :], op=ALU.add)
                    pending.append(out_tile)
        while pending:
            pending.pop(0)()
        kb.barrier()


def _din(nc, name, shape, dt=F32):
    return nc.dram_tensor(name, list(shape), dt, kind="ExternalInput").ap()


def _dout(nc, name, shape, dt=F32):
    return nc.dram_tensor(name, list(shape), dt, kind="ExternalOutput").ap()


def load_vecs(kb, es, c, specs):
    out = {}
    for name, (ap, shape, dt) in specs.items():
        t = kb.sb(es, shape, dt, name)
        kb.sp.e.dma_start(out=t[:], in_=ap).then_inc(c.dsem.sem, 16)
        c.dsem.cnt += 16
        out[name] = t
    c.b_vec = Buf()
    c.b_vec.w = Tok(c.dsem.sem, c.dsem.cnt, c.dsem.name)
    return out


def emit_rope_tables(kb, es, pos_i, b_pos, n, invf, sgn, b_vec, cosT, sinT, b_tab, p0=0, nparts=32):
    nc = kb.nc
    ang = kb.sb(es, [p0 + nparts, n], F32, "ang")[p0:p0 + nparts]
    kf = kb.sb(es, [p0 + nparts, n], F32, "kf")[p0:p0 + nparts]
    ki = kb.sb(es, [p0 + nparts, n], I32, "ki")[p0:p0 + nparts]
    m = kb.sb(es, [p0 + nparts, n], F32, "mm")[p0:p0 + nparts]
    b = Buf()
    C1 = 6.28125
    C2 = TWO_PI - C1
    V = nc.vector
    op = lambda fn, **kw: kb.op(kb.dve, fn, reads=[b_pos, b_vec], writes=[b, b_tab], **kw)
    op(V.tensor_copy, out=ang[:], in_=pos_i)
    op(V.tensor_scalar, out=ang[:], in0=ang[:], scalar1=invf[:, 0:1], scalar2=None, op0=ALU.mult)
    op(V.tensor_scalar, out=kf[:], in0=ang[:], scalar1=1.0 / TWO_PI, scalar2=None, op0=ALU.mult)
    op(V.tensor_copy, out=ki[:], in_=kf[:])
    op(V.tensor_copy, out=kf[:], in_=ki[:])
    op(V.scalar_tensor_tensor, out=ang[:], in0=kf[:], scalar=-C1, in1=ang[:], op0=ALU.mult, op1=ALU.add)
    op(V.scalar_tensor_tensor, out=ang[:], in0=kf[:], scalar=-C2, in1=ang[:], op0=ALU.mult, op1=ALU.add)

    def wrap(t):
        op(V.tensor_scalar, out=m[:], in0=t[:], scalar1=PI, scalar2=TWO_PI, op0=ALU.is_gt, op1=ALU.mult)
        op(V.tensor_tensor, out=t[:], in0=t[:], in1=m[:], op=ALU.subtract)
        op(V.tensor_scalar, out=m[:], in0=t[:], scalar1=-PI, scalar2=TWO_PI, op0=ALU.is_lt, op1=ALU.mult)
        op(V.tensor_tensor, out=t[:], in0=t[:], in1=m[:], op=ALU.add)
    wrap(ang)
    kb.op(kb.act, nc.scalar.activation, reads=[b], writes=[b_tab], out=sinT, in_=ang[:], func=AF.Sin)
    op(V.tensor_scalar, out=sinT, in0=sinT, scalar1=sgn[:, 0:1], scalar2=None, op0=ALU.mult)
    op(V.tensor_scalar, out=kf[:], in0=ang[:], scalar1=PI / 2, scalar2=None, op0=ALU.add)
    wrap(kf)
    kb.op(kb.act, nc.scalar.activation, reads=[b], writes=[b_tab], out=cosT, in_=kf[:], func=AF.Sin)


def emit_rope_packed(kb, es, pos_d, invf128, sgn128, b_vec, cos_out, sin_out, b_out, tag):
    nc = kb.nc
    with ExitStack() as sr:
        posi = kb.sb(sr, [128, 512], I32, "posi")
        cosf = kb.sb(sr, [128, 512], F32, "cosf")
        sinf = kb.sb(sr, [128, 512], F32, "sinf")
        b_pos, b_tf = Buf(), Buf()
        dsp = DSem(kb, "pos" + tag)
        for g in range(4):
            kb.load(kb.sp, dsp, posi[g * 32:(g + 1) * 32, :], pos_d[0:1, g * 512:(g + 1) * 512].partition_broadcast(32),
                    writes=[b_pos], chain=False)
        b_pos.w = Tok(dsp.sem, dsp.cnt, dsp.name)
        emit_rope_tables(kb, sr, posi[:], b_pos, 512, invf128, sgn128, b_vec, cosf[:], sinf[:], b_tf, p0=0, nparts=128)
        for g in range(4):
            kb.op(kb.dve, nc.vector.tensor_copy, reads=[b_tf], writes=[b_out], out=cos_out[:, g * 512:(g + 1) * 512],
                  in_=cosf[g * 32:(g + 1) * 32, :])
            kb.op(kb.dve, nc.vector.tensor_copy, reads=[b_tf], writes=[b_out], out=sin_out[:, g * 512:(g + 1) * 512],
                  in_=sinf[g * 32:(g + 1) * 32, :])
        kb.barrier()


def emit_latents(kb, es, c, d, h, b_h, ssq, b_ssq, akv, b_akv, shkv, b_shkv, a1n, b_a1n, sh1n, b_sh1n, vec, bv,
                 pairs, banks, o_ckv, o_kr, o_cq, stop=99):
    nc = kb.nc
    V, A, G = nc.vector, nc.scalar, nc.gpsimd
    with ExitStack() as s6:
        junk = kb.sb(s6, [128, 1024], BF16, "junk6")
        b_junk = Buf()
        kb.op(kb.dve, V.memset, writes=[b_ssq], ap=ssq[:], constant=0.0)
        for te in range(1, NEXT):
            kb.op(kb.act, A.activation, reads=[b_h[te]], writes=[b_junk, b_ssq], out=junk[:], in_=h[:, te, :],
                  func=AF.Square, accum_out=ssq[:, te:te + 1])
        emit_rstd(kb, es, ssq, b_ssq, NEXT, 1.0 / D)
        vec = dict(vec)
        vec["latg"] = kb.sb(s6, [128, 256], F32, "latg")
        vec["qg"] = kb.sb(s6, [128, 384], F32, "qg")
        b_lg = Buf()
        ds_lg = DSem(kb, "latg")
        kb.load(kb.sp, ds_lg, vec["latg"][:], d["latg"].partition_broadcast(128), writes=[b_lg], chain=False)
        kb.load(kb.sp, ds_lg, vec["qg"][:], d["qg"].partition_broadcast(128), writes=[b_lg], chain=False)
        wdkv = kb.sb(s6, [128, 8, 256], BF16, "wdkv")
        wdq = kb.sb(s6, [128, 8, 384], BF16, "wdq")
        wkr = kb.sb(s6, [128, 8, 64], BF16, "wkr")
        b_w = Buf()
        dsw = [DSem(kb, f"lw{i}") for i in range(3)]
        kb.load(kb.pool, dsw[0], wdkv[:], d["wdkv"].rearrange("(k p) n -> p k n", p=128), writes=[b_w])
        t2 = kb.load(kb.pool, dsw[1], wdq[:], d["wdq"].rearrange("(k p) n -> p k n", p=128), writes=[])
        t3 = kb.load(kb.pool, dsw[2], wkr[:], d["wkr2"].rearrange("(k p) n -> p k n", p=128), writes=[])
        b_w2, b_w3 = Buf(), Buf()
        b_w2.w, b_w3.w = t2, t3
        ssl = kb.sb(s6, [128, NOWN, 2], F32, "ssl")
        b_ssl = [Buf() for _ in range(NOWN)]
        kb.dve.done(V.memset(ap=ssl[:], constant=0.0))
        for ti in range(NOWN):
            b_ssl[ti].w = Tok(kb.dve.sem, kb.dve.cnt, "dve")
        krT = kb.sb(s6, [32, OWN], BF16, "krT")
        b_krT = Buf()
        ckvT = kb.sb(s6, [128, 2, OWN], BF16, "ckvTo")
        cqT = kb.sb(s6, [128, 3, OWN], BF16, "cqTo")
        b_ckvT, b_cqT = Buf(), Buf()
        cosT = kb.sb(s6, [32, OWN], F32, "cosT")
        sinT = kb.sb(s6, [32, OWN], F32, "sinT")
        b_tab = Buf()
        with ExitStack() as sr:
            posi = kb.sb(sr, [32, OWN], I32, "posi")
            b_pos = Buf()
            dsp = DSem(kb, "posk")
            kb.load(kb.sp, dsp, posi[:], d["posk"].partition_broadcast(32), writes=[b_pos])
            emit_rope_tables(kb, sr, posi[:], b_pos, OWN, vec["invf"], vec["sgn"], bv, cosT[:], sinT[:], b_tab)
            kb.barrier()
        if stop == 7:
            kb.barrier()
            return
        hkvT = kb.sb(s6, [128, 8, 512], BF16, "hkvT")
        hqT = kb.sb(s6, [128, 8, 512], BF16, "hqT")
        b_hkvT, b_hqT = Buf(), Buf()
        xnb = [kb.sb(s6, [128, 1024], BF16, "xnb6") for _ in range(4)]
        b_xnb = [Buf() for _ in range(4)]
        t1 = kb.sb(s6, [32, 512], F32, "rt1")
        t2_ = kb.sb(s6, [32, 512], F32, "rt2")
        b_t1, b_t2 = Buf(), Buf()
        lnb = [kb.sb(s6, [128, 640], BF16, "lnb") for _ in range(2)]
        b_lnb = [Buf(), Buf()]
        for blk in range(4):
            tl = [(h[:, 1 + 4 * blk + j, :], b_h[1 + 4 * blk + j], 1 + 4 * blk + j, j * 128) for j in range(4)]
            emit_norm_T(kb, c, tl, ssq, b_ssq, [(hkvT, b_hkvT, akv, b_akv, shkv, b_shkv),
                                                (hqT, b_hqT, a1n, b_a1n, sh1n, b_sh1n)], pairs[0:2], xnb, b_xnb)
            for j in range(4):
                ti = 4 * blk + j
                pbs = []
                for (src, b_src, w, b_ww, ncol, row, inv_n) in ((hkvT, b_hkvT, wdkv, b_w, 256, 0, 1.0 / 256),
                                                              (hqT, b_hqT, wdq, b_w2, 384, 1, 1.0 / 384)):
                    pb, b_pb = banks[4 + (2 * ti + row) % 4]
                    pbs.append((pb, b_pb))

                    def emit(pb=pb, src=src, w=w, ncol=ncol, j=j):
                        for k in range(8):
                            ins = nc.tensor.matmul(pb[:, 0:ncol], lhsT=src[:, k, j * 128:(j + 1) * 128], rhs=w[:, k, :],
                                                   start=(k == 0), stop=(k == 7))
                        return ins
                    kb.mm(emit, reads=[b_src, b_ww], writes=[b_pb])
                    kb.op(kb.act, A.activation, reads=[b_pb], writes=[b_junk, b_ssl[ti]], out=junk[:, 0:ncol],
                          in_=pb[:, 0:ncol], func=AF.Square, accum_out=ssl[:, ti, row:row + 1])
                    kb.op(kb.dve, V.tensor_scalar, reads=[], writes=[b_ssl[ti]], out=ssl[:, ti, row:row + 1],
                          in0=ssl[:, ti, row:row + 1], scalar1=inv_n, scalar2=1e-6, op0=ALU.mult, op1=ALU.add)
                kb.op(kb.act, A.activation, reads=[], writes=[b_ssl[ti]], out=ssl[:, ti, :], in_=ssl[:, ti, :],
                      func=AF.Sqrt)
                kb.op(kb.dve, V.reciprocal, reads=[], writes=[b_ssl[ti]], out=ssl[:, ti, :], in_=ssl[:, ti, :])
                lb, b_lb = lnb[ti % 2], b_lnb[ti % 2]
                kb.op(kb.dve, V.scalar_tensor_tensor, reads=[pbs[0][1], b_ssl[ti], b_lg], writes=[b_lb], out=lb[:, 0:256],
                      in0=pbs[0][0][:, 0:256], scalar=ssl[:, ti, 0:1], in1=vec["latg"][:], op0=ALU.mult, op1=ALU.mult)
                kb.op(kb.dve, V.scalar_tensor_tensor, reads=[pbs[1][1], b_ssl[ti], b_lg], writes=[b_lb], out=lb[:, 256:640],
                      in0=pbs[1][0][:, 0:384], scalar=ssl[:, ti, 1:2], in1=vec["qg"][:], op0=ALU.mult, op1=ALU.mult)
                if os.environ.get("SKIP_TR"):
                    continue
                pb, b_pb = banks[2 + ti % 2]
                pv = pairs[1][0].bitcast(BF16)[:, (ti % 2) * 1024:(ti % 2) * 1024 + 1024]

                def emit(pv=pv, lb=lb):
                    for cc in range(5):
                        ins = nc.tensor.transpose(out=pv[:, cc * 128:(cc + 1) * 128], in_=lb[:, cc * 128:(cc + 1) * 128],
                                                  identity=c.idb[:])
                    return ins
                kb.mm(emit, reads=[b_lb, c.b_idb], writes=[b_pb])
                for cc in range(5):
                    dst, b_dst = (ckvT[:, cc, ti * 128:(ti + 1) * 128], b_ckvT) if cc < 2 else \
                        (cqT[:, cc - 2, ti * 128:(ti + 1) * 128], b_cqT)
                    if cc % 2 == 0:
                        kb.op(kb.dve, V.tensor_copy, reads=[b_pb], writes=[b_dst], out=dst, in_=pv[:, cc * 128:(cc + 1) * 128])
                    else:
                        kb.op(kb.act, A.activation, reads=[b_pb], writes=[b_dst], out=dst, in_=pv[:, cc * 128:(cc + 1) * 128],
                              func=AF.Copy)
            if os.environ.get("SKIP_ROPE"):
                continue
            pa, b_pa = banks[0]
            pbw, b_pbw = banks[1]
            for (pp, b_pp, c0) in ((pa, b_pa, 0), (pbw, b_pbw, 32)):
                def emit(pp=pp, c0=c0):
                    for k in range(8):
                        ins = nc.tensor.matmul(pp[0:32, :], lhsT=wkr[:, k, c0:c0 + 32], rhs=hkvT[:, k, :],
                                               start=(k == 0), stop=(k == 7))
                    return ins
                kb.mm(emit, reads=[b_hkvT, b_w3], writes=[b_pp])
            tk = slice(blk * 512, (blk + 1) * 512)
            kb.op(kb.dve, V.tensor_tensor, reads=[b_pa, b_tab], writes=[b_t1], out=t1[:], in0=pa[0:32, :],
                  in1=cosT[:, tk], op=ALU.mult)
            kb.op(kb.dve, V.tensor_tensor, reads=[b_pbw, b_tab], writes=[b_t2], out=t2_[:], in0=pbw[0:32, :],
                  in1=sinT[:, tk], op=ALU.mult)
            kb.op(kb.dve, V.tensor_tensor, reads=[b_t1, b_t2], writes=[b_krT], out=krT[:, tk], in0=t1[:], in1=t2_[:],
                  op=ALU.add)
        if stop == 8:
            kb.barrier()
            return
        kb.store(kb.sp, o_ckv.rearrange("(c p) n -> p c n", p=128), ckvT[:], reads=[b_ckvT])
        kb.store(kb.sp, o_cq.rearrange("(c p) n -> p c n", p=128), cqT[:], reads=[b_cqT])
        kb.store(kb.sp, o_kr, krT[:], reads=[b_krT])
        kb.barrier()


def build_phaseA(dbg=False, stop=99):
    nc = bass.Bass("TRN2", target_bir_lowering=False)
    d = {}
    for name, shape, dt in [
        ("xe", [NALL * 128, D], F32), ("valid", [128, NALL], F32), ("posk", [1, OWN], I32), ("cvec", [128, 8], F32),
        ("invf", [32, 1], F32), ("sgn", [32, 1], F32), ("hv", [128, 1], F32), ("ident", [128, 128], F32),
        ("mw0", [D, 6 * D], F32), ("mb0", [1, 6 * D], F32), ("mw1", [D, 2 * D], F32), ("mb1", [1, 2 * D], F32),
        ("kvmw", [D, 2 * D], F32), ("kvmb", [1, 2 * D], F32),
        ("n1g0", [128, 8], F32), ("n2g0", [128, 8], F32), ("n1g1", [128, 8], F32), ("kvng", [128, 8], F32),
        ("wqkv_t", [8, 128, 8 * 384], F32), ("wo", [D, D], F32), ("tb", [128, 16 * 640], F32),
        ("win_t", [NF, 128, 8 * 256], F32), ("cw", [128, 2 * NF * 3], F32), ("cb", [128, 2 * NF], F32),
        ("wout", [FF, D], F32), ("wdkv", [D, 256], F32), ("latg", [1, 256], F32), ("wkr2", [D, 64], F32),
        ("wdq", [D, 384], F32), ("qg", [1, 384], F32),
    ]:
        d[name] = _din(nc, name, shape, dt)
    o_h1 = _dout(nc, "h1", [NEXT * 128, D], F32)
    o_ckv = _dout(nc, "ckvT", [256, OWN], BF16)
    o_kr = _dout(nc, "krT", [32, OWN], BF16)
    o_cq = _dout(nc, "cqT", [384, OWN], BF16)
    if dbg:
        o_dbg = _dout(nc, "dbg", [NEXT * 128, D], F32)

    kb = KB(nc)
    es = kb.es
    V, A, G = nc.vector, nc.scalar, nc.gpsimd
    pairs, banks = psum_banks(kb, es)
    c = emit_consts(kb, es, d)
    vec = load_vecs(kb, es, c, {
        "valid": (d["valid"], [128, NALL], F32), "hv": (d["hv"], [128, 1], F32),
        "invf": (d["invf"], [32, 1], F32), "sgn": (d["sgn"], [32, 1], F32),
        "n1g0": (d["n1g0"], [128, 8], F32), "n2g0": (d["n2g0"], [128, 8], F32),
        "n1g1": (d["n1g1"], [128, 8], F32), "kvng": (d["kvng"], [128, 8], F32),
        "cw": (d["cw"].rearrange("p (f t) -> p f t", t=3), [128, 2 * NF, 3], F32), "cb": (d["cb"], [128, 2 * NF], F32),
    })
    bv = c.b_vec
    fm0, bc0 = emit_mod(kb, es, c, [banks[0], banks[1]], d["cvec"], d["mw0"], d["mb0"], 6 * D,
                        want_fm=[0, 1024, 3072, 4096], want_bc=[2048, 5120], tag="0")
    fm1, _ = emit_mod(kb, es, c, [banks[0], banks[1]], d["cvec"], d["mw1"], d["mb1"], 2 * D,
                      want_fm=[0, 1024], want_bc=[], tag="1")
    fmk, _ = emit_mod(kb, es, c, [banks[0], banks[1]], d["cvec"], d["kvmw"], d["kvmb"], 2 * D,
                      want_fm=[0, 1024], want_bc=[], tag="k")

    def mk_a(sc, gname):
        t = kb.sb(es, [128, 8], F32, "a_" + gname)
        b = Buf()
        kb.op(kb.dve, V.scalar_tensor_tensor, reads=[sc[1], bv], writes=[b], out=t[:], in0=sc[0][:], scalar=1.0,
              in1=vec[gname][:], op0=ALU.add, op1=ALU.mult)
        return t, b
    a1, b_a1 = mk_a(fm0[1024], "n1g0")
    a2, b_a2 = mk_a(fm0[4096], "n2g0")
    a1n, b_a1n = mk_a(fm1[1024], "n1g1")
    akv, b_akv = mk_a(fmk[1024], "kvng")
    sh1, b_sh1 = fm0[0]
    sh2, b_sh2 = fm0[3072]
    sh1n, b_sh1n = fm1[0]
    shkv, b_shkv = fmk[0]
    g1bc, b_g1 = bc0[2048]
    g2bc, b_g2 = bc0[5120]

    if stop == 0:
        kb.finish()
        return nc
    ssq = kb.sb(es, [128, 32], F32, "ssq")
    b_ssq = Buf()
    xev = d["xe"].rearrange("(t p) n -> p t n", p=128)
    with ExitStack() as s12:
        hnT = kb.sb(s12, [128, 8, NALL * 128], BF16, "hnT")
        b_hnT = Buf()
        with ExitStack() as s1:
            xall = kb.sb(s1, [128, NALL, 1024], F32, "xall")
            b_x = [Buf() for _ in range(NALL)]
            junk = kb.sb(s1, [128, 1024], BF16, "junk")
            b_junk = Buf()
            xnb = [kb.sb(s1, [128, 1024], BF16, "xnb") for _ in range(4)]
            b_xnb = [Buf() for _ in range(4)]
            dsx = [DSem(kb, f"x{i}") for i in range(6)]
            kb.op(kb.dve, V.memset, writes=[b_ssq], ap=ssq[:], constant=0.0)
            for t in range(NALL):
                kb.load(kb.sp, dsx[t % 6], xall[:, t, :], xev[:, t, :], writes=[b_x[t]])
            for t in range(NALL):
                kb.op(kb.act, A.activation, reads=[b_x[t]], writes=[b_junk, b_ssq], out=junk[:], in_=xall[:, t, :],
                      func=AF.Square, accum_out=ssq[:, t:t + 1])
            emit_rstd(kb, es, ssq, b_ssq, NALL, 1.0 / D)
            tl = [(xall[:, t, :], b_x[t], t, t * 128) for t in range(NALL)]
            emit_norm_T(kb, c, tl, ssq, b_ssq, [(hnT, b_hnT, a1, b_a1, sh1, b_sh1)], pairs[0:2], xnb, b_xnb)
            kb.barrier()
        if stop == 1:
            kb.finish()
            return nc
        s_o = ExitStack()
        oT = kb.sb(s_o, [128, 8, NEXT * 128], BF16, "oT")
        if True:
            b_oT = Buf()
            with ExitStack() as s2:
                tb = kb.sb(s2, [128, 16, 640], BF16, "tb")
                b_tb = Buf()
                ds_tb = DSem(kb, "tb")
                tbv = d["tb"].rearrange("p (h n) -> p h n", h=16)
                for q4 in range(4):
                    kb.load(kb.pool, ds_tb, tb[:, q4 * 4:q4 * 4 + 4, :], tbv[:, q4 * 4:q4 * 4 + 4, :], writes=[b_tb])
                wsl = [kb.sb(s2, [128, 8, 384], BF16, "wqkv") for _ in range(2)]
                b_wsl = [Buf(), Buf()]
                ds_w = [DSem(kb, f"wqkv{i}") for i in range(2)]
                qT = [kb.sb(s2, [128, NEXT * 128], BF16, "qT") for _ in range(2)]
                kT = [kb.sb(s2, [128, NALL * 128], BF16, "kT") for _ in range(2)]
                va = [kb.sb(s2, [128, NALL, 2, 128], BF16, "va") for _ in range(2)]
                b_qT, b_kT, b_va = [Buf(), Buf()], [Buf(), Buf()], [Buf(), Buf()]
                PT = [kb.sb(s2, [128, 640], BF16, "PT") for _ in range(6)]
                b_PT = [Buf() for _ in range(6)]
                rs = [kb.sb(s2, [128, 256], F32, "rs") for _ in range(2)]
                b_rs = [Buf(), Buf()]
                for sl in range(2):
                    for t in range(NALL):
                        kb.op(kb.dve, V.tensor_scalar, reads=[bv, c.b_ones], writes=[b_va[sl]],
                              out=va[sl][:, t, 0, 64:128], in0=c.ones[:, 0:64], scalar1=vec["valid"][:, t:t + 1],
                              scalar2=None, op0=ALU.mult)
                        kb.op(kb.dve, V.tensor_scalar, reads=[bv, c.b_ones], writes=[b_va[sl]],
                              out=va[sl][:, t, 1, 0:64], in0=c.ones[:, 0:64], scalar1=vec["valid"][:, t:t + 1],
                              scalar2=None, op0=ALU.mult)

                def load_w(p):
                    kb.load(kb.pool, ds_w[p % 2], wsl[p % 2][:],
                            d["wqkv_t"][p].rearrange("p (k n) -> p k n", k=8), writes=[b_wsl[p % 2]])
                load_w(0)
                pcnt = 0
                acnt = 0
                for p in range(8):
                    sl = p % 2
                    if p + 1 < 8:
                        load_w(p + 1)
                    w = wsl[sl]
                    for (dst, b_dst, col0, tok0, ntok, scale) in ((qT[sl], b_qT[sl], 0, 512, NEXT * 128, 0.125),
                                                                  (kT[sl], b_kT[sl], 128, 0, NALL * 128, None)):
                        for e0 in range(0, ntok, 512):
                            n = min(512, ntok - e0)
                            pb, b_pb = banks[4 + pcnt % 2]
                            pcnt += 1

                            def emit(pb=pb, col0=col0, a0=tok0 + e0, n=n):
                                for k in range(8):
                                    ins = nc.tensor.matmul(pb[:, 0:n], lhsT=w[:, k, col0:col0 + 128],
                                                           rhs=hnT[:, k, a0:a0 + n], start=(k == 0), stop=(k == 7))
                                return ins
                            kb.mm(emit, reads=[b_wsl[sl], b_hnT], writes=[b_pb])
                            if scale is not None:
                                kb.op(kb.act, A.activation, reads=[b_pb], writes=[b_dst], out=dst[:, e0:e0 + n],
                                      in_=pb[:, 0:n], func=AF.Copy, scale=scale)
                            else:
                                kb.op(kb.dve, V.tensor_copy, reads=[b_pb], writes=[b_dst], out=dst[:, e0:e0 + n],
                                      in_=pb[:, 0:n])
                    for t0 in range(0, NALL, 4):
                        nt_ = min(4, NALL - t0)
                        pb, b_pb = banks[4 + pcnt % 2]
                        pcnt += 1

                        def emit(pb=pb, t0=t0, nt_=nt_):
                            for j in range(nt_):
                                for k in range(8):
                                    ins = nc.tensor.matmul(pb[:, j * 128:(j + 1) * 128],
                                                           lhsT=hnT[:, k, (t0 + j) * 128:(t0 + j + 1) * 128],
                                                           rhs=w[:, k, 256:384], start=(k == 0), stop=(k == 7))
                            return ins
                        kb.mm(emit, reads=[b_wsl[sl], b_hnT], writes=[b_pb])
                        for j in range(nt_):
                            kb.op(kb.dve, V.tensor_scalar, reads=[b_pb, bv], writes=[b_va[sl]],
                                  out=va[sl][:, t0 + j, 0, 0:64], in0=pb[:, j * 128:j * 128 + 64],
                                  scalar1=vec["valid"][:, t0 + j:t0 + j + 1], scalar2=None, op0=ALU.mult)
                            kb.op(kb.act, A.activation, reads=[b_pb, bv], writes=[b_va[sl]],
                                  out=va[sl][:, t0 + j, 1, 64:128], in_=pb[:, j * 128 + 64:j * 128 + 128],
                                  func=AF.Copy, scale=vec["valid"][:, t0 + j:t0 + j + 1])
                    def tail_fn(te, pts):
                        ob, b_ob = banks[6 + te % 2]
                        for hi in range(2):
                            pi = pts[hi]

                            def emit(hi=hi, pi=pi, te=te, ob=ob):
                                for m in range(5):
                                    ins = nc.tensor.matmul(ob[:, hi * 128:(hi + 1) * 128], lhsT=va[sl][:, te + m, hi, :],
                                                           rhs=PT[pi][:, m * 128:(m + 1) * 128], start=(m == 0),
                                                           stop=(m == 4), skip_group_check=True)
                                return ins
                            kb.mm(emit, reads=[b_va[sl], b_PT[pi]], writes=[b_ob])
                        rsb, b_rsb = rs[te % 2], b_rs[te % 2]
                        kb.op(kb.dve, V.tensor_scalar, reads=[b_ob], writes=[b_rsb], out=rsb[0:64, 0:128],
                              in0=ob[64:128, 0:128], scalar1=1e-30, scalar2=None, op0=ALU.max)
                        kb.op(kb.dve, V.tensor_scalar, reads=[b_ob], writes=[b_rsb], out=rsb[64:128, 128:256],
                              in0=ob[0:64, 128:256], scalar1=1e-30, scalar2=None, op0=ALU.max)
                        kb.op(kb.dve, V.reciprocal, reads=[b_rsb], writes=[b_rsb], out=rsb[0:64, 0:128],
                              in_=rsb[0:64, 0:128])
                        kb.op(kb.dve, V.reciprocal, reads=[b_rsb], writes=[b_rsb], out=rsb[64:128, 128:256],
                              in_=rsb[64:128, 128:256])
                        kb.op(kb.dve, V.tensor_tensor, reads=[b_ob, b_rsb], writes=[b_oT],
                              out=oT[0:64, p, te * 128:(te + 1) * 128], in0=ob[0:64, 0:128], in1=rsb[0:64, 0:128],
                              op=ALU.mult)
                        kb.op(kb.dve, V.tensor_tensor, reads=[b_ob, b_rsb], writes=[b_oT],
                              out=oT[64:128, p, te * 128:(te + 1) * 128], in0=ob[64:128, 128:256],
                              in1=rsb[64:128, 128:256], op=ALU.mult)

                    prev = None
                    for te in range(NEXT):
                        pts = []
                        for hi in range(2):
                            h16 = 2 * p + hi
                            sp_, sb_ = pairs[hi]
                            r0 = hi * 64

                            def emit(sp_=sp_, h16=h16, r0=r0, te=te):
                                nc.tensor.matmul(sp_[:, 0:512], lhsT=c.idb[:], rhs=tb[:, h16, 0:512], start=True,
                                                 stop=False, skip_group_check=True)
                                nc.tensor.matmul(sp_[:, 512:640], lhsT=c.idb[:], rhs=tb[:, h16, 512:640], start=True,
                                                 stop=False, skip_group_check=True)
                                for m in range(5):
                                    ins = nc.tensor.matmul(sp_[:, m * 128:(m + 1) * 128],
                                                           lhsT=kT[sl][r0:r0 + 64, (te + m) * 128:(te + m + 1) * 128],
                                                           rhs=qT[sl][r0:r0 + 64, te * 128:(te + 1) * 128],
                                                           start=False, stop=True, skip_group_check=True)
                                return ins
                            kb.mm(emit, reads=[c.b_idb, b_tb, b_kT[sl], b_qT[sl]], writes=sb_)
                            pi = acnt % 6
                            acnt += 1
                            kb.op(kb.act, A.activation, reads=sb_, writes=[b_PT[pi]], out=PT[pi][:], in_=sp_[:, 0:640],
                                  func=AF.Exp)
                            pts.append(pi)
                        if prev is not None:
                            tail_fn(*prev)
                        prev = (te, pts)
                    tail_fn(*prev)
                kb.barrier()
    if stop == 2:
        kb.finish()
        return nc
    h = kb.sb(es, [128, NEXT, 1024], F32, "h")
    b_h = [Buf() for _ in range(NEXT)]
    with ExitStack() as s3:
        wo = kb.sb(s3, [128, 8, 1024], BF16, "wo")
        b_wo = Buf()
        ds_wo = DSem(kb, "wo")
        kb.load(kb.pool, ds_wo, wo[:], d["wo"].rearrange("(c p) n -> p c n", p=128), writes=[b_wo])
        tmp = kb.sb(s3, [128, 1024], F32, "tmp3")
        b_tmp = Buf()
        dsh = [DSem(kb, f"h{i}") for i in range(4)]
        for te in range(NEXT):
            kb.load(kb.sp, dsh[te % 4], h[:, te, :], xev[:, te + 4, :], writes=[b_h[te]])
        for te in range(NEXT):
            pair, pb2 = pairs[te % 2]

            def emit(pair=pair, te=te):
                for hh in range(2):
                    for cc in range(8):
                        ins = nc.tensor.matmul(pair[:, hh * 512:(hh + 1) * 512],
                                               lhsT=oT[:, cc, te * 128:(te + 1) * 128],
                                               rhs=wo[:, cc, hh * 512:(hh + 1) * 512], start=(cc == 0),
                                               stop=(cc == 7))
                return ins
            kb.mm(emit, reads=[b_oT, b_wo], writes=pb2)
            kb.op(kb.dve, V.tensor_tensor, reads=pb2 + [b_g1], writes=[b_tmp], out=tmp[:], in0=pair[:],
                  in1=g1bc[:], op=ALU.mult)
            kb.op(kb.dve, V.tensor_tensor, reads=[b_tmp], writes=[b_h[te]], out=h[:, te, :], in0=h[:, te, :],
                  in1=tmp[:], op=ALU.add)
        kb.barrier()
    s_o.close()
    if dbg:
        for te in range(NEXT):
            kb.store(kb.sp, o_dbg[te * 128:(te + 1) * 128, :], h[:, te, :], reads=[b_h[te]])
    if stop == 3:
        kb.finish()
        return nc
    with ExitStack() as s4:
        junk = kb.sb(s4, [128, 1024], BF16, "junk2")
        b_junk = Buf()
        kb.op(kb.dve, V.memset, writes=[b_ssq], ap=ssq[:], constant=0.0)
        for te in range(NEXT):
            kb.op(kb.act, A.activation, reads=[b_h[te]], writes=[b_junk, b_ssq], out=junk[:], in_=h[:, te, :],
                  func=AF.Square, accum_out=ssq[:, te:te + 1])
        emit_rstd(kb, es, ssq, b_ssq, NEXT, 1.0 / D)
        kb.barrier()
    emit_ffn(kb, es, c, h, b_h, NEXT, ssq, b_ssq, a2, b_a2, sh2, b_sh2, g2bc, b_g2, d["win_t"], d["wout"],
             vec["cw"], bv, vec["cb"], bv, vec["hv"], bv, pairs, banks, 128, "A")
    if stop == 5:
        kb.finish()
        return nc
    for te in range(NEXT):
        kb.store(kb.sp, o_h1[te * 128:(te + 1) * 128, :], h[:, te, :], reads=[b_h[te]])
    if stop == 6:
        kb.finish()
        return nc
    emit_latents(kb, es, c, d, h, b_h, ssq, b_ssq, akv, b_akv, shkv, b_shkv, a1n, b_a1n, sh1n, b_sh1n, vec, bv,
                 pairs, banks, o_ckv, o_kr, o_cq, stop=stop)
    kb.finish()
    return nc


def toeplitz_bias(relb):
    k = np.arange(128)[:, None, None]
    m = np.arange(5)[None, :, None]
    q = np.arange(128)[None, None, :]
    idx = np.clip(q - k + 128 * (4 - m), -128, 128) + 128
    T = relb[:, idx]
    a, j = q // 64, k // 64
    masked = ((m == 0) & (j == 0) & (a == 1)) | ((m == 4) & (j == 1) & (a == 0))
    T = np.where(masked[None], np.float32(NEG), T).astype(np.float32)
    return np.ascontiguousarray(T.transpose(1, 0, 2, 3).reshape(128, 16 * 640))


def tile_win(win):
    w = np.asarray(win).reshape(8, 128, 2, NF, 128)
    return np.ascontiguousarray(w.transpose(3, 1, 0, 2, 4).reshape(NF, 128, 8 * 256))


def tile_wqkv(w):
    w = np.asarray(w).reshape(8, 128, 3, 8, 128)
    return np.ascontiguousarray(w.transpose(3, 1, 0, 2, 4).reshape(8, 128, 8 * 384))


def conv_fm(conv_w, conv_b):
    cw = np.ascontiguousarray(np.asarray(conv_w).T.reshape(2 * NF, 128, 3).transpose(1, 0, 2).reshape(128, 2 * NF * 3))
    return cw, fm(conv_b)


def rope_consts():
    invf = np.power(np.float32(10000.0), -np.arange(16, dtype=np.float32) * np.float32(2.0 / 32)).astype(np.float32)
    invf = np.concatenate([invf, invf]).reshape(32, 1)
    sgn = np.concatenate([-np.ones(16, np.float32), np.ones(16, np.float32)]).reshape(32, 1)
    return invf, sgn


def prep_A(inp):
    x, cc, pos = inp["x"], inp["c"], inp["positions"]
    invf, sgn = rope_consts()
    cw0, cb0 = conv_fm(inp["f_conv_w"][0], inp["f_conv_b"][0])
    wkr = np.asarray(inp["b_wkr"])
    shared = {
        "invf": invf, "sgn": sgn, "ident": np.eye(128, dtype=np.float32),
        "mw0": np.ascontiguousarray(inp["mod_w"][0]), "mb0": np.ascontiguousarray(inp["mod_b"][0][None]),
        "mw1": np.ascontiguousarray(inp["mod_w"][1][:, 0:2 * D]), "mb1": np.ascontiguousarray(inp["mod_b"][1][None, 0:2 * D]),
        "kvmw": np.ascontiguousarray(inp["kv_mod_w"]), "kvmb": np.ascontiguousarray(inp["kv_mod_b"][None]),
        "n1g0": fm(inp["norm1_g"][0]), "n2g0": fm(inp["norm2_g"][0]), "n1g1": fm(inp["norm1_g"][1]),
        "kvng": fm(inp["kv_norm_g"]),
        "wqkv_t": tile_wqkv(inp["a_wqkv"][0]), "wo": np.ascontiguousarray(inp["a_wo"][0]),
        "tb": toeplitz_bias(np.asarray(inp["a_rel_bias"][0])),
        "win_t": tile_win(inp["f_win"][0]), "cw": cw0, "cb": cb0, "wout": np.ascontiguousarray(inp["f_wout"][0]),
        "wdkv": np.ascontiguousarray(inp["b_wdkv"]), "latg": np.ascontiguousarray(inp["b_kv_lat_norm_g"][None]),
        "wkr2": np.ascontiguousarray(np.concatenate([wkr, wkr[:, 16:32], wkr[:, 0:16]], axis=1)),
        "wdq": np.ascontiguousarray(inp["b_wdq"][0]), "qg": np.ascontiguousarray(inp["b_q_norm_g"][0][None]),
    }
    maps = []
    for core in range(NCORE):
        b, j = core // 4, core % 4
        s0 = j * OWN
        xe = np.zeros((NALL * 128, D), np.float32)
        lo = s0 - 640
        src0 = max(lo, 0)
        xe[src0 - lo:] = x[b, src0:s0 + OWN]
        tok = lo + np.arange(NALL * 128)
        valid = np.ascontiguousarray((tok >= 0).astype(np.float32).reshape(NALL, 128).T)
        m = dict(shared)
        m.update({"xe": xe, "valid": valid, "posk": np.ascontiguousarray(pos[b, s0:s0 + OWN][None]).astype(np.int32),
                  "cvec": fm(cc[b]), "hv": np.full((128, 1), 1.0 if s0 > 0 else 0.0, np.float32)})
        maps.append(m)
    return maps


def mla_mask():
    k = np.arange(128)[:, None, None]
    jj = np.arange(4)[None, :, None]
    q = np.arange(512)[None, None, :]
    allowed = (2 * jj + k // 64) <= (q // 64)
    return np.ascontiguousarray(np.where(allowed, np.float32(0.0), np.float32(NEG)).astype(np.float32).reshape(128, 4 * 512))


def build_phaseB():
    nc = bass.Bass("TRN2", target_bir_lowering=False)
    d = {}
    for name, shape, dt in [
        ("cqT", [384, S], BF16), ("ckvT", [256, S], BF16), ("krT", [32, S], BF16), ("posq", [1, S], I32),
        ("invf", [32, 1], F32), ("sgn", [32, 1], F32), ("ident", [128, 128], F32),
        ("wq96", [384, 4 * 96], F32), ("wq96s", [384, 4 * 96], F32), ("wuk", [256, 256], F32), ("wuv", [256, 256], F32),
        ("mask", [128, 4 * 512], F32),
    ]:
        d[name] = _din(nc, name, shape, dt)
    o_oT = _dout(nc, "oT", [256, S], BF16)
    kb = KB(nc)
    es = kb.es
    V, A, G = nc.vector, nc.scalar, nc.gpsimd
    pairs, banks = psum_banks(kb, es)
    c = emit_consts(kb, es, d)
    invf128 = kb.sb(es, [128, 1], F32, "invf")
    sgn128 = kb.sb(es, [128, 1], F32, "sgn")
    dsv = DSem(kb, "bvec")
    bv = Buf()
    for g in range(4):
        kb.load(kb.sp, dsv, invf128[g * 32:(g + 1) * 32, :], d["invf"], writes=[bv], chain=False)
        kb.load(kb.sp, dsv, sgn128[g * 32:(g + 1) * 32, :], d["sgn"], writes=[bv], chain=False)
    bv.w = Tok(dsv.sem, dsv.cnt, dsv.name)
    SC = 96.0 ** -0.5
    NQB = S // 512
    cqs = [kb.sb(es, [128, 3, 512], BF16, "cqs") for _ in range(3)]
    b_cqs = [Buf() for _ in range(3)]
    ds_cq = [DSem(kb, f"cqs{i}") for i in range(3)]
    cq_d = d["cqT"].rearrange("(c p) n -> p c n", p=128)
    ckvT = kb.sb(es, [128, 2, S], BF16, "ckvT")
    KT = kb.sb(es, [96, S], BF16, "KT")
    QT = kb.sb(es, [96, S], BF16, "QT")
    b_ckv, b_KT, b_QT = Buf(), Buf(), Buf()
    dsl = [DSem(kb, f"bl{i}") for i in range(3)]
    kb.load(kb.sp, dsl[1], ckvT[:], d["ckvT"].rearrange("(c p) n -> p c n", p=128), writes=[b_ckv])
    kb.load(kb.sp, dsl[2], KT[64:96, :], d["krT"], writes=[b_KT])
    wq = kb.sb(es, [128, 3, 4 * 96], BF16, "wq96")
    wqs = kb.sb(es, [128, 3, 4 * 96], BF16, "wq96s")
    wuk = kb.sb(es, [128, 2, 256], BF16, "wuk")
    wuv = kb.sb(es, [128, 2, 256], BF16, "wuv")
    mask = kb.sb(es, [128, 4, 512], BF16, "mask")
    dsw = [DSem(kb, f"bw{i}") for i in range(5)]
    toks = [
        kb.load(kb.pool, dsw[0], wq[:], d["wq96"].rearrange("(c p) n -> p c n", p=128)),
        kb.load(kb.pool, dsw[1], wqs[:], d["wq96s"].rearrange("(c p) n -> p c n", p=128)),
        kb.load(kb.pool, dsw[2], wuk[:], d["wuk"].rearrange("(c p) n -> p c n", p=128)),
        kb.load(kb.pool, dsw[3], wuv[:], d["wuv"].rearrange("(c p) n -> p c n", p=128)),
        kb.load(kb.pool, dsw[4], mask[:], d["mask"].rearrange("p (j n) -> p j n", j=4)),
    ]
    b_wl = [Buf() for _ in toks]
    for bb, t in zip(b_wl, toks):
        bb.w = t
    cosb = kb.sb(es, [96, S], BF16, "cosb")[64:96]
    sinb = kb.sb(es, [96, S], BF16, "sinb")[64:96]
    b_tab = Buf()
    for part in range(4):
        tk = slice(part * 2048, (part + 1) * 2048)
        emit_rope_packed(kb, es, d["posq"][0:1, tk], invf128, sgn128, bv, cosb[:, tk], sinb[:, tk], b_tab, f"B{part}")
    va = kb.sb(es, [128, S // 128, 128], BF16, "va")
    b_va = Buf()
    kb.op(kb.dve, V.memset, writes=[b_va], ap=va[:, :, 64:128], constant=1.0)
    PT = [kb.sb(es, [128, 1024], BF16, "PT") for _ in range(3)]
    b_PT = [Buf() for _ in range(3)]
    rs = kb.sb(es, [64, 512], F32, "rs")
    b_rs = Buf()
    ost = [kb.sb(es, [64, 512], BF16, "ost") for _ in range(3)]
    b_ost = [Buf() for _ in range(3)]
    t1 = [kb.sb(es, [96, 512], F32, "t1")[64:96] for _ in range(2)]
    t2 = [kb.sb(es, [96, 512], F32, "t2")[64:96] for _ in range(2)]
    b_t1, b_t2 = [Buf(), Buf()], [Buf(), Buf()]
    pcnt = 0
    acnt = 0
    ocnt = 0

    def load_cq(i):
        blk = i % NQB
        kb.load(kb.sp, ds_cq[i % 3], cqs[i % 3][:], cq_d[:, :, blk * 512:(blk + 1) * 512], writes=[b_cqs[i % 3]])
    load_cq(0)
    load_cq(1)
    for hh in range(4):
        hc = slice(hh * 64, (hh + 1) * 64)
        h96 = slice(hh * 96, (hh + 1) * 96)
        for blk in range(NQB):
            ci = hh * NQB + blk
            if ci + 2 < 4 * NQB:
                load_cq(ci + 2)
            cqT, b_cq = cqs[ci % 3], b_cqs[ci % 3]
            tk = slice(blk * 512, (blk + 1) * 512)
            p1, b_p1 = banks[4 + pcnt % 4]
            pcnt += 1
            p2, b_p2 = banks[4 + pcnt % 4]
            pcnt += 1

            def emit(p1=p1, cqT=cqT):
                for kc in range(3):
                    ins = nc.tensor.matmul(p1[0:96, :], lhsT=wq[:, kc, h96], rhs=cqT[:, kc, :], start=(kc == 0),
                                           stop=(kc == 2))
                return ins
            kb.mm(emit, reads=[b_wl[0], b_cq], writes=[b_p1])

            def emit(p2=p2, cqT=cqT):
                for kc in range(3):
                    ins = nc.tensor.matmul(p2[0:96, :], lhsT=wqs[:, kc, h96], rhs=cqT[:, kc, :], start=(kc == 0),
                                           stop=(kc == 2))
                return ins
            kb.mm(emit, reads=[b_wl[1], b_cq], writes=[b_p2])
            kb.op(kb.act, A.activation, reads=[b_p1], writes=[b_QT], out=QT[0:64, tk], in_=p1[0:64, :], func=AF.Copy,
                  scale=SC)
            ti_ = blk % 2
            kb.op(kb.dve, V.tensor_tensor, reads=[b_p1, b_tab], writes=[b_t1[ti_]], out=t1[ti_], in0=p1[64:96, :],
                  in1=cosb[:, tk], op=ALU.mult)
            kb.op(kb.dve, V.scalar_tensor_tensor, reads=[b_p2, b_tab], writes=[b_t2[ti_]], out=t2[ti_],
                  in0=p2[64:96, :], scalar=SC, in1=sinb[:, tk], op0=ALU.mult, op1=ALU.mult)
            kb.op(kb.dve, V.scalar_tensor_tensor, reads=[b_t1[ti_], b_t2[ti_]], writes=[b_QT], out=QT[64:96, tk],
                  in0=t1[ti_], scalar=SC, in1=t2[ti_], op0=ALU.mult, op1=ALU.add)
            pb, b_pb = banks[4 + pcnt % 4]
            pcnt += 1

            def emit(pb=pb, tk=tk):
                for kc in range(2):
                    ins = nc.tensor.matmul(pb[0:64, :], lhsT=wuk[:, kc, hc], rhs=ckvT[:, kc, tk], start=(kc == 0),
                                           stop=(kc == 1))
                return ins
            kb.mm(emit, reads=[b_wl[2], b_ckv], writes=[b_pb])
            kb.op(kb.act, A.activation, reads=[b_pb], writes=[b_KT], out=KT[0:64, tk], in_=pb[0:64, :], func=AF.Copy)
        for t0 in range(0, S // 128, 8):
            pb, b_pb = banks[4 + pcnt % 4]
            pcnt += 1

            def emit(pb=pb, t0=t0):
                for j in range(8):
                    for kc in range(2):
                        ins = nc.tensor.matmul(pb[:, j * 64:(j + 1) * 64],
                                               lhsT=ckvT[:, kc, (t0 + j) * 128:(t0 + j + 1) * 128], rhs=wuv[:, kc, hc],
                                               start=(kc == 0), stop=(kc == 1))
                return ins
            kb.mm(emit, reads=[b_wl[3], b_ckv], writes=[b_pb])
            for j in range(8):
                if j % 2 == 0:
                    kb.op(kb.act, A.activation, reads=[b_pb], writes=[b_va], out=va[:, t0 + j, 0:64],
                          in_=pb[:, j * 64:(j + 1) * 64], func=AF.Copy)
                else:
                    kb.op(kb.dve, V.tensor_copy, reads=[b_pb], writes=[b_va], out=va[:, t0 + j, 0:64],
                          in_=pb[:, j * 64:(j + 1) * 64])
        for qb in range(NQB):
            qk = slice(qb * 512, (qb + 1) * 512)
            nkt = 4 * (qb + 1)
            ob, b_ob = banks[4 + ocnt % 2]
            ocnt += 1
            pend = []
            for kp in range(nkt // 2):
                sp_, sbufs = pairs[acnt % 2]
                pi = acnt % 3
                acnt += 1

                def emit(sp_=sp_, kp=kp):
                    for u in range(2):
                        kt = 2 * kp + u
                        ks = slice(kt * 128, (kt + 1) * 128)
                        o_ = sp_[:, u * 512:(u + 1) * 512]
                        diag = kt >= 4 * qb
                        ins = nc.tensor.matmul(o_, lhsT=KT[:, ks], rhs=QT[:, qk], start=True, stop=not diag,
                                               skip_group_check=True)
                        if diag:
                            ins = nc.tensor.matmul(o_, lhsT=c.idb[:], rhs=mask[:, kt - 4 * qb, :], start=False,
                                                   stop=True, skip_group_check=True)
                    return ins
                kb.mm(emit, reads=[b_KT, b_QT, c.b_idb, b_wl[4]], writes=sbufs)
                kb.op(kb.act, A.activation, reads=sbufs, writes=[b_PT[pi]], out=PT[pi][:], in_=sp_, func=AF.Exp)

                def emit2(pi=pi, kp=kp):
                    for u in range(2):
                        kt = 2 * kp + u
                        ins = nc.tensor.matmul(ob, lhsT=va[:, kt, :], rhs=PT[pi][:, u * 512:(u + 1) * 512],
                                               start=(kt == 0), stop=(kt == nkt - 1), skip_group_check=True)
                    return ins
                pend.append((emit2, pi))
                if len(pend) > 1:
                    e2, p2_ = pend.pop(0)
                    kb.mm(e2, reads=[b_va, b_PT[p2_]], writes=[b_ob])
            while pend:
                e2, p2_ = pend.pop(0)
                kb.mm(e2, reads=[b_va, b_PT[p2_]], writes=[b_ob])
            oi = ocnt % 3
            kb.op(kb.dve, V.tensor_scalar, reads=[b_ob], writes=[b_rs], out=rs[:], in0=ob[64:128, :], scalar1=1e-30,
                  scalar2=None, op0=ALU.max)
            kb.op(kb.dve, V.reciprocal, reads=[b_rs], writes=[b_rs], out=rs[:], in_=rs[:])
            kb.op(kb.dve, V.tensor_tensor, reads=[b_ob, b_rs], writes=[b_ost[oi]], out=ost[oi][:], in0=ob[0:64, :],
                  in1=rs[:], op=ALU.mult)
            kb.store(kb.sp, o_oT[hh * 64:(hh + 1) * 64, qk], ost[oi][:], reads=[b_ost[oi]])
    kb.finish()
    return nc


def build_phaseC():
    nc = bass.Bass("TRN2", target_bir_lowering=False)
    d = {}
    for name, shape, dt in [
        ("h1e", [NEXT * 128, D], F32), ("oTe", [D, NEXT * 128], BF16), ("cvec", [128, 8], F32), ("hv", [128, 1], F32),
        ("ident", [128, 128], F32), ("mw1", [D, 6 * D], F32), ("mb1", [1, 6 * D], F32), ("n2g1", [128, 8], F32),
        ("wo1", [D, D], F32), ("win_t", [NF, 128, 8 * 256], F32), ("cw", [128, 2 * NF * 3], F32),
        ("cb", [128, 2 * NF], F32), ("wout", [FF, D], F32), ("fg", [1, D], F32),
    ]:
        d[name] = _din(nc, name, shape, dt)
    o_out = _dout(nc, "out", [OWN, D], F32)
    kb = KB(nc)
    es = kb.es
    V, A, G = nc.vector, nc.scalar, nc.gpsimd
    pairs, banks = psum_banks(kb, es)
    c = emit_consts(kb, es, d)
    vec = load_vecs(kb, es, c, {
        "hv": (d["hv"], [128, 1], F32), "n2g1": (d["n2g1"], [128, 8], F32),
        "cw": (d["cw"].rearrange("p (f t) -> p f t", t=3), [128, 2 * NF, 3], F32), "cb": (d["cb"], [128, 2 * NF], F32),
    })
    bv = c.b_vec
    fm1, bc1 = emit_mod(kb, es, c, [banks[0], banks[1]], d["cvec"], d["mw1"], d["mb1"], 6 * D,
                        want_fm=[3072, 4096], want_bc=[2048, 5120], tag="c")
    a2 = kb.sb(es, [128, 8], F32, "a2")
    b_a2 = Buf()
    kb.op(kb.dve, V.scalar_tensor_tensor, reads=[fm1[4096][1], bv], writes=[b_a2], out=a2[:], in0=fm1[4096][0][:],
          scalar=1.0, in1=vec["n2g1"][:], op0=ALU.add, op1=ALU.mult)
    sh2, b_sh2 = fm1[3072]
    g1bc, b_g1 = bc1[2048]
    g2bc, b_g2 = bc1[5120]
    ssq = kb.sb(es, [128, 32], F32, "ssq")
    b_ssq = Buf()
    h = kb.sb(es, [128, NEXT, 1024], F32, "h")
    b_h = [Buf() for _ in range(NEXT)]
    hv_d = d["h1e"].rearrange("(t p) n -> p t n", p=128)
    with ExitStack() as s3:
        oT = kb.sb(s3, [128, 8, NEXT * 128], BF16, "oT")
        b_oT = Buf()
        ds_o = DSem(kb, "oT")
        kb.load(kb.sp, ds_o, oT[:], d["oTe"].rearrange("(c p) n -> p c n", p=128), writes=[b_oT])
        wo = kb.sb(s3, [128, 8, 1024], BF16, "wo")
        b_wo = Buf()
        ds_wo = DSem(kb, "wo")
        kb.load(kb.pool, ds_wo, wo[:], d["wo1"].rearrange("(c p) n -> p c n", p=128), writes=[b_wo])
        tmp = kb.sb(s3, [128, 1024], F32, "tmp3")
        b_tmp = Buf()
        dsh = [DSem(kb, f"h{i}") for i in range(4)]
        for te in range(NEXT):
            kb.load(kb.sp, dsh[te % 4], h[:, te, :], hv_d[:, te, :], writes=[b_h[te]])
        for te in range(NEXT):
            pair, pb2 = pairs[te % 2]

            def emit(pair=pair, te=te):
                for hh in range(2):
                    for cc in range(8):
                        ins = nc.tensor.matmul(pair[:, hh * 512:(hh + 1) * 512], lhsT=oT[:, cc, te * 128:(te + 1) * 128],
                                               rhs=wo[:, cc, hh * 512:(hh + 1) * 512], start=(cc == 0), stop=(cc == 7))
                return ins
            kb.mm(emit, reads=[b_oT, b_wo], writes=pb2)
            kb.op(kb.dve, V.tensor_tensor, reads=pb2 + [b_g1], writes=[b_tmp], out=tmp[:], in0=pair[:], in1=g1bc[:],
                  op=ALU.mult)
            kb.op(kb.dve, V.tensor_tensor, reads=[b_tmp], writes=[b_h[te]], out=h[:, te, :], in0=h[:, te, :],
                  in1=tmp[:], op=ALU.add)
        kb.barrier()

    def stats(lo):
        with ExitStack() as s4:
            junk = kb.sb(s4, [128, 1024], BF16, "junk2")
            b_junk = Buf()
            kb.op(kb.dve, V.memset, writes=[b_ssq], ap=ssq[:], constant=0.0)
            for te in range(lo, NEXT):
                kb.op(kb.act, A.activation, reads=[b_h[te]], writes=[b_junk, b_ssq], out=junk[:], in_=h[:, te, :],
                      func=AF.Square, accum_out=ssq[:, te:te + 1])
            emit_rstd(kb, es, ssq, b_ssq, NEXT, 1.0 / D)
            kb.barrier()
    stats(0)
    emit_ffn(kb, es, c, h, b_h, NEXT, ssq, b_ssq, a2, b_a2, sh2, b_sh2, g2bc, b_g2, d["win_t"], d["wout"],
             vec["cw"], bv, vec["cb"], bv, vec["hv"], bv, pairs, banks, 128, "C")
    stats(1)
    with ExitStack() as s5:
        ot = [kb.sb(s5, [128, 1024], F32, "ot") for _ in range(3)]
        b_ot = [Buf() for _ in range(3)]
        fg = kb.sb(s5, [128, D], F32, "fg")
        b_fg = Buf()
        kb.load(kb.sp, DSem(kb, "fg"), fg[:], d["fg"].partition_broadcast(128), writes=[b_fg])
        for te in range(1, NEXT):
            i = te % 3
            kb.op(kb.dve, V.scalar_tensor_tensor, reads=[b_h[te], b_ssq, b_fg], writes=[b_ot[i]], out=ot[i][:],
                  in0=h[:, te, :], scalar=ssq[:, te:te + 1], in1=fg[:], op0=ALU.mult, op1=ALU.mult)
            kb.store(kb.sp, o_out[(te - 1) * 128:te * 128, :], ot[i][:], reads=[b_ot[i]])
        kb.barrier()
    kb.finish()
    return nc


_CACHE = {}


def _get(name, fn):
    if name not in _CACHE:
        _CACHE[name] = fn()
    return _CACHE[name]


def kernel(**inp):
    inp = {k: np.asarray(v) for k, v in inp.items()}
    ident = np.eye(128, dtype=np.float32)
    invf, sgn = rope_consts()
    cores = list(range(NCORE))
    ncA = _get("A", build_phaseA)
    resA = run_bass_kernel_spmd(ncA, prep_A(inp), core_ids=cores).results
    ncB = _get("B", build_phaseB)
    wqr = np.asarray(inp["b_wqr"][0]).reshape(384, 16, 32)
    wuq_ = np.asarray(inp["b_wuq"][0]).reshape(384, 16, 64)
    wq96 = np.concatenate([wuq_, wqr], axis=2)
    wq96s = np.concatenate([np.zeros_like(wuq_), wqr[:, :, 16:32], wqr[:, :, 0:16]], axis=2)
    mask = mla_mask()
    mapsB = []
    for core in cores:
        b, j = core // 4, core % 4
        g = [resA[b * 4 + q] for q in range(4)]
        hs = slice(j * 256, (j + 1) * 256)
        mapsB.append({
            "cqT": np.ascontiguousarray(np.concatenate([np.asarray(r["cqT"]) for r in g], axis=1)),
            "ckvT": np.ascontiguousarray(np.concatenate([np.asarray(r["ckvT"]) for r in g], axis=1)),
            "krT": np.ascontiguousarray(np.concatenate([np.asarray(r["krT"]) for r in g], axis=1)),
            "posq": np.ascontiguousarray(inp["positions"][b][None]).astype(np.int32),
            "invf": invf, "sgn": sgn, "ident": ident,
            "wq96": np.ascontiguousarray(wq96[:, 4 * j:4 * j + 4, :].reshape(384, 4 * 96)),
            "wq96s": np.ascontiguousarray(wq96s[:, 4 * j:4 * j + 4, :].reshape(384, 4 * 96)),
            "wuk": np.ascontiguousarray(inp["b_wuk"][:, hs]), "wuv": np.ascontiguousarray(inp["b_wuv"][:, hs]),
            "mask": mask,
        })
    resB = run_bass_kernel_spmd(ncB, mapsB, core_ids=cores).results
    ncC = _get("C", build_phaseC)
    cw1, cb1 = conv_fm(inp["f_conv_w"][1], inp["f_conv_b"][1])
    sharedC = {
        "ident": ident, "mw1": np.ascontiguousarray(inp["mod_w"][1]), "mb1": np.ascontiguousarray(inp["mod_b"][1][None]),
        "n2g1": fm(inp["norm2_g"][1]), "wo1": np.ascontiguousarray(inp["b_wo"][0]),
        "win_t": tile_win(inp["f_win"][1]), "cw": cw1, "cb": cb1, "wout": np.ascontiguousarray(inp["f_wout"][1]),
        "fg": np.ascontiguousarray(inp["final_g"][None]),
    }
    mapsC = []
    for core in cores:
        b, j = core // 4, core % 4
        s0 = j * OWN
        oT_b = np.concatenate([np.asarray(resB[b * 4 + q]["oT"]) for q in range(4)], axis=0)
        oTe = np.zeros((D, NEXT * 128), oT_b.dtype)
        lo = s0 - 128
        src0 = max(lo, 0)
        oTe[:, src0 - lo:] = oT_b[:, src0:s0 + OWN]
        m = dict(sharedC)
        m.update({"h1e": np.ascontiguousarray(np.asarray(resA[core]["h1"])), "oTe": oTe, "cvec": fm(inp["c"][b]),
                  "hv": np.full((128, 1), 1.0 if s0 > 0 else 0.0, np.float32)})
        mapsC.append(m)
    resC = run_bass_kernel_spmd(ncC, mapsC, core_ids=cores).results
    out = np.zeros((2, S, D), np.float32)
    for core in cores:
        b, j = core // 4, core % 4
        out[b, j * OWN:(j + 1) * OWN] = np.asarray(resC[core]["out"])
    return out
```
